# Optimizing a Trainium2 kernel written in Bass

```python
import math
import jax, jax.numpy as jnp
from jax import lax
import numpy as np

D_MODEL = 2048
BATCH = 4
SEQ = 2048
DEPTH = 4

CHUNK = 64
Q_BLOCK = 128
MIX_W = D_MODEL // 2
N_BRANCH = 3
MLA_HEADS = 8
MLA_NOPE = 128
MLA_ROPE = 64
MLA_QK = MLA_NOPE + MLA_ROPE
MLA_V = MIX_W // MLA_HEADS
Q_LORA = D_MODEL // 4
KV_LORA = D_MODEL // 8
ROPE_THETA = 10000.0
FOX_DH = 128
FOX_HEADS = MIX_W // FOX_DH
CH_DH = 128
CH_HEADS = MIX_W // CH_DH
LEFT_CHUNKS = 8
BAND = (LEFT_CHUNKS + 1) * CHUNK
REL_CLIP = 128
N_REL = 2 * REL_CLIP + 1
MEM_LEN = 256
X_HEADS = 4
X_DH = 128
D_FF = 4 * D_MODEL
EPS = 1e-6
NEG = -1e30

SPLIT_SIZES = (Q_LORA, KV_LORA, MLA_ROPE, 3 * FOX_HEADS * FOX_DH, FOX_HEADS,
               3 * CH_HEADS * CH_DH, N_BRANCH * D_MODEL)
D_IN = sum(SPLIT_SIZES)
SPLIT_CUTS = tuple(int(c) for c in np.cumsum(SPLIT_SIZES)[:-1])

kernel_name = 'hybrid_mla_fox_chunkrel_gated_encoder'


def rms_norm(x, g):
    xf = x.astype(jnp.float32)
    y = xf * lax.rsqrt(jnp.mean(xf * xf, axis=-1, keepdims=True) + EPS)
    return (y * g.astype(jnp.float32)).astype(x.dtype)


def rope_tables(seq):
    pos = jnp.arange(seq, dtype=jnp.float32)
    inv = ROPE_THETA ** (-jnp.arange(0, MLA_ROPE, 2, dtype=jnp.float32) / MLA_ROPE)
    ang = pos[:, None] * inv[None, :]
    return jnp.cos(ang), jnp.sin(ang)


def apply_rope(x, cos, sin):
    x1, x2 = jnp.split(x, 2, axis=-1)
    c = cos.astype(x.dtype)[None, :, None, :]
    s = sin.astype(x.dtype)[None, :, None, :]
    return jnp.concatenate([x1 * c - x2 * s, x1 * s + x2 * c], axis=-1)


def block_sweep_attention(q, k, v, cum=None):
    B, S, H, Dk = q.shape
    scale = Dk ** -0.5
    nq = S // Q_BLOCK
    qb = jnp.moveaxis(q.reshape(B, nq, Q_BLOCK, H, Dk), 1, 0)
    k_pos = jnp.arange(S)
    if cum is None:
        xs = (jnp.arange(nq), qb)
    else:
        cb = jnp.moveaxis(cum.reshape(B, nq, Q_BLOCK, H), 1, 0)
        cum_k = jnp.transpose(cum, (0, 2, 1))[:, :, None, :]
        xs = (jnp.arange(nq), qb, cb)

    def body(xs_):
        i, q_i = xs_[0], xs_[1]
        q_pos = i * Q_BLOCK + jnp.arange(Q_BLOCK)
        s = jnp.einsum('bqhd,bkhd->bhqk', q_i, k).astype(jnp.float32) * scale
        if cum is None:
            allowed = (k_pos // CHUNK)[None, :] <= (q_pos // CHUNK)[:, None]
        else:
            allowed = k_pos[None, :] <= q_pos[:, None]
            s = s + jnp.transpose(xs_[2], (0, 2, 1))[..., None] - cum_k
        s = jnp.where(allowed[None, None], s, NEG)
        p = jax.nn.softmax(s, axis=-1).astype(v.dtype)
        return jnp.einsum('bhqk,bkhd->bqhd', p, v)

    out = lax.map(body, xs)
    return jnp.moveaxis(out, 0, 1).reshape(B, S, H, v.shape[-1])


def mla_branch(c_q_raw, c_kv_raw, k_r_raw, g_cq, w_uq, g_ckv, w_ukv, g_qn, g_kn, cos, sin):
    B, S, _ = c_q_raw.shape
    q = (rms_norm(c_q_raw, g_cq) @ w_uq).reshape(B, S, MLA_HEADS, MLA_QK)
    kv = (rms_norm(c_kv_raw, g_ckv) @ w_ukv).reshape(B, S, MLA_HEADS, MLA_NOPE + MLA_V)
    k_nope, v = kv[..., :MLA_NOPE], kv[..., MLA_NOPE:]
    k_rope = jnp.broadcast_to(k_r_raw[:, :, None, :], (B, S, MLA_HEADS, MLA_ROPE))
    k = jnp.concatenate([k_nope, k_rope], axis=-1)
    q = rms_norm(q, g_qn)
    k = rms_norm(k, g_kn)
    q = jnp.concatenate([q[..., :MLA_NOPE], apply_rope(q[..., MLA_NOPE:], cos, sin)], axis=-1)
    k = jnp.concatenate([k[..., :MLA_NOPE], apply_rope(k[..., MLA_NOPE:], cos, sin)], axis=-1)
    o = block_sweep_attention(q, k, v)
    return o.reshape(B, S, MIX_W)


def fox_branch(qkv, f_logit, b_f, g_qn, g_kn):
    B, S, _ = qkv.shape
    qkv = qkv.reshape(B, S, 3, FOX_HEADS, FOX_DH)
    q = rms_norm(qkv[:, :, 0], g_qn)
    k = rms_norm(qkv[:, :, 1], g_kn)
    v = qkv[:, :, 2]
    log_f = jax.nn.log_sigmoid(f_logit.astype(jnp.float32) + b_f.astype(jnp.float32))
    cum = jnp.cumsum(log_f, axis=1)
    o = block_sweep_attention(q, k, v, cum)
    return o.reshape(B, S, MIX_W)


def chunk_band_branch(qkv, rel_bias, g_qn, g_kn):
    B, S, _ = qkv.shape
    n_chunks = S // CHUNK
    pad = LEFT_CHUNKS * CHUNK
    qkv = qkv.reshape(B, S, 3, CH_HEADS, CH_DH)
    q = rms_norm(qkv[:, :, 0], g_qn)
    k = rms_norm(qkv[:, :, 1], g_kn)
    v = qkv[:, :, 2]
    band_idx = (jnp.arange(n_chunks) * CHUNK)[:, None] + jnp.arange(BAND)[None, :]
    kp = jnp.pad(k, ((0, 0), (pad, 0), (0, 0), (0, 0)))
    vp = jnp.pad(v, ((0, 0), (pad, 0), (0, 0), (0, 0)))
    kb = kp[:, band_idx]
    vb = vp[:, band_idx]
    qc = q.reshape(B, n_chunks, CHUNK, CH_HEADS, CH_DH)
    s = jnp.einsum('bcqhd,bckhd->bchqk', qc, kb).astype(jnp.float32) * (CH_DH ** -0.5)
    rel = (jnp.arange(CHUNK)[:, None] + pad) - jnp.arange(BAND)[None, :]
    bias = rel_bias.astype(jnp.float32)[:, jnp.clip(rel, -REL_CLIP, REL_CLIP) + REL_CLIP]
    valid = band_idx >= pad
    s = jnp.where(valid[None, :, None, None, :], s + bias[None, None], NEG)
    p = jax.nn.softmax(s, axis=-1).astype(v.dtype)
    o = jnp.einsum('bchqk,bckhd->bcqhd', p, vb)
    return o.reshape(B, S, MIX_W)


def memory_cross_attention(h, mem_n, w_xq, w_xkv, g_qn, g_kn, w_xo):
    B, S, _ = h.shape
    M = mem_n.shape[1]
    q = rms_norm((h @ w_xq).reshape(B, S, X_HEADS, X_DH), g_qn)
    kv = (mem_n @ w_xkv).reshape(B, M, 2, X_HEADS, X_DH)
    k = rms_norm(kv[:, :, 0], g_kn)
    v = kv[:, :, 1]
    s = jnp.einsum('bshd,bmhd->bhsm', q, k).astype(jnp.float32) * (X_DH ** -0.5)
    p = jax.nn.softmax(s, axis=-1).astype(v.dtype)
    o = jnp.einsum('bhsm,bmhd->bshd', p, v).reshape(B, S, X_HEADS * X_DH)
    return o @ w_xo


def setup_inputs(seed: int = 0) -> dict:
    key = jax.random.key(seed)
    ks = jax.random.split(key, 32)
    f32 = jnp.float32

    def nrm(k, shape, scale):
        return jax.random.normal(k, shape, f32) * scale

    def gain(k, shape):
        return 1.0 + 0.02 * jax.random.normal(k, shape, f32)

    L = DEPTH
    return {
        'x': nrm(ks[0], (BATCH, SEQ, D_MODEL), 1.0),
        'mem': nrm(ks[1], (BATCH, MEM_LEN, D_MODEL), 1.0),
        'g_mix': gain(ks[2], (L, D_MODEL)),
        'w_in': nrm(ks[3], (L, D_MODEL, D_IN), D_MODEL ** -0.5),
        'g_cq': gain(ks[4], (L, Q_LORA)),
        'w_uq': nrm(ks[5], (L, Q_LORA, MLA_HEADS * MLA_QK), Q_LORA ** -0.5),
        'g_ckv': gain(ks[6], (L, KV_LORA)),
        'w_ukv': nrm(ks[7], (L, KV_LORA, MLA_HEADS * (MLA_NOPE + MLA_V)), KV_LORA ** -0.5),
        'g_mla_q': gain(ks[8], (L, MLA_QK)),
        'g_mla_k': gain(ks[9], (L, MLA_QK)),
        'b_f': 3.0 + 0.1 * jax.random.normal(ks[10], (L, FOX_HEADS), f32),
        'g_fox_q': gain(ks[11], (L, FOX_DH)),
        'g_fox_k': gain(ks[12], (L, FOX_DH)),
        'rel_bias': nrm(ks[13], (L, CH_HEADS, N_REL), 0.5),
        'g_ch_q': gain(ks[14], (L, CH_DH)),
        'g_ch_k': gain(ks[15], (L, CH_DH)),
        'w_br': nrm(ks[16], (L, N_BRANCH, MIX_W, D_MODEL), MIX_W ** -0.5),
        'w_out': nrm(ks[17], (L, D_MODEL, D_MODEL), D_MODEL ** -0.5),
        'g_cross': gain(ks[18], (L, D_MODEL)),
        'g_mem': gain(ks[19], (L, D_MODEL)),
        'w_xq': nrm(ks[20], (L, D_MODEL, X_HEADS * X_DH), D_MODEL ** -0.5),
        'w_xkv': nrm(ks[21], (L, D_MODEL, 2 * X_HEADS * X_DH), D_MODEL ** -0.5),
        'g_x_q': gain(ks[22], (L, X_DH)),
        'g_x_k': gain(ks[23], (L, X_DH)),
        'w_xo': nrm(ks[24], (L, X_HEADS * X_DH, D_MODEL), (X_HEADS * X_DH) ** -0.5),
        'g_mlp': gain(ks[25], (L, D_MODEL)),
        'w_1': nrm(ks[26], (L, D_MODEL, D_FF), D_MODEL ** -0.5),
        'w_2': nrm(ks[27], (L, D_FF, D_MODEL), D_FF ** -0.5),
    }


def reference(x, mem, g_mix, w_in, g_cq, w_uq, g_ckv, w_ukv, g_mla_q, g_mla_k, b_f,
              g_fox_q, g_fox_k, rel_bias, g_ch_q, g_ch_k, w_br, w_out, g_cross, g_mem,
              w_xq, w_xkv, g_x_q, g_x_k, w_xo, g_mlp, w_1, w_2):
    B, S, _ = x.shape
    cos, sin = rope_tables(S)
    for l in range(DEPTH):
        h = rms_norm(x, g_mix[l])
        z = h @ w_in[l]
        c_q, c_kv, k_r, fox_qkv, fox_f, ch_qkv, gate_logits = jnp.split(z, SPLIT_CUTS, axis=-1)
        y_a = mla_branch(c_q, c_kv, k_r, g_cq[l], w_uq[l], g_ckv[l], w_ukv[l],
                         g_mla_q[l], g_mla_k[l], cos, sin)
        y_b = fox_branch(fox_qkv, fox_f, b_f[l], g_fox_q[l], g_fox_k[l])
        y_c = chunk_band_branch(ch_qkv, rel_bias[l], g_ch_q[l], g_ch_k[l])
        ys = jnp.stack([y_a, y_b, y_c], axis=2)
        proj = jnp.einsum('bsnc,ncd->bsnd', ys, w_br[l])
        gates = jax.nn.sigmoid(gate_logits.astype(jnp.float32)).astype(x.dtype)
        gates = gates.reshape(B, S, N_BRANCH, D_MODEL)
        merged = jnp.einsum('bsnd,bsnd->bsd', gates, proj)
        x = x + merged @ w_out[l]
        x = x + memory_cross_attention(rms_norm(x, g_cross[l]), rms_norm(mem, g_mem[l]),
                                       w_xq[l], w_xkv[l], g_x_q[l], g_x_k[l], w_xo[l])
        hm = rms_norm(x, g_mlp[l])
        x = x + jnp.square(jax.nn.relu(hm @ w_1[l])) @ w_2[l]
    return x
```

```python
import numpy as np
import ml_dtypes
import concourse.bass as bass
import concourse.mybir as mybir
from concourse.bass_utils import run_bass_kernel_spmd
from contextlib import ExitStack
from bisect import bisect_left

F32 = mybir.dt.float32
BF16 = mybir.dt.bfloat16
AF = mybir.ActivationFunctionType
ALU = mybir.AluOpType

D = 2048
S = 2048
T = 1024
DEPTH = 4
D_IN = 13128
EPS = 1e-6
C_FOX = 832
C_FOXF = 3904
C_CH = 3912
C_GATE = 6984
NCORES = 4
G_MIX, G_CROSS, G_MLP, G_MEM, G_CQ, G_CKV = 0, 16, 32, 48, 64, 68
G_MQN, G_MQR, G_MQS, G_MKN, G_MKR, G_MKS = 70, 71, 72, 73, 74, 75
G_FQ, G_FK, G_CQ2, G_CK2, G_XQ, G_XK = 76, 77, 78, 79, 80, 81
G_BF = 82
NG = 146
EW = 1408


class Res:
    __slots__ = ("w", "wd", "re", "rd")

    def __init__(self):
        self.w = None
        self.wd = []
        self.re = {}
        self.rd = []


class EngQ:
    def __init__(self, name, eng, sem):
        self.name, self.eng, self.sem = name, eng, sem
        self.n = 0
        self.count = 0
        self.inc_idx = []
        self.inc_cnt = []
        self.last_handle = None
        self.last_idx = 0
        self.waited = {}

    def ensure_inc(self, idx):
        pos = bisect_left(self.inc_idx, idx)
        if pos < len(self.inc_idx):
            return self.inc_cnt[pos]
        assert self.last_idx >= idx
        self.count += 1
        self.last_handle.then_inc(self.sem, 1)
        self.inc_idx.append(self.last_idx)
        self.inc_cnt.append(self.count)
        return self.count


class FW:
    NDS = 24

    def __init__(self, nc, es):
        self.nc = nc
        self.dry = False
        self.engs = {}
        for name, eng in [("pe", nc.tensor), ("act", nc.scalar), ("dve", nc.vector),
                          ("pool", nc.gpsimd), ("sp", nc.sync)]:
            self.engs[name] = EngQ(name, eng, es.enter_context(nc.semaphore("s_" + name)))
        self.dsem = [es.enter_context(nc.semaphore("s_d%d" % i)) for i in range(self.NDS)]
        self.dcnt = [0] * self.NDS
        self.di = 0
        self.out_tokens = []

    def _wait(self, E, tok):
        if tok[0] == "e":
            Sq = self.engs[tok[1]]
            if Sq is E and E.name == "pe":
                return
            cnt = Sq.ensure_inc(tok[2])
            if E.waited.get(Sq.name, 0) >= cnt:
                return
            E.eng.wait_ge(Sq.sem, cnt)
            E.waited[Sq.name] = cnt
        else:
            key = ("d", tok[1])
            if E.waited.get(key, 0) >= tok[2]:
                return
            E.eng.wait_ge(self.dsem[tok[1]], tok[2])
            E.waited[key] = tok[2]

    def _deps(self, E, r, w, dma_write=False):
        deps = []
        for x in r:
            if x.w is not None:
                deps.append(x.w)
            deps.extend(x.wd)
        for x in w:
            if x.w is not None:
                deps.append(x.w)
            if not dma_write:
                deps.extend(x.wd)
            deps.extend(x.re.values())
            deps.extend(x.rd)
        for t in deps:
            self._wait(E, t)

    def _mark(self, tok, r, w):
        for x in r:
            if tok[0] == "e":
                x.re[tok[1]] = tok
            else:
                x.rd.append(tok)
        for x in w:
            if tok[0] == "e":
                x.w = tok
                x.wd = []
            else:
                x.w = None
                x.wd.append(tok)
                if len(x.wd) > 64:
                    x.wd = x.wd[-64:]
            x.re = {}
            x.rd = []

    def op(self, ename, fn, r=(), w=()):
        if self.dry:
            return
        E = self.engs[ename]
        self._deps(E, r, w)
        h = fn(E.eng)
        E.n += 1
        E.last_handle = h
        E.last_idx = E.n
        self._mark(("e", ename, E.n), r, w)

    def dma(self, qname, out, in_, r=(), w=(), is_out=False):
        if self.dry:
            return
        E = self.engs[qname]
        self._deps(E, r, w, dma_write=True)
        i = self.di % self.NDS
        self.di += 1
        if self.dcnt[i] > 0:
            self._wait(E, ("d", i, self.dcnt[i]))
        self.dcnt[i] += 16
        E.eng.dma_start(out=out, in_=in_).then_inc(self.dsem[i], 16)
        tok = ("d", i, self.dcnt[i])
        self._mark(tok, r, w)
        if is_out:
            self.out_tokens.append(tok)

    def finish(self):
        E = self.engs["sp"]
        for tok in self.out_tokens:
            self._wait(E, tok)


def build_program(nl, first, last, dbg=False):
    nc = bass.Bass("TRN2", target_bir_lowering=False)

    def din(name, shape, dt=F32):
        return nc.dram_tensor(name, shape, dt, kind="ExternalInput").ap()

    def dscr(name, shape, dt):
        return nc.dram_tensor(name, shape, dt, kind="Internal").ap()

    xT_in = din("xT", [D, S])
    memT = din("memT", [D, 256])
    w_in = din("w_in", [nl, D, D_IN])
    w_krx = din("w_krx", [nl, D, 128])
    w_uqx = din("w_uqx", [nl, 512, 2048])
    w_ukv = din("w_ukv", [nl, 256, 2048])
    w_br = din("w_br", [nl, 3, 1024, D])
    w_out = din("w_out", [nl, D, D])
    w_xq = din("w_xq", [nl, D, 512])
    w_xkv = din("w_xkv", [nl, D, 1024])
    w_xo = din("w_xo", [nl, 512, D])
    w_1 = din("w_1", [nl, D, 8192])
    w_2 = din("w_2", [nl, 8192, D])
    Gd = din("G", [nl, 128, NG])
    Tzd = din("Tz", [nl, 8, 128, EW])
    ctab_d = din("ctab", [64, S])
    stab_d = din("stab", [64, S])
    fmask_d = din("fmask", [128, 2048], BF16)
    mmask_d = din("mmask", [128, 2048], BF16)
    band_d = din("band", [128, EW], BF16)
    tri_d = din("tri", [128, 128])
    oT = nc.dram_tensor("oT", [D, S], F32, kind="ExternalOutput").ap()

    xs = dscr("xs", [D, S], F32)
    qTn = {b: dscr("qTn_" + b, [8, 128, T], BF16) for b in ("m", "f", "c")}
    qTr = dscr("qTr_m", [8, 64, T], BF16)
    kTn = {b: dscr("kTn_" + b, [8, 128, S], BF16) for b in ("m", "f", "c")}
    kTr = dscr("kTr_m", [8, 64, S], BF16)
    vv = {b: dscr("v_" + b, [S, 1024], BF16) for b in ("m", "f", "c")}
    gat = dscr("gat", [48, 128, T], BF16)
    rem0 = dscr("rem0", [128, 64], F32)

    es = ExitStack()
    with es:
        fw = FW(nc, es)

        def sb(name, shape, dt):
            return es.enter_context(nc.sbuf_tensor("sb_" + name, shape, dt))

        x = sb("x", [128, 16, T], F32)
        h = sb("h", [128, 16, T], BF16)
        big = sb("big", [128, 16, T], BF16)
        NSLOT = 3
        wr = [sb("wr%d" % i, [128, 4096], BF16) for i in range(NSLOT)]
        qn_t = sb("qn_t", [128, T], BF16)
        qr_t = sb("qr_t", [64, T], BF16)
        kn_t = sb("kn_t", [128, S], BF16)
        kr_t = sb("kr_t", [64, S], BF16)
        v_t = sb("v_t", [128, 16, 128], BF16)
        sq_t = [sb("sq%d" % i, [128, 512], BF16) for i in range(3)]
        rs_t = [sb("rs%d" % i, [128, 512], F32) for i in range(2)]
        tmp_t = [sb("tmp%d" % i, [128, 512], F32) for i in range(2)]
        st_t = [sb("st%d" % i, [128, 512], BF16) for i in range(4)]
        gt_t = [sb("gt%d" % i, [128, 512], BF16) for i in range(2)]
        p_t = [sb("p%d" % i, [128, 512], BF16) for i in range(3)]
        mask_t = sb("mask", [128, 2048], BF16)
        E_t = sb("E", [128, EW], BF16)
        band_t = sb("band", [128, EW], BF16)
        kbase = sb("kbase", [64, T], BF16)
        sqkr = sb("sqkr", [64, T], BF16)
        G = sb("G", [128, NG], F32)
        ones_bf = sb("ones_bf", [128, 128], BF16)
        ones_f = sb("ones_f", [128, 128], F32)
        tri = sb("tri", [128, 128], F32)
        ff = sb("ff", [128, 64], F32)
        prefx = sb("prefx", [128, 72], F32)
        Dtok = sb("Dtok", [128, 64], F32)
        Rsb = sb("Rsb", [128, 72], F32)
        rem_t = sb("rem_t", [128, 64], F32)
        bias_own = sb("bias_own", [128, 2, 8, 8], F32)
        bias_oth = sb("bias_oth", [128, 2, 8, 8], F32)
        ps = [es.enter_context(nc.psum_tensor("pp%d" % i, [128, 512], F32)) for i in range(8)]

        RR = {}

        def R(*key):
            r = RR.get(key)
            if r is None:
                r = RR[key] = Res()
            return r

        def fence(new_keys, old_prefixes):
            toks = []
            for k, r in RR.items():
                if k[0] in old_prefixes:
                    if r.w is not None:
                        toks.append(r.w)
                    toks.extend(r.wd)
                    toks.extend(r.re.values())
                    toks.extend(r.rd)
            for k in new_keys:
                r = R(*k)
                for t in toks:
                    if t[0] == "e":
                        old = r.re.get(t[1])
                        if old is None or old[2] < t[2]:
                            r.re[t[1]] = t
                    else:
                        r.rd.append(t)

        rot = {}

        def mark(name):
            if not fw.dry:
                PHASES.append((name, fw.engs["pe"].n))

        def nxt(name, n):
            i = rot.get(name, 0)
            rot[name] = i + 1
            return i % n

        plan = []
        wstate = {"i": 0, "issued": 0}
        wsrc = {"w_in": w_in, "w_krx": w_krx, "w_uqx": w_uqx, "w_ukv": w_ukv, "w_out": w_out,
                "w_xq": w_xq, "w_xkv": w_xkv, "w_xo": w_xo, "w_1": w_1, "w_2": w_2}

        def issue_slab(j):
            key = plan[j]
            name, li, k0, nk, c0, ncol = key
            if name.startswith("w_br"):
                src = w_br[li, int(name[4])]
            else:
                src = wsrc[name][li]
            slot = wr[j % NSLOT]
            fw.dma("pool", slot[:, 0:nk * ncol].rearrange("p (k c) -> p k c", k=nk),
                   src[k0 * 128:(k0 + nk) * 128, c0:c0 + ncol].rearrange("(k p) c -> p k c", p=128),
                   w=[R("wr", j % NSLOT)])

        def wget(name, li, k0, nk, c0, ncol):
            key = (name, li, k0, nk, c0, ncol)
            i = wstate["i"]
            wstate["i"] = i + 1
            if fw.dry:
                plan.append(key)
            else:
                assert plan[i] == key, (plan[i], key)
                while wstate["issued"] < min(len(plan), i + NSLOT):
                    issue_slab(wstate["issued"])
                    wstate["issued"] += 1
            slot = wr[i % NSLOT]
            return slot[:, 0:nk * ncol].rearrange("p (k c) -> p k c", k=nk), R("wr", i % NSLOT)

        def mm(out, lhsT, rhs, start, stop, r, w):
            fw.op("pe", lambda e: e.matmul(out, lhsT, rhs, start=start, stop=stop), r=r, w=w)

        def rstd_from(bank, npart, ncol, scale, rdeps):
            ri = nxt("rs", 2)
            rs = rs_t[ri]
            fw.op("act", lambda e: e.activation(out=rs[0:npart, 0:ncol], in_=ps[bank][0:npart, 0:ncol],
                                                func=AF.Sqrt, bias=EPS, scale=scale),
                  r=[R("ps", bank)], w=[R("rs", ri)])
            fw.op("dve", lambda e: e.reciprocal(out=rs[0:npart, 0:ncol], in_=rs[0:npart, 0:ncol]),
                  r=[R("rs", ri)], w=[R("rs", ri)])
            return ri

        def rmsnorm_fm(src, src_res, dst, dst_res, nch, ncols, gcol, ntile):
            for tt in range(ntile):
                t0, t1 = tt * 512, min(ncols, tt * 512 + 512)
                n = t1 - t0
                bank = 6 + nxt("aux", 2)
                for c in range(nch):
                    si = nxt("sq", 3)
                    fw.op("act", lambda e, c=c, si=si: e.activation(out=sq_t[si][:, 0:n], in_=src(c, t0, t1),
                                                                  func=AF.Square),
                          r=[src_res(c, tt)], w=[R("sq", si)])
                    mm(ps[bank][:, 0:n], ones_bf[:, :], sq_t[si][:, 0:n], c == 0, c == nch - 1,
                       r=[R("sq", si), R("ones")], w=[R("ps", bank)])
                ri = rstd_from(bank, 128, n, 1.0 / (nch * 128), None)
                for c in range(nch):
                    fw.op("dve", lambda e, c=c: e.scalar_tensor_tensor(
                        out=dst(c, t0, t1), in0=src(c, t0, t1), scalar=G[:, gcol + c:gcol + c + 1],
                        in1=rs_t[ri][:, 0:n], op0=ALU.mult, op1=ALU.mult),
                        r=[src_res(c, tt), R("rs", ri), R("G")], w=[dst_res(c, tt)])

        def head_norm(bank, n, gcol, dst_ap, dst_res, extra=None, div=128.0, sb_dst=None, sb_res=None):
            si = nxt("sq", 3)
            fw.op("act", lambda e: e.activation(out=sq_t[si][:, 0:n], in_=ps[bank][:, 0:n], func=AF.Square),
                  r=[R("ps", bank)], w=[R("sq", si)])
            ab = 6 + nxt("aux", 2)
            mm(ps[ab][:, 0:n], ones_bf[:, :], sq_t[si][:, 0:n], True, extra is None,
               r=[R("sq", si), R("ones")], w=[R("ps", ab)])
            if extra is not None:
                mm(ps[ab][:, 0:n], ones_bf[0:64, :], extra[0], False, True,
                   r=[extra[1], R("ones")], w=[R("ps", ab)])
            ri = rstd_from(ab, 128, n, 1.0 / div, None)
            if sb_dst is not None:
                fw.op("dve", lambda e: e.scalar_tensor_tensor(
                    out=sb_dst, in0=ps[bank][:, 0:n], scalar=G[:, gcol:gcol + 1], in1=rs_t[ri][:, 0:n],
                    op0=ALU.mult, op1=ALU.mult), r=[R("ps", bank), R("rs", ri), R("G")], w=[sb_res])
            else:
                sti = nxt("st", 4)
                fw.op("dve", lambda e: e.scalar_tensor_tensor(
                    out=st_t[sti][:, 0:n], in0=ps[bank][:, 0:n], scalar=G[:, gcol:gcol + 1],
                    in1=rs_t[ri][:, 0:n], op0=ALU.mult, op1=ALU.mult),
                    r=[R("ps", bank), R("rs", ri), R("G")], w=[R("st", sti)])
                fw.dma("sp", dst_ap, st_t[sti][:, 0:n], r=[R("st", sti)], w=[dst_res])
            return ri

        def evac_store(bank, npart, n, dst_ap, dst_res, func=AF.Copy):
            sti = nxt("st", 4)
            fw.op("act", lambda e: e.activation(out=st_t[sti][0:npart, 0:n], in_=ps[bank][0:npart, 0:n], func=func),
                  r=[R("ps", bank)], w=[R("st", sti)])
            fw.dma("sp", dst_ap, st_t[sti][0:npart, 0:n], r=[R("st", sti)], w=[dst_res])

        def proj_fm(slab, slab_res, m0, msz, nk, rhs, rhs_res, tt, bank, ncol=512):
            for kc in range(nk):
                mm(ps[bank][0:msz, 0:ncol], slab[:, kc, m0:m0 + msz], rhs(kc), kc == 0, kc == nk - 1,
                   r=[slab_res, rhs_res(kc, tt)], w=[R("ps", bank)])

        hres = lambda c, tt: R("h", c, tt)
        xres = lambda c, tt: R("x", c, tt)

        def emit_consts():
            fw.op("dve", lambda e: e.memset(ones_bf[:, :], 1.0), w=[R("ones")])
            fw.op("dve", lambda e: e.memset(ones_f[:, :], 1.0), w=[R("onesf")])
            fw.dma("sp", tri[:, :], tri_d, w=[R("tri")])
            fw.dma("sp", band_t[:, :], band_d, w=[R("band")])

        def attention(br, hf, li):
            scale = (192.0 if br == "m" else 128.0) ** -0.5
            if br == "c":
                kb_lo = max(0, hf * 8 - 4)
            else:
                kb_lo = 0
            kb_hi = hf * 8 + 8
            nkb = kb_hi - kb_lo
            if br in ("m", "f"):
                fw.dma("sp", mask_t[:, :], mmask_d if br == "m" else fmask_d, w=[R("mask")])
            for hd in range(8):
                fw.dma("sp", qn_t[:, :], qTn[br][hd], r=[R("qTn", br, hd)], w=[R("qn")])
                fw.dma("sp", kn_t[:, 0:nkb * 128], kTn[br][hd][:, kb_lo * 128:kb_hi * 128],
                       r=[R("kTn", br, hd)], w=[R("kn")])
                if br == "m":
                    fw.dma("sp", qr_t[:, :], qTr[hd], r=[R("qTr", hd)], w=[R("qr")])
                    fw.dma("sp", kr_t[:, 0:nkb * 128], kTr[hd][:, kb_lo * 128:kb_hi * 128],
                           r=[R("kTr", hd)], w=[R("kr")])
                fw.dma("sp", v_t[:, 0:nkb, :],
                       vv[br][kb_lo * 128:kb_hi * 128, hd * 128:(hd + 1) * 128].rearrange("(b p) d -> p b d", p=128),
                       r=[R("v", br)], w=[R("vt")])
                if br == "c":
                    for q0 in range(0, EW, 512):
                        q1 = min(EW, q0 + 512)
                        ti = nxt("tmp", 2)
                        fw.dma("sp", tmp_t[ti][:, 0:q1 - q0], Tzd[li, hd][:, q0:q1], w=[R("tmp", ti)])
                        fw.op("act", lambda e, ti=ti, q0=q0, q1=q1: e.activation(
                            out=E_t[:, q0:q1], in_=tmp_t[ti][:, 0:q1 - q0], func=AF.Exp),
                            r=[R("tmp", ti)], w=[R("E")])
                    fw.op("dve", lambda e: e.tensor_tensor(out=E_t[:, :], in0=E_t[:, :], in1=band_t[:, :],
                                                           op=ALU.mult), r=[R("E"), R("band")], w=[R("E")])
                seq = []
                for j in range(2):
                    qs = hf * 8 + j * 4
                    if br == "c":
                        blocks = list(range(max(0, qs - 4), qs + 4))
                    else:
                        blocks = list(range(0, qs + 4))
                    for bi, kb in enumerate(blocks):
                        seq.append((j, bi, kb, len(blocks), qs))
                sbank_of = {}
                olb = {}

                def emit_qk(t):
                    j, bi, kb, nb, qs = seq[t]
                    kl = kb - kb_lo
                    sbk = nxt("sbank", 4)
                    sbank_of[t] = sbk
                    ksl = slice(kl * 128, kl * 128 + 128)
                    qsl = slice(j * 512, j * 512 + 512)
                    mm(ps[sbk][:, :], kn_t[:, ksl], qn_t[:, qsl], True, br != "m",
                       r=[R("kn"), R("qn")], w=[R("ps", sbk)])
                    if br == "m":
                        mm(ps[sbk][:, :], kr_t[:, ksl], qr_t[:, qsl], False, True,
                           r=[R("kr"), R("qr")], w=[R("ps", sbk)])

                def emit_exp(t):
                    j, bi, kb, nb, qs = seq[t]
                    sbk = sbank_of[t]
                    pi = nxt("p", 3)
                    if br == "f":
                        if kb >= hf * 8:
                            bias_ap = bias_own[:, j, kb - hf * 8, hd:hd + 1]
                            bres = R("bias_own")
                        else:
                            bias_ap = bias_oth[:, j, kb, hd:hd + 1]
                            bres = R("bias_oth")
                        fw.op("act", lambda e: e.activation(
                            out=p_t[pi][:, :], in_=ps[sbk][:, :], func=AF.Exp, bias=bias_ap, scale=scale),
                            r=[R("ps", sbk), bres], w=[R("p", pi)])
                    else:
                        fw.op("act", lambda e: e.activation(
                            out=p_t[pi][:, :], in_=ps[sbk][:, :], func=AF.Exp, scale=scale),
                            r=[R("ps", sbk)], w=[R("p", pi)])
                    if br == "c":
                        i_off = kb - (qs - 4)
                        e0 = 896 - 128 * i_off
                        fw.op("dve", lambda e: e.tensor_tensor(
                            out=p_t[pi][:, :], in0=p_t[pi][:, :], in1=E_t[:, e0:e0 + 512], op=ALU.mult),
                            r=[R("p", pi), R("E")], w=[R("p", pi)])
                    elif kb >= qs:
                        m0 = (kb - qs) * 512
                        fw.op("dve", lambda e: e.tensor_tensor(
                            out=p_t[pi][:, :], in0=p_t[pi][:, :], in1=mask_t[:, m0:m0 + 512], op=ALU.mult),
                            r=[R("p", pi), R("mask")], w=[R("p", pi)])
                    return pi

                def emit_pv(t, pi):
                    j, bi, kb, nb, qs = seq[t]
                    kl = kb - kb_lo
                    if bi == 0:
                        oi = nxt("ol", 2)
                        olb[j] = (4 + 2 * oi, 5 + 2 * oi)
                    ob, lb = olb[j]
                    first_b, last_b = bi == 0, bi == nb - 1
                    mm(ps[ob][:, :], v_t[:, kl, :], p_t[pi][:, :], first_b, last_b,
                       r=[R("vt"), R("p", pi)], w=[R("ps", ob)])
                    mm(ps[lb][:, :], ones_bf[:, :], p_t[pi][:, :], first_b, last_b,
                       r=[R("ones"), R("p", pi)], w=[R("ps", lb)])
                    if last_b:
                        qsl = slice(j * 512, j * 512 + 512)
                        ri = nxt("rs", 2)
                        fw.op("dve", lambda e: e.reciprocal(out=rs_t[ri][:, :], in_=ps[lb][:, :]),
                              r=[R("ps", lb)], w=[R("rs", ri)])
                        fw.op("dve", lambda e: e.tensor_tensor(
                            out=big[:, hd, qsl], in0=ps[ob][:, :], in1=rs_t[ri][:, :], op=ALU.mult),
                            r=[R("ps", ob), R("rs", ri)], w=[R("big", hd, j)])

                LA = 2
                for t in range(min(LA, len(seq))):
                    emit_qk(t)
                for t in range(len(seq)):
                    pi = emit_exp(t)
                    if t + LA < len(seq):
                        emit_qk(t + LA)
                    emit_pv(t, pi)

        def merge(n, li):
            for sl in range(4):
                slab, sres = wget("w_br%d" % n, li, 0, 8, sl * 512, 512)
                for m in range(4):
                    c = sl * 4 + m
                    for tt in range(2):
                        tsl = slice(tt * 512, tt * 512 + 512)
                        bank = nxt("main", 4)
                        proj_fm(slab, sres, m * 128, 128, 8, lambda kc: big[:, kc, tsl],
                                lambda kc, tt_: R("big", kc, tt_), tt, bank)
                        gi = nxt("gt", 2)
                        fw.dma("sp", gt_t[gi][:, :], gat[n * 16 + c][:, tsl], r=[R("gat", n * 16 + c, tt)],
                               w=[R("gt", gi)])
                        if n == 0:
                            fw.op("dve", lambda e, bank=bank, gi=gi, c=c, tsl=tsl: e.tensor_tensor(
                                out=h[:, c, tsl], in0=ps[bank][:, :], in1=gt_t[gi][:, :], op=ALU.mult),
                                r=[R("ps", bank), R("gt", gi)], w=[R("h", c, tt)])
                        else:
                            ti = nxt("tmp", 2)
                            fw.op("dve", lambda e, bank=bank, gi=gi, ti=ti: e.tensor_tensor(
                                out=tmp_t[ti][:, :], in0=ps[bank][:, :], in1=gt_t[gi][:, :], op=ALU.mult),
                                r=[R("ps", bank), R("gt", gi)], w=[R("tmp", ti)])
                            fw.op("dve", lambda e, ti=ti, c=c, tsl=tsl: e.tensor_tensor(
                                out=h[:, c, tsl], in0=h[:, c, tsl], in1=tmp_t[ti][:, :], op=ALU.add),
                                r=[R("tmp", ti), R("h", c, tt)], w=[R("h", c, tt)])

        def add_to_x(bank, c, tt):
            tsl = slice(tt * 512, tt * 512 + 512)
            fw.op("dve", lambda e: e.tensor_tensor(out=x[:, c, tsl], in0=x[:, c, tsl], in1=ps[bank][:, :],
                                                   op=ALU.add), r=[R("ps", bank), R("x", c, tt)], w=[R("x", c, tt)])

        def emit_pass(li, hf, src_ap, dst_ap, is_final):
            tok0 = hf * T
            mark("norm")
            fw.dma("sp", x[:, :, :], src_ap[:, tok0:tok0 + T].rearrange("(c p) t -> p c t", p=128),
                   r=[R("xs", hf)], w=[R("x", c, tt) for c in range(16) for tt in range(2)])
            fw.dma("sp", G[:, :], Gd[li], w=[R("G")])
            rmsnorm_fm(lambda c, a, b: x[:, c, a:b], xres, lambda c, a, b: h[:, c, a:b], hres, 16, T, G_MIX, 2)

            hr = lambda tt: (lambda kc: h[:, kc, tt * 512:tt * 512 + 512])
            mark("z_mla")
            for sl in range(3):
                slab, sres = wget("w_in", li, 0, 16, sl * 256, 256)
                for m in range(2):
                    zc = sl * 2 + m
                    for tt in range(2):
                        bank = nxt("main", 6)
                        proj_fm(slab, sres, m * 128, 128, 16, hr(tt), hres, tt, bank)
                        fw.op("act", lambda e, bank=bank, zc=zc, tt=tt: e.activation(
                            out=big[:, 8 + zc, tt * 512:tt * 512 + 512], in_=ps[bank][:, :], func=AF.Copy),
                            r=[R("ps", bank)], w=[R("big", 8 + zc, tt)])
            slab, sres = wget("w_krx", li, 0, 16, 0, 128)
            for which in range(2):
                for tt in range(2):
                    bank = nxt("main", 6)
                    proj_fm(slab, sres, which * 64, 64, 16, hr(tt), hres, tt, bank)
                    fw.op("act", lambda e, bank=bank, which=which, tt=tt: e.activation(
                        out=big[0:64, 14 + which, tt * 512:tt * 512 + 512], in_=ps[bank][0:64, :], func=AF.Copy),
                        r=[R("ps", bank)], w=[R("big", 14 + which, tt)])
            mark("z_qkv")
            for br, cbase, gq, gk in (("f", C_FOX, G_FQ, G_FK), ("c", C_CH, G_CQ2, G_CK2)):
                for sl in range(8):
                    slab, sres = wget("w_in", li, 0, 16, cbase + sl * 256, 256)
                    for m in range(2):
                        hidx = sl * 2 + m
                        for tt in range(2):
                            bank = nxt("main", 6)
                            proj_fm(slab, sres, m * 128, 128, 16, hr(tt), hres, tt, bank)
                            if hidx < 8:
                                head_norm(bank, 512, gq, qTn[br][hidx][:, tt * 512:tt * 512 + 512],
                                          R("qTn", br, hidx))
                            else:
                                head_norm(bank, 512, gk,
                                          kTn[br][hidx - 8][:, tok0 + tt * 512:tok0 + tt * 512 + 512],
                                          R("kTn", br, hidx - 8))
                for sl in range(4):
                    slab, sres = wget("w_in", li, 0, 16, cbase + 2048 + sl * 256, 256)
                    for tb in range(8):
                        bank = nxt("main", 6)
                        for kc in range(16):
                            mm(ps[bank][:, 0:256], h[:, kc, tb * 128:tb * 128 + 128], slab[:, kc, :], kc == 0, kc == 15,
                               r=[sres, R("h", kc, tb // 4)], w=[R("ps", bank)])
                        evac_store(bank, 128, 256, vv[br][tok0 + tb * 128:tok0 + tb * 128 + 128, sl * 256:sl * 256 + 256],
                                   R("v", br))
                if br == "f":
                    slab, sres = wget("w_in", li, 0, 16, C_FOXF, 8)
                    fbank = 6 + nxt("aux", 2)
                    for tb in range(8):
                        for kc in range(16):
                            mm(ps[fbank][:, tb * 8:tb * 8 + 8], h[:, kc, tb * 128:tb * 128 + 128], slab[:, kc, :],
                               kc == 0, kc == 15, r=[sres, R("h", kc, tb // 4)], w=[R("ps", fbank)])
                    fw.op("dve", lambda e: e.tensor_tensor(out=ff[:, :], in0=ps[fbank][:, 0:64], in1=G[:, G_BF:G_BF + 64],
                                                           op=ALU.add), r=[R("ps", fbank), R("G")], w=[R("ff")])
                    fw.op("act", lambda e: e.activation(out=ff[:, :], in_=ff[:, :], func=AF.Exp, scale=-1.0),
                          r=[R("ff")], w=[R("ff")])
                    fw.op("act", lambda e: e.activation(out=ff[:, :], in_=ff[:, :], func=AF.Ln, bias=1.0),
                          r=[R("ff")], w=[R("ff")])
                    fw.op("dve", lambda e: e.memset(prefx[:, 0:8], 0.0), w=[R("prefx")])
                    for b in range(1, 9):
                        fw.op("dve", lambda e, b=b: e.tensor_tensor(
                            out=prefx[:, b * 8:b * 8 + 8], in0=prefx[:, b * 8 - 8:b * 8], in1=ff[:, b * 8 - 8:b * 8],
                            op=ALU.add), r=[R("prefx"), R("ff")], w=[R("prefx")])
                    cb = 6 + nxt("aux", 2)
                    mm(ps[cb][:, 0:64], tri[:, :], ff[:, :], True, True, r=[R("tri"), R("ff")], w=[R("ps", cb)])
                    mm(ps[cb][:, 64:136], ones_f[:, :], prefx[:, :], True, True, r=[R("onesf"), R("prefx")],
                       w=[R("ps", cb)])
                    fw.op("act", lambda e: e.activation(out=Rsb[:, :], in_=ps[cb][:, 64:136], func=AF.Copy),
                          r=[R("ps", cb)], w=[R("Rsb")])
                    fw.op("dve", lambda e: e.tensor_tensor(out=Dtok[:, :], in0=ps[cb][:, 0:64], in1=Rsb[:, 0:64],
                                                           op=ALU.add), r=[R("ps", cb), R("Rsb")], w=[R("Dtok")])
                    for j in range(2):
                        rc = (4 * j + 2) * 8
                        for kb in range(8):
                            fw.op("dve", lambda e, j=j, kb=kb, rc=rc: e.tensor_tensor(
                                out=bias_own[:, j, kb, :], in0=Dtok[:, kb * 8:kb * 8 + 8], in1=Rsb[:, rc:rc + 8],
                                op=ALU.subtract), r=[R("Dtok"), R("Rsb")], w=[R("bias_own")])
                    if hf == 0:
                        for kb in range(8):
                            fw.op("dve", lambda e, kb=kb: e.tensor_tensor(
                                out=rem_t[:, kb * 8:kb * 8 + 8], in0=Rsb[:, 64:72], in1=Dtok[:, kb * 8:kb * 8 + 8],
                                op=ALU.subtract), r=[R("Dtok"), R("Rsb")], w=[R("rem_t")])
                        fw.dma("sp", rem0, rem_t[:, :], r=[R("rem_t")], w=[R("rem0")])
                    else:
                        fw.dma("sp", rem_t[:, :], rem0, r=[R("rem0")], w=[R("rem_t")])
                        for j in range(2):
                            rc = (4 * j + 2) * 8
                            for kb in range(8):
                                fw.op("dve", lambda e, j=j, kb=kb, rc=rc: e.scalar_tensor_tensor(
                                    out=bias_oth[:, j, kb, :], in0=rem_t[:, kb * 8:kb * 8 + 8], scalar=-1.0,
                                    in1=Rsb[:, rc:rc + 8], op0=ALU.mult, op1=ALU.subtract),
                                    r=[R("rem_t"), R("Rsb")], w=[R("bias_oth")])
            mark("z_gates")
            for sl in range(24):
                slab, sres = wget("w_in", li, 0, 16, C_GATE + sl * 256, 256)
                for m in range(2):
                    gc = sl * 2 + m
                    for tt in range(2):
                        bank = nxt("main", 6)
                        proj_fm(slab, sres, m * 128, 128, 16, hr(tt), hres, tt, bank)
                        evac_store(bank, 128, 512, gat[gc][:, tt * 512:tt * 512 + 512], R("gat", gc, tt),
                                   func=AF.Sigmoid)

            mark("mla_prep")
            cqs = lambda c, a, b: big[:, 8 + c, a:b]
            cqr = lambda c, tt: R("big", 8 + c, tt)
            rmsnorm_fm(cqs, cqr, cqs, cqr, 4, T, G_CQ, 2)
            cks = lambda c, a, b: big[:, 12 + c, a:b]
            ckr = lambda c, tt: R("big", 12 + c, tt)
            rmsnorm_fm(cks, ckr, cks, ckr, 2, T, G_CKV, 2)
            for tt in range(2):
                tsl = slice(tt * 512, tt * 512 + 512)
                gsl = slice(tok0 + tt * 512, tok0 + tt * 512 + 512)
                t1, t2 = nxt("tmp", 2), nxt("tmp", 2)
                fw.dma("sp", tmp_t[t1][0:64, :], ctab_d[:, gsl], w=[R("tmp", t1)])
                fw.dma("sp", tmp_t[t2][0:64, :], stab_d[:, gsl], w=[R("tmp", t2)])
                fw.op("dve", lambda e, t1=t1, tsl=tsl: e.scalar_tensor_tensor(
                    out=tmp_t[t1][0:64, :], in0=big[0:64, 14, tsl], scalar=G[0:64, G_MKR:G_MKR + 1],
                    in1=tmp_t[t1][0:64, :], op0=ALU.mult, op1=ALU.mult),
                    r=[R("big", 14, tt), R("G"), R("tmp", t1)], w=[R("tmp", t1)])
                fw.op("dve", lambda e, t2=t2, tsl=tsl: e.scalar_tensor_tensor(
                    out=tmp_t[t2][0:64, :], in0=big[0:64, 15, tsl], scalar=G[0:64, G_MKS:G_MKS + 1],
                    in1=tmp_t[t2][0:64, :], op0=ALU.mult, op1=ALU.mult),
                    r=[R("big", 15, tt), R("G"), R("tmp", t2)], w=[R("tmp", t2)])
                fw.op("dve", lambda e, t1=t1, t2=t2, tsl=tsl: e.tensor_tensor(
                    out=kbase[:, tsl], in0=tmp_t[t1][0:64, :], in1=tmp_t[t2][0:64, :], op=ALU.add),
                    r=[R("tmp", t1), R("tmp", t2)], w=[R("kbase", tt)])
                fw.op("act", lambda e, tsl=tsl: e.activation(out=sqkr[:, tsl], in_=big[0:64, 14, tsl], func=AF.Square),
                      r=[R("big", 14, tt)], w=[R("sqkr", tt)])
            for sl in range(2):
                slab, sres = wget("w_uqx", li, 0, 4, sl * 1024, 1024)
                for hh in range(4):
                    hd = sl * 4 + hh
                    for tt in range(2):
                        tsl = slice(tt * 512, tt * 512 + 512)
                        gsl = slice(tok0 + tt * 512, tok0 + tt * 512 + 512)
                        rhs = lambda kc: big[:, 8 + kc, tsl]
                        bA, b1, b2 = nxt("main", 6), nxt("main", 6), nxt("main", 6)
                        proj_fm(slab, sres, hh * 256, 128, 4, rhs, cqr, tt, bA)
                        proj_fm(slab, sres, hh * 256 + 128, 64, 4, rhs, cqr, tt, b1)
                        proj_fm(slab, sres, hh * 256 + 192, 64, 4, rhs, cqr, tt, b2)
                        si = nxt("sq", 3)
                        fw.op("act", lambda e, si=si, b1=b1: e.activation(out=sq_t[si][0:64, :], in_=ps[b1][0:64, :],
                                                                        func=AF.Square),
                              r=[R("ps", b1)], w=[R("sq", si)])
                        ri = head_norm(bA, 512, G_MQN, qTn["m"][hd][:, tsl], R("qTn", "m", hd),
                                       extra=(sq_t[si][0:64, :], R("sq", si)), div=192.0)
                        t1, t2 = nxt("tmp", 2), nxt("tmp", 2)
                        fw.dma("sp", tmp_t[t1][0:64, :], ctab_d[:, gsl], w=[R("tmp", t1)])
                        fw.dma("sp", tmp_t[t2][0:64, :], stab_d[:, gsl], w=[R("tmp", t2)])
                        fw.op("dve", lambda e, t1=t1: e.tensor_tensor(
                            out=tmp_t[t1][0:64, :], in0=tmp_t[t1][0:64, :], in1=rs_t[ri][0:64, :], op=ALU.mult),
                            r=[R("tmp", t1), R("rs", ri)], w=[R("tmp", t1)])
                        fw.op("dve", lambda e, t2=t2: e.tensor_tensor(
                            out=tmp_t[t2][0:64, :], in0=tmp_t[t2][0:64, :], in1=rs_t[ri][0:64, :], op=ALU.mult),
                            r=[R("tmp", t2), R("rs", ri)], w=[R("tmp", t2)])
                        fw.op("dve", lambda e, t1=t1, b1=b1: e.scalar_tensor_tensor(
                            out=tmp_t[t1][0:64, :], in0=ps[b1][0:64, :], scalar=G[0:64, G_MQR:G_MQR + 1],
                            in1=tmp_t[t1][0:64, :], op0=ALU.mult, op1=ALU.mult),
                            r=[R("ps", b1), R("G"), R("tmp", t1)], w=[R("tmp", t1)])
                        fw.op("dve", lambda e, t2=t2, b2=b2: e.scalar_tensor_tensor(
                            out=tmp_t[t2][0:64, :], in0=ps[b2][0:64, :], scalar=G[0:64, G_MQS:G_MQS + 1],
                            in1=tmp_t[t2][0:64, :], op0=ALU.mult, op1=ALU.mult),
                            r=[R("ps", b2), R("G"), R("tmp", t2)], w=[R("tmp", t2)])
                        sti = nxt("st", 4)
                        fw.op("dve", lambda e, t1=t1, t2=t2, sti=sti: e.tensor_tensor(
                            out=st_t[sti][0:64, :], in0=tmp_t[t1][0:64, :], in1=tmp_t[t2][0:64, :], op=ALU.add),
                            r=[R("tmp", t1), R("tmp", t2)], w=[R("st", sti)])
                        fw.dma("sp", qTr[hd][:, tsl], st_t[sti][0:64, :], r=[R("st", sti)], w=[R("qTr", hd)])
            slab, sres = wget("w_ukv", li, 0, 2, 0, 2048)
            for hd in range(8):
                for tt in range(2):
                    tsl = slice(tt * 512, tt * 512 + 512)
                    gsl = slice(tok0 + tt * 512, tok0 + tt * 512 + 512)
                    bA = nxt("main", 6)
                    proj_fm(slab, sres, hd * 256, 128, 2, lambda kc: big[:, 12 + kc, tsl], ckr, tt, bA)
                    ri = head_norm(bA, 512, G_MKN, kTn["m"][hd][:, gsl], R("kTn", "m", hd),
                                   extra=(sqkr[:, tsl], R("sqkr", tt)), div=192.0)
                    sti = nxt("st", 4)
                    fw.op("dve", lambda e, sti=sti, tsl=tsl, ri=ri: e.tensor_tensor(
                        out=st_t[sti][0:64, :], in0=kbase[:, tsl], in1=rs_t[ri][0:64, :], op=ALU.mult),
                        r=[R("kbase", tt), R("rs", ri)], w=[R("st", sti)])
                    fw.dma("sp", kTr[hd][:, gsl], st_t[sti][0:64, :], r=[R("st", sti)], w=[R("kTr", hd)])
            for tb in range(8):
                for hg in range(2):
                    bank = nxt("main", 6)
                    for hh in range(4):
                        hd = hg * 4 + hh
                        for kc in range(2):
                            mm(ps[bank][:, hh * 128:hh * 128 + 128], big[:, 12 + kc, tb * 128:tb * 128 + 128],
                               slab[:, kc, hd * 256 + 128:hd * 256 + 256], kc == 0, kc == 1,
                               r=[sres, R("big", 12 + kc, tb // 4)], w=[R("ps", bank)])
                    evac_store(bank, 128, 512, vv["m"][tok0 + tb * 128:tok0 + tb * 128 + 128, hg * 512:hg * 512 + 512],
                               R("v", "m"))

            for n, br in enumerate(("m", "f", "c")):
                mark("att_" + br)
                attention(br, hf, li)
                mark("merge_" + br)
                merge(n, li)
            mark("w_out")
            for sl in range(8):
                slab, sres = wget("w_out", li, 0, 16, sl * 256, 256)
                for m in range(2):
                    c = sl * 2 + m
                    for tt in range(2):
                        bank = nxt("main", 6)
                        proj_fm(slab, sres, m * 128, 128, 16, hr(tt), hres, tt, bank)
                        add_to_x(bank, c, tt)

            mark("cross")
            rmsnorm_fm(lambda c, a, b: x[:, c, a:b], xres, lambda c, a, b: h[:, c, a:b], hres, 16, T, G_CROSS, 2)
            fw.dma("pool", big[:, :, 0:256], memT.rearrange("(c p) m -> p c m", p=128),
                   w=[R("big", c, 0) for c in range(16)])
            mems = lambda c, a, b: big[:, c, a:b]
            memr = lambda c, tt: R("big", c, 0)
            rmsnorm_fm(mems, memr, mems, memr, 16, 256, G_MEM, 1)
            for sl in range(4):
                slab, sres = wget("w_xkv", li, 0, 16, sl * 256, 256)
                if sl < 2:
                    for m in range(2):
                        hd = sl * 2 + m
                        bank = nxt("main", 6)
                        proj_fm(slab, sres, m * 128, 128, 16, lambda kc: big[:, kc, 0:256], memr, 0, bank, ncol=256)
                        head_norm(bank, 256, G_XK, None, None, sb_dst=big[:, hd, 256:512], sb_res=R("xk", hd))
                else:
                    for mb in range(2):
                        bank = nxt("main", 6)
                        for kc in range(16):
                            mm(ps[bank][:, 0:256], big[:, kc, mb * 128:mb * 128 + 128], slab[:, kc, :], kc == 0, kc == 15,
                               r=[sres, R("big", kc, 0)], w=[R("ps", bank)])
                        c0 = 512 + (sl - 2) * 256
                        fw.op("act", lambda e, bank=bank, mb=mb, c0=c0: e.activation(
                            out=big[:, mb, c0:c0 + 256], in_=ps[bank][:, 0:256], func=AF.Copy),
                            r=[R("ps", bank)], w=[R("xv", mb, sl - 2)])
            for sl in range(2):
                slab, sres = wget("w_xq", li, 0, 16, sl * 256, 256)
                for m in range(2):
                    hd = sl * 2 + m
                    for tt in range(2):
                        bank = nxt("main", 6)
                        proj_fm(slab, sres, m * 128, 128, 16, hr(tt), hres, tt, bank)
                        head_norm(bank, 512, G_XQ, None, None, sb_dst=big[:, 4 + hd, tt * 512:tt * 512 + 512],
                                  sb_res=R("xq", hd, tt))
            xscale = 128.0 ** -0.5
            xseq = [(hd, j, mb) for hd in range(4) for j in range(2) for mb in range(2)]
            xsb = {}
            xol = {}

            def x_qk(t):
                hd, j, mb = xseq[t]
                sbk = nxt("sbank", 4)
                xsb[t] = sbk
                mm(ps[sbk][:, :], big[:, hd, 256 + mb * 128:256 + mb * 128 + 128], big[:, 4 + hd, j * 512:j * 512 + 512],
                   True, True, r=[R("xk", hd), R("xq", hd, j)], w=[R("ps", sbk)])

            def x_rest(t):
                hd, j, mb = xseq[t]
                sbk = xsb[t]
                pi = nxt("p", 3)
                fw.op("act", lambda e: e.activation(out=p_t[pi][:, :], in_=ps[sbk][:, :], func=AF.Exp, scale=xscale),
                      r=[R("ps", sbk)], w=[R("p", pi)])
                if t + 2 < len(xseq):
                    x_qk(t + 2)
                if mb == 0:
                    oi = nxt("ol", 2)
                    xol[(hd, j)] = (4 + 2 * oi, 5 + 2 * oi)
                ob, lb = xol[(hd, j)]
                mm(ps[ob][:, :], big[:, mb, 512 + hd * 128:512 + hd * 128 + 128], p_t[pi][:, :], mb == 0, mb == 1,
                   r=[R("xv", mb, hd // 2), R("p", pi)], w=[R("ps", ob)])
                mm(ps[lb][:, :], ones_bf[:, :], p_t[pi][:, :], mb == 0, mb == 1,
                   r=[R("ones"), R("p", pi)], w=[R("ps", lb)])
                if mb == 1:
                    qsl = slice(j * 512, j * 512 + 512)
                    ri = nxt("rs", 2)
                    fw.op("dve", lambda e: e.reciprocal(out=rs_t[ri][:, :], in_=ps[lb][:, :]),
                          r=[R("ps", lb)], w=[R("rs", ri)])
                    fw.op("dve", lambda e: e.tensor_tensor(
                        out=big[:, 8 + hd, qsl], in0=ps[ob][:, :], in1=rs_t[ri][:, :], op=ALU.mult),
                        r=[R("ps", ob), R("rs", ri)], w=[R("xo", hd, j)])

            x_qk(0)
            x_qk(1)
            for t in range(len(xseq)):
                x_rest(t)
            for sl in range(4):
                slab, sres = wget("w_xo", li, 0, 4, sl * 512, 512)
                for m in range(4):
                    c = sl * 4 + m
                    for tt in range(2):
                        tsl = slice(tt * 512, tt * 512 + 512)
                        bank = nxt("main", 4)
                        proj_fm(slab, sres, m * 128, 128, 4, lambda kc: big[:, 8 + kc, tsl],
                                lambda kc, tt_: R("xo", kc, tt_), tt, bank)
                        add_to_x(bank, c, tt)

            mark("mlp")
            rmsnorm_fm(lambda c, a, b: x[:, c, a:b], xres, lambda c, a, b: h[:, c, a:b], hres, 16, T, G_MLP, 2)
            fence([("big", c, tt) for c in range(16) for tt in range(2)],
                  ("big", "xk", "xv", "xq", "xo"))
            for fs in range(4):
                for sl in range(8):
                    slab, sres = wget("w_1", li, 0, 16, fs * 2048 + sl * 256, 256)
                    for m in range(2):
                        uc = sl * 2 + m
                        for tt in range(2):
                            tsl = slice(tt * 512, tt * 512 + 512)
                            bank = nxt("main", 6)
                            proj_fm(slab, sres, m * 128, 128, 16, hr(tt), hres, tt, bank)
                            ti = nxt("tmp", 2)
                            fw.op("act", lambda e, ti=ti, bank=bank: e.activation(out=tmp_t[ti][:, :], in_=ps[bank][:, :],
                                                                                func=AF.Relu),
                                  r=[R("ps", bank)], w=[R("tmp", ti)])
                            fw.op("dve", lambda e, ti=ti, uc=uc, tsl=tsl: e.tensor_tensor(
                                out=big[:, uc, tsl], in0=tmp_t[ti][:, :], in1=tmp_t[ti][:, :], op=ALU.mult),
                                r=[R("tmp", ti)], w=[R("big", uc, tt)])
                for sl in range(8):
                    slab, sres = wget("w_2", li, fs * 16, 16, sl * 256, 256)
                    for m in range(2):
                        c = sl * 2 + m
                        for tt in range(2):
                            tsl = slice(tt * 512, tt * 512 + 512)
                            bank = nxt("main", 6)
                            proj_fm(slab, sres, m * 128, 128, 16, lambda kc: big[:, kc, tsl],
                                    lambda kc, tt_: R("big", kc, tt_), tt, bank)
                            add_to_x(bank, c, tt)
            fw.dma("sp", dst_ap[:, tok0:tok0 + T].rearrange("(c p) t -> p c t", p=128), x[:, :, :],
                   r=[R("x", c, tt) for c in range(16) for tt in range(2)], w=[R("xs", hf)], is_out=is_final)

        def emit_all():
            rot.clear()
            wstate["i"] = 0
            emit_consts()
            for li in range(nl):
                for hf in range(2):
                    src = xT_in if li == 0 else xs
                    fin = li == nl - 1
                    emit_pass(li, hf, src, oT if fin else xs, fin)
            fw.finish()

        fw.dry = True
        emit_all()
        fw.dry = False
        emit_all()
    return nc


def _host_consts():
    bf = ml_dtypes.bfloat16
    s = np.arange(128)[:, None]
    t = np.arange(512)[None, :]
    fm = np.concatenate([((o * 128 + s) <= t) for o in range(4)], axis=1).astype(bf)
    mm_ = np.concatenate([(((o * 128 + s) // 64) <= (t // 64)) for o in range(4)], axis=1).astype(bf)
    u = np.arange(EW)[None, :]
    qc = (u - 384) // 64
    kc = s // 64
    band = ((kc >= qc - 8) & (kc <= qc)).astype(bf)
    tri = (s <= np.arange(128)[None, :]).astype(np.float32)
    pos = np.arange(S, dtype=np.float32)
    inv = (10000.0 ** (-np.arange(0, 64, 2, dtype=np.float32) / 64)).astype(np.float32)
    ang = pos[:, None] * inv[None, :]
    cos, sin = np.cos(ang).astype(np.float32).T, np.sin(ang).astype(np.float32).T
    ctab = np.ascontiguousarray(np.concatenate([cos, cos], 0))
    stab = np.ascontiguousarray(np.concatenate([-sin, sin], 0))
    return dict(fmask=np.ascontiguousarray(fm), mmask=np.ascontiguousarray(mm_), band=np.ascontiguousarray(band),
                tri=tri, ctab=ctab, stab=stab)


def _host_layer_tables(inp, ls):
    L = len(ls)
    G = np.zeros((L, 128, NG), np.float32)
    sw = np.concatenate([np.arange(32, 64), np.arange(0, 32)])
    for i, l in enumerate(ls):
        def fm(v, n):
            return np.asarray(v).reshape(n, 128).T
        G[i, :, G_MIX:G_MIX + 16] = fm(inp["g_mix"][l], 16)
        G[i, :, G_CROSS:G_CROSS + 16] = fm(inp["g_cross"][l], 16)
        G[i, :, G_MLP:G_MLP + 16] = fm(inp["g_mlp"][l], 16)
        G[i, :, G_MEM:G_MEM + 16] = fm(inp["g_mem"][l], 16)
        G[i, :, G_CQ:G_CQ + 4] = fm(inp["g_cq"][l], 4)
        G[i, :, G_CKV:G_CKV + 2] = fm(inp["g_ckv"][l], 2)
        gq, gk = np.asarray(inp["g_mla_q"][l]), np.asarray(inp["g_mla_k"][l])
        G[i, :, G_MQN] = gq[:128]
        G[i, :64, G_MQR] = gq[128:]
        G[i, :64, G_MQS] = gq[128:][sw]
        G[i, :, G_MKN] = gk[:128]
        G[i, :64, G_MKR] = gk[128:]
        G[i, :64, G_MKS] = gk[128:][sw]
        for col, nm in ((G_FQ, "g_fox_q"), (G_FK, "g_fox_k"), (G_CQ2, "g_ch_q"), (G_CK2, "g_ch_k"),
                        (G_XQ, "g_x_q"), (G_XK, "g_x_k")):
            G[i, :, col] = np.asarray(inp[nm][l])
        G[i, :, G_BF:G_BF + 64] = np.tile(np.asarray(inp["b_f"][l])[None, :], (128, 8))
    sidx = np.arange(128)[:, None]
    uidx = np.arange(EW)[None, :]
    idx = np.clip(uidx - 384 - sidx, -128, 128) + 128
    Tz = np.ascontiguousarray(np.stack([np.asarray(inp["rel_bias"][l])[:, idx] for l in ls], 0)).astype(np.float32)
    w_uq = np.stack([np.asarray(inp["w_uq"][l]) for l in ls], 0)
    cols = []
    for hd in range(8):
        b = hd * 192
        cols += list(range(b, b + 128)) + list(range(b + 128, b + 192)) + list(b + 128 + sw)
    w_uqx = np.ascontiguousarray(w_uq[:, :, cols])
    krc = 768 + np.concatenate([np.arange(64), sw])
    w_krx = np.ascontiguousarray(np.stack([np.asarray(inp["w_in"][l][:, krc]) for l in ls], 0))
    return G, Tz, w_uqx, w_krx


_CACHE = {}
PHASES = []


def _get_program(nl, first, last):
    key = (nl, first, last)
    if key not in _CACHE:
        _CACHE[key] = build_program(nl, first, last)
    return _CACHE[key]


def _run_layers(inp, xT_list, ls, consts):
    nl = len(ls)
    nc = _get_program(nl, True, True)
    G, Tz, w_uqx, w_krx = _host_layer_tables(inp, ls)
    sel = (lambda a: np.ascontiguousarray(np.asarray(a)[ls[0]:ls[-1] + 1]))
    shared = dict(w_in=sel(inp["w_in"]), w_krx=w_krx, w_uqx=w_uqx, w_ukv=sel(inp["w_ukv"]), w_br=sel(inp["w_br"]),
                  w_out=sel(inp["w_out"]), w_xq=sel(inp["w_xq"]), w_xkv=sel(inp["w_xkv"]), w_xo=sel(inp["w_xo"]),
                  w_1=sel(inp["w_1"]), w_2=sel(inp["w_2"]), G=G, Tz=Tz, **consts)
    in_maps = []
    mem = np.asarray(inp["mem"])
    for c in range(len(xT_list)):
        m = dict(shared)
        m["xT"] = xT_list[c]
        m["memT"] = np.ascontiguousarray(mem[c % mem.shape[0]].T)
        in_maps.append(m)
    res = run_bass_kernel_spmd(nc, in_maps, core_ids=list(range(len(xT_list))))
    return [np.asarray(r["oT"]) for r in res.results]


LAUNCH_GROUPS = [[0, 1, 2, 3]]


def kernel(**inp):
    x = np.asarray(inp["x"])
    consts = _host_consts()
    xT = [np.ascontiguousarray(x[b].T) for b in range(NCORES)]
    for ls in LAUNCH_GROUPS:
        xT = _run_layers(inp, xT, ls, consts)
    out = np.stack([xT[b].T for b in range(4)], 0).astype(np.float32)
    return np.ascontiguousarray(out)
```

```python
import numpy as np
import ml_dtypes
import concourse.bass as bass
import concourse.mybir as mybir
from concourse.bass_utils import run_bass_kernel_spmd
from contextlib import ExitStack
from bisect import bisect_left

F32 = mybir.dt.float32
BF16 = mybir.dt.bfloat16
AF = mybir.ActivationFunctionType
ALU = mybir.AluOpType

D = 2048
S = 2048
T = 1024
DEPTH = 4
D_IN = 13128
EPS = 1e-6
C_FOX = 832
C_FOXF = 3904
C_CH = 3912
C_GATE = 6984
NCORES = 4
G_MIX, G_CROSS, G_MLP, G_MEM, G_CQ, G_CKV = 0, 16, 32, 48, 64, 68
G_MQN, G_MQR, G_MQS, G_MKN, G_MKR, G_MKS = 70, 71, 72, 73, 74, 75
G_FQ, G_FK, G_CQ2, G_CK2, G_XQ, G_XK = 76, 77, 78, 79, 80, 81
G_BF = 82
NG = 146
EW = 1408


class Res:
    __slots__ = ("w", "wd", "re", "rd")

    def __init__(self):
        self.w = None
        self.wd = []
        self.re = {}
        self.rd = []


class EngQ:
    def __init__(self, name, eng, sem):
        self.name, self.eng, self.sem = name, eng, sem
        self.n = 0
        self.count = 0
        self.inc_idx = []
        self.inc_cnt = []
        self.last_handle = None
        self.last_idx = 0
        self.waited = {}

    def ensure_inc(self, idx):
        pos = bisect_left(self.inc_idx, idx)
        if pos < len(self.inc_idx):
            return self.inc_cnt[pos]
        assert self.last_idx >= idx
        self.count += 1
        self.last_handle.then_inc(self.sem, 1)
        self.inc_idx.append(self.last_idx)
        self.inc_cnt.append(self.count)
        return self.count


class FW:
    NDS = 24

    def __init__(self, nc, es):
        self.nc = nc
        self.dry = False
        self.engs = {}
        for name, eng in [("pe", nc.tensor), ("act", nc.scalar), ("dve", nc.vector),
                          ("pool", nc.gpsimd), ("sp", nc.sync)]:
            self.engs[name] = EngQ(name, eng, es.enter_context(nc.semaphore("s_" + name)))
        self.dsem = [es.enter_context(nc.semaphore("s_d%d" % i)) for i in range(self.NDS)]
        self.dcnt = [0] * self.NDS
        self.di = 0
        self.out_tokens = []

    def _wait(self, E, tok):
        if tok[0] == "e":
            Sq = self.engs[tok[1]]
            if Sq is E and E.name == "pe":
                return
            cnt = Sq.ensure_inc(tok[2])
            if E.waited.get(Sq.name, 0) >= cnt:
                return
            E.eng.wait_ge(Sq.sem, cnt)
            E.waited[Sq.name] = cnt
        else:
            key = ("d", tok[1])
            if E.waited.get(key, 0) >= tok[2]:
                return
            E.eng.wait_ge(self.dsem[tok[1]], tok[2])
            E.waited[key] = tok[2]

    def _deps(self, E, r, w, dma_write=False):
        deps = []
        for x in r:
            if x.w is not None:
                deps.append(x.w)
            deps.extend(x.wd)
        for x in w:
            if x.w is not None:
                deps.append(x.w)
            if not dma_write:
                deps.extend(x.wd)
            deps.extend(x.re.values())
            deps.extend(x.rd)
        for t in deps:
            self._wait(E, t)

    def _mark(self, tok, r, w):
        for x in r:
            if tok[0] == "e":
                x.re[tok[1]] = tok
            else:
                x.rd.append(tok)
        for x in w:
            if tok[0] == "e":
                x.w = tok
                x.wd = []
            else:
                x.w = None
                x.wd.append(tok)
                if len(x.wd) > 64:
                    x.wd = x.wd[-64:]
            x.re = {}
            x.rd = []

    def op(self, ename, fn, r=(), w=()):
        if self.dry:
            return
        E = self.engs[ename]
        self._deps(E, r, w)
        h = fn(E.eng)
        E.n += 1
        E.last_handle = h
        E.last_idx = E.n
        self._mark(("e", ename, E.n), r, w)

    def dma(self, qname, out, in_, r=(), w=(), is_out=False):
        if self.dry:
            return
        E = self.engs[qname]
        self._deps(E, r, w, dma_write=True)
        i = self.di % self.NDS
        self.di += 1
        if self.dcnt[i] > 0:
            self._wait(E, ("d", i, self.dcnt[i]))
        self.dcnt[i] += 16
        E.eng.dma_start(out=out, in_=in_).then_inc(self.dsem[i], 16)
        tok = ("d", i, self.dcnt[i])
        self._mark(tok, r, w)
        if is_out:
            self.out_tokens.append(tok)

    def finish(self):
        E = self.engs["sp"]
        for tok in self.out_tokens:
            self._wait(E, tok)


def build_program(nl, first, last, dbg=False):
    nc = bass.Bass("TRN2", target_bir_lowering=False)

    def din(name, shape, dt=F32):
        return nc.dram_tensor(name, shape, dt, kind="ExternalInput").ap()

    def dscr(name, shape, dt):
        return nc.dram_tensor(name, shape, dt, kind="Internal").ap()

    xT_in = din("xT", [D, S])
    memT = din("memT", [D, 256])
    w_in = din("w_in", [nl, D, D_IN])
    w_krx = din("w_krx", [nl, D, 128])
    w_uqx = din("w_uqx", [nl, 512, 2048])
    w_ukv = din("w_ukv", [nl, 256, 2048])
    w_br = din("w_br", [nl, 3, 1024, D])
    w_out = din("w_out", [nl, D, D])
    w_xq = din("w_xq", [nl, D, 512])
    w_xkv = din("w_xkv", [nl, D, 1024])
    w_xo = din("w_xo", [nl, 512, D])
    w_1 = din("w_1", [nl, D, 8192])
    w_2 = din("w_2", [nl, 8192, D])
    Gd = din("G", [nl, 128, NG])
    Tzd = din("Tz", [nl, 8, 128, EW])
    ctab_d = din("ctab", [64, S])
    stab_d = din("stab", [64, S])
    fmask_d = din("fmask", [128, 2048], BF16)
    mmask_d = din("mmask", [128, 2048], BF16)
    band_d = din("band", [128, EW], BF16)
    tri_d = din("tri", [128, 128])
    oT = nc.dram_tensor("oT", [D, S], F32, kind="ExternalOutput").ap()

    xs = dscr("xs", [D, S], F32)
    qTn = {b: dscr("qTn_" + b, [8, 128, T], BF16) for b in ("m", "f", "c")}
    qTr = dscr("qTr_m", [8, 64, T], BF16)
    kTn = {b: dscr("kTn_" + b, [8, 128, S], BF16) for b in ("m", "f", "c")}
    kTr = dscr("kTr_m", [8, 64, S], BF16)
    vv = {b: dscr("v_" + b, [S, 1024], BF16) for b in ("m", "f", "c")}
    gat = dscr("gat", [48, 128, T], BF16)
    rem0 = dscr("rem0", [128, 64], F32)

    es = ExitStack()
    with es:
        fw = FW(nc, es)

        def sb(name, shape, dt):
            return es.enter_context(nc.sbuf_tensor("sb_" + name, shape, dt))

        x = sb("x", [128, 16, T], F32)
        h = sb("h", [128, 16, T], BF16)
        big = sb("big", [128, 16, T], BF16)
        NSLOT = 3
        wr = [sb("wr%d" % i, [128, 4096], BF16) for i in range(NSLOT)]
        qn_t = sb("qn_t", [128, T], BF16)
        qr_t = sb("qr_t", [64, T], BF16)
        kn_t = sb("kn_t", [128, S], BF16)
        kr_t = sb("kr_t", [64, S], BF16)
        v_t = sb("v_t", [128, 16, 128], BF16)
        sq_t = [sb("sq%d" % i, [128, 512], BF16) for i in range(3)]
        rs_t = [sb("rs%d" % i, [128, 512], F32) for i in range(2)]
        tmp_t = [sb("tmp%d" % i, [128, 512], F32) for i in range(2)]
        st_t = [sb("st%d" % i, [128, 512], BF16) for i in range(4)]
        gt_t = [sb("gt%d" % i, [128, 512], BF16) for i in range(2)]
        p_t = [sb("p%d" % i, [128, 512], BF16) for i in range(3)]
        ME = sb("ME", [128, 3 * EW], BF16)
        mask_t = ME[:, 0:2048]
        band_t = ME[:, 0:EW]
        E_ts = [ME[:, EW:2 * EW], ME[:, 2 * EW:3 * EW]]
        kbase = sb("kbase", [64, T], BF16)
        sqkr = sb("sqkr", [64, T], BF16)
        G = sb("G", [128, NG], F32)
        ones_bf = sb("ones_bf", [128, 128], BF16)
        ones_f = sb("ones_f", [128, 128], F32)
        tri = sb("tri", [128, 128], F32)
        ff = sb("ff", [128, 64], F32)
        prefx = sb("prefx", [128, 72], F32)
        Dtok = sb("Dtok", [128, 64], F32)
        Rsb = sb("Rsb", [128, 72], F32)
        rem_t = sb("rem_t", [128, 64], F32)
        bias_own = sb("bias_own", [128, 2, 8, 8], F32)
        bias_oth = sb("bias_oth", [128, 2, 8, 8], F32)
        ps = [es.enter_context(nc.psum_tensor("pp%d" % i, [128, 512], F32)) for i in range(8)]

        RR = {}

        def R(*key):
            r = RR.get(key)
            if r is None:
                r = RR[key] = Res()
            return r

        def fence(new_keys, old_prefixes):
            toks = []
            for k, r in RR.items():
                if k[0] in old_prefixes:
                    if r.w is not None:
                        toks.append(r.w)
                    toks.extend(r.wd)
                    toks.extend(r.re.values())
                    toks.extend(r.rd)
            for k in new_keys:
                r = R(*k)
                for t in toks:
                    if t[0] == "e":
                        old = r.re.get(t[1])
                        if old is None or old[2] < t[2]:
                            r.re[t[1]] = t
                    else:
                        r.rd.append(t)

        rot = {}

        def mark(name):
            if not fw.dry:
                PHASES.append((name, fw.engs["pe"].n))

        def nxt(name, n):
            i = rot.get(name, 0)
            rot[name] = i + 1
            return i % n

        plan = []
        wstate = {"i": 0, "issued": 0}
        wsrc = {"w_in": w_in, "w_krx": w_krx, "w_uqx": w_uqx, "w_ukv": w_ukv, "w_out": w_out,
                "w_xq": w_xq, "w_xkv": w_xkv, "w_xo": w_xo, "w_1": w_1, "w_2": w_2}

        def issue_slab(j):
            key = plan[j]
            name, li, k0, nk, c0, ncol = key
            if name.startswith("w_br"):
                src = w_br[li, int(name[4])]
            else:
                src = wsrc[name][li]
            slot = wr[j % NSLOT]
            fw.dma("pool", slot[:, 0:nk * ncol].rearrange("p (k c) -> p k c", k=nk),
                   src[k0 * 128:(k0 + nk) * 128, c0:c0 + ncol].rearrange("(k p) c -> p k c", p=128),
                   w=[R("wr", j % NSLOT)])

        def wget(name, li, k0, nk, c0, ncol):
            key = (name, li, k0, nk, c0, ncol)
            i = wstate["i"]
            wstate["i"] = i + 1
            if fw.dry:
                plan.append(key)
            else:
                assert plan[i] == key, (plan[i], key)
                while wstate["issued"] < min(len(plan), i + NSLOT):
                    issue_slab(wstate["issued"])
                    wstate["issued"] += 1
            slot = wr[i % NSLOT]
            return slot[:, 0:nk * ncol].rearrange("p (k c) -> p k c", k=nk), R("wr", i % NSLOT)

        def mm(out, lhsT, rhs, start, stop, r, w):
            fw.op("pe", lambda e: e.matmul(out, lhsT, rhs, start=start, stop=stop), r=r, w=w)

        def rstd_from(bank, npart, ncol, scale, rdeps):
            ri = nxt("rs", 2)
            rs = rs_t[ri]
            fw.op("act", lambda e: e.activation(out=rs[0:npart, 0:ncol], in_=ps[bank][0:npart, 0:ncol],
                                                func=AF.Sqrt, bias=EPS, scale=scale),
                  r=[R("ps", bank)], w=[R("rs", ri)])
            fw.op("dve", lambda e: e.reciprocal(out=rs[0:npart, 0:ncol], in_=rs[0:npart, 0:ncol]),
                  r=[R("rs", ri)], w=[R("rs", ri)])
            return ri

        def rmsnorm_fm(src, src_res, dst, dst_res, nch, ncols, gcol, ntile):
            for tt in range(ntile):
                t0, t1 = tt * 512, min(ncols, tt * 512 + 512)
                n = t1 - t0
                bank = 6 + nxt("aux", 2)
                for c in range(nch):
                    si = nxt("sq", 3)
                    fw.op("act", lambda e, c=c, si=si: e.activation(out=sq_t[si][:, 0:n], in_=src(c, t0, t1),
                                                                  func=AF.Square),
                          r=[src_res(c, tt)], w=[R("sq", si)])
                    mm(ps[bank][:, 0:n], ones_bf[:, :], sq_t[si][:, 0:n], c == 0, c == nch - 1,
                       r=[R("sq", si), R("ones")], w=[R("ps", bank)])
                ri = rstd_from(bank, 128, n, 1.0 / (nch * 128), None)
                for c in range(nch):
                    fw.op("dve", lambda e, c=c: e.scalar_tensor_tensor(
                        out=dst(c, t0, t1), in0=src(c, t0, t1), scalar=G[:, gcol + c:gcol + c + 1],
                        in1=rs_t[ri][:, 0:n], op0=ALU.mult, op1=ALU.mult),
                        r=[src_res(c, tt), R("rs", ri), R("G")], w=[dst_res(c, tt)])

        def head_norm(bank, n, gcol, dst_ap, dst_res, extra=None, div=128.0, sb_dst=None, sb_res=None):
            si = nxt("sq", 3)
            fw.op("act", lambda e: e.activation(out=sq_t[si][:, 0:n], in_=ps[bank][:, 0:n], func=AF.Square),
                  r=[R("ps", bank)], w=[R("sq", si)])
            ab = 6 + nxt("aux", 2)
            mm(ps[ab][:, 0:n], ones_bf[:, :], sq_t[si][:, 0:n], True, extra is None,
               r=[R("sq", si), R("ones")], w=[R("ps", ab)])
            if extra is not None:
                mm(ps[ab][:, 0:n], ones_bf[0:64, :], extra[0], False, True,
                   r=[extra[1], R("ones")], w=[R("ps", ab)])
            ri = rstd_from(ab, 128, n, 1.0 / div, None)
            if sb_dst is not None:
                fw.op("dve", lambda e: e.scalar_tensor_tensor(
                    out=sb_dst, in0=ps[bank][:, 0:n], scalar=G[:, gcol:gcol + 1], in1=rs_t[ri][:, 0:n],
                    op0=ALU.mult, op1=ALU.mult), r=[R("ps", bank), R("rs", ri), R("G")], w=[sb_res])
            else:
                sti = nxt("st", 4)
                fw.op("dve", lambda e: e.scalar_tensor_tensor(
                    out=st_t[sti][:, 0:n], in0=ps[bank][:, 0:n], scalar=G[:, gcol:gcol + 1],
                    in1=rs_t[ri][:, 0:n], op0=ALU.mult, op1=ALU.mult),
                    r=[R("ps", bank), R("rs", ri), R("G")], w=[R("st", sti)])
                fw.dma("sp", dst_ap, st_t[sti][:, 0:n], r=[R("st", sti)], w=[dst_res])
            return ri

        def evac_store(bank, npart, n, dst_ap, dst_res, func=AF.Copy):
            sti = nxt("st", 4)
            fw.op("act", lambda e: e.activation(out=st_t[sti][0:npart, 0:n], in_=ps[bank][0:npart, 0:n], func=func),
                  r=[R("ps", bank)], w=[R("st", sti)])
            fw.dma("sp", dst_ap, st_t[sti][0:npart, 0:n], r=[R("st", sti)], w=[dst_res])

        def proj_fm(slab, slab_res, m0, msz, nk, rhs, rhs_res, tt, bank, ncol=512):
            for kc in range(nk):
                mm(ps[bank][0:msz, 0:ncol], slab[:, kc, m0:m0 + msz], rhs(kc), kc == 0, kc == nk - 1,
                   r=[slab_res, rhs_res(kc, tt)], w=[R("ps", bank)])

        hres = lambda c, tt: R("h", c, tt)
        xres = lambda c, tt: R("x", c, tt)

        def emit_consts():
            fw.op("dve", lambda e: e.memset(ones_bf[:, :], 1.0), w=[R("ones")])
            fw.op("dve", lambda e: e.memset(ones_f[:, :], 1.0), w=[R("onesf")])
            fw.dma("sp", tri[:, :], tri_d, w=[R("tri")])

        def attention(br, hf, li):
            scale = (192.0 if br == "m" else 128.0) ** -0.5
            if br == "c":
                kb_lo = max(0, hf * 8 - 4)
            else:
                kb_lo = 0
            kb_hi = hf * 8 + 8
            nkb = kb_hi - kb_lo
            if br == "m":
                fence([("qn", 1), ("qr", 1), ("kn", 1), ("kr", 1), ("vt", 1)], ("big",))
            if br in ("m", "f"):
                fence([("mask",)], ("mask", "band", "E"))
                fw.dma("sp", mask_t, mmask_d if br == "m" else fmask_d, w=[R("mask")])
            else:
                fence([("band",), ("E", 0), ("E", 1)], ("mask", "band", "E"))
                fw.dma("sp", band_t, band_d, w=[R("band")])
            bufs = [dict(qn=qn_t, qr=qr_t, kn=kn_t, kr=kr_t, v=v_t),
                    dict(qn=big[:, 8, :], qr=big[0:64, 9, :],
                         kn=big[:, 10:12, :].rearrange("p a t -> p (a t)"),
                         kr=big[0:64, 12:14, :].rearrange("p a t -> p (a t)"),
                         v=big[:, 14:16, :].rearrange("p a (b d) -> p (a b) d", d=128))]

            def load_head(hd, si):
                bs = bufs[si]
                fw.dma("sp", bs["qn"][:, :], qTn[br][hd], r=[R("qTn", br, hd)], w=[R("qn", si)])
                fw.dma("sp", bs["kn"][:, 0:nkb * 128], kTn[br][hd][:, kb_lo * 128:kb_hi * 128],
                       r=[R("kTn", br, hd)], w=[R("kn", si)])
                if br == "m":
                    fw.dma("sp", bs["qr"][:, :], qTr[hd], r=[R("qTr", hd)], w=[R("qr", si)])
                    fw.dma("sp", bs["kr"][:, 0:nkb * 128], kTr[hd][:, kb_lo * 128:kb_hi * 128],
                           r=[R("kTr", hd)], w=[R("kr", si)])
                fw.dma("sp", bs["v"][:, 0:nkb, :],
                       vv[br][kb_lo * 128:kb_hi * 128, hd * 128:(hd + 1) * 128].rearrange("(b p) d -> p b d", p=128),
                       r=[R("v", br)], w=[R("vt", si)])
                pend = []
                if br == "c":
                    for q0 in range(0, EW, 512):
                        q1 = min(EW, q0 + 512)
                        ti = nxt("tmp", 2)
                        fw.dma("sp", tmp_t[ti][:, 0:q1 - q0], Tzd[li, hd][:, q0:q1], w=[R("tmp", ti)])
                        pend.append((ti, q0, q1))
                return pend

            def post_load(si, pend):
                for ti, q0, q1 in pend:
                    fw.op("act", lambda e: e.activation(
                        out=E_ts[si][:, q0:q1], in_=tmp_t[ti][:, 0:q1 - q0], func=AF.Exp),
                        r=[R("tmp", ti)], w=[R("E", si)])
                if pend:
                    fw.op("dve", lambda e: e.tensor_tensor(out=E_ts[si], in0=E_ts[si], in1=band_t,
                                                           op=ALU.mult), r=[R("E", si), R("band")], w=[R("E", si)])

            pend0 = load_head(0, 0)
            post_load(0, pend0)
            for hd in range(8):
                si = hd % 2
                bs = bufs[si]
                qn_b, qr_b, kn_b, kr_b, v_b = bs["qn"], bs["qr"], bs["kn"], bs["kr"], bs["v"]
                E_t = E_ts[si]
                pend_next = load_head(hd + 1, 1 - si) if hd + 1 < 8 else None
                seq = []
                for j in range(2):
                    qs = hf * 8 + j * 4
                    if br == "c":
                        blocks = list(range(max(0, qs - 4), qs + 4))
                    else:
                        blocks = list(range(0, qs + 4))
                    for bi, kb in enumerate(blocks):
                        seq.append((j, bi, kb, len(blocks), qs))
                sbank_of = {}
                olb = {}

                def emit_qk(t):
                    j, bi, kb, nb, qs = seq[t]
                    kl = kb - kb_lo
                    sbk = nxt("sbank", 4)
                    sbank_of[t] = sbk
                    ksl = slice(kl * 128, kl * 128 + 128)
                    qsl = slice(j * 512, j * 512 + 512)
                    mm(ps[sbk][:, :], kn_b[:, ksl], qn_b[:, qsl], True, br != "m",
                       r=[R("kn", si), R("qn", si)], w=[R("ps", sbk)])
                    if br == "m":
                        mm(ps[sbk][:, :], kr_b[:, ksl], qr_b[:, qsl], False, True,
                           r=[R("kr", si), R("qr", si)], w=[R("ps", sbk)])

                def emit_exp(t):
                    j, bi, kb, nb, qs = seq[t]
                    sbk = sbank_of[t]
                    pi = nxt("p", 3)
                    if br == "f":
                        if kb >= hf * 8:
                            bias_ap = bias_own[:, j, kb - hf * 8, hd:hd + 1]
                            bres = R("bias_own")
                        else:
                            bias_ap = bias_oth[:, j, kb, hd:hd + 1]
                            bres = R("bias_oth")
                        fw.op("act", lambda e: e.activation(
                            out=p_t[pi][:, :], in_=ps[sbk][:, :], func=AF.Exp, bias=bias_ap, scale=scale),
                            r=[R("ps", sbk), bres], w=[R("p", pi)])
                    else:
                        fw.op("act", lambda e: e.activation(
                            out=p_t[pi][:, :], in_=ps[sbk][:, :], func=AF.Exp, scale=scale),
                            r=[R("ps", sbk)], w=[R("p", pi)])
                    if br == "c":
                        i_off = kb - (qs - 4)
                        e0 = 896 - 128 * i_off
                        fw.op("dve", lambda e: e.tensor_tensor(
                            out=p_t[pi][:, :], in0=p_t[pi][:, :], in1=E_t[:, e0:e0 + 512], op=ALU.mult),
                            r=[R("p", pi), R("E", si)], w=[R("p", pi)])
                    elif kb >= qs:
                        m0 = (kb - qs) * 512
                        fw.op("dve", lambda e: e.tensor_tensor(
                            out=p_t[pi][:, :], in0=p_t[pi][:, :], in1=mask_t[:, m0:m0 + 512], op=ALU.mult),
                            r=[R("p", pi), R("mask")], w=[R("p", pi)])
                    return pi

                def emit_pv(t, pi):
                    j, bi, kb, nb, qs = seq[t]
                    kl = kb - kb_lo
                    if bi == 0:
                        oi = nxt("ol", 2)
                        olb[j] = (4 + 2 * oi, 5 + 2 * oi)
                    ob, lb = olb[j]
                    first_b, last_b = bi == 0, bi == nb - 1
                    mm(ps[ob][:, :], v_b[:, kl, :], p_t[pi][:, :], first_b, last_b,
                       r=[R("vt", si), R("p", pi)], w=[R("ps", ob)])
                    mm(ps[lb][:, :], ones_bf[:, :], p_t[pi][:, :], first_b, last_b,
                       r=[R("ones"), R("p", pi)], w=[R("ps", lb)])
                    if last_b:
                        qsl = slice(j * 512, j * 512 + 512)
                        ri = nxt("rs", 2)
                        fw.op("dve", lambda e: e.reciprocal(out=rs_t[ri][:, :], in_=ps[lb][:, :]),
                              r=[R("ps", lb)], w=[R("rs", ri)])
                        fw.op("dve", lambda e: e.tensor_tensor(
                            out=big[:, hd, qsl], in0=ps[ob][:, :], in1=rs_t[ri][:, :], op=ALU.mult),
                            r=[R("ps", ob), R("rs", ri)], w=[R("big", hd, j)])

                LA = 2
                for t in range(min(LA, len(seq))):
                    emit_qk(t)
                for t in range(len(seq)):
                    pi = emit_exp(t)
                    if t + LA < len(seq):
                        emit_qk(t + LA)
                    emit_pv(t, pi)
                    if t == len(seq) // 2 and pend_next is not None:
                        post_load(1 - si, pend_next)
            if br == "c":
                fence([("big", c, tt) for c in range(8, 16) for tt in range(2)], ("qn", "qr", "kn", "kr", "vt"))

        def merge(n, li):
            for sl in range(4):
                slab, sres = wget("w_br%d" % n, li, 0, 8, sl * 512, 512)
                for m in range(4):
                    c = sl * 4 + m
                    for tt in range(2):
                        tsl = slice(tt * 512, tt * 512 + 512)
                        bank = nxt("main", 4)
                        proj_fm(slab, sres, m * 128, 128, 8, lambda kc: big[:, kc, tsl],
                                lambda kc, tt_: R("big", kc, tt_), tt, bank)
                        gi = nxt("gt", 2)
                        fw.dma("sp", gt_t[gi][:, :], gat[n * 16 + c][:, tsl], r=[R("gat", n * 16 + c, tt)],
                               w=[R("gt", gi)])
                        if n == 0:
                            fw.op("dve", lambda e, bank=bank, gi=gi, c=c, tsl=tsl: e.tensor_tensor(
                                out=h[:, c, tsl], in0=ps[bank][:, :], in1=gt_t[gi][:, :], op=ALU.mult),
                                r=[R("ps", bank), R("gt", gi)], w=[R("h", c, tt)])
                        else:
                            ti = nxt("tmp", 2)
                            fw.op("dve", lambda e, bank=bank, gi=gi, ti=ti: e.tensor_tensor(
                                out=tmp_t[ti][:, :], in0=ps[bank][:, :], in1=gt_t[gi][:, :], op=ALU.mult),
                                r=[R("ps", bank), R("gt", gi)], w=[R("tmp", ti)])
                            fw.op("dve", lambda e, ti=ti, c=c, tsl=tsl: e.tensor_tensor(
                                out=h[:, c, tsl], in0=h[:, c, tsl], in1=tmp_t[ti][:, :], op=ALU.add),
                                r=[R("tmp", ti), R("h", c, tt)], w=[R("h", c, tt)])

        def add_to_x(bank, c, tt):
            tsl = slice(tt * 512, tt * 512 + 512)
            fw.op("dve", lambda e: e.tensor_tensor(out=x[:, c, tsl], in0=x[:, c, tsl], in1=ps[bank][:, :],
                                                   op=ALU.add), r=[R("ps", bank), R("x", c, tt)], w=[R("x", c, tt)])

        def emit_pass(li, hf, src_ap, dst_ap, is_final):
            tok0 = hf * T
            mark("norm")
            fw.dma("sp", x[:, :, :], src_ap[:, tok0:tok0 + T].rearrange("(c p) t -> p c t", p=128),
                   r=[R("xs", hf)], w=[R("x", c, tt) for c in range(16) for tt in range(2)])
            fw.dma("sp", G[:, :], Gd[li], w=[R("G")])
            rmsnorm_fm(lambda c, a, b: x[:, c, a:b], xres, lambda c, a, b: h[:, c, a:b], hres, 16, T, G_MIX, 2)

            hr = lambda tt: (lambda kc: h[:, kc, tt * 512:tt * 512 + 512])
            mark("z_mla")
            for sl in range(3):
                slab, sres = wget("w_in", li, 0, 16, sl * 256, 256)
                for m in range(2):
                    zc = sl * 2 + m
                    for tt in range(2):
                        bank = nxt("main", 6)
                        proj_fm(slab, sres, m * 128, 128, 16, hr(tt), hres, tt, bank)
                        fw.op("act", lambda e, bank=bank, zc=zc, tt=tt: e.activation(
                            out=big[:, 8 + zc, tt * 512:tt * 512 + 512], in_=ps[bank][:, :], func=AF.Copy),
                            r=[R("ps", bank)], w=[R("big", 8 + zc, tt)])
            slab, sres = wget("w_krx", li, 0, 16, 0, 128)
            for which in range(2):
                for tt in range(2):
                    bank = nxt("main", 6)
                    proj_fm(slab, sres, which * 64, 64, 16, hr(tt), hres, tt, bank)
                    fw.op("act", lambda e, bank=bank, which=which, tt=tt: e.activation(
                        out=big[0:64, 14 + which, tt * 512:tt * 512 + 512], in_=ps[bank][0:64, :], func=AF.Copy),
                        r=[R("ps", bank)], w=[R("big", 14 + which, tt)])
            mark("z_qkv")
            for br, cbase, gq, gk in (("f", C_FOX, G_FQ, G_FK), ("c", C_CH, G_CQ2, G_CK2)):
                for sl in range(8):
                    slab, sres = wget("w_in", li, 0, 16, cbase + sl * 256, 256)
                    for m in range(2):
                        hidx = sl * 2 + m
                        for tt in range(2):
                            bank = nxt("main", 6)
                            proj_fm(slab, sres, m * 128, 128, 16, hr(tt), hres, tt, bank)
                            if hidx < 8:
                                head_norm(bank, 512, gq, qTn[br][hidx][:, tt * 512:tt * 512 + 512],
                                          R("qTn", br, hidx))
                            else:
                                head_norm(bank, 512, gk,
                                          kTn[br][hidx - 8][:, tok0 + tt * 512:tok0 + tt * 512 + 512],
                                          R("kTn", br, hidx - 8))
                for sl in range(4):
                    slab, sres = wget("w_in", li, 0, 16, cbase + 2048 + sl * 256, 256)
                    for tb in range(8):
                        bank = nxt("main", 6)
                        for kc in range(16):
                            mm(ps[bank][:, 0:256], h[:, kc, tb * 128:tb * 128 + 128], slab[:, kc, :], kc == 0, kc == 15,
                               r=[sres, R("h", kc, tb // 4)], w=[R("ps", bank)])
                        evac_store(bank, 128, 256, vv[br][tok0 + tb * 128:tok0 + tb * 128 + 128, sl * 256:sl * 256 + 256],
                                   R("v", br))
                if br == "f":
                    slab, sres = wget("w_in", li, 0, 16, C_FOXF, 8)
                    fbank = 6 + nxt("aux", 2)
                    for tb in range(8):
                        for kc in range(16):
                            mm(ps[fbank][:, tb * 8:tb * 8 + 8], h[:, kc, tb * 128:tb * 128 + 128], slab[:, kc, :],
                               kc == 0, kc == 15, r=[sres, R("h", kc, tb // 4)], w=[R("ps", fbank)])
                    fw.op("dve", lambda e: e.tensor_tensor(out=ff[:, :], in0=ps[fbank][:, 0:64], in1=G[:, G_BF:G_BF + 64],
                                                           op=ALU.add), r=[R("ps", fbank), R("G")], w=[R("ff")])
                    fw.op("act", lambda e: e.activation(out=ff[:, :], in_=ff[:, :], func=AF.Exp, scale=-1.0),
                          r=[R("ff")], w=[R("ff")])
                    fw.op("act", lambda e: e.activation(out=ff[:, :], in_=ff[:, :], func=AF.Ln, bias=1.0),
                          r=[R("ff")], w=[R("ff")])
                    fw.op("dve", lambda e: e.memset(prefx[:, 0:8], 0.0), w=[R("prefx")])
                    for b in range(1, 9):
                        fw.op("dve", lambda e, b=b: e.tensor_tensor(
                            out=prefx[:, b * 8:b * 8 + 8], in0=prefx[:, b * 8 - 8:b * 8], in1=ff[:, b * 8 - 8:b * 8],
                            op=ALU.add), r=[R("prefx"), R("ff")], w=[R("prefx")])
                    cb = 6 + nxt("aux", 2)
                    mm(ps[cb][:, 0:64], tri[:, :], ff[:, :], True, True, r=[R("tri"), R("ff")], w=[R("ps", cb)])
                    mm(ps[cb][:, 64:136], ones_f[:, :], prefx[:, :], True, True, r=[R("onesf"), R("prefx")],
                       w=[R("ps", cb)])
                    fw.op("act", lambda e: e.activation(out=Rsb[:, :], in_=ps[cb][:, 64:136], func=AF.Copy),
                          r=[R("ps", cb)], w=[R("Rsb")])
                    fw.op("dve", lambda e: e.tensor_tensor(out=Dtok[:, :], in0=ps[cb][:, 0:64], in1=Rsb[:, 0:64],
                                                           op=ALU.add), r=[R("ps", cb), R("Rsb")], w=[R("Dtok")])
                    for j in range(2):
                        rc = (4 * j + 2) * 8
                        for kb in range(8):
                            fw.op("dve", lambda e, j=j, kb=kb, rc=rc: e.tensor_tensor(
                                out=bias_own[:, j, kb, :], in0=Dtok[:, kb * 8:kb * 8 + 8], in1=Rsb[:, rc:rc + 8],
                                op=ALU.subtract), r=[R("Dtok"), R("Rsb")], w=[R("bias_own")])
                    if hf == 0:
                        for kb in range(8):
                            fw.op("dve", lambda e, kb=kb: e.tensor_tensor(
                                out=rem_t[:, kb * 8:kb * 8 + 8], in0=Rsb[:, 64:72], in1=Dtok[:, kb * 8:kb * 8 + 8],
                                op=ALU.subtract), r=[R("Dtok"), R("Rsb")], w=[R("rem_t")])
                        fw.dma("sp", rem0, rem_t[:, :], r=[R("rem_t")], w=[R("rem0")])
                    else:
                        fw.dma("sp", rem_t[:, :], rem0, r=[R("rem0")], w=[R("rem_t")])
                        for j in range(2):
                            rc = (4 * j + 2) * 8
                            for kb in range(8):
                                fw.op("dve", lambda e, j=j, kb=kb, rc=rc: e.scalar_tensor_tensor(
                                    out=bias_oth[:, j, kb, :], in0=rem_t[:, kb * 8:kb * 8 + 8], scalar=-1.0,
                                    in1=Rsb[:, rc:rc + 8], op0=ALU.mult, op1=ALU.subtract),
                                    r=[R("rem_t"), R("Rsb")], w=[R("bias_oth")])
            mark("z_gates")
            for sl in range(24):
                slab, sres = wget("w_in", li, 0, 16, C_GATE + sl * 256, 256)
                for m in range(2):
                    gc = sl * 2 + m
                    for tt in range(2):
                        bank = nxt("main", 6)
                        proj_fm(slab, sres, m * 128, 128, 16, hr(tt), hres, tt, bank)
                        evac_store(bank, 128, 512, gat[gc][:, tt * 512:tt * 512 + 512], R("gat", gc, tt),
                                   func=AF.Sigmoid)

            mark("mla_prep")
            cqs = lambda c, a, b: big[:, 8 + c, a:b]
            cqr = lambda c, tt: R("big", 8 + c, tt)
            rmsnorm_fm(cqs, cqr, cqs, cqr, 4, T, G_CQ, 2)
            cks = lambda c, a, b: big[:, 12 + c, a:b]
            ckr = lambda c, tt: R("big", 12 + c, tt)
            rmsnorm_fm(cks, ckr, cks, ckr, 2, T, G_CKV, 2)
            for tt in range(2):
                tsl = slice(tt * 512, tt * 512 + 512)
                gsl = slice(tok0 + tt * 512, tok0 + tt * 512 + 512)
                t1, t2 = nxt("tmp", 2), nxt("tmp", 2)
                fw.dma("sp", tmp_t[t1][0:64, :], ctab_d[:, gsl], w=[R("tmp", t1)])
                fw.dma("sp", tmp_t[t2][0:64, :], stab_d[:, gsl], w=[R("tmp", t2)])
                fw.op("dve", lambda e, t1=t1, tsl=tsl: e.scalar_tensor_tensor(
                    out=tmp_t[t1][0:64, :], in0=big[0:64, 14, tsl], scalar=G[0:64, G_MKR:G_MKR + 1],
                    in1=tmp_t[t1][0:64, :], op0=ALU.mult, op1=ALU.mult),
                    r=[R("big", 14, tt), R("G"), R("tmp", t1)], w=[R("tmp", t1)])
                fw.op("dve", lambda e, t2=t2, tsl=tsl: e.scalar_tensor_tensor(
                    out=tmp_t[t2][0:64, :], in0=big[0:64, 15, tsl], scalar=G[0:64, G_MKS:G_MKS + 1],
                    in1=tmp_t[t2][0:64, :], op0=ALU.mult, op1=ALU.mult),
                    r=[R("big", 15, tt), R("G"), R("tmp", t2)], w=[R("tmp", t2)])
                fw.op("dve", lambda e, t1=t1, t2=t2, tsl=tsl: e.tensor_tensor(
                    out=kbase[:, tsl], in0=tmp_t[t1][0:64, :], in1=tmp_t[t2][0:64, :], op=ALU.add),
                    r=[R("tmp", t1), R("tmp", t2)], w=[R("kbase", tt)])
                fw.op("act", lambda e, tsl=tsl: e.activation(out=sqkr[:, tsl], in_=big[0:64, 14, tsl], func=AF.Square),
                      r=[R("big", 14, tt)], w=[R("sqkr", tt)])
            qit = [(sl, hh, tt) for sl in range(2) for hh in range(4) for tt in range(2)]
            qslab = {}
            qbanks = {}

            def q_s1(i):
                sl, hh, tt = qit[i]
                if sl not in qslab:
                    qslab[sl] = wget("w_uqx", li, 0, 4, sl * 1024, 1024)
                slab, sres = qslab[sl]
                tsl = slice(tt * 512, tt * 512 + 512)
                rhs = lambda kc: big[:, 8 + kc, tsl]
                bA, b1, b2 = nxt("main", 6), nxt("main", 6), nxt("main", 6)
                proj_fm(slab, sres, hh * 256, 128, 4, rhs, cqr, tt, bA)
                proj_fm(slab, sres, hh * 256 + 128, 64, 4, rhs, cqr, tt, b1)
                proj_fm(slab, sres, hh * 256 + 192, 64, 4, rhs, cqr, tt, b2)
                qbanks[i] = (bA, b1, b2)

            def q_s2(i):
                sl, hh, tt = qit[i]
                hd = sl * 4 + hh
                bA, b1, b2 = qbanks[i]
                tsl = slice(tt * 512, tt * 512 + 512)
                gsl = slice(tok0 + tt * 512, tok0 + tt * 512 + 512)
                si = nxt("sq", 3)
                fw.op("act", lambda e: e.activation(out=sq_t[si][0:64, :], in_=ps[b1][0:64, :], func=AF.Square),
                      r=[R("ps", b1)], w=[R("sq", si)])
                t1, t2 = nxt("tmp", 2), nxt("tmp", 2)
                fw.dma("sp", tmp_t[t1][0:64, :], ctab_d[:, gsl], w=[R("tmp", t1)])
                fw.dma("sp", tmp_t[t2][0:64, :], stab_d[:, gsl], w=[R("tmp", t2)])
                ri = head_norm(bA, 512, G_MQN, qTn["m"][hd][:, tsl], R("qTn", "m", hd),
                               extra=(sq_t[si][0:64, :], R("sq", si)), div=192.0)
                fw.op("dve", lambda e: e.tensor_tensor(
                    out=tmp_t[t1][0:64, :], in0=tmp_t[t1][0:64, :], in1=rs_t[ri][0:64, :], op=ALU.mult),
                    r=[R("tmp", t1), R("rs", ri)], w=[R("tmp", t1)])
                fw.op("dve", lambda e: e.tensor_tensor(
                    out=tmp_t[t2][0:64, :], in0=tmp_t[t2][0:64, :], in1=rs_t[ri][0:64, :], op=ALU.mult),
                    r=[R("tmp", t2), R("rs", ri)], w=[R("tmp", t2)])
                fw.op("dve", lambda e: e.scalar_tensor_tensor(
                    out=tmp_t[t1][0:64, :], in0=ps[b1][0:64, :], scalar=G[0:64, G_MQR:G_MQR + 1],
                    in1=tmp_t[t1][0:64, :], op0=ALU.mult, op1=ALU.mult),
                    r=[R("ps", b1), R("G"), R("tmp", t1)], w=[R("tmp", t1)])
                fw.op("dve", lambda e: e.scalar_tensor_tensor(
                    out=tmp_t[t2][0:64, :], in0=ps[b2][0:64, :], scalar=G[0:64, G_MQS:G_MQS + 1],
                    in1=tmp_t[t2][0:64, :], op0=ALU.mult, op1=ALU.mult),
                    r=[R("ps", b2), R("G"), R("tmp", t2)], w=[R("tmp", t2)])
                sti = nxt("st", 4)
                fw.op("dve", lambda e: e.tensor_tensor(
                    out=st_t[sti][0:64, :], in0=tmp_t[t1][0:64, :], in1=tmp_t[t2][0:64, :], op=ALU.add),
                    r=[R("tmp", t1), R("tmp", t2)], w=[R("st", sti)])
                fw.dma("sp", qTr[hd][:, tsl], st_t[sti][0:64, :], r=[R("st", sti)], w=[R("qTr", hd)])

            q_s1(0)
            for i in range(len(qit)):
                if i + 1 < len(qit):
                    q_s1(i + 1)
                q_s2(i)
            slab, sres = wget("w_ukv", li, 0, 2, 0, 2048)
            kit = [(hd, tt) for hd in range(8) for tt in range(2)]
            kbank = {}

            def k_s1(i):
                hd, tt = kit[i]
                tsl = slice(tt * 512, tt * 512 + 512)
                bA = nxt("main", 6)
                proj_fm(slab, sres, hd * 256, 128, 2, lambda kc: big[:, 12 + kc, tsl], ckr, tt, bA)
                kbank[i] = bA

            def k_s2(i):
                hd, tt = kit[i]
                bA = kbank[i]
                tsl = slice(tt * 512, tt * 512 + 512)
                gsl = slice(tok0 + tt * 512, tok0 + tt * 512 + 512)
                ri = head_norm(bA, 512, G_MKN, kTn["m"][hd][:, gsl], R("kTn", "m", hd),
                               extra=(sqkr[:, tsl], R("sqkr", tt)), div=192.0)
                sti = nxt("st", 4)
                fw.op("dve", lambda e: e.tensor_tensor(
                    out=st_t[sti][0:64, :], in0=kbase[:, tsl], in1=rs_t[ri][0:64, :], op=ALU.mult),
                    r=[R("kbase", tt), R("rs", ri)], w=[R("st", sti)])
                fw.dma("sp", kTr[hd][:, gsl], st_t[sti][0:64, :], r=[R("st", sti)], w=[R("kTr", hd)])

            k_s1(0)
            k_s1(1)
            for i in range(len(kit)):
                if i + 2 < len(kit):
                    k_s1(i + 2)
                k_s2(i)
            for tb in range(8):
                for hg in range(2):
                    bank = nxt("main", 6)
                    for hh in range(4):
                        hd = hg * 4 + hh
                        for kc in range(2):
                            mm(ps[bank][:, hh * 128:hh * 128 + 128], big[:, 12 + kc, tb * 128:tb * 128 + 128],
                               slab[:, kc, hd * 256 + 128:hd * 256 + 256], kc == 0, kc == 1,
                               r=[sres, R("big", 12 + kc, tb // 4)], w=[R("ps", bank)])
                    evac_store(bank, 128, 512, vv["m"][tok0 + tb * 128:tok0 + tb * 128 + 128, hg * 512:hg * 512 + 512],
                               R("v", "m"))

            for n, br in enumerate(("m", "f", "c")):
                mark("att_" + br)
                attention(br, hf, li)
                mark("merge_" + br)
                merge(n, li)
            mark("w_out")
            for sl in range(8):
                slab, sres = wget("w_out", li, 0, 16, sl * 256, 256)
                for m in range(2):
                    c = sl * 2 + m
                    for tt in range(2):
                        bank = nxt("main", 6)
                        proj_fm(slab, sres, m * 128, 128, 16, hr(tt), hres, tt, bank)
                        add_to_x(bank, c, tt)

            mark("cross")
            rmsnorm_fm(lambda c, a, b: x[:, c, a:b], xres, lambda c, a, b: h[:, c, a:b], hres, 16, T, G_CROSS, 2)
            fw.dma("pool", big[:, :, 0:256], memT.rearrange("(c p) m -> p c m", p=128),
                   w=[R("big", c, 0) for c in range(16)])
            mems = lambda c, a, b: big[:, c, a:b]
            memr = lambda c, tt: R("big", c, 0)
            rmsnorm_fm(mems, memr, mems, memr, 16, 256, G_MEM, 1)
            for sl in range(4):
                slab, sres = wget("w_xkv", li, 0, 16, sl * 256, 256)
                if sl < 2:
                    for m in range(2):
                        hd = sl * 2 + m
                        bank = nxt("main", 6)
                        proj_fm(slab, sres, m * 128, 128, 16, lambda kc: big[:, kc, 0:256], memr, 0, bank, ncol=256)
                        head_norm(bank, 256, G_XK, None, None, sb_dst=big[:, hd, 256:512], sb_res=R("xk", hd))
                else:
                    for mb in range(2):
                        bank = nxt("main", 6)
                        for kc in range(16):
                            mm(ps[bank][:, 0:256], big[:, kc, mb * 128:mb * 128 + 128], slab[:, kc, :], kc == 0, kc == 15,
                               r=[sres, R("big", kc, 0)], w=[R("ps", bank)])
                        c0 = 512 + (sl - 2) * 256
                        fw.op("act", lambda e, bank=bank, mb=mb, c0=c0: e.activation(
                            out=big[:, mb, c0:c0 + 256], in_=ps[bank][:, 0:256], func=AF.Copy),
                            r=[R("ps", bank)], w=[R("xv", mb, sl - 2)])
            for sl in range(2):
                slab, sres = wget("w_xq", li, 0, 16, sl * 256, 256)
                for m in range(2):
                    hd = sl * 2 + m
                    for tt in range(2):
                        bank = nxt("main", 6)
                        proj_fm(slab, sres, m * 128, 128, 16, hr(tt), hres, tt, bank)
                        head_norm(bank, 512, G_XQ, None, None, sb_dst=big[:, 4 + hd, tt * 512:tt * 512 + 512],
                                  sb_res=R("xq", hd, tt))
            xscale = 128.0 ** -0.5
            xseq = [(hd, j, mb) for hd in range(4) for j in range(2) for mb in range(2)]
            xsb = {}
            xol = {}

            def x_qk(t):
                hd, j, mb = xseq[t]
                sbk = nxt("sbank", 4)
                xsb[t] = sbk
                mm(ps[sbk][:, :], big[:, hd, 256 + mb * 128:256 + mb * 128 + 128], big[:, 4 + hd, j * 512:j * 512 + 512],
                   True, True, r=[R("xk", hd), R("xq", hd, j)], w=[R("ps", sbk)])

            def x_rest(t):
                hd, j, mb = xseq[t]
                sbk = xsb[t]
                pi = nxt("p", 3)
                fw.op("act", lambda e: e.activation(out=p_t[pi][:, :], in_=ps[sbk][:, :], func=AF.Exp, scale=xscale),
                      r=[R("ps", sbk)], w=[R("p", pi)])
                if t + 2 < len(xseq):
                    x_qk(t + 2)
                if mb == 0:
                    oi = nxt("ol", 2)
                    xol[(hd, j)] = (4 + 2 * oi, 5 + 2 * oi)
                ob, lb = xol[(hd, j)]
                mm(ps[ob][:, :], big[:, mb, 512 + hd * 128:512 + hd * 128 + 128], p_t[pi][:, :], mb == 0, mb == 1,
                   r=[R("xv", mb, hd // 2), R("p", pi)], w=[R("ps", ob)])
                mm(ps[lb][:, :], ones_bf[:, :], p_t[pi][:, :], mb == 0, mb == 1,
                   r=[R("ones"), R("p", pi)], w=[R("ps", lb)])
                if mb == 1:
                    qsl = slice(j * 512, j * 512 + 512)
                    ri = nxt("rs", 2)
                    fw.op("dve", lambda e: e.reciprocal(out=rs_t[ri][:, :], in_=ps[lb][:, :]),
                          r=[R("ps", lb)], w=[R("rs", ri)])
                    fw.op("dve", lambda e: e.tensor_tensor(
                        out=big[:, 8 + hd, qsl], in0=ps[ob][:, :], in1=rs_t[ri][:, :], op=ALU.mult),
                        r=[R("ps", ob), R("rs", ri)], w=[R("xo", hd, j)])

            x_qk(0)
            x_qk(1)
            for t in range(len(xseq)):
                x_rest(t)
            for sl in range(4):
                slab, sres = wget("w_xo", li, 0, 4, sl * 512, 512)
                for m in range(4):
                    c = sl * 4 + m
                    for tt in range(2):
                        tsl = slice(tt * 512, tt * 512 + 512)
                        bank = nxt("main", 4)
                        proj_fm(slab, sres, m * 128, 128, 4, lambda kc: big[:, 8 + kc, tsl],
                                lambda kc, tt_: R("xo", kc, tt_), tt, bank)
                        add_to_x(bank, c, tt)

            mark("mlp")
            rmsnorm_fm(lambda c, a, b: x[:, c, a:b], xres, lambda c, a, b: h[:, c, a:b], hres, 16, T, G_MLP, 2)
            fence([("big", c, tt) for c in range(16) for tt in range(2)],
                  ("big", "xk", "xv", "xq", "xo"))
            for fs in range(4):
                for sl in range(8):
                    slab, sres = wget("w_1", li, 0, 16, fs * 2048 + sl * 256, 256)
                    for m in range(2):
                        uc = sl * 2 + m
                        for tt in range(2):
                            tsl = slice(tt * 512, tt * 512 + 512)
                            bank = nxt("main", 6)
                            proj_fm(slab, sres, m * 128, 128, 16, hr(tt), hres, tt, bank)
                            ti = nxt("tmp", 2)
                            fw.op("act", lambda e, ti=ti, bank=bank: e.activation(out=tmp_t[ti][:, :], in_=ps[bank][:, :],
                                                                                func=AF.Relu),
                                  r=[R("ps", bank)], w=[R("tmp", ti)])
                            fw.op("dve", lambda e, ti=ti, uc=uc, tsl=tsl: e.tensor_tensor(
                                out=big[:, uc, tsl], in0=tmp_t[ti][:, :], in1=tmp_t[ti][:, :], op=ALU.mult),
                                r=[R("tmp", ti)], w=[R("big", uc, tt)])
                for sl in range(8):
                    slab, sres = wget("w_2", li, fs * 16, 16, sl * 256, 256)
                    for m in range(2):
                        c = sl * 2 + m
                        for tt in range(2):
                            tsl = slice(tt * 512, tt * 512 + 512)
                            bank = nxt("main", 6)
                            proj_fm(slab, sres, m * 128, 128, 16, lambda kc: big[:, kc, tsl],
                                    lambda kc, tt_: R("big", kc, tt_), tt, bank)
                            add_to_x(bank, c, tt)
            fw.dma("sp", dst_ap[:, tok0:tok0 + T].rearrange("(c p) t -> p c t", p=128), x[:, :, :],
                   r=[R("x", c, tt) for c in range(16) for tt in range(2)], w=[R("xs", hf)], is_out=is_final)

        def emit_all():
            rot.clear()
            wstate["i"] = 0
            emit_consts()
            for li in range(nl):
                for hf in range(2):
                    src = xT_in if li == 0 else xs
                    fin = li == nl - 1
                    emit_pass(li, hf, src, oT if fin else xs, fin)
            fw.finish()

        fw.dry = True
        emit_all()
        fw.dry = False
        emit_all()
    return nc


def _host_consts():
    bf = ml_dtypes.bfloat16
    s = np.arange(128)[:, None]
    t = np.arange(512)[None, :]
    fm = np.concatenate([((o * 128 + s) <= t) for o in range(4)], axis=1).astype(bf)
    mm_ = np.concatenate([(((o * 128 + s) // 64) <= (t // 64)) for o in range(4)], axis=1).astype(bf)
    u = np.arange(EW)[None, :]
    qc = (u - 384) // 64
    kc = s // 64
    band = ((kc >= qc - 8) & (kc <= qc)).astype(bf)
    tri = (s <= np.arange(128)[None, :]).astype(np.float32)
    pos = np.arange(S, dtype=np.float32)
    inv = (10000.0 ** (-np.arange(0, 64, 2, dtype=np.float32) / 64)).astype(np.float32)
    ang = pos[:, None] * inv[None, :]
    cos, sin = np.cos(ang).astype(np.float32).T, np.sin(ang).astype(np.float32).T
    ctab = np.ascontiguousarray(np.concatenate([cos, cos], 0))
    stab = np.ascontiguousarray(np.concatenate([-sin, sin], 0))
    return dict(fmask=np.ascontiguousarray(fm), mmask=np.ascontiguousarray(mm_), band=np.ascontiguousarray(band),
                tri=tri, ctab=ctab, stab=stab)


def _host_layer_tables(inp, ls):
    L = len(ls)
    G = np.zeros((L, 128, NG), np.float32)
    sw = np.concatenate([np.arange(32, 64), np.arange(0, 32)])
    for i, l in enumerate(ls):
        def fm(v, n):
            return np.asarray(v).reshape(n, 128).T
        G[i, :, G_MIX:G_MIX + 16] = fm(inp["g_mix"][l], 16)
        G[i, :, G_CROSS:G_CROSS + 16] = fm(inp["g_cross"][l], 16)
        G[i, :, G_MLP:G_MLP + 16] = fm(inp["g_mlp"][l], 16)
        G[i, :, G_MEM:G_MEM + 16] = fm(inp["g_mem"][l], 16)
        G[i, :, G_CQ:G_CQ + 4] = fm(inp["g_cq"][l], 4)
        G[i, :, G_CKV:G_CKV + 2] = fm(inp["g_ckv"][l], 2)
        gq, gk = np.asarray(inp["g_mla_q"][l]), np.asarray(inp["g_mla_k"][l])
        G[i, :, G_MQN] = gq[:128]
        G[i, :64, G_MQR] = gq[128:]
        G[i, :64, G_MQS] = gq[128:][sw]
        G[i, :, G_MKN] = gk[:128]
        G[i, :64, G_MKR] = gk[128:]
        G[i, :64, G_MKS] = gk[128:][sw]
        for col, nm in ((G_FQ, "g_fox_q"), (G_FK, "g_fox_k"), (G_CQ2, "g_ch_q"), (G_CK2, "g_ch_k"),
                        (G_XQ, "g_x_q"), (G_XK, "g_x_k")):
            G[i, :, col] = np.asarray(inp[nm][l])
        G[i, :, G_BF:G_BF + 64] = np.tile(np.asarray(inp["b_f"][l])[None, :], (128, 8))
    sidx = np.arange(128)[:, None]
    uidx = np.arange(EW)[None, :]
    idx = np.clip(uidx - 384 - sidx, -128, 128) + 128
    Tz = np.ascontiguousarray(np.stack([np.asarray(inp["rel_bias"][l])[:, idx] for l in ls], 0)).astype(np.float32)
    w_uq = np.stack([np.asarray(inp["w_uq"][l]) for l in ls], 0)
    cols = []
    for hd in range(8):
        b = hd * 192
        cols += list(range(b, b + 128)) + list(range(b + 128, b + 192)) + list(b + 128 + sw)
    w_uqx = np.ascontiguousarray(w_uq[:, :, cols])
    krc = 768 + np.concatenate([np.arange(64), sw])
    w_krx = np.ascontiguousarray(np.stack([np.asarray(inp["w_in"][l][:, krc]) for l in ls], 0))
    return G, Tz, w_uqx, w_krx


_CACHE = {}
PHASES = []


def _get_program(nl, first, last):
    key = (nl, first, last)
    if key not in _CACHE:
        _CACHE[key] = build_program(nl, first, last)
    return _CACHE[key]


def _run_layers(inp, xT_list, ls, consts):
    nl = len(ls)
    nc = _get_program(nl, True, True)
    G, Tz, w_uqx, w_krx = _host_layer_tables(inp, ls)
    sel = (lambda a: np.ascontiguousarray(np.asarray(a)[ls[0]:ls[-1] + 1]))
    shared = dict(w_in=sel(inp["w_in"]), w_krx=w_krx, w_uqx=w_uqx, w_ukv=sel(inp["w_ukv"]), w_br=sel(inp["w_br"]),
                  w_out=sel(inp["w_out"]), w_xq=sel(inp["w_xq"]), w_xkv=sel(inp["w_xkv"]), w_xo=sel(inp["w_xo"]),
                  w_1=sel(inp["w_1"]), w_2=sel(inp["w_2"]), G=G, Tz=Tz, **consts)
    in_maps = []
    mem = np.asarray(inp["mem"])
    for c in range(len(xT_list)):
        m = dict(shared)
        m["xT"] = xT_list[c]
        m["memT"] = np.ascontiguousarray(mem[c % mem.shape[0]].T)
        in_maps.append(m)
    res = run_bass_kernel_spmd(nc, in_maps, core_ids=list(range(len(xT_list))))
    return [np.asarray(r["oT"]) for r in res.results]


LAUNCH_GROUPS = [[0, 1, 2, 3]]


def kernel(**inp):
    x = np.asarray(inp["x"])
    consts = _host_consts()
    xT = [np.ascontiguousarray(x[b].T) for b in range(NCORES)]
    for ls in LAUNCH_GROUPS:
        xT = _run_layers(inp, xT, ls, consts)
    out = np.stack([xT[b].T for b in range(4)], 0).astype(np.float32)
    return np.ascontiguousarray(out)
```

```python
import numpy as np
import ml_dtypes
import concourse.bass as bass
import concourse.mybir as mybir
from concourse.bass_utils import run_bass_kernel_spmd
from contextlib import ExitStack
from bisect import bisect_left

F32 = mybir.dt.float32
BF16 = mybir.dt.bfloat16
AF = mybir.ActivationFunctionType
ALU = mybir.AluOpType

D = 2048
S = 2048
T = 1024
DEPTH = 4
D_IN = 13128
EPS = 1e-6
C_FOX = 832
C_FOXF = 3904
C_CH = 3912
C_GATE = 6984
NCORES = 4
G_MIX, G_CROSS, G_MLP, G_MEM, G_CQ, G_CKV = 0, 16, 32, 48, 64, 68
G_MQN, G_MQR, G_MQS, G_MKN, G_MKR, G_MKS = 70, 71, 72, 73, 74, 75
G_FQ, G_FK, G_CQ2, G_CK2, G_XQ, G_XK = 76, 77, 78, 79, 80, 81
G_BF = 82
NG = 146
EW = 1408


class Res:
    __slots__ = ("w", "wd", "re", "rd")

    def __init__(self):
        self.w = None
        self.wd = []
        self.re = {}
        self.rd = []


class EngQ:
    def __init__(self, name, eng, sem):
        self.name, self.eng, self.sem = name, eng, sem
        self.n = 0
        self.count = 0
        self.inc_idx = []
        self.inc_cnt = []
        self.last_handle = None
        self.last_idx = 0
        self.waited = {}

    def ensure_inc(self, idx):
        pos = bisect_left(self.inc_idx, idx)
        if pos < len(self.inc_idx):
            return self.inc_cnt[pos]
        assert self.last_idx >= idx
        self.count += 1
        self.last_handle.then_inc(self.sem, 1)
        self.inc_idx.append(self.last_idx)
        self.inc_cnt.append(self.count)
        return self.count


class FW:
    NDS = 24

    def __init__(self, nc, es):
        self.nc = nc
        self.dry = False
        self.engs = {}
        for name, eng in [("pe", nc.tensor), ("act", nc.scalar), ("dve", nc.vector),
                          ("pool", nc.gpsimd), ("sp", nc.sync)]:
            self.engs[name] = EngQ(name, eng, es.enter_context(nc.semaphore("s_" + name)))
        self.dsem = [es.enter_context(nc.semaphore("s_d%d" % i)) for i in range(self.NDS)]
        self.dcnt = [0] * self.NDS
        self.di = 0
        self.out_tokens = []

    def _wait(self, E, tok):
        if tok[0] == "e":
            Sq = self.engs[tok[1]]
            if Sq is E and E.name == "pe":
                return
            cnt = Sq.ensure_inc(tok[2])
            if E.waited.get(Sq.name, 0) >= cnt:
                return
            E.eng.wait_ge(Sq.sem, cnt)
            E.waited[Sq.name] = cnt
        else:
            key = ("d", tok[1])
            if E.waited.get(key, 0) >= tok[2]:
                return
            E.eng.wait_ge(self.dsem[tok[1]], tok[2])
            E.waited[key] = tok[2]

    def _deps(self, E, r, w, dma_write=False):
        deps = []
        for x in r:
            if x.w is not None:
                deps.append(x.w)
            deps.extend(x.wd)
        for x in w:
            if x.w is not None:
                deps.append(x.w)
            if not dma_write:
                deps.extend(x.wd)
            deps.extend(x.re.values())
            deps.extend(x.rd)
        for t in deps:
            self._wait(E, t)

    def _mark(self, tok, r, w):
        for x in r:
            if tok[0] == "e":
                x.re[tok[1]] = tok
            else:
                x.rd.append(tok)
        for x in w:
            if tok[0] == "e":
                x.w = tok
                x.wd = []
            else:
                x.w = None
                x.wd.append(tok)
                if len(x.wd) > 64:
                    x.wd = x.wd[-64:]
            x.re = {}
            x.rd = []

    def op(self, ename, fn, r=(), w=()):
        if self.dry:
            return
        E = self.engs[ename]
        self._deps(E, r, w)
        h = fn(E.eng)
        E.n += 1
        E.last_handle = h
        E.last_idx = E.n
        self._mark(("e", ename, E.n), r, w)

    def dma(self, qname, out, in_, r=(), w=(), is_out=False):
        if self.dry:
            return
        E = self.engs[qname]
        self._deps(E, r, w, dma_write=True)
        i = self.di % self.NDS
        self.di += 1
        if self.dcnt[i] > 0:
            self._wait(E, ("d", i, self.dcnt[i]))
        self.dcnt[i] += 16
        E.eng.dma_start(out=out, in_=in_).then_inc(self.dsem[i], 16)
        tok = ("d", i, self.dcnt[i])
        self._mark(tok, r, w)
        if is_out:
            self.out_tokens.append(tok)

    def finish(self):
        E = self.engs["sp"]
        for tok in self.out_tokens:
            self._wait(E, tok)


def build_program(nl, first, last, dbg=False):
    nc = bass.Bass("TRN2", target_bir_lowering=False)

    def din(name, shape, dt=F32):
        return nc.dram_tensor(name, shape, dt, kind="ExternalInput").ap()

    def dscr(name, shape, dt):
        return nc.dram_tensor(name, shape, dt, kind="Internal").ap()

    xT_in = din("xT", [D, S])
    memT = din("memT", [D, 256])
    w_in = din("w_in", [nl, D, D_IN])
    w_krx = din("w_krx", [nl, D, 128])
    w_uqx = din("w_uqx", [nl, 512, 2048])
    w_ukv = din("w_ukv", [nl, 256, 2048])
    w_br = din("w_br", [nl, 3, 1024, D])
    w_out = din("w_out", [nl, D, D])
    w_xq = din("w_xq", [nl, D, 512])
    w_xkv = din("w_xkv", [nl, D, 1024])
    w_xo = din("w_xo", [nl, 512, D])
    w_1 = din("w_1", [nl, D, 8192])
    w_2 = din("w_2", [nl, 8192, D])
    Gd = din("G", [nl, 128, NG])
    Tzd = din("Tz", [nl, 8, 128, EW])
    ctab_d = din("ctab", [64, S])
    stab_d = din("stab", [64, S])
    fmask_d = din("fmask", [128, 2048], BF16)
    mmask_d = din("mmask", [128, 2048], BF16)
    band_d = din("band", [128, EW], BF16)
    tri_d = din("tri", [128, 128])
    oT = nc.dram_tensor("oT", [D, S], F32, kind="ExternalOutput").ap()

    xs = dscr("xs", [D, S], F32)
    qTn = {b: dscr("qTn_" + b, [8, 128, T], BF16) for b in ("m", "f", "c")}
    qTr = dscr("qTr_m", [8, 64, T], BF16)
    kTn = {b: dscr("kTn_" + b, [8, 128, S], BF16) for b in ("m", "f", "c")}
    kTr = dscr("kTr_m", [8, 64, S], BF16)
    vv = {b: dscr("v_" + b, [S, 1024], BF16) for b in ("m", "f", "c")}
    gat = dscr("gat", [48, 128, T], BF16)
    rem0 = dscr("rem0", [128, 64], F32)

    es = ExitStack()
    with es:
        fw = FW(nc, es)

        def sb(name, shape, dt):
            return es.enter_context(nc.sbuf_tensor("sb_" + name, shape, dt))

        x = sb("x", [128, 16, T], F32)
        h = sb("h", [128, 16, T], BF16)
        big = sb("big", [128, 16, T], BF16)
        NSLOT = 3
        wr = [sb("wr%d" % i, [128, 4096], BF16) for i in range(NSLOT)]
        qn_t = sb("qn_t", [128, T], BF16)
        qr_t = sb("qr_t", [64, T], BF16)
        kn_t = sb("kn_t", [128, S], BF16)
        kr_t = sb("kr_t", [64, S], BF16)
        v_t = sb("v_t", [128, 16, 128], BF16)
        sq_t = [sb("sq%d" % i, [128, 512], BF16) for i in range(3)]
        rs_t = [sb("rs%d" % i, [128, 512], F32) for i in range(2)]
        tmp_t = [sb("tmp%d" % i, [128, 512], F32) for i in range(2)]
        st_t = [sb("st%d" % i, [128, 512], BF16) for i in range(4)]
        gt_t = [sb("gt%d" % i, [128, 512], BF16) for i in range(2)]
        p_t = [sb("p%d" % i, [128, 512], BF16) for i in range(3)]
        ME = sb("ME", [128, 3 * EW], BF16)
        mask_t = ME[:, 0:2048]
        band_t = ME[:, 0:EW]
        E_ts = [ME[:, EW:2 * EW], ME[:, 2 * EW:3 * EW]]
        kbase = sb("kbase", [64, T], BF16)
        sqkr = sb("sqkr", [64, T], BF16)
        G = sb("G", [128, NG], F32)
        ones_bf = sb("ones_bf", [128, 128], BF16)
        ones_f = sb("ones_f", [128, 128], F32)
        tri = sb("tri", [128, 128], F32)
        ff = sb("ff", [128, 64], F32)
        prefx = sb("prefx", [128, 72], F32)
        Dtok = sb("Dtok", [128, 64], F32)
        Rsb = sb("Rsb", [128, 72], F32)
        rem_t = sb("rem_t", [128, 64], F32)
        bias_own = sb("bias_own", [128, 2, 8, 8], F32)
        bias_oth = sb("bias_oth", [128, 2, 8, 8], F32)
        ps = [es.enter_context(nc.psum_tensor("pp%d" % i, [128, 512], F32)) for i in range(8)]

        RR = {}

        def R(*key):
            r = RR.get(key)
            if r is None:
                r = RR[key] = Res()
            return r

        def fence(new_keys, old_prefixes):
            toks = []
            for k, r in RR.items():
                if k[0] in old_prefixes:
                    if r.w is not None:
                        toks.append(r.w)
                    toks.extend(r.wd)
                    toks.extend(r.re.values())
                    toks.extend(r.rd)
            for k in new_keys:
                r = R(*k)
                for t in toks:
                    if t[0] == "e":
                        old = r.re.get(t[1])
                        if old is None or old[2] < t[2]:
                            r.re[t[1]] = t
                    else:
                        r.rd.append(t)

        rot = {}

        def ve(name="ve"):
            return "dve" if nxt(name, 2) == 0 else "pool"

        def mark(name):
            if not fw.dry:
                PHASES.append((name, fw.engs["pe"].n))

        def nxt(name, n):
            i = rot.get(name, 0)
            rot[name] = i + 1
            return i % n

        plan = []
        wstate = {"i": 0, "issued": 0}
        wsrc = {"w_in": w_in, "w_krx": w_krx, "w_uqx": w_uqx, "w_ukv": w_ukv, "w_out": w_out,
                "w_xq": w_xq, "w_xkv": w_xkv, "w_xo": w_xo, "w_1": w_1, "w_2": w_2}

        def issue_slab(j):
            key = plan[j]
            name, li, k0, nk, c0, ncol = key
            if name.startswith("w_br"):
                src = w_br[li, int(name[4])]
            else:
                src = wsrc[name][li]
            slot = wr[j % NSLOT]
            fw.dma("pool", slot[:, 0:nk * ncol].rearrange("p (k c) -> p k c", k=nk),
                   src[k0 * 128:(k0 + nk) * 128, c0:c0 + ncol].rearrange("(k p) c -> p k c", p=128),
                   w=[R("wr", j % NSLOT)])

        def wget(name, li, k0, nk, c0, ncol):
            key = (name, li, k0, nk, c0, ncol)
            i = wstate["i"]
            wstate["i"] = i + 1
            if fw.dry:
                plan.append(key)
            else:
                assert plan[i] == key, (plan[i], key)
                while wstate["issued"] < min(len(plan), i + NSLOT):
                    issue_slab(wstate["issued"])
                    wstate["issued"] += 1
            slot = wr[i % NSLOT]
            return slot[:, 0:nk * ncol].rearrange("p (k c) -> p k c", k=nk), R("wr", i % NSLOT)

        def mm(out, lhsT, rhs, start, stop, r, w):
            fw.op("pe", lambda e: e.matmul(out, lhsT, rhs, start=start, stop=stop), r=r, w=w)

        def rstd_from(bank, npart, ncol, scale, rdeps):
            ri = nxt("rs", 2)
            rs = rs_t[ri]
            fw.op("act", lambda e: e.activation(out=rs[0:npart, 0:ncol], in_=ps[bank][0:npart, 0:ncol],
                                                func=AF.Sqrt, bias=EPS, scale=scale),
                  r=[R("ps", bank)], w=[R("rs", ri)])
            fw.op("dve", lambda e: e.reciprocal(out=rs[0:npart, 0:ncol], in_=rs[0:npart, 0:ncol]),
                  r=[R("rs", ri)], w=[R("rs", ri)])
            return ri

        def rmsnorm_fm(src, src_res, dst, dst_res, nch, ncols, gcol, ntile):
            for tt in range(ntile):
                t0, t1 = tt * 512, min(ncols, tt * 512 + 512)
                n = t1 - t0
                bank = 6 + nxt("aux", 2)
                for c in range(nch):
                    si = nxt("sq", 3)
                    fw.op("act", lambda e, c=c, si=si: e.activation(out=sq_t[si][:, 0:n], in_=src(c, t0, t1),
                                                                  func=AF.Square),
                          r=[src_res(c, tt)], w=[R("sq", si)])
                    mm(ps[bank][:, 0:n], ones_bf[:, :], sq_t[si][:, 0:n], c == 0, c == nch - 1,
                       r=[R("sq", si), R("ones")], w=[R("ps", bank)])
                ri = rstd_from(bank, 128, n, 1.0 / (nch * 128), None)
                for c in range(nch):
                    fw.op("dve", lambda e, c=c: e.scalar_tensor_tensor(
                        out=dst(c, t0, t1), in0=src(c, t0, t1), scalar=G[:, gcol + c:gcol + c + 1],
                        in1=rs_t[ri][:, 0:n], op0=ALU.mult, op1=ALU.mult),
                        r=[src_res(c, tt), R("rs", ri), R("G")], w=[dst_res(c, tt)])

        def head_norm(bank, n, gcol, dst_ap, dst_res, extra=None, div=128.0, sb_dst=None, sb_res=None):
            si = nxt("sq", 3)
            fw.op("act", lambda e: e.activation(out=sq_t[si][:, 0:n], in_=ps[bank][:, 0:n], func=AF.Square),
                  r=[R("ps", bank)], w=[R("sq", si)])
            ab = 6 + nxt("aux", 2)
            mm(ps[ab][:, 0:n], ones_bf[:, :], sq_t[si][:, 0:n], True, extra is None,
               r=[R("sq", si), R("ones")], w=[R("ps", ab)])
            if extra is not None:
                mm(ps[ab][:, 0:n], ones_bf[0:64, :], extra[0], False, True,
                   r=[extra[1], R("ones")], w=[R("ps", ab)])
            ri = rstd_from(ab, 128, n, 1.0 / div, None)
            if sb_dst is not None:
                fw.op("dve", lambda e: e.scalar_tensor_tensor(
                    out=sb_dst, in0=ps[bank][:, 0:n], scalar=G[:, gcol:gcol + 1], in1=rs_t[ri][:, 0:n],
                    op0=ALU.mult, op1=ALU.mult), r=[R("ps", bank), R("rs", ri), R("G")], w=[sb_res])
            else:
                sti = nxt("st", 4)
                fw.op("dve", lambda e: e.scalar_tensor_tensor(
                    out=st_t[sti][:, 0:n], in0=ps[bank][:, 0:n], scalar=G[:, gcol:gcol + 1],
                    in1=rs_t[ri][:, 0:n], op0=ALU.mult, op1=ALU.mult),
                    r=[R("ps", bank), R("rs", ri), R("G")], w=[R("st", sti)])
                fw.dma("sp", dst_ap, st_t[sti][:, 0:n], r=[R("st", sti)], w=[dst_res])
            return ri

        def evac_store(bank, npart, n, dst_ap, dst_res, func=AF.Copy):
            sti = nxt("st", 4)
            fw.op("act", lambda e: e.activation(out=st_t[sti][0:npart, 0:n], in_=ps[bank][0:npart, 0:n], func=func),
                  r=[R("ps", bank)], w=[R("st", sti)])
            fw.dma("sp", dst_ap, st_t[sti][0:npart, 0:n], r=[R("st", sti)], w=[dst_res])

        def proj_fm(slab, slab_res, m0, msz, nk, rhs, rhs_res, tt, bank, ncol=512):
            for kc in range(nk):
                mm(ps[bank][0:msz, 0:ncol], slab[:, kc, m0:m0 + msz], rhs(kc), kc == 0, kc == nk - 1,
                   r=[slab_res, rhs_res(kc, tt)], w=[R("ps", bank)])

        hres = lambda c, tt: R("h", c, tt)
        xres = lambda c, tt: R("x", c, tt)

        def emit_consts():
            fw.op("dve", lambda e: e.memset(ones_bf[:, :], 1.0), w=[R("ones")])
            fw.op("dve", lambda e: e.memset(ones_f[:, :], 1.0), w=[R("onesf")])
            fw.dma("sp", tri[:, :], tri_d, w=[R("tri")])

        def attention(br, hf, li):
            scale = (192.0 if br == "m" else 128.0) ** -0.5
            if br == "c":
                kb_lo = max(0, hf * 8 - 4)
            else:
                kb_lo = 0
            kb_hi = hf * 8 + 8
            nkb = kb_hi - kb_lo
            if br == "m":
                fence([("qn", 1), ("qr", 1), ("kn", 1), ("kr", 1), ("vt", 1)], ("big",))
            if br in ("m", "f"):
                fence([("mask",)], ("mask", "band", "E"))
                fw.dma("sp", mask_t, mmask_d if br == "m" else fmask_d, w=[R("mask")])
            else:
                fence([("band",), ("E", 0), ("E", 1)], ("mask", "band", "E"))
                fw.dma("sp", band_t, band_d, w=[R("band")])
            bufs = [dict(qn=qn_t, qr=qr_t, kn=kn_t, kr=kr_t, v=v_t),
                    dict(qn=big[:, 8, :], qr=big[0:64, 9, :],
                         kn=big[:, 10:12, :].rearrange("p a t -> p (a t)"),
                         kr=big[0:64, 12:14, :].rearrange("p a t -> p (a t)"),
                         v=big[:, 14:16, :].rearrange("p a (b d) -> p (a b) d", d=128))]

            def load_head(hd, si):
                bs = bufs[si]
                fw.dma("sp", bs["qn"][:, :], qTn[br][hd], r=[R("qTn", br, hd)], w=[R("qn", si)])
                fw.dma("sp", bs["kn"][:, 0:nkb * 128], kTn[br][hd][:, kb_lo * 128:kb_hi * 128],
                       r=[R("kTn", br, hd)], w=[R("kn", si)])
                if br == "m":
                    fw.dma("sp", bs["qr"][:, :], qTr[hd], r=[R("qTr", hd)], w=[R("qr", si)])
                    fw.dma("sp", bs["kr"][:, 0:nkb * 128], kTr[hd][:, kb_lo * 128:kb_hi * 128],
                           r=[R("kTr", hd)], w=[R("kr", si)])
                fw.dma("sp", bs["v"][:, 0:nkb, :],
                       vv[br][kb_lo * 128:kb_hi * 128, hd * 128:(hd + 1) * 128].rearrange("(b p) d -> p b d", p=128),
                       r=[R("v", br)], w=[R("vt", si)])
                pend = []
                if br == "c":
                    for q0 in range(0, EW, 512):
                        q1 = min(EW, q0 + 512)
                        ti = nxt("tmp", 2)
                        fw.dma("sp", tmp_t[ti][:, 0:q1 - q0], Tzd[li, hd][:, q0:q1], w=[R("tmp", ti)])
                        pend.append((ti, q0, q1))
                return pend

            def post_load(si, pend):
                for ti, q0, q1 in pend:
                    fw.op("act", lambda e: e.activation(
                        out=E_ts[si][:, q0:q1], in_=tmp_t[ti][:, 0:q1 - q0], func=AF.Exp),
                        r=[R("tmp", ti)], w=[R("E", si)])
                if pend:
                    fw.op("dve", lambda e: e.tensor_tensor(out=E_ts[si], in0=E_ts[si], in1=band_t,
                                                           op=ALU.mult), r=[R("E", si), R("band")], w=[R("E", si)])

            pend0 = load_head(0, 0)
            post_load(0, pend0)
            for hd in range(8):
                si = hd % 2
                bs = bufs[si]
                qn_b, qr_b, kn_b, kr_b, v_b = bs["qn"], bs["qr"], bs["kn"], bs["kr"], bs["v"]
                E_t = E_ts[si]
                pend_next = load_head(hd + 1, 1 - si) if hd + 1 < 8 else None
                seq = []
                for j in range(2):
                    qs = hf * 8 + j * 4
                    if br == "c":
                        blocks = list(range(max(0, qs - 4), qs + 4))
                    else:
                        blocks = list(range(0, qs + 4))
                    for bi, kb in enumerate(blocks):
                        seq.append((j, bi, kb, len(blocks), qs))
                sbank_of = {}
                olb = {}

                def emit_qk(t):
                    j, bi, kb, nb, qs = seq[t]
                    kl = kb - kb_lo
                    sbk = nxt("sbank", 4)
                    sbank_of[t] = sbk
                    ksl = slice(kl * 128, kl * 128 + 128)
                    qsl = slice(j * 512, j * 512 + 512)
                    mm(ps[sbk][:, :], kn_b[:, ksl], qn_b[:, qsl], True, br != "m",
                       r=[R("kn", si), R("qn", si)], w=[R("ps", sbk)])
                    if br == "m":
                        mm(ps[sbk][:, :], kr_b[:, ksl], qr_b[:, qsl], False, True,
                           r=[R("kr", si), R("qr", si)], w=[R("ps", sbk)])

                def emit_exp(t):
                    j, bi, kb, nb, qs = seq[t]
                    sbk = sbank_of[t]
                    pi = nxt("p", 3)
                    if br == "f":
                        if kb >= hf * 8:
                            bias_ap = bias_own[:, j, kb - hf * 8, hd:hd + 1]
                            bres = R("bias_own")
                        else:
                            bias_ap = bias_oth[:, j, kb, hd:hd + 1]
                            bres = R("bias_oth")
                        fw.op("act", lambda e: e.activation(
                            out=p_t[pi][:, :], in_=ps[sbk][:, :], func=AF.Exp, bias=bias_ap, scale=scale),
                            r=[R("ps", sbk), bres], w=[R("p", pi)])
                    else:
                        fw.op("act", lambda e: e.activation(
                            out=p_t[pi][:, :], in_=ps[sbk][:, :], func=AF.Exp, scale=scale),
                            r=[R("ps", sbk)], w=[R("p", pi)])
                    if br == "c":
                        i_off = kb - (qs - 4)
                        e0 = 896 - 128 * i_off
                        fw.op(ve("att"), lambda e: e.tensor_tensor(
                            out=p_t[pi][:, :], in0=p_t[pi][:, :], in1=E_t[:, e0:e0 + 512], op=ALU.mult),
                            r=[R("p", pi), R("E", si)], w=[R("p", pi)])
                    elif kb >= qs:
                        m0 = (kb - qs) * 512
                        fw.op(ve("att"), lambda e: e.tensor_tensor(
                            out=p_t[pi][:, :], in0=p_t[pi][:, :], in1=mask_t[:, m0:m0 + 512], op=ALU.mult),
                            r=[R("p", pi), R("mask")], w=[R("p", pi)])
                    return pi

                def emit_pv(t, pi):
                    j, bi, kb, nb, qs = seq[t]
                    kl = kb - kb_lo
                    if bi == 0:
                        oi = nxt("ol", 2)
                        olb[j] = (4 + 2 * oi, 5 + 2 * oi)
                    ob, lb = olb[j]
                    first_b, last_b = bi == 0, bi == nb - 1
                    mm(ps[ob][:, :], v_b[:, kl, :], p_t[pi][:, :], first_b, last_b,
                       r=[R("vt", si), R("p", pi)], w=[R("ps", ob)])
                    mm(ps[lb][:, :], ones_bf[:, :], p_t[pi][:, :], first_b, last_b,
                       r=[R("ones"), R("p", pi)], w=[R("ps", lb)])
                    if last_b:
                        qsl = slice(j * 512, j * 512 + 512)
                        ri = nxt("rs", 2)
                        fw.op("dve", lambda e: e.reciprocal(out=rs_t[ri][:, :], in_=ps[lb][:, :]),
                              r=[R("ps", lb)], w=[R("rs", ri)])
                        fw.op("dve", lambda e: e.tensor_tensor(
                            out=big[:, hd, qsl], in0=ps[ob][:, :], in1=rs_t[ri][:, :], op=ALU.mult),
                            r=[R("ps", ob), R("rs", ri)], w=[R("big", hd, j)])

                LA = 2
                for t in range(min(LA, len(seq))):
                    emit_qk(t)
                for t in range(len(seq)):
                    pi = emit_exp(t)
                    if t + LA < len(seq):
                        emit_qk(t + LA)
                    emit_pv(t, pi)
                    if t == len(seq) // 2 and pend_next is not None:
                        post_load(1 - si, pend_next)
            if br == "c":
                fence([("big", c, tt) for c in range(8, 16) for tt in range(2)], ("qn", "qr", "kn", "kr", "vt"))

        def merge(n, li):
            for sl in range(4):
                slab, sres = wget("w_br%d" % n, li, 0, 8, sl * 512, 512)
                for m in range(4):
                    c = sl * 4 + m
                    for tt in range(2):
                        tsl = slice(tt * 512, tt * 512 + 512)
                        bank = nxt("main", 4)
                        proj_fm(slab, sres, m * 128, 128, 8, lambda kc: big[:, kc, tsl],
                                lambda kc, tt_: R("big", kc, tt_), tt, bank)
                        gi = nxt("gt", 2)
                        fw.dma("sp", gt_t[gi][:, :], gat[n * 16 + c][:, tsl], r=[R("gat", n * 16 + c, tt)],
                               w=[R("gt", gi)])
                        if n == 0:
                            fw.op("dve", lambda e, bank=bank, gi=gi, c=c, tsl=tsl: e.tensor_tensor(
                                out=h[:, c, tsl], in0=ps[bank][:, :], in1=gt_t[gi][:, :], op=ALU.mult),
                                r=[R("ps", bank), R("gt", gi)], w=[R("h", c, tt)])
                        else:
                            ti = nxt("tmp", 2)
                            fw.op("dve", lambda e, bank=bank, gi=gi, ti=ti: e.tensor_tensor(
                                out=tmp_t[ti][:, :], in0=ps[bank][:, :], in1=gt_t[gi][:, :], op=ALU.mult),
                                r=[R("ps", bank), R("gt", gi)], w=[R("tmp", ti)])
                            fw.op("pool", lambda e, ti=ti, c=c, tsl=tsl: e.tensor_tensor(
                                out=h[:, c, tsl], in0=h[:, c, tsl], in1=tmp_t[ti][:, :], op=ALU.add),
                                r=[R("tmp", ti), R("h", c, tt)], w=[R("h", c, tt)])

        def add_to_x(bank, c, tt):
            tsl = slice(tt * 512, tt * 512 + 512)
            fw.op("dve", lambda e: e.tensor_tensor(out=x[:, c, tsl], in0=x[:, c, tsl], in1=ps[bank][:, :],
                                                   op=ALU.add), r=[R("ps", bank), R("x", c, tt)], w=[R("x", c, tt)])

        def emit_pass(li, hf, src_ap, dst_ap, is_final):
            tok0 = hf * T
            mark("norm")
            fw.dma("sp", x[:, :, :], src_ap[:, tok0:tok0 + T].rearrange("(c p) t -> p c t", p=128),
                   r=[R("xs", hf)], w=[R("x", c, tt) for c in range(16) for tt in range(2)])
            fw.dma("sp", G[:, :], Gd[li], w=[R("G")])
            rmsnorm_fm(lambda c, a, b: x[:, c, a:b], xres, lambda c, a, b: h[:, c, a:b], hres, 16, T, G_MIX, 2)

            hr = lambda tt: (lambda kc: h[:, kc, tt * 512:tt * 512 + 512])
            mark("z_mla")
            for sl in range(3):
                slab, sres = wget("w_in", li, 0, 16, sl * 256, 256)
                for m in range(2):
                    zc = sl * 2 + m
                    for tt in range(2):
                        bank = nxt("main", 6)
                        proj_fm(slab, sres, m * 128, 128, 16, hr(tt), hres, tt, bank)
                        fw.op("act", lambda e, bank=bank, zc=zc, tt=tt: e.activation(
                            out=big[:, 8 + zc, tt * 512:tt * 512 + 512], in_=ps[bank][:, :], func=AF.Copy),
                            r=[R("ps", bank)], w=[R("big", 8 + zc, tt)])
            slab, sres = wget("w_krx", li, 0, 16, 0, 128)
            for which in range(2):
                for tt in range(2):
                    bank = nxt("main", 6)
                    proj_fm(slab, sres, which * 64, 64, 16, hr(tt), hres, tt, bank)
                    fw.op("act", lambda e, bank=bank, which=which, tt=tt: e.activation(
                        out=big[0:64, 14 + which, tt * 512:tt * 512 + 512], in_=ps[bank][0:64, :], func=AF.Copy),
                        r=[R("ps", bank)], w=[R("big", 14 + which, tt)])
            mark("z_qkv")
            for br, cbase, gq, gk in (("f", C_FOX, G_FQ, G_FK), ("c", C_CH, G_CQ2, G_CK2)):
                for sl in range(8):
                    slab, sres = wget("w_in", li, 0, 16, cbase + sl * 256, 256)
                    for m in range(2):
                        hidx = sl * 2 + m
                        for tt in range(2):
                            bank = nxt("main", 6)
                            proj_fm(slab, sres, m * 128, 128, 16, hr(tt), hres, tt, bank)
                            if hidx < 8:
                                head_norm(bank, 512, gq, qTn[br][hidx][:, tt * 512:tt * 512 + 512],
                                          R("qTn", br, hidx))
                            else:
                                head_norm(bank, 512, gk,
                                          kTn[br][hidx - 8][:, tok0 + tt * 512:tok0 + tt * 512 + 512],
                                          R("kTn", br, hidx - 8))
                for sl in range(4):
                    slab, sres = wget("w_in", li, 0, 16, cbase + 2048 + sl * 256, 256)
                    for tb in range(8):
                        bank = nxt("main", 6)
                        for kc in range(16):
                            mm(ps[bank][:, 0:256], h[:, kc, tb * 128:tb * 128 + 128], slab[:, kc, :], kc == 0, kc == 15,
                               r=[sres, R("h", kc, tb // 4)], w=[R("ps", bank)])
                        evac_store(bank, 128, 256, vv[br][tok0 + tb * 128:tok0 + tb * 128 + 128, sl * 256:sl * 256 + 256],
                                   R("v", br))
                if br == "f":
                    slab, sres = wget("w_in", li, 0, 16, C_FOXF, 8)
                    fbank = 6 + nxt("aux", 2)
                    for tb in range(8):
                        for kc in range(16):
                            mm(ps[fbank][:, tb * 8:tb * 8 + 8], h[:, kc, tb * 128:tb * 128 + 128], slab[:, kc, :],
                               kc == 0, kc == 15, r=[sres, R("h", kc, tb // 4)], w=[R("ps", fbank)])
                    fw.op("dve", lambda e: e.tensor_tensor(out=ff[:, :], in0=ps[fbank][:, 0:64], in1=G[:, G_BF:G_BF + 64],
                                                           op=ALU.add), r=[R("ps", fbank), R("G")], w=[R("ff")])
                    fw.op("act", lambda e: e.activation(out=ff[:, :], in_=ff[:, :], func=AF.Exp, scale=-1.0),
                          r=[R("ff")], w=[R("ff")])
                    fw.op("act", lambda e: e.activation(out=ff[:, :], in_=ff[:, :], func=AF.Ln, bias=1.0),
                          r=[R("ff")], w=[R("ff")])
                    fw.op("dve", lambda e: e.memset(prefx[:, 0:8], 0.0), w=[R("prefx")])
                    for b in range(1, 9):
                        fw.op("dve", lambda e, b=b: e.tensor_tensor(
                            out=prefx[:, b * 8:b * 8 + 8], in0=prefx[:, b * 8 - 8:b * 8], in1=ff[:, b * 8 - 8:b * 8],
                            op=ALU.add), r=[R("prefx"), R("ff")], w=[R("prefx")])
                    cb = 6 + nxt("aux", 2)
                    mm(ps[cb][:, 0:64], tri[:, :], ff[:, :], True, True, r=[R("tri"), R("ff")], w=[R("ps", cb)])
                    mm(ps[cb][:, 64:136], ones_f[:, :], prefx[:, :], True, True, r=[R("onesf"), R("prefx")],
                       w=[R("ps", cb)])
                    fw.op("act", lambda e: e.activation(out=Rsb[:, :], in_=ps[cb][:, 64:136], func=AF.Copy),
                          r=[R("ps", cb)], w=[R("Rsb")])
                    fw.op("dve", lambda e: e.tensor_tensor(out=Dtok[:, :], in0=ps[cb][:, 0:64], in1=Rsb[:, 0:64],
                                                           op=ALU.add), r=[R("ps", cb), R("Rsb")], w=[R("Dtok")])
                    for j in range(2):
                        rc = (4 * j + 2) * 8
                        for kb in range(8):
                            fw.op("dve", lambda e, j=j, kb=kb, rc=rc: e.tensor_tensor(
                                out=bias_own[:, j, kb, :], in0=Dtok[:, kb * 8:kb * 8 + 8], in1=Rsb[:, rc:rc + 8],
                                op=ALU.subtract), r=[R("Dtok"), R("Rsb")], w=[R("bias_own")])
                    if hf == 0:
                        for kb in range(8):
                            fw.op("dve", lambda e, kb=kb: e.tensor_tensor(
                                out=rem_t[:, kb * 8:kb * 8 + 8], in0=Rsb[:, 64:72], in1=Dtok[:, kb * 8:kb * 8 + 8],
                                op=ALU.subtract), r=[R("Dtok"), R("Rsb")], w=[R("rem_t")])
                        fw.dma("sp", rem0, rem_t[:, :], r=[R("rem_t")], w=[R("rem0")])
                    else:
                        fw.dma("sp", rem_t[:, :], rem0, r=[R("rem0")], w=[R("rem_t")])
                        for j in range(2):
                            rc = (4 * j + 2) * 8
                            for kb in range(8):
                                fw.op("dve", lambda e, j=j, kb=kb, rc=rc: e.scalar_tensor_tensor(
                                    out=bias_oth[:, j, kb, :], in0=rem_t[:, kb * 8:kb * 8 + 8], scalar=-1.0,
                                    in1=Rsb[:, rc:rc + 8], op0=ALU.mult, op1=ALU.subtract),
                                    r=[R("rem_t"), R("Rsb")], w=[R("bias_oth")])
            mark("z_gates")
            for sl in range(24):
                slab, sres = wget("w_in", li, 0, 16, C_GATE + sl * 256, 256)
                for m in range(2):
                    gc = sl * 2 + m
                    for tt in range(2):
                        bank = nxt("main", 6)
                        proj_fm(slab, sres, m * 128, 128, 16, hr(tt), hres, tt, bank)
                        evac_store(bank, 128, 512, gat[gc][:, tt * 512:tt * 512 + 512], R("gat", gc, tt),
                                   func=AF.Sigmoid)

            mark("mla_prep")
            cqs = lambda c, a, b: big[:, 8 + c, a:b]
            cqr = lambda c, tt: R("big", 8 + c, tt)
            rmsnorm_fm(cqs, cqr, cqs, cqr, 4, T, G_CQ, 2)
            cks = lambda c, a, b: big[:, 12 + c, a:b]
            ckr = lambda c, tt: R("big", 12 + c, tt)
            rmsnorm_fm(cks, ckr, cks, ckr, 2, T, G_CKV, 2)
            for tt in range(2):
                tsl = slice(tt * 512, tt * 512 + 512)
                gsl = slice(tok0 + tt * 512, tok0 + tt * 512 + 512)
                t1, t2 = nxt("tmp", 2), nxt("tmp", 2)
                fw.dma("sp", tmp_t[t1][0:64, :], ctab_d[:, gsl], w=[R("tmp", t1)])
                fw.dma("sp", tmp_t[t2][0:64, :], stab_d[:, gsl], w=[R("tmp", t2)])
                fw.op("dve", lambda e, t1=t1, tsl=tsl: e.scalar_tensor_tensor(
                    out=tmp_t[t1][0:64, :], in0=big[0:64, 14, tsl], scalar=G[0:64, G_MKR:G_MKR + 1],
                    in1=tmp_t[t1][0:64, :], op0=ALU.mult, op1=ALU.mult),
                    r=[R("big", 14, tt), R("G"), R("tmp", t1)], w=[R("tmp", t1)])
                fw.op("dve", lambda e, t2=t2, tsl=tsl: e.scalar_tensor_tensor(
                    out=tmp_t[t2][0:64, :], in0=big[0:64, 15, tsl], scalar=G[0:64, G_MKS:G_MKS + 1],
                    in1=tmp_t[t2][0:64, :], op0=ALU.mult, op1=ALU.mult),
                    r=[R("big", 15, tt), R("G"), R("tmp", t2)], w=[R("tmp", t2)])
                fw.op("dve", lambda e, t1=t1, t2=t2, tsl=tsl: e.tensor_tensor(
                    out=kbase[:, tsl], in0=tmp_t[t1][0:64, :], in1=tmp_t[t2][0:64, :], op=ALU.add),
                    r=[R("tmp", t1), R("tmp", t2)], w=[R("kbase", tt)])
                fw.op("act", lambda e, tsl=tsl: e.activation(out=sqkr[:, tsl], in_=big[0:64, 14, tsl], func=AF.Square),
                      r=[R("big", 14, tt)], w=[R("sqkr", tt)])
            qit = [(sl, hh, tt) for sl in range(2) for hh in range(4) for tt in range(2)]
            qslab = {}
            qbanks = {}

            def q_s1(i):
                sl, hh, tt = qit[i]
                if sl not in qslab:
                    qslab[sl] = wget("w_uqx", li, 0, 4, sl * 1024, 1024)
                slab, sres = qslab[sl]
                tsl = slice(tt * 512, tt * 512 + 512)
                rhs = lambda kc: big[:, 8 + kc, tsl]
                bA, b1, b2 = nxt("main", 6), nxt("main", 6), nxt("main", 6)
                proj_fm(slab, sres, hh * 256, 128, 4, rhs, cqr, tt, bA)
                proj_fm(slab, sres, hh * 256 + 128, 64, 4, rhs, cqr, tt, b1)
                proj_fm(slab, sres, hh * 256 + 192, 64, 4, rhs, cqr, tt, b2)
                qbanks[i] = (bA, b1, b2)

            def q_s2(i):
                sl, hh, tt = qit[i]
                hd = sl * 4 + hh
                bA, b1, b2 = qbanks[i]
                tsl = slice(tt * 512, tt * 512 + 512)
                gsl = slice(tok0 + tt * 512, tok0 + tt * 512 + 512)
                si = nxt("sq", 3)
                fw.op("act", lambda e: e.activation(out=sq_t[si][0:64, :], in_=ps[b1][0:64, :], func=AF.Square),
                      r=[R("ps", b1)], w=[R("sq", si)])
                t1, t2 = nxt("tmp", 2), nxt("tmp", 2)
                fw.dma("sp", tmp_t[t1][0:64, :], ctab_d[:, gsl], w=[R("tmp", t1)])
                fw.dma("sp", tmp_t[t2][0:64, :], stab_d[:, gsl], w=[R("tmp", t2)])
                ri = head_norm(bA, 512, G_MQN, qTn["m"][hd][:, tsl], R("qTn", "m", hd),
                               extra=(sq_t[si][0:64, :], R("sq", si)), div=192.0)
                fw.op("pool", lambda e: e.tensor_tensor(
                    out=tmp_t[t1][0:64, :], in0=tmp_t[t1][0:64, :], in1=rs_t[ri][0:64, :], op=ALU.mult),
                    r=[R("tmp", t1), R("rs", ri)], w=[R("tmp", t1)])
                fw.op("pool", lambda e: e.tensor_tensor(
                    out=tmp_t[t2][0:64, :], in0=tmp_t[t2][0:64, :], in1=rs_t[ri][0:64, :], op=ALU.mult),
                    r=[R("tmp", t2), R("rs", ri)], w=[R("tmp", t2)])
                fw.op("dve", lambda e: e.scalar_tensor_tensor(
                    out=tmp_t[t1][0:64, :], in0=ps[b1][0:64, :], scalar=G[0:64, G_MQR:G_MQR + 1],
                    in1=tmp_t[t1][0:64, :], op0=ALU.mult, op1=ALU.mult),
                    r=[R("ps", b1), R("G"), R("tmp", t1)], w=[R("tmp", t1)])
                fw.op("dve", lambda e: e.scalar_tensor_tensor(
                    out=tmp_t[t2][0:64, :], in0=ps[b2][0:64, :], scalar=G[0:64, G_MQS:G_MQS + 1],
                    in1=tmp_t[t2][0:64, :], op0=ALU.mult, op1=ALU.mult),
                    r=[R("ps", b2), R("G"), R("tmp", t2)], w=[R("tmp", t2)])
                sti = nxt("st", 4)
                fw.op("pool", lambda e: e.tensor_tensor(
                    out=st_t[sti][0:64, :], in0=tmp_t[t1][0:64, :], in1=tmp_t[t2][0:64, :], op=ALU.add),
                    r=[R("tmp", t1), R("tmp", t2)], w=[R("st", sti)])
                fw.dma("sp", qTr[hd][:, tsl], st_t[sti][0:64, :], r=[R("st", sti)], w=[R("qTr", hd)])

            q_s1(0)
            for i in range(len(qit)):
                if i + 1 < len(qit):
                    q_s1(i + 1)
                q_s2(i)
            slab, sres = wget("w_ukv", li, 0, 2, 0, 2048)
            kit = [(hd, tt) for hd in range(8) for tt in range(2)]
            kbank = {}

            def k_s1(i):
                hd, tt = kit[i]
                tsl = slice(tt * 512, tt * 512 + 512)
                bA = nxt("main", 6)
                proj_fm(slab, sres, hd * 256, 128, 2, lambda kc: big[:, 12 + kc, tsl], ckr, tt, bA)
                kbank[i] = bA

            def k_s2(i):
                hd, tt = kit[i]
                bA = kbank[i]
                tsl = slice(tt * 512, tt * 512 + 512)
                gsl = slice(tok0 + tt * 512, tok0 + tt * 512 + 512)
                ri = head_norm(bA, 512, G_MKN, kTn["m"][hd][:, gsl], R("kTn", "m", hd),
                               extra=(sqkr[:, tsl], R("sqkr", tt)), div=192.0)
                sti = nxt("st", 4)
                fw.op("pool", lambda e: e.tensor_tensor(
                    out=st_t[sti][0:64, :], in0=kbase[:, tsl], in1=rs_t[ri][0:64, :], op=ALU.mult),
                    r=[R("kbase", tt), R("rs", ri)], w=[R("st", sti)])
                fw.dma("sp", kTr[hd][:, gsl], st_t[sti][0:64, :], r=[R("st", sti)], w=[R("kTr", hd)])

            k_s1(0)
            k_s1(1)
            for i in range(len(kit)):
                if i + 2 < len(kit):
                    k_s1(i + 2)
                k_s2(i)
            for tb in range(8):
                for hg in range(2):
                    bank = nxt("main", 6)
                    for hh in range(4):
                        hd = hg * 4 + hh
                        for kc in range(2):
                            mm(ps[bank][:, hh * 128:hh * 128 + 128], big[:, 12 + kc, tb * 128:tb * 128 + 128],
                               slab[:, kc, hd * 256 + 128:hd * 256 + 256], kc == 0, kc == 1,
                               r=[sres, R("big", 12 + kc, tb // 4)], w=[R("ps", bank)])
                    evac_store(bank, 128, 512, vv["m"][tok0 + tb * 128:tok0 + tb * 128 + 128, hg * 512:hg * 512 + 512],
                               R("v", "m"))

            for n, br in enumerate(("m", "f", "c")):
                mark("att_" + br)
                attention(br, hf, li)
                mark("merge_" + br)
                merge(n, li)
            mark("w_out")
            for sl in range(8):
                slab, sres = wget("w_out", li, 0, 16, sl * 256, 256)
                for m in range(2):
                    c = sl * 2 + m
                    for tt in range(2):
                        bank = nxt("main", 6)
                        proj_fm(slab, sres, m * 128, 128, 16, hr(tt), hres, tt, bank)
                        add_to_x(bank, c, tt)

            mark("cross")
            rmsnorm_fm(lambda c, a, b: x[:, c, a:b], xres, lambda c, a, b: h[:, c, a:b], hres, 16, T, G_CROSS, 2)
            fw.dma("pool", big[:, :, 0:256], memT.rearrange("(c p) m -> p c m", p=128),
                   w=[R("big", c, 0) for c in range(16)])
            mems = lambda c, a, b: big[:, c, a:b]
            memr = lambda c, tt: R("big", c, 0)
            rmsnorm_fm(mems, memr, mems, memr, 16, 256, G_MEM, 1)
            for sl in range(4):
                slab, sres = wget("w_xkv", li, 0, 16, sl * 256, 256)
                if sl < 2:
                    for m in range(2):
                        hd = sl * 2 + m
                        bank = nxt("main", 6)
                        proj_fm(slab, sres, m * 128, 128, 16, lambda kc: big[:, kc, 0:256], memr, 0, bank, ncol=256)
                        head_norm(bank, 256, G_XK, None, None, sb_dst=big[:, hd, 256:512], sb_res=R("xk", hd))
                else:
                    for mb in range(2):
                        bank = nxt("main", 6)
                        for kc in range(16):
                            mm(ps[bank][:, 0:256], big[:, kc, mb * 128:mb * 128 + 128], slab[:, kc, :], kc == 0, kc == 15,
                               r=[sres, R("big", kc, 0)], w=[R("ps", bank)])
                        c0 = 512 + (sl - 2) * 256
                        fw.op("act", lambda e, bank=bank, mb=mb, c0=c0: e.activation(
                            out=big[:, mb, c0:c0 + 256], in_=ps[bank][:, 0:256], func=AF.Copy),
                            r=[R("ps", bank)], w=[R("xv", mb, sl - 2)])
            for sl in range(2):
                slab, sres = wget("w_xq", li, 0, 16, sl * 256, 256)
                for m in range(2):
                    hd = sl * 2 + m
                    for tt in range(2):
                        bank = nxt("main", 6)
                        proj_fm(slab, sres, m * 128, 128, 16, hr(tt), hres, tt, bank)
                        head_norm(bank, 512, G_XQ, None, None, sb_dst=big[:, 4 + hd, tt * 512:tt * 512 + 512],
                                  sb_res=R("xq", hd, tt))
            xscale = 128.0 ** -0.5
            xseq = [(hd, j, mb) for hd in range(4) for j in range(2) for mb in range(2)]
            xsb = {}
            xol = {}

            def x_qk(t):
                hd, j, mb = xseq[t]
                sbk = nxt("sbank", 4)
                xsb[t] = sbk
                mm(ps[sbk][:, :], big[:, hd, 256 + mb * 128:256 + mb * 128 + 128], big[:, 4 + hd, j * 512:j * 512 + 512],
                   True, True, r=[R("xk", hd), R("xq", hd, j)], w=[R("ps", sbk)])

            def x_rest(t):
                hd, j, mb = xseq[t]
                sbk = xsb[t]
                pi = nxt("p", 3)
                fw.op("act", lambda e: e.activation(out=p_t[pi][:, :], in_=ps[sbk][:, :], func=AF.Exp, scale=xscale),
                      r=[R("ps", sbk)], w=[R("p", pi)])
                if t + 2 < len(xseq):
                    x_qk(t + 2)
                if mb == 0:
                    oi = nxt("ol", 2)
                    xol[(hd, j)] = (4 + 2 * oi, 5 + 2 * oi)
                ob, lb = xol[(hd, j)]
                mm(ps[ob][:, :], big[:, mb, 512 + hd * 128:512 + hd * 128 + 128], p_t[pi][:, :], mb == 0, mb == 1,
                   r=[R("xv", mb, hd // 2), R("p", pi)], w=[R("ps", ob)])
                mm(ps[lb][:, :], ones_bf[:, :], p_t[pi][:, :], mb == 0, mb == 1,
                   r=[R("ones"), R("p", pi)], w=[R("ps", lb)])
                if mb == 1:
                    qsl = slice(j * 512, j * 512 + 512)
                    ri = nxt("rs", 2)
                    fw.op("dve", lambda e: e.reciprocal(out=rs_t[ri][:, :], in_=ps[lb][:, :]),
                          r=[R("ps", lb)], w=[R("rs", ri)])
                    fw.op("dve", lambda e: e.tensor_tensor(
                        out=big[:, 8 + hd, qsl], in0=ps[ob][:, :], in1=rs_t[ri][:, :], op=ALU.mult),
                        r=[R("ps", ob), R("rs", ri)], w=[R("xo", hd, j)])

            x_qk(0)
            x_qk(1)
            for t in range(len(xseq)):
                x_rest(t)
            for sl in range(4):
                slab, sres = wget("w_xo", li, 0, 4, sl * 512, 512)
                for m in range(4):
                    c = sl * 4 + m
                    for tt in range(2):
                        tsl = slice(tt * 512, tt * 512 + 512)
                        bank = nxt("main", 4)
                        proj_fm(slab, sres, m * 128, 128, 4, lambda kc: big[:, 8 + kc, tsl],
                                lambda kc, tt_: R("xo", kc, tt_), tt, bank)
                        add_to_x(bank, c, tt)

            mark("mlp")
            rmsnorm_fm(lambda c, a, b: x[:, c, a:b], xres, lambda c, a, b: h[:, c, a:b], hres, 16, T, G_MLP, 2)
            fence([("big", c, tt) for c in range(16) for tt in range(2)],
                  ("big", "xk", "xv", "xq", "xo"))
            for fs in range(4):
                for sl in range(8):
                    slab, sres = wget("w_1", li, 0, 16, fs * 2048 + sl * 256, 256)
                    for m in range(2):
                        uc = sl * 2 + m
                        for tt in range(2):
                            tsl = slice(tt * 512, tt * 512 + 512)
                            bank = nxt("main", 6)
                            proj_fm(slab, sres, m * 128, 128, 16, hr(tt), hres, tt, bank)
                            ti = nxt("tmp", 2)
                            fw.op("act", lambda e, ti=ti, bank=bank: e.activation(out=tmp_t[ti][:, :], in_=ps[bank][:, :],
                                                                                func=AF.Relu),
                                  r=[R("ps", bank)], w=[R("tmp", ti)])
                            fw.op(ve("mlp"), lambda e, ti=ti, uc=uc, tsl=tsl: e.tensor_tensor(
                                out=big[:, uc, tsl], in0=tmp_t[ti][:, :], in1=tmp_t[ti][:, :], op=ALU.mult),
                                r=[R("tmp", ti)], w=[R("big", uc, tt)])
                for sl in range(8):
                    slab, sres = wget("w_2", li, fs * 16, 16, sl * 256, 256)
                    for m in range(2):
                        c = sl * 2 + m
                        for tt in range(2):
                            tsl = slice(tt * 512, tt * 512 + 512)
                            bank = nxt("main", 6)
                            proj_fm(slab, sres, m * 128, 128, 16, lambda kc: big[:, kc, tsl],
                                    lambda kc, tt_: R("big", kc, tt_), tt, bank)
                            add_to_x(bank, c, tt)
            fw.dma("sp", dst_ap[:, tok0:tok0 + T].rearrange("(c p) t -> p c t", p=128), x[:, :, :],
                   r=[R("x", c, tt) for c in range(16) for tt in range(2)], w=[R("xs", hf)], is_out=is_final)

        def emit_all():
            rot.clear()
            wstate["i"] = 0
            emit_consts()
            for li in range(nl):
                for hf in range(2):
                    src = xT_in if li == 0 else xs
                    fin = li == nl - 1
                    emit_pass(li, hf, src, oT if fin else xs, fin)
            fw.finish()

        fw.dry = True
        emit_all()
        fw.dry = False
        emit_all()
    return nc


def _host_consts():
    bf = ml_dtypes.bfloat16
    s = np.arange(128)[:, None]
    t = np.arange(512)[None, :]
    fm = np.concatenate([((o * 128 + s) <= t) for o in range(4)], axis=1).astype(bf)
    mm_ = np.concatenate([(((o * 128 + s) // 64) <= (t // 64)) for o in range(4)], axis=1).astype(bf)
    u = np.arange(EW)[None, :]
    qc = (u - 384) // 64
    kc = s // 64
    band = ((kc >= qc - 8) & (kc <= qc)).astype(bf)
    tri = (s <= np.arange(128)[None, :]).astype(np.float32)
    pos = np.arange(S, dtype=np.float32)
    inv = (10000.0 ** (-np.arange(0, 64, 2, dtype=np.float32) / 64)).astype(np.float32)
    ang = pos[:, None] * inv[None, :]
    cos, sin = np.cos(ang).astype(np.float32).T, np.sin(ang).astype(np.float32).T
    ctab = np.ascontiguousarray(np.concatenate([cos, cos], 0))
    stab = np.ascontiguousarray(np.concatenate([-sin, sin], 0))
    return dict(fmask=np.ascontiguousarray(fm), mmask=np.ascontiguousarray(mm_), band=np.ascontiguousarray(band),
                tri=tri, ctab=ctab, stab=stab)


def _host_layer_tables(inp, ls):
    L = len(ls)
    G = np.zeros((L, 128, NG), np.float32)
    sw = np.concatenate([np.arange(32, 64), np.arange(0, 32)])
    for i, l in enumerate(ls):
        def fm(v, n):
            return np.asarray(v).reshape(n, 128).T
        G[i, :, G_MIX:G_MIX + 16] = fm(inp["g_mix"][l], 16)
        G[i, :, G_CROSS:G_CROSS + 16] = fm(inp["g_cross"][l], 16)
        G[i, :, G_MLP:G_MLP + 16] = fm(inp["g_mlp"][l], 16)
        G[i, :, G_MEM:G_MEM + 16] = fm(inp["g_mem"][l], 16)
        G[i, :, G_CQ:G_CQ + 4] = fm(inp["g_cq"][l], 4)
        G[i, :, G_CKV:G_CKV + 2] = fm(inp["g_ckv"][l], 2)
        gq, gk = np.asarray(inp["g_mla_q"][l]), np.asarray(inp["g_mla_k"][l])
        G[i, :, G_MQN] = gq[:128]
        G[i, :64, G_MQR] = gq[128:]
        G[i, :64, G_MQS] = gq[128:][sw]
        G[i, :, G_MKN] = gk[:128]
        G[i, :64, G_MKR] = gk[128:]
        G[i, :64, G_MKS] = gk[128:][sw]
        for col, nm in ((G_FQ, "g_fox_q"), (G_FK, "g_fox_k"), (G_CQ2, "g_ch_q"), (G_CK2, "g_ch_k"),
                        (G_XQ, "g_x_q"), (G_XK, "g_x_k")):
            G[i, :, col] = np.asarray(inp[nm][l])
        G[i, :, G_BF:G_BF + 64] = np.tile(np.asarray(inp["b_f"][l])[None, :], (128, 8))
    sidx = np.arange(128)[:, None]
    uidx = np.arange(EW)[None, :]
    idx = np.clip(uidx - 384 - sidx, -128, 128) + 128
    Tz = np.ascontiguousarray(np.stack([np.asarray(inp["rel_bias"][l])[:, idx] for l in ls], 0)).astype(np.float32)
    w_uq = np.stack([np.asarray(inp["w_uq"][l]) for l in ls], 0)
    cols = []
    for hd in range(8):
        b = hd * 192
        cols += list(range(b, b + 128)) + list(range(b + 128, b + 192)) + list(b + 128 + sw)
    w_uqx = np.ascontiguousarray(w_uq[:, :, cols])
    krc = 768 + np.concatenate([np.arange(64), sw])
    w_krx = np.ascontiguousarray(np.stack([np.asarray(inp["w_in"][l][:, krc]) for l in ls], 0))
    return G, Tz, w_uqx, w_krx


_CACHE = {}
PHASES = []


def _get_program(nl, first, last):
    key = (nl, first, last)
    if key not in _CACHE:
        _CACHE[key] = build_program(nl, first, last)
    return _CACHE[key]


def _run_layers(inp, xT_list, ls, consts):
    nl = len(ls)
    nc = _get_program(nl, True, True)
    G, Tz, w_uqx, w_krx = _host_layer_tables(inp, ls)
    sel = (lambda a: np.ascontiguousarray(np.asarray(a)[ls[0]:ls[-1] + 1]))
    shared = dict(w_in=sel(inp["w_in"]), w_krx=w_krx, w_uqx=w_uqx, w_ukv=sel(inp["w_ukv"]), w_br=sel(inp["w_br"]),
                  w_out=sel(inp["w_out"]), w_xq=sel(inp["w_xq"]), w_xkv=sel(inp["w_xkv"]), w_xo=sel(inp["w_xo"]),
                  w_1=sel(inp["w_1"]), w_2=sel(inp["w_2"]), G=G, Tz=Tz, **consts)
    in_maps = []
    mem = np.asarray(inp["mem"])
    for c in range(len(xT_list)):
        m = dict(shared)
        m["xT"] = xT_list[c]
        m["memT"] = np.ascontiguousarray(mem[c % mem.shape[0]].T)
        in_maps.append(m)
    res = run_bass_kernel_spmd(nc, in_maps, core_ids=list(range(len(xT_list))))
    return [np.asarray(r["oT"]) for r in res.results]


LAUNCH_GROUPS = [[0, 1, 2, 3]]


def kernel(**inp):
    x = np.asarray(inp["x"])
    consts = _host_consts()
    xT = [np.ascontiguousarray(x[b].T) for b in range(NCORES)]
    for ls in LAUNCH_GROUPS:
        xT = _run_layers(inp, xT, ls, consts)
    out = np.stack([xT[b].T for b in range(4)], 0).astype(np.float32)
    return np.ascontiguousarray(out)
```

```python
import numpy as np
import ml_dtypes
import concourse.bass as bass
import concourse.mybir as mybir
from concourse.bass_utils import run_bass_kernel_spmd
from contextlib import ExitStack
from bisect import bisect_left

F32 = mybir.dt.float32
BF16 = mybir.dt.bfloat16
AF = mybir.ActivationFunctionType
ALU = mybir.AluOpType

D = 2048
S = 2048
T = 1024
DEPTH = 4
D_IN = 13128
EPS = 1e-6
C_FOX = 832
C_FOXF = 3904
C_CH = 3912
C_GATE = 6984
NCORES = 4
G_MIX, G_CROSS, G_MLP, G_MEM, G_CQ, G_CKV = 0, 16, 32, 48, 64, 68
G_MQN, G_MQR, G_MQS, G_MKN, G_MKR, G_MKS = 70, 71, 72, 73, 74, 75
G_FQ, G_FK, G_CQ2, G_CK2, G_XQ, G_XK = 76, 77, 78, 79, 80, 81
G_BF = 82
NG = 146
EW = 1408


class Res:
    __slots__ = ("w", "wd", "re", "rd")

    def __init__(self):
        self.w = None
        self.wd = []
        self.re = {}
        self.rd = []


class EngQ:
    def __init__(self, name, eng, sem):
        self.name, self.eng, self.sem = name, eng, sem
        self.n = 0
        self.count = 0
        self.inc_idx = []
        self.inc_cnt = []
        self.last_handle = None
        self.last_idx = 0
        self.waited = {}

    def ensure_inc(self, idx):
        pos = bisect_left(self.inc_idx, idx)
        if pos < len(self.inc_idx):
            return self.inc_cnt[pos]
        assert self.last_idx >= idx
        self.count += 1
        self.last_handle.then_inc(self.sem, 1)
        self.inc_idx.append(self.last_idx)
        self.inc_cnt.append(self.count)
        return self.count


class FW:
    NDS = 24

    def __init__(self, nc, es):
        self.nc = nc
        self.dry = False
        self.engs = {}
        for name, eng in [("pe", nc.tensor), ("act", nc.scalar), ("dve", nc.vector),
                          ("pool", nc.gpsimd), ("sp", nc.sync)]:
            self.engs[name] = EngQ(name, eng, es.enter_context(nc.semaphore("s_" + name)))
        self.dsem = [es.enter_context(nc.semaphore("s_d%d" % i)) for i in range(self.NDS)]
        self.dcnt = [0] * self.NDS
        self.di = 0
        self.out_tokens = []

    def _wait(self, E, tok):
        if tok[0] == "e":
            Sq = self.engs[tok[1]]
            if Sq is E and E.name == "pe":
                return
            cnt = Sq.ensure_inc(tok[2])
            if E.waited.get(Sq.name, 0) >= cnt:
                return
            E.eng.wait_ge(Sq.sem, cnt)
            E.waited[Sq.name] = cnt
        else:
            key = ("d", tok[1])
            if E.waited.get(key, 0) >= tok[2]:
                return
            E.eng.wait_ge(self.dsem[tok[1]], tok[2])
            E.waited[key] = tok[2]

    def _deps(self, E, r, w, dma_write=False):
        deps = []
        for x in r:
            if x.w is not None:
                deps.append(x.w)
            deps.extend(x.wd)
        for x in w:
            if x.w is not None:
                deps.append(x.w)
            if not dma_write:
                deps.extend(x.wd)
            deps.extend(x.re.values())
            deps.extend(x.rd)
        for t in deps:
            self._wait(E, t)

    def _mark(self, tok, r, w):
        for x in r:
            if tok[0] == "e":
                x.re[tok[1]] = tok
            else:
                x.rd.append(tok)
        for x in w:
            if tok[0] == "e":
                x.w = tok
                x.wd = []
            else:
                x.w = None
                x.wd.append(tok)
                if len(x.wd) > 64:
                    x.wd = x.wd[-64:]
            x.re = {}
            x.rd = []

    def op(self, ename, fn, r=(), w=()):
        if self.dry:
            return
        E = self.engs[ename]
        self._deps(E, r, w)
        h = fn(E.eng)
        E.n += 1
        E.last_handle = h
        E.last_idx = E.n
        self._mark(("e", ename, E.n), r, w)

    def dma(self, qname, out, in_, r=(), w=(), is_out=False):
        if self.dry:
            return
        E = self.engs[qname]
        self._deps(E, r, w, dma_write=True)
        i = self.di % self.NDS
        self.di += 1
        if self.dcnt[i] > 0:
            self._wait(E, ("d", i, self.dcnt[i]))
        self.dcnt[i] += 16
        E.eng.dma_start(out=out, in_=in_).then_inc(self.dsem[i], 16)
        tok = ("d", i, self.dcnt[i])
        self._mark(tok, r, w)
        if is_out:
            self.out_tokens.append(tok)

    def finish(self):
        E = self.engs["sp"]
        for tok in self.out_tokens:
            self._wait(E, tok)


def build_program(nl, first, last, dbg=False):
    nc = bass.Bass("TRN2", target_bir_lowering=False)

    def din(name, shape, dt=F32):
        return nc.dram_tensor(name, shape, dt, kind="ExternalInput").ap()

    def dscr(name, shape, dt):
        return nc.dram_tensor(name, shape, dt, kind="Internal").ap()

    xT_in = din("xT", [D, S])
    memT = din("memT", [D, 256])
    w_in = din("w_in", [nl, D, D_IN])
    w_krx = din("w_krx", [nl, D, 128])
    w_uqx = din("w_uqx", [nl, 512, 2048])
    w_ukv = din("w_ukv", [nl, 256, 2048])
    w_br = din("w_br", [nl, 3, 1024, D])
    w_out = din("w_out", [nl, D, D])
    w_xq = din("w_xq", [nl, D, 512])
    w_xkv = din("w_xkv", [nl, D, 1024])
    w_xo = din("w_xo", [nl, 512, D])
    w_1 = din("w_1", [nl, D, 8192])
    w_2 = din("w_2", [nl, 8192, D])
    Gd = din("G", [nl, 128, NG])
    Tzd = din("Tz", [nl, 8, 128, EW])
    ctab_d = din("ctab", [64, S])
    stab_d = din("stab", [64, S])
    fmask_d = din("fmask", [128, 2048], BF16)
    mmask_d = din("mmask", [128, 2048], BF16)
    band_d = din("band", [128, EW], BF16)
    tri_d = din("tri", [128, 128])
    ident_d = din("ident", [128, 128], BF16)
    oT = nc.dram_tensor("oT", [D, S], F32, kind="ExternalOutput").ap()

    xs = dscr("xs", [D, S], F32)
    qTn = {b: dscr("qTn_" + b, [8, 128, T], BF16) for b in ("m", "f", "c")}
    qTr = dscr("qTr_m", [8, 64, T], BF16)
    kTn = {b: dscr("kTn_" + b, [8, 128, S], BF16) for b in ("m", "f", "c")}
    kTr = dscr("kTr_m", [8, 64, S], BF16)
    vv = {b: dscr("v_" + b, [S, 1024], BF16) for b in ("m", "f", "c")}
    gat = dscr("gat", [48, 128, T], BF16)
    rem0 = dscr("rem0", [128, 64], F32)

    es = ExitStack()
    with es:
        fw = FW(nc, es)

        def sb(name, shape, dt):
            return es.enter_context(nc.sbuf_tensor("sb_" + name, shape, dt))

        x = sb("x", [128, 16, T], F32)
        h = sb("h", [128, 16, T], BF16)
        big = sb("big", [128, 16, T], BF16)
        NSLOT = 3
        wr = [sb("wr%d" % i, [128, 4096], BF16) for i in range(NSLOT)]
        qn_t = sb("qn_t", [128, T], BF16)
        qr_t = sb("qr_t", [64, T], BF16)
        kn_t = sb("kn_t", [128, S], BF16)
        kr_t = sb("kr_t", [64, S], BF16)
        v_t = sb("v_t", [128, 16, 128], BF16)
        sq_t = [sb("sq%d" % i, [128, 512], BF16) for i in range(3)]
        rs_t = [sb("rs%d" % i, [128, 512], F32) for i in range(2)]
        tmp_t = [sb("tmp%d" % i, [128, 512], F32) for i in range(2)]
        st_t = [sb("st%d" % i, [128, 512], BF16) for i in range(4)]
        gt_t = [sb("gt%d" % i, [128, 512], BF16) for i in range(2)]
        p_t = [sb("p%d" % i, [128, 512], BF16) for i in range(4)]
        ME = sb("ME", [128, 3 * EW], BF16)
        mask_t = ME[:, 0:2048]
        band_t = ME[:, 0:EW]
        E_ts = [ME[:, EW:2 * EW], ME[:, 2 * EW:3 * EW]]
        kbase = sb("kbase", [64, T], BF16)
        sqkr = sb("sqkr", [64, T], BF16)
        G = sb("G", [128, NG], F32)
        ones_bf = sb("ones_bf", [128, 128], BF16)
        ones_f = sb("ones_f", [128, 128], F32)
        tri = sb("tri", [128, 128], F32)
        ident = sb("ident", [128, 128], BF16)
        ff = sb("ff", [128, 64], F32)
        prefx = sb("prefx", [128, 72], F32)
        Dtok = sb("Dtok", [128, 64], F32)
        Rsb = sb("Rsb", [128, 72], F32)
        rem_t = sb("rem_t", [128, 64], F32)
        bias_own = sb("bias_own", [128, 2, 8, 8], F32)
        bias_oth = sb("bias_oth", [128, 2, 8, 8], F32)
        ps = [es.enter_context(nc.psum_tensor("pp%d" % i, [128, 512], F32)) for i in range(8)]

        RR = {}

        def R(*key):
            r = RR.get(key)
            if r is None:
                r = RR[key] = Res()
            return r

        def fence(new_keys, old_prefixes):
            toks = []
            for k, r in RR.items():
                if k[0] in old_prefixes:
                    if r.w is not None:
                        toks.append(r.w)
                    toks.extend(r.wd)
                    toks.extend(r.re.values())
                    toks.extend(r.rd)
            for k in new_keys:
                r = R(*k)
                for t in toks:
                    if t[0] == "e":
                        old = r.re.get(t[1])
                        if old is None or old[2] < t[2]:
                            r.re[t[1]] = t
                    else:
                        r.rd.append(t)

        rot = {}

        def mark(name):
            if not fw.dry:
                PHASES.append((name, fw.engs["pe"].n))

        def nxt(name, n):
            i = rot.get(name, 0)
            rot[name] = i + 1
            return i % n

        plan = []
        wstate = {"i": 0, "issued": 0}
        wsrc = {"w_in": w_in, "w_krx": w_krx, "w_uqx": w_uqx, "w_ukv": w_ukv, "w_out": w_out,
                "w_xq": w_xq, "w_xkv": w_xkv, "w_xo": w_xo, "w_1": w_1, "w_2": w_2}

        def issue_slab(j):
            key = plan[j]
            name, li, k0, nk, c0, ncol = key
            if name.startswith("w_br"):
                src = w_br[li, int(name[4])]
            else:
                src = wsrc[name][li]
            slot = wr[j % NSLOT]
            fw.dma("pool", slot[:, 0:nk * ncol].rearrange("p (k c) -> p k c", k=nk),
                   src[k0 * 128:(k0 + nk) * 128, c0:c0 + ncol].rearrange("(k p) c -> p k c", p=128),
                   w=[R("wr", j % NSLOT)])

        def wget(name, li, k0, nk, c0, ncol):
            key = (name, li, k0, nk, c0, ncol)
            i = wstate["i"]
            wstate["i"] = i + 1
            if fw.dry:
                plan.append(key)
            else:
                assert plan[i] == key, (plan[i], key)
                while wstate["issued"] < min(len(plan), i + NSLOT):
                    issue_slab(wstate["issued"])
                    wstate["issued"] += 1
            slot = wr[i % NSLOT]
            return slot[:, 0:nk * ncol].rearrange("p (k c) -> p k c", k=nk), R("wr", i % NSLOT)

        def mm(out, lhsT, rhs, start, stop, r, w):
            fw.op("pe", lambda e: e.matmul(out, lhsT, rhs, start=start, stop=stop), r=r, w=w)

        def rstd_from(bank, npart, ncol, scale, rdeps):
            ri = nxt("rs", 2)
            rs = rs_t[ri]
            fw.op("act", lambda e: e.activation(out=rs[0:npart, 0:ncol], in_=ps[bank][0:npart, 0:ncol],
                                                func=AF.Sqrt, bias=EPS, scale=scale),
                  r=[R("ps", bank)], w=[R("rs", ri)])
            fw.op("dve", lambda e: e.reciprocal(out=rs[0:npart, 0:ncol], in_=rs[0:npart, 0:ncol]),
                  r=[R("rs", ri)], w=[R("rs", ri)])
            return ri

        def rmsnorm_fm(src, src_res, dst, dst_res, nch, ncols, gcol, ntile):
            for tt in range(ntile):
                t0, t1 = tt * 512, min(ncols, tt * 512 + 512)
                n = t1 - t0
                bank = 6 + nxt("aux", 2)
                for c in range(nch):
                    si = nxt("sq", 3)
                    fw.op("act", lambda e, c=c, si=si: e.activation(out=sq_t[si][:, 0:n], in_=src(c, t0, t1),
                                                                  func=AF.Square),
                          r=[src_res(c, tt)], w=[R("sq", si)])
                    mm(ps[bank][:, 0:n], ones_bf[:, :], sq_t[si][:, 0:n], c == 0, c == nch - 1,
                       r=[R("sq", si), R("ones")], w=[R("ps", bank)])
                ri = rstd_from(bank, 128, n, 1.0 / (nch * 128), None)
                for c in range(nch):
                    fw.op("dve", lambda e, c=c: e.scalar_tensor_tensor(
                        out=dst(c, t0, t1), in0=src(c, t0, t1), scalar=G[:, gcol + c:gcol + c + 1],
                        in1=rs_t[ri][:, 0:n], op0=ALU.mult, op1=ALU.mult),
                        r=[src_res(c, tt), R("rs", ri), R("G")], w=[dst_res(c, tt)])

        def head_norm(bank, n, gcol, dst_ap, dst_res, extra=None, div=128.0, sb_dst=None, sb_res=None):
            si = nxt("sq", 3)
            fw.op("act", lambda e: e.activation(out=sq_t[si][:, 0:n], in_=ps[bank][:, 0:n], func=AF.Square),
                  r=[R("ps", bank)], w=[R("sq", si)])
            ab = 6 + nxt("aux", 2)
            mm(ps[ab][:, 0:n], ones_bf[:, :], sq_t[si][:, 0:n], True, extra is None,
               r=[R("sq", si), R("ones")], w=[R("ps", ab)])
            if extra is not None:
                mm(ps[ab][:, 0:n], ones_bf[0:64, :], extra[0], False, True,
                   r=[extra[1], R("ones")], w=[R("ps", ab)])
            ri = rstd_from(ab, 128, n, 1.0 / div, None)
            if sb_dst is not None:
                fw.op("dve", lambda e: e.scalar_tensor_tensor(
                    out=sb_dst, in0=ps[bank][:, 0:n], scalar=G[:, gcol:gcol + 1], in1=rs_t[ri][:, 0:n],
                    op0=ALU.mult, op1=ALU.mult), r=[R("ps", bank), R("rs", ri), R("G")], w=[sb_res])
            else:
                sti = nxt("st", 4)
                fw.op("dve", lambda e: e.scalar_tensor_tensor(
                    out=st_t[sti][:, 0:n], in0=ps[bank][:, 0:n], scalar=G[:, gcol:gcol + 1],
                    in1=rs_t[ri][:, 0:n], op0=ALU.mult, op1=ALU.mult),
                    r=[R("ps", bank), R("rs", ri), R("G")], w=[R("st", sti)])
                fw.dma("sp", dst_ap, st_t[sti][:, 0:n], r=[R("st", sti)], w=[dst_res])
            return ri

        def evac_store(bank, npart, n, dst_ap, dst_res, func=AF.Copy):
            sti = nxt("st", 4)
            fw.op("act", lambda e: e.activation(out=st_t[sti][0:npart, 0:n], in_=ps[bank][0:npart, 0:n], func=func),
                  r=[R("ps", bank)], w=[R("st", sti)])
            fw.dma("sp", dst_ap, st_t[sti][0:npart, 0:n], r=[R("st", sti)], w=[dst_res])

        def proj_fm(slab, slab_res, m0, msz, nk, rhs, rhs_res, tt, bank, ncol=512):
            for kc in range(nk):
                mm(ps[bank][0:msz, 0:ncol], slab[:, kc, m0:m0 + msz], rhs(kc), kc == 0, kc == nk - 1,
                   r=[slab_res, rhs_res(kc, tt)], w=[R("ps", bank)])

        hres = lambda c, tt: R("h", c, tt)
        xres = lambda c, tt: R("x", c, tt)

        def emit_consts():
            fw.op("dve", lambda e: e.memset(ones_bf[:, :], 1.0), w=[R("ones")])
            fw.op("dve", lambda e: e.memset(ones_f[:, :], 1.0), w=[R("onesf")])
            fw.dma("sp", tri[:, :], tri_d, w=[R("tri")])
            fw.dma("sp", ident[:, :], ident_d, w=[R("ident")])

        def attention(br, hf, li):
            scale = (192.0 if br == "m" else 128.0) ** -0.5
            if br == "c":
                kb_lo = max(0, hf * 8 - 4)
            else:
                kb_lo = 0
            kb_hi = hf * 8 + 8
            nkb = kb_hi - kb_lo
            if br == "m":
                fence([("qn", 1), ("qr", 1), ("kn", 1), ("kr", 1), ("vt", 1)], ("big",))
            if br in ("m", "f"):
                fence([("mask",)], ("mask", "band", "E"))
                fw.dma("sp", mask_t, mmask_d if br == "m" else fmask_d, w=[R("mask")])
            else:
                fence([("band",), ("E", 0), ("E", 1)], ("mask", "band", "E"))
                fw.dma("sp", band_t, band_d, w=[R("band")])
            bufs = [dict(qn=qn_t, qr=qr_t, kn=kn_t, kr=kr_t, v=v_t),
                    dict(qn=big[:, 8, :], qr=big[0:64, 9, :],
                         kn=big[:, 10:12, :].rearrange("p a t -> p (a t)"),
                         kr=big[0:64, 12:14, :].rearrange("p a t -> p (a t)"),
                         v=big[:, 14:16, :].rearrange("p a (b d) -> p (a b) d", d=128))]

            def load_head(hd, si):
                bs = bufs[si]
                fw.dma("sp", bs["qn"][:, :], qTn[br][hd], r=[R("qTn", br, hd)], w=[R("qn", si)])
                fw.dma("sp", bs["kn"][:, 0:nkb * 128], kTn[br][hd][:, kb_lo * 128:kb_hi * 128],
                       r=[R("kTn", br, hd)], w=[R("kn", si)])
                if br == "m":
                    fw.dma("sp", bs["qr"][:, :], qTr[hd], r=[R("qTr", hd)], w=[R("qr", si)])
                    fw.dma("sp", bs["kr"][:, 0:nkb * 128], kTr[hd][:, kb_lo * 128:kb_hi * 128],
                           r=[R("kTr", hd)], w=[R("kr", si)])
                fw.dma("sp", bs["v"][:, 0:nkb, :],
                       vv[br][kb_lo * 128:kb_hi * 128, hd * 128:(hd + 1) * 128].rearrange("(b p) d -> p b d", p=128),
                       r=[R("v", br)], w=[R("vt", si)])
                pend = []
                if br == "c":
                    for q0 in range(0, EW, 512):
                        q1 = min(EW, q0 + 512)
                        ti = nxt("tmp", 2)
                        fw.dma("sp", tmp_t[ti][:, 0:q1 - q0], Tzd[li, hd][:, q0:q1], w=[R("tmp", ti)])
                        pend.append((ti, q0, q1))
                return pend

            def post_load(si, pend):
                for ti, q0, q1 in pend:
                    fw.op("dve", lambda e: e.scalar_tensor_tensor(
                        out=E_ts[si][:, q0:q1], in0=tmp_t[ti][:, 0:q1 - q0], scalar=1.0 / scale,
                        in1=band_t[:, q0:q1], op0=ALU.mult, op1=ALU.add),
                        r=[R("tmp", ti), R("band")], w=[R("E", si)])

            pend0 = load_head(0, 0)
            post_load(0, pend0)
            for hd in range(8):
                si = hd % 2
                bs = bufs[si]
                qn_b, qr_b, kn_b, kr_b, v_b = bs["qn"], bs["qr"], bs["kn"], bs["kr"], bs["v"]
                E_t = E_ts[si]
                pend_next = load_head(hd + 1, 1 - si) if hd + 1 < 8 else None
                seq = []
                for j in range(2):
                    qs = hf * 8 + j * 4
                    if br == "c":
                        blocks = list(range(max(0, qs - 4), qs + 4))
                    else:
                        blocks = list(range(0, qs + 4))
                    for bi, kb in enumerate(blocks):
                        seq.append((j, bi, kb, len(blocks), qs))
                sbank_of = {}
                olb = {}

                def emit_qk(t):
                    j, bi, kb, nb, qs = seq[t]
                    kl = kb - kb_lo
                    sbk = nxt("sbank", 4)
                    sbank_of[t] = sbk
                    ksl = slice(kl * 128, kl * 128 + 128)
                    qsl = slice(j * 512, j * 512 + 512)
                    if br == "c":
                        add_ap, add_res = E_t[:, 896 - 128 * (kb - (qs - 4)):896 - 128 * (kb - (qs - 4)) + 512], R("E", si)
                    elif kb >= qs:
                        add_ap, add_res = mask_t[:, (kb - qs) * 512:(kb - qs) * 512 + 512], R("mask")
                    else:
                        add_ap = None
                    mm(ps[sbk][:, :], kn_b[:, ksl], qn_b[:, qsl], True, br != "m" and add_ap is None,
                       r=[R("kn", si), R("qn", si)], w=[R("ps", sbk)])
                    if br == "m":
                        mm(ps[sbk][:, :], kr_b[:, ksl], qr_b[:, qsl], False, add_ap is None,
                           r=[R("kr", si), R("qr", si)], w=[R("ps", sbk)])
                    if add_ap is not None:
                        mm(ps[sbk][:, :], ident[:, :], add_ap, False, True,
                           r=[R("ident"), add_res], w=[R("ps", sbk)])

                def emit_exp(t):
                    j, bi, kb, nb, qs = seq[t]
                    sbk = sbank_of[t]
                    pi = nxt("p", 4)
                    if br == "f":
                        if kb >= hf * 8:
                            bias_ap = bias_own[:, j, kb - hf * 8, hd:hd + 1]
                            bres = R("bias_own")
                        else:
                            bias_ap = bias_oth[:, j, kb, hd:hd + 1]
                            bres = R("bias_oth")
                        fw.op("act", lambda e: e.activation(
                            out=p_t[pi][:, :], in_=ps[sbk][:, :], func=AF.Exp, bias=bias_ap, scale=scale),
                            r=[R("ps", sbk), bres], w=[R("p", pi)])
                    else:
                        fw.op("act", lambda e: e.activation(
                            out=p_t[pi][:, :], in_=ps[sbk][:, :], func=AF.Exp, scale=scale),
                            r=[R("ps", sbk)], w=[R("p", pi)])
                    return pi

                def emit_pv(t, pi):
                    j, bi, kb, nb, qs = seq[t]
                    kl = kb - kb_lo
                    if bi == 0:
                        oi = nxt("ol", 2)
                        olb[j] = (4 + 2 * oi, 5 + 2 * oi)
                    ob, lb = olb[j]
                    first_b, last_b = bi == 0, bi == nb - 1
                    mm(ps[ob][:, :], v_b[:, kl, :], p_t[pi][:, :], first_b, last_b,
                       r=[R("vt", si), R("p", pi)], w=[R("ps", ob)])
                    mm(ps[lb][:, :], ones_bf[:, :], p_t[pi][:, :], first_b, last_b,
                       r=[R("ones"), R("p", pi)], w=[R("ps", lb)])
                    if last_b:
                        qsl = slice(j * 512, j * 512 + 512)
                        ri = nxt("rs", 2)
                        fw.op("dve", lambda e: e.reciprocal(out=rs_t[ri][:, :], in_=ps[lb][:, :]),
                              r=[R("ps", lb)], w=[R("rs", ri)])
                        fw.op("dve", lambda e: e.tensor_tensor(
                            out=big[:, hd, qsl], in0=ps[ob][:, :], in1=rs_t[ri][:, :], op=ALU.mult),
                            r=[R("ps", ob), R("rs", ri)], w=[R("big", hd, j)])

                LA = 3
                for t in range(min(LA, len(seq))):
                    emit_qk(t)
                for t in range(len(seq)):
                    pi = emit_exp(t)
                    if t + LA < len(seq):
                        emit_qk(t + LA)
                    emit_pv(t, pi)
                    if t == len(seq) // 2 and pend_next is not None:
                        post_load(1 - si, pend_next)
            if br == "c":
                fence([("big", c, tt) for c in range(8, 16) for tt in range(2)], ("qn", "qr", "kn", "kr", "vt"))

        def merge(n, li):
            for sl in range(4):
                slab, sres = wget("w_br%d" % n, li, 0, 8, sl * 512, 512)
                for m in range(4):
                    c = sl * 4 + m
                    for tt in range(2):
                        tsl = slice(tt * 512, tt * 512 + 512)
                        bank = nxt("main", 4)
                        proj_fm(slab, sres, m * 128, 128, 8, lambda kc: big[:, kc, tsl],
                                lambda kc, tt_: R("big", kc, tt_), tt, bank)
                        gi = nxt("gt", 2)
                        fw.dma("sp", gt_t[gi][:, :], gat[n * 16 + c][:, tsl], r=[R("gat", n * 16 + c, tt)],
                               w=[R("gt", gi)])
                        if n == 0:
                            fw.op("dve", lambda e, bank=bank, gi=gi, c=c, tsl=tsl: e.tensor_tensor(
                                out=h[:, c, tsl], in0=ps[bank][:, :], in1=gt_t[gi][:, :], op=ALU.mult),
                                r=[R("ps", bank), R("gt", gi)], w=[R("h", c, tt)])
                        else:
                            ti = nxt("tmp", 2)
                            fw.op("dve", lambda e, bank=bank, gi=gi, ti=ti: e.tensor_tensor(
                                out=tmp_t[ti][:, :], in0=ps[bank][:, :], in1=gt_t[gi][:, :], op=ALU.mult),
                                r=[R("ps", bank), R("gt", gi)], w=[R("tmp", ti)])
                            fw.op("dve", lambda e, ti=ti, c=c, tsl=tsl: e.tensor_tensor(
                                out=h[:, c, tsl], in0=h[:, c, tsl], in1=tmp_t[ti][:, :], op=ALU.add),
                                r=[R("tmp", ti), R("h", c, tt)], w=[R("h", c, tt)])

        def add_to_x(bank, c, tt):
            tsl = slice(tt * 512, tt * 512 + 512)
            fw.op("dve", lambda e: e.tensor_tensor(out=x[:, c, tsl], in0=x[:, c, tsl], in1=ps[bank][:, :],
                                                   op=ALU.add), r=[R("ps", bank), R("x", c, tt)], w=[R("x", c, tt)])

        def emit_pass(li, hf, src_ap, dst_ap, is_final):
            tok0 = hf * T
            mark("norm")
            fw.dma("sp", x[:, :, :], src_ap[:, tok0:tok0 + T].rearrange("(c p) t -> p c t", p=128),
                   r=[R("xs", hf)], w=[R("x", c, tt) for c in range(16) for tt in range(2)])
            fw.dma("sp", G[:, :], Gd[li], w=[R("G")])
            rmsnorm_fm(lambda c, a, b: x[:, c, a:b], xres, lambda c, a, b: h[:, c, a:b], hres, 16, T, G_MIX, 2)

            hr = lambda tt: (lambda kc: h[:, kc, tt * 512:tt * 512 + 512])
            mark("z_mla")
            for sl in range(3):
                slab, sres = wget("w_in", li, 0, 16, sl * 256, 256)
                for m in range(2):
                    zc = sl * 2 + m
                    for tt in range(2):
                        bank = nxt("main", 6)
                        proj_fm(slab, sres, m * 128, 128, 16, hr(tt), hres, tt, bank)
                        fw.op("act", lambda e, bank=bank, zc=zc, tt=tt: e.activation(
                            out=big[:, 8 + zc, tt * 512:tt * 512 + 512], in_=ps[bank][:, :], func=AF.Copy),
                            r=[R("ps", bank)], w=[R("big", 8 + zc, tt)])
            slab, sres = wget("w_krx", li, 0, 16, 0, 128)
            for which in range(2):
                for tt in range(2):
                    bank = nxt("main", 6)
                    proj_fm(slab, sres, which * 64, 64, 16, hr(tt), hres, tt, bank)
                    fw.op("act", lambda e, bank=bank, which=which, tt=tt: e.activation(
                        out=big[0:64, 14 + which, tt * 512:tt * 512 + 512], in_=ps[bank][0:64, :], func=AF.Copy),
                        r=[R("ps", bank)], w=[R("big", 14 + which, tt)])
            mark("z_qkv")
            for br, cbase, gq, gk in (("f", C_FOX, G_FQ, G_FK), ("c", C_CH, G_CQ2, G_CK2)):
                for sl in range(8):
                    slab, sres = wget("w_in", li, 0, 16, cbase + sl * 256, 256)
                    for m in range(2):
                        hidx = sl * 2 + m
                        for tt in range(2):
                            bank = nxt("main", 6)
                            proj_fm(slab, sres, m * 128, 128, 16, hr(tt), hres, tt, bank)
                            if hidx < 8:
                                head_norm(bank, 512, gq, qTn[br][hidx][:, tt * 512:tt * 512 + 512],
                                          R("qTn", br, hidx))
                            else:
                                head_norm(bank, 512, gk,
                                          kTn[br][hidx - 8][:, tok0 + tt * 512:tok0 + tt * 512 + 512],
                                          R("kTn", br, hidx - 8))
                for sl in range(4):
                    slab, sres = wget("w_in", li, 0, 16, cbase + 2048 + sl * 256, 256)
                    for tb in range(8):
                        bank = nxt("main", 6)
                        for kc in range(16):
                            mm(ps[bank][:, 0:256], h[:, kc, tb * 128:tb * 128 + 128], slab[:, kc, :], kc == 0, kc == 15,
                               r=[sres, R("h", kc, tb // 4)], w=[R("ps", bank)])
                        evac_store(bank, 128, 256, vv[br][tok0 + tb * 128:tok0 + tb * 128 + 128, sl * 256:sl * 256 + 256],
                                   R("v", br))
                if br == "f":
                    slab, sres = wget("w_in", li, 0, 16, C_FOXF, 8)
                    fbank = 6 + nxt("aux", 2)
                    for tb in range(8):
                        for kc in range(16):
                            mm(ps[fbank][:, tb * 8:tb * 8 + 8], h[:, kc, tb * 128:tb * 128 + 128], slab[:, kc, :],
                               kc == 0, kc == 15, r=[sres, R("h", kc, tb // 4)], w=[R("ps", fbank)])
                    fw.op("dve", lambda e: e.tensor_tensor(out=ff[:, :], in0=ps[fbank][:, 0:64], in1=G[:, G_BF:G_BF + 64],
                                                           op=ALU.add), r=[R("ps", fbank), R("G")], w=[R("ff")])
                    fw.op("act", lambda e: e.activation(out=ff[:, :], in_=ff[:, :], func=AF.Exp, scale=-1.0),
                          r=[R("ff")], w=[R("ff")])
                    fw.op("act", lambda e: e.activation(out=ff[:, :], in_=ff[:, :], func=AF.Ln, bias=1.0),
                          r=[R("ff")], w=[R("ff")])
                    fw.op("dve", lambda e: e.memset(prefx[:, 0:8], 0.0), w=[R("prefx")])
                    for b in range(1, 9):
                        fw.op("dve", lambda e, b=b: e.tensor_tensor(
                            out=prefx[:, b * 8:b * 8 + 8], in0=prefx[:, b * 8 - 8:b * 8], in1=ff[:, b * 8 - 8:b * 8],
                            op=ALU.add), r=[R("prefx"), R("ff")], w=[R("prefx")])
                    cb = 6 + nxt("aux", 2)
                    mm(ps[cb][:, 0:64], tri[:, :], ff[:, :], True, True, r=[R("tri"), R("ff")], w=[R("ps", cb)])
                    mm(ps[cb][:, 64:136], ones_f[:, :], prefx[:, :], True, True, r=[R("onesf"), R("prefx")],
                       w=[R("ps", cb)])
                    fw.op("act", lambda e: e.activation(out=Rsb[:, :], in_=ps[cb][:, 64:136], func=AF.Copy),
                          r=[R("ps", cb)], w=[R("Rsb")])
                    fw.op("dve", lambda e: e.tensor_tensor(out=Dtok[:, :], in0=ps[cb][:, 0:64], in1=Rsb[:, 0:64],
                                                           op=ALU.add), r=[R("ps", cb), R("Rsb")], w=[R("Dtok")])
                    for j in range(2):
                        rc = (4 * j + 2) * 8
                        for kb in range(8):
                            fw.op("dve", lambda e, j=j, kb=kb, rc=rc: e.tensor_tensor(
                                out=bias_own[:, j, kb, :], in0=Dtok[:, kb * 8:kb * 8 + 8], in1=Rsb[:, rc:rc + 8],
                                op=ALU.subtract), r=[R("Dtok"), R("Rsb")], w=[R("bias_own")])
                    if hf == 0:
                        for kb in range(8):
                            fw.op("dve", lambda e, kb=kb: e.tensor_tensor(
                                out=rem_t[:, kb * 8:kb * 8 + 8], in0=Rsb[:, 64:72], in1=Dtok[:, kb * 8:kb * 8 + 8],
                                op=ALU.subtract), r=[R("Dtok"), R("Rsb")], w=[R("rem_t")])
                        fw.dma("sp", rem0, rem_t[:, :], r=[R("rem_t")], w=[R("rem0")])
                    else:
                        fw.dma("sp", rem_t[:, :], rem0, r=[R("rem0")], w=[R("rem_t")])
                        for j in range(2):
                            rc = (4 * j + 2) * 8
                            for kb in range(8):
                                fw.op("dve", lambda e, j=j, kb=kb, rc=rc: e.scalar_tensor_tensor(
                                    out=bias_oth[:, j, kb, :], in0=rem_t[:, kb * 8:kb * 8 + 8], scalar=-1.0,
                                    in1=Rsb[:, rc:rc + 8], op0=ALU.mult, op1=ALU.subtract),
                                    r=[R("rem_t"), R("Rsb")], w=[R("bias_oth")])
            mark("z_gates")
            for sl in range(24):
                slab, sres = wget("w_in", li, 0, 16, C_GATE + sl * 256, 256)
                for m in range(2):
                    gc = sl * 2 + m
                    for tt in range(2):
                        bank = nxt("main", 6)
                        proj_fm(slab, sres, m * 128, 128, 16, hr(tt), hres, tt, bank)
                        evac_store(bank, 128, 512, gat[gc][:, tt * 512:tt * 512 + 512], R("gat", gc, tt),
                                   func=AF.Sigmoid)

            mark("mla_prep")
            cqs = lambda c, a, b: big[:, 8 + c, a:b]
            cqr = lambda c, tt: R("big", 8 + c, tt)
            rmsnorm_fm(cqs, cqr, cqs, cqr, 4, T, G_CQ, 2)
            cks = lambda c, a, b: big[:, 12 + c, a:b]
            ckr = lambda c, tt: R("big", 12 + c, tt)
            rmsnorm_fm(cks, ckr, cks, ckr, 2, T, G_CKV, 2)
            for tt in range(2):
                tsl = slice(tt * 512, tt * 512 + 512)
                gsl = slice(tok0 + tt * 512, tok0 + tt * 512 + 512)
                t1, t2 = nxt("tmp", 2), nxt("tmp", 2)
                fw.dma("sp", tmp_t[t1][0:64, :], ctab_d[:, gsl], w=[R("tmp", t1)])
                fw.dma("sp", tmp_t[t2][0:64, :], stab_d[:, gsl], w=[R("tmp", t2)])
                fw.op("dve", lambda e, t1=t1, tsl=tsl: e.scalar_tensor_tensor(
                    out=tmp_t[t1][0:64, :], in0=big[0:64, 14, tsl], scalar=G[0:64, G_MKR:G_MKR + 1],
                    in1=tmp_t[t1][0:64, :], op0=ALU.mult, op1=ALU.mult),
                    r=[R("big", 14, tt), R("G"), R("tmp", t1)], w=[R("tmp", t1)])
                fw.op("dve", lambda e, t2=t2, tsl=tsl: e.scalar_tensor_tensor(
                    out=tmp_t[t2][0:64, :], in0=big[0:64, 15, tsl], scalar=G[0:64, G_MKS:G_MKS + 1],
                    in1=tmp_t[t2][0:64, :], op0=ALU.mult, op1=ALU.mult),
                    r=[R("big", 15, tt), R("G"), R("tmp", t2)], w=[R("tmp", t2)])
                fw.op("dve", lambda e, t1=t1, t2=t2, tsl=tsl: e.tensor_tensor(
                    out=kbase[:, tsl], in0=tmp_t[t1][0:64, :], in1=tmp_t[t2][0:64, :], op=ALU.add),
                    r=[R("tmp", t1), R("tmp", t2)], w=[R("kbase", tt)])
                fw.op("act", lambda e, tsl=tsl: e.activation(out=sqkr[:, tsl], in_=big[0:64, 14, tsl], func=AF.Square),
                      r=[R("big", 14, tt)], w=[R("sqkr", tt)])
            qit = [(sl, hh, tt) for sl in range(2) for hh in range(4) for tt in range(2)]
            qslab = {}
            qbanks = {}

            def q_s1(i):
                sl, hh, tt = qit[i]
                if sl not in qslab:
                    qslab[sl] = wget("w_uqx", li, 0, 4, sl * 1024, 1024)
                slab, sres = qslab[sl]
                tsl = slice(tt * 512, tt * 512 + 512)
                rhs = lambda kc: big[:, 8 + kc, tsl]
                bA, b1, b2 = nxt("main", 6), nxt("main", 6), nxt("main", 6)
                proj_fm(slab, sres, hh * 256, 128, 4, rhs, cqr, tt, bA)
                proj_fm(slab, sres, hh * 256 + 128, 64, 4, rhs, cqr, tt, b1)
                proj_fm(slab, sres, hh * 256 + 192, 64, 4, rhs, cqr, tt, b2)
                qbanks[i] = (bA, b1, b2)

            def q_s2(i):
                sl, hh, tt = qit[i]
                hd = sl * 4 + hh
                bA, b1, b2 = qbanks[i]
                tsl = slice(tt * 512, tt * 512 + 512)
                gsl = slice(tok0 + tt * 512, tok0 + tt * 512 + 512)
                si = nxt("sq", 3)
                fw.op("act", lambda e: e.activation(out=sq_t[si][0:64, :], in_=ps[b1][0:64, :], func=AF.Square),
                      r=[R("ps", b1)], w=[R("sq", si)])
                t1, t2 = nxt("tmp", 2), nxt("tmp", 2)
                fw.dma("sp", tmp_t[t1][0:64, :], ctab_d[:, gsl], w=[R("tmp", t1)])
                fw.dma("sp", tmp_t[t2][0:64, :], stab_d[:, gsl], w=[R("tmp", t2)])
                ri = head_norm(bA, 512, G_MQN, qTn["m"][hd][:, tsl], R("qTn", "m", hd),
                               extra=(sq_t[si][0:64, :], R("sq", si)), div=192.0)
                fw.op("dve", lambda e: e.tensor_tensor(
                    out=tmp_t[t1][0:64, :], in0=tmp_t[t1][0:64, :], in1=rs_t[ri][0:64, :], op=ALU.mult),
                    r=[R("tmp", t1), R("rs", ri)], w=[R("tmp", t1)])
                fw.op("dve", lambda e: e.tensor_tensor(
                    out=tmp_t[t2][0:64, :], in0=tmp_t[t2][0:64, :], in1=rs_t[ri][0:64, :], op=ALU.mult),
                    r=[R("tmp", t2), R("rs", ri)], w=[R("tmp", t2)])
                fw.op("dve", lambda e: e.scalar_tensor_tensor(
                    out=tmp_t[t1][0:64, :], in0=ps[b1][0:64, :], scalar=G[0:64, G_MQR:G_MQR + 1],
                    in1=tmp_t[t1][0:64, :], op0=ALU.mult, op1=ALU.mult),
                    r=[R("ps", b1), R("G"), R("tmp", t1)], w=[R("tmp", t1)])
                fw.op("dve", lambda e: e.scalar_tensor_tensor(
                    out=tmp_t[t2][0:64, :], in0=ps[b2][0:64, :], scalar=G[0:64, G_MQS:G_MQS + 1],
                    in1=tmp_t[t2][0:64, :], op0=ALU.mult, op1=ALU.mult),
                    r=[R("ps", b2), R("G"), R("tmp", t2)], w=[R("tmp", t2)])
                sti = nxt("st", 4)
                fw.op("dve", lambda e: e.tensor_tensor(
                    out=st_t[sti][0:64, :], in0=tmp_t[t1][0:64, :], in1=tmp_t[t2][0:64, :], op=ALU.add),
                    r=[R("tmp", t1), R("tmp", t2)], w=[R("st", sti)])
                fw.dma("sp", qTr[hd][:, tsl], st_t[sti][0:64, :], r=[R("st", sti)], w=[R("qTr", hd)])

            q_s1(0)
            for i in range(len(qit)):
                if i + 1 < len(qit):
                    q_s1(i + 1)
                q_s2(i)
            slab, sres = wget("w_ukv", li, 0, 2, 0, 2048)
            kit = [(hd, tt) for hd in range(8) for tt in range(2)]
            kbank = {}

            def k_s1(i):
                hd, tt = kit[i]
                tsl = slice(tt * 512, tt * 512 + 512)
                bA = nxt("main", 6)
                proj_fm(slab, sres, hd * 256, 128, 2, lambda kc: big[:, 12 + kc, tsl], ckr, tt, bA)
                kbank[i] = bA

            def k_s2(i):
                hd, tt = kit[i]
                bA = kbank[i]
                tsl = slice(tt * 512, tt * 512 + 512)
                gsl = slice(tok0 + tt * 512, tok0 + tt * 512 + 512)
                ri = head_norm(bA, 512, G_MKN, kTn["m"][hd][:, gsl], R("kTn", "m", hd),
                               extra=(sqkr[:, tsl], R("sqkr", tt)), div=192.0)
                sti = nxt("st", 4)
                fw.op("dve", lambda e: e.tensor_tensor(
                    out=st_t[sti][0:64, :], in0=kbase[:, tsl], in1=rs_t[ri][0:64, :], op=ALU.mult),
                    r=[R("kbase", tt), R("rs", ri)], w=[R("st", sti)])
                fw.dma("sp", kTr[hd][:, gsl], st_t[sti][0:64, :], r=[R("st", sti)], w=[R("kTr", hd)])

            k_s1(0)
            k_s1(1)
            for i in range(len(kit)):
                if i + 2 < len(kit):
                    k_s1(i + 2)
                k_s2(i)
            for tb in range(8):
                for hg in range(2):
                    bank = nxt("main", 6)
                    for hh in range(4):
                        hd = hg * 4 + hh
                        for kc in range(2):
                            mm(ps[bank][:, hh * 128:hh * 128 + 128], big[:, 12 + kc, tb * 128:tb * 128 + 128],
                               slab[:, kc, hd * 256 + 128:hd * 256 + 256], kc == 0, kc == 1,
                               r=[sres, R("big", 12 + kc, tb // 4)], w=[R("ps", bank)])
                    evac_store(bank, 128, 512, vv["m"][tok0 + tb * 128:tok0 + tb * 128 + 128, hg * 512:hg * 512 + 512],
                               R("v", "m"))

            for n, br in enumerate(("m", "f", "c")):
                mark("att_" + br)
                attention(br, hf, li)
                mark("merge_" + br)
                merge(n, li)
            mark("w_out")
            for sl in range(8):
                slab, sres = wget("w_out", li, 0, 16, sl * 256, 256)
                for m in range(2):
                    c = sl * 2 + m
                    for tt in range(2):
                        bank = nxt("main", 6)
                        proj_fm(slab, sres, m * 128, 128, 16, hr(tt), hres, tt, bank)
                        add_to_x(bank, c, tt)

            mark("cross")
            rmsnorm_fm(lambda c, a, b: x[:, c, a:b], xres, lambda c, a, b: h[:, c, a:b], hres, 16, T, G_CROSS, 2)
            fw.dma("pool", big[:, :, 0:256], memT.rearrange("(c p) m -> p c m", p=128),
                   w=[R("big", c, 0) for c in range(16)])
            mems = lambda c, a, b: big[:, c, a:b]
            memr = lambda c, tt: R("big", c, 0)
            rmsnorm_fm(mems, memr, mems, memr, 16, 256, G_MEM, 1)
            for sl in range(4):
                slab, sres = wget("w_xkv", li, 0, 16, sl * 256, 256)
                if sl < 2:
                    for m in range(2):
                        hd = sl * 2 + m
                        bank = nxt("main", 6)
                        proj_fm(slab, sres, m * 128, 128, 16, lambda kc: big[:, kc, 0:256], memr, 0, bank, ncol=256)
                        head_norm(bank, 256, G_XK, None, None, sb_dst=big[:, hd, 256:512], sb_res=R("xk", hd))
                else:
                    for mb in range(2):
                        bank = nxt("main", 6)
                        for kc in range(16):
                            mm(ps[bank][:, 0:256], big[:, kc, mb * 128:mb * 128 + 128], slab[:, kc, :], kc == 0, kc == 15,
                               r=[sres, R("big", kc, 0)], w=[R("ps", bank)])
                        c0 = 512 + (sl - 2) * 256
                        fw.op("act", lambda e, bank=bank, mb=mb, c0=c0: e.activation(
                            out=big[:, mb, c0:c0 + 256], in_=ps[bank][:, 0:256], func=AF.Copy),
                            r=[R("ps", bank)], w=[R("xv", mb, sl - 2)])
            for sl in range(2):
                slab, sres = wget("w_xq", li, 0, 16, sl * 256, 256)
                for m in range(2):
                    hd = sl * 2 + m
                    for tt in range(2):
                        bank = nxt("main", 6)
                        proj_fm(slab, sres, m * 128, 128, 16, hr(tt), hres, tt, bank)
                        head_norm(bank, 512, G_XQ, None, None, sb_dst=big[:, 4 + hd, tt * 512:tt * 512 + 512],
                                  sb_res=R("xq", hd, tt))
            xscale = 128.0 ** -0.5
            xseq = [(hd, j, mb) for hd in range(4) for j in range(2) for mb in range(2)]
            xsb = {}
            xol = {}

            def x_qk(t):
                hd, j, mb = xseq[t]
                sbk = nxt("sbank", 4)
                xsb[t] = sbk
                mm(ps[sbk][:, :], big[:, hd, 256 + mb * 128:256 + mb * 128 + 128], big[:, 4 + hd, j * 512:j * 512 + 512],
                   True, True, r=[R("xk", hd), R("xq", hd, j)], w=[R("ps", sbk)])

            def x_rest(t):
                hd, j, mb = xseq[t]
                sbk = xsb[t]
                pi = nxt("p", 4)
                fw.op("act", lambda e: e.activation(out=p_t[pi][:, :], in_=ps[sbk][:, :], func=AF.Exp, scale=xscale),
                      r=[R("ps", sbk)], w=[R("p", pi)])
                if t + 2 < len(xseq):
                    x_qk(t + 2)
                if mb == 0:
                    oi = nxt("ol", 2)
                    xol[(hd, j)] = (4 + 2 * oi, 5 + 2 * oi)
                ob, lb = xol[(hd, j)]
                mm(ps[ob][:, :], big[:, mb, 512 + hd * 128:512 + hd * 128 + 128], p_t[pi][:, :], mb == 0, mb == 1,
                   r=[R("xv", mb, hd // 2), R("p", pi)], w=[R("ps", ob)])
                mm(ps[lb][:, :], ones_bf[:, :], p_t[pi][:, :], mb == 0, mb == 1,
                   r=[R("ones"), R("p", pi)], w=[R("ps", lb)])
                if mb == 1:
                    qsl = slice(j * 512, j * 512 + 512)
                    ri = nxt("rs", 2)
                    fw.op("dve", lambda e: e.reciprocal(out=rs_t[ri][:, :], in_=ps[lb][:, :]),
                          r=[R("ps", lb)], w=[R("rs", ri)])
                    fw.op("dve", lambda e: e.tensor_tensor(
                        out=big[:, 8 + hd, qsl], in0=ps[ob][:, :], in1=rs_t[ri][:, :], op=ALU.mult),
                        r=[R("ps", ob), R("rs", ri)], w=[R("xo", hd, j)])

            x_qk(0)
            x_qk(1)
            for t in range(len(xseq)):
                x_rest(t)
            for sl in range(4):
                slab, sres = wget("w_xo", li, 0, 4, sl * 512, 512)
                for m in range(4):
                    c = sl * 4 + m
                    for tt in range(2):
                        tsl = slice(tt * 512, tt * 512 + 512)
                        bank = nxt("main", 4)
                        proj_fm(slab, sres, m * 128, 128, 4, lambda kc: big[:, 8 + kc, tsl],
                                lambda kc, tt_: R("xo", kc, tt_), tt, bank)
                        add_to_x(bank, c, tt)

            mark("mlp")
            rmsnorm_fm(lambda c, a, b: x[:, c, a:b], xres, lambda c, a, b: h[:, c, a:b], hres, 16, T, G_MLP, 2)
            fence([("big", c, tt) for c in range(16) for tt in range(2)],
                  ("big", "xk", "xv", "xq", "xo"))
            for fs in range(4):
                for sl in range(8):
                    slab, sres = wget("w_1", li, 0, 16, fs * 2048 + sl * 256, 256)
                    for m in range(2):
                        uc = sl * 2 + m
                        for tt in range(2):
                            tsl = slice(tt * 512, tt * 512 + 512)
                            bank = nxt("main", 6)
                            proj_fm(slab, sres, m * 128, 128, 16, hr(tt), hres, tt, bank)
                            ti = nxt("tmp", 2)
                            fw.op("act", lambda e, ti=ti, bank=bank: e.activation(out=tmp_t[ti][:, :], in_=ps[bank][:, :],
                                                                                func=AF.Relu),
                                  r=[R("ps", bank)], w=[R("tmp", ti)])
                            fw.op("dve", lambda e, ti=ti, uc=uc, tsl=tsl: e.tensor_tensor(
                                out=big[:, uc, tsl], in0=tmp_t[ti][:, :], in1=tmp_t[ti][:, :], op=ALU.mult),
                                r=[R("tmp", ti)], w=[R("big", uc, tt)])
                for sl in range(8):
                    slab, sres = wget("w_2", li, fs * 16, 16, sl * 256, 256)
                    for m in range(2):
                        c = sl * 2 + m
                        for tt in range(2):
                            tsl = slice(tt * 512, tt * 512 + 512)
                            bank = nxt("main", 6)
                            proj_fm(slab, sres, m * 128, 128, 16, lambda kc: big[:, kc, tsl],
                                    lambda kc, tt_: R("big", kc, tt_), tt, bank)
                            add_to_x(bank, c, tt)
            fw.dma("sp", dst_ap[:, tok0:tok0 + T].rearrange("(c p) t -> p c t", p=128), x[:, :, :],
                   r=[R("x", c, tt) for c in range(16) for tt in range(2)], w=[R("xs", hf)], is_out=is_final)

        def emit_all():
            rot.clear()
            wstate["i"] = 0
            emit_consts()
            for li in range(nl):
                for hf in range(2):
                    src = xT_in if li == 0 else xs
                    fin = li == nl - 1
                    emit_pass(li, hf, src, oT if fin else xs, fin)
            fw.finish()

        fw.dry = True
        emit_all()
        fw.dry = False
        emit_all()
    return nc


def _host_consts():
    bf = ml_dtypes.bfloat16
    s = np.arange(128)[:, None]
    t = np.arange(512)[None, :]
    NEGM = np.float32(-30000.0)
    fm = np.where(np.concatenate([((o * 128 + s) <= t) for o in range(4)], axis=1), np.float32(0), NEGM).astype(bf)
    mm_ = np.where(np.concatenate([(((o * 128 + s) // 64) <= (t // 64)) for o in range(4)], axis=1),
                   np.float32(0), NEGM).astype(bf)
    u = np.arange(EW)[None, :]
    qc = (u - 384) // 64
    kc = s // 64
    band = np.where((kc >= qc - 8) & (kc <= qc), np.float32(0), NEGM).astype(bf)
    tri = (s <= np.arange(128)[None, :]).astype(np.float32)
    pos = np.arange(S, dtype=np.float32)
    inv = (10000.0 ** (-np.arange(0, 64, 2, dtype=np.float32) / 64)).astype(np.float32)
    ang = pos[:, None] * inv[None, :]
    cos, sin = np.cos(ang).astype(np.float32).T, np.sin(ang).astype(np.float32).T
    ctab = np.ascontiguousarray(np.concatenate([cos, cos], 0))
    stab = np.ascontiguousarray(np.concatenate([-sin, sin], 0))
    return dict(fmask=np.ascontiguousarray(fm), mmask=np.ascontiguousarray(mm_), band=np.ascontiguousarray(band),
                tri=tri, ctab=ctab, stab=stab, ident=np.eye(128, dtype=np.float32).astype(bf))


def _host_layer_tables(inp, ls):
    L = len(ls)
    G = np.zeros((L, 128, NG), np.float32)
    sw = np.concatenate([np.arange(32, 64), np.arange(0, 32)])
    for i, l in enumerate(ls):
        def fm(v, n):
            return np.asarray(v).reshape(n, 128).T
        G[i, :, G_MIX:G_MIX + 16] = fm(inp["g_mix"][l], 16)
        G[i, :, G_CROSS:G_CROSS + 16] = fm(inp["g_cross"][l], 16)
        G[i, :, G_MLP:G_MLP + 16] = fm(inp["g_mlp"][l], 16)
        G[i, :, G_MEM:G_MEM + 16] = fm(inp["g_mem"][l], 16)
        G[i, :, G_CQ:G_CQ + 4] = fm(inp["g_cq"][l], 4)
        G[i, :, G_CKV:G_CKV + 2] = fm(inp["g_ckv"][l], 2)
        gq, gk = np.asarray(inp["g_mla_q"][l]), np.asarray(inp["g_mla_k"][l])
        G[i, :, G_MQN] = gq[:128]
        G[i, :64, G_MQR] = gq[128:]
        G[i, :64, G_MQS] = gq[128:][sw]
        G[i, :, G_MKN] = gk[:128]
        G[i, :64, G_MKR] = gk[128:]
        G[i, :64, G_MKS] = gk[128:][sw]
        for col, nm in ((G_FQ, "g_fox_q"), (G_FK, "g_fox_k"), (G_CQ2, "g_ch_q"), (G_CK2, "g_ch_k"),
                        (G_XQ, "g_x_q"), (G_XK, "g_x_k")):
            G[i, :, col] = np.asarray(inp[nm][l])
        G[i, :, G_BF:G_BF + 64] = np.tile(np.asarray(inp["b_f"][l])[None, :], (128, 8))
    sidx = np.arange(128)[:, None]
    uidx = np.arange(EW)[None, :]
    idx = np.clip(uidx - 384 - sidx, -128, 128) + 128
    Tz = np.ascontiguousarray(np.stack([np.asarray(inp["rel_bias"][l])[:, idx] for l in ls], 0)).astype(np.float32)
    w_uq = np.stack([np.asarray(inp["w_uq"][l]) for l in ls], 0)
    cols = []
    for hd in range(8):
        b = hd * 192
        cols += list(range(b, b + 128)) + list(range(b + 128, b + 192)) + list(b + 128 + sw)
    w_uqx = np.ascontiguousarray(w_uq[:, :, cols])
    krc = 768 + np.concatenate([np.arange(64), sw])
    w_krx = np.ascontiguousarray(np.stack([np.asarray(inp["w_in"][l][:, krc]) for l in ls], 0))
    return G, Tz, w_uqx, w_krx


_CACHE = {}
PHASES = []


def _get_program(nl, first, last):
    key = (nl, first, last)
    if key not in _CACHE:
        _CACHE[key] = build_program(nl, first, last)
    return _CACHE[key]


def _run_layers(inp, xT_list, ls, consts):
    nl = len(ls)
    nc = _get_program(nl, True, True)
    G, Tz, w_uqx, w_krx = _host_layer_tables(inp, ls)
    sel = (lambda a: np.ascontiguousarray(np.asarray(a)[ls[0]:ls[-1] + 1]))
    shared = dict(w_in=sel(inp["w_in"]), w_krx=w_krx, w_uqx=w_uqx, w_ukv=sel(inp["w_ukv"]), w_br=sel(inp["w_br"]),
                  w_out=sel(inp["w_out"]), w_xq=sel(inp["w_xq"]), w_xkv=sel(inp["w_xkv"]), w_xo=sel(inp["w_xo"]),
                  w_1=sel(inp["w_1"]), w_2=sel(inp["w_2"]), G=G, Tz=Tz, **consts)
    in_maps = []
    mem = np.asarray(inp["mem"])
    for c in range(len(xT_list)):
        m = dict(shared)
        m["xT"] = xT_list[c]
        m["memT"] = np.ascontiguousarray(mem[c % mem.shape[0]].T)
        in_maps.append(m)
    res = run_bass_kernel_spmd(nc, in_maps, core_ids=list(range(len(xT_list))))
    return [np.asarray(r["oT"]) for r in res.results]


LAUNCH_GROUPS = [[0, 1, 2, 3]]


def kernel(**inp):
    x = np.asarray(inp["x"])
    consts = _host_consts()
    xT = [np.ascontiguousarray(x[b].T) for b in range(NCORES)]
    for ls in LAUNCH_GROUPS:
        xT = _run_layers(inp, xT, ls, consts)
    out = np.stack([xT[b].T for b in range(4)], 0).astype(np.float32)
    return np.ascontiguousarray(out)
```

```python
import numpy as np
import ml_dtypes
import concourse.bass as bass
import concourse.mybir as mybir
from concourse.bass_utils import run_bass_kernel_spmd
from contextlib import ExitStack
from bisect import bisect_left

F32 = mybir.dt.float32
BF16 = mybir.dt.bfloat16
AF = mybir.ActivationFunctionType
ALU = mybir.AluOpType

D = 2048
S = 2048
T = 1024
DEPTH = 4
D_IN = 13128
EPS = 1e-6
C_FOX = 832
C_FOXF = 3904
C_CH = 3912
C_GATE = 6984
NCORES = 4
G_MIX, G_CROSS, G_MLP, G_MEM, G_CQ, G_CKV = 0, 16, 32, 48, 64, 68
G_MQN, G_MQR, G_MQS, G_MKN, G_MKR, G_MKS = 70, 71, 72, 73, 74, 75
G_FQ, G_FK, G_CQ2, G_CK2, G_XQ, G_XK = 76, 77, 78, 79, 80, 81
G_BF = 82
NG = 146
EW = 1408


class Res:
    __slots__ = ("w", "wd", "re", "rd")

    def __init__(self):
        self.w = None
        self.wd = []
        self.re = {}
        self.rd = []


class EngQ:
    def __init__(self, name, eng, sem):
        self.name, self.eng, self.sem = name, eng, sem
        self.n = 0
        self.count = 0
        self.inc_idx = []
        self.inc_cnt = []
        self.last_handle = None
        self.last_idx = 0
        self.waited = {}

    def ensure_inc(self, idx):
        pos = bisect_left(self.inc_idx, idx)
        if pos < len(self.inc_idx):
            return self.inc_cnt[pos]
        assert self.last_idx >= idx
        self.count += 1
        self.last_handle.then_inc(self.sem, 1)
        self.inc_idx.append(self.last_idx)
        self.inc_cnt.append(self.count)
        return self.count


class FW:
    NDS = 24

    def __init__(self, nc, es):
        self.nc = nc
        self.dry = False
        self.engs = {}
        for name, eng in [("pe", nc.tensor), ("act", nc.scalar), ("dve", nc.vector),
                          ("pool", nc.gpsimd), ("sp", nc.sync)]:
            self.engs[name] = EngQ(name, eng, es.enter_context(nc.semaphore("s_" + name)))
        self.dsem = [es.enter_context(nc.semaphore("s_d%d" % i)) for i in range(self.NDS)]
        self.dcnt = [0] * self.NDS
        self.di = 0
        self.out_tokens = []

    def _wait(self, E, tok):
        if tok[0] == "e":
            Sq = self.engs[tok[1]]
            if Sq is E and E.name == "pe":
                return
            cnt = Sq.ensure_inc(tok[2])
            if E.waited.get(Sq.name, 0) >= cnt:
                return
            E.eng.wait_ge(Sq.sem, cnt)
            E.waited[Sq.name] = cnt
        else:
            key = ("d", tok[1])
            if E.waited.get(key, 0) >= tok[2]:
                return
            E.eng.wait_ge(self.dsem[tok[1]], tok[2])
            E.waited[key] = tok[2]

    def _deps(self, E, r, w, dma_write=False):
        deps = []
        for x in r:
            if x.w is not None:
                deps.append(x.w)
            deps.extend(x.wd)
        for x in w:
            if x.w is not None:
                deps.append(x.w)
            if not dma_write:
                deps.extend(x.wd)
            deps.extend(x.re.values())
            deps.extend(x.rd)
        for t in deps:
            self._wait(E, t)

    def _mark(self, tok, r, w):
        for x in r:
            if tok[0] == "e":
                x.re[tok[1]] = tok
            else:
                x.rd.append(tok)
        for x in w:
            if tok[0] == "e":
                x.w = tok
                x.wd = []
            else:
                x.w = None
                x.wd.append(tok)
                if len(x.wd) > 64:
                    x.wd = x.wd[-64:]
            x.re = {}
            x.rd = []

    def op(self, ename, fn, r=(), w=()):
        if self.dry:
            return
        E = self.engs[ename]
        self._deps(E, r, w)
        h = fn(E.eng)
        E.n += 1
        E.last_handle = h
        E.last_idx = E.n
        self._mark(("e", ename, E.n), r, w)

    def dma(self, qname, out, in_, r=(), w=(), is_out=False):
        if self.dry:
            return
        E = self.engs[qname]
        self._deps(E, r, w, dma_write=True)
        i = self.di % self.NDS
        self.di += 1
        if self.dcnt[i] > 0:
            self._wait(E, ("d", i, self.dcnt[i]))
        self.dcnt[i] += 16
        E.eng.dma_start(out=out, in_=in_).then_inc(self.dsem[i], 16)
        tok = ("d", i, self.dcnt[i])
        self._mark(tok, r, w)
        if is_out:
            self.out_tokens.append(tok)

    def finish(self):
        E = self.engs["sp"]
        for tok in self.out_tokens:
            self._wait(E, tok)


def build_program(nl, first, last, dbg=False):
    nc = bass.Bass("TRN2", target_bir_lowering=False)

    def din(name, shape, dt=F32):
        return nc.dram_tensor(name, shape, dt, kind="ExternalInput").ap()

    def dscr(name, shape, dt):
        return nc.dram_tensor(name, shape, dt, kind="Internal").ap()

    xT_in = din("xT", [D, S])
    memT = din("memT", [D, 256])
    w_in = din("w_in", [nl, D, D_IN])
    w_krx = din("w_krx", [nl, D, 128])
    w_uqx = din("w_uqx", [nl, 512, 2048])
    w_ukv = din("w_ukv", [nl, 256, 2048])
    w_br = din("w_br", [nl, 3, 1024, D])
    w_out = din("w_out", [nl, D, D])
    w_xq = din("w_xq", [nl, D, 512])
    w_xkv = din("w_xkv", [nl, D, 1024])
    w_xo = din("w_xo", [nl, 512, D])
    w_1 = din("w_1", [nl, D, 8192])
    w_2 = din("w_2", [nl, 8192, D])
    Gd = din("G", [nl, 128, NG])
    Tzd = din("Tz", [nl, 8, 128, EW])
    ctab_d = din("ctab", [64, S])
    stab_d = din("stab", [64, S])
    fmask_d = din("fmask", [128, 2048], BF16)
    mmask_d = din("mmask", [128, 2048], BF16)
    band_d = din("band", [128, EW], BF16)
    tri_d = din("tri", [128, 128])
    ident_d = din("ident", [128, 128], BF16)
    oT = nc.dram_tensor("oT", [D, S], F32, kind="ExternalOutput").ap()

    xs = dscr("xs", [D, S], F32)
    qTn = {b: dscr("qTn_" + b, [8, 128, T], BF16) for b in ("m", "f", "c")}
    qTr = dscr("qTr_m", [8, 64, T], BF16)
    kTn = {b: dscr("kTn_" + b, [8, 128, S], BF16) for b in ("m", "f", "c")}
    kTr = dscr("kTr_m", [8, 64, S], BF16)
    vv = {b: dscr("v_" + b, [S, 1024], BF16) for b in ("m", "f", "c")}
    gat = dscr("gat", [48, 128, T], BF16)
    rem0 = dscr("rem0", [128, 64], F32)

    es = ExitStack()
    with es:
        fw = FW(nc, es)

        def sb(name, shape, dt):
            return es.enter_context(nc.sbuf_tensor("sb_" + name, shape, dt))

        x = sb("x", [128, 16, T], F32)
        h = sb("h", [128, 16, T], BF16)
        big = sb("big", [128, 16, T], BF16)
        NSLOT = 3
        wr = [sb("wr%d" % i, [128, 4096], BF16) for i in range(NSLOT)]
        qn_t = sb("qn_t", [128, T], BF16)
        qr_t = sb("qr_t", [64, T], BF16)
        kn_t = sb("kn_t", [128, S], BF16)
        kr_t = sb("kr_t", [64, S], BF16)
        v_t = sb("v_t", [128, 16, 128], BF16)
        sq_t = [sb("sq%d" % i, [128, 512], BF16) for i in range(3)]
        rs_t = [sb("rs%d" % i, [128, 512], F32) for i in range(2)]
        tmp_t = [sb("tmp%d" % i, [128, 512], F32) for i in range(2)]
        st_t = [sb("st%d" % i, [128, 512], BF16) for i in range(4)]
        gt_t = [sb("gt%d" % i, [128, 512], BF16) for i in range(2)]
        p_t = [sb("p%d" % i, [128, 512], BF16) for i in range(4)]
        ME = sb("ME", [128, 3 * EW], BF16)
        mask_t = ME[:, 0:2048]
        band_t = ME[:, 0:EW]
        E_ts = [ME[:, EW:2 * EW], ME[:, 2 * EW:3 * EW]]
        kbase = sb("kbase", [64, T], BF16)
        sqkr = sb("sqkr", [64, T], BF16)
        G = sb("G", [128, NG], F32)
        ones_bf = sb("ones_bf", [128, 128], BF16)
        ones_f = sb("ones_f", [128, 128], F32)
        tri = sb("tri", [128, 128], F32)
        ident = sb("ident", [128, 128], BF16)
        ff = sb("ff", [128, 64], F32)
        prefx = sb("prefx", [128, 72], F32)
        Dtok = sb("Dtok", [128, 64], F32)
        Rsb = sb("Rsb", [128, 72], F32)
        rem_t = sb("rem_t", [128, 64], F32)
        bias_own = sb("bias_own", [128, 2, 8, 8], F32)
        bias_oth = sb("bias_oth", [128, 2, 8, 8], F32)
        ps = [es.enter_context(nc.psum_tensor("pp%d" % i, [128, 512], F32)) for i in range(8)]

        RR = {}

        def R(*key):
            r = RR.get(key)
            if r is None:
                r = RR[key] = Res()
            return r

        def fence(new_keys, old_prefixes):
            toks = []
            for k, r in RR.items():
                if k[0] in old_prefixes:
                    if r.w is not None:
                        toks.append(r.w)
                    toks.extend(r.wd)
                    toks.extend(r.re.values())
                    toks.extend(r.rd)
            for k in new_keys:
                r = R(*k)
                for t in toks:
                    if t[0] == "e":
                        old = r.re.get(t[1])
                        if old is None or old[2] < t[2]:
                            r.re[t[1]] = t
                    else:
                        r.rd.append(t)

        rot = {}

        def mark(name):
            if not fw.dry:
                PHASES.append((name, fw.engs["pe"].n))

        def nxt(name, n):
            i = rot.get(name, 0)
            rot[name] = i + 1
            return i % n

        plan = []
        wstate = {"i": 0, "issued": 0}
        wsrc = {"w_in": w_in, "w_krx": w_krx, "w_uqx": w_uqx, "w_ukv": w_ukv, "w_out": w_out,
                "w_xq": w_xq, "w_xkv": w_xkv, "w_xo": w_xo, "w_1": w_1, "w_2": w_2}

        def issue_slab(j):
            key = plan[j]
            name, li, k0, nk, c0, ncol = key
            if name.startswith("w_br"):
                src = w_br[li, int(name[4])]
            else:
                src = wsrc[name][li]
            slot = wr[j % NSLOT]
            fw.dma("pool", slot[:, 0:nk * ncol].rearrange("p (k c) -> p k c", k=nk),
                   src[k0 * 128:(k0 + nk) * 128, c0:c0 + ncol].rearrange("(k p) c -> p k c", p=128),
                   w=[R("wr", j % NSLOT)])

        def wget(name, li, k0, nk, c0, ncol):
            key = (name, li, k0, nk, c0, ncol)
            i = wstate["i"]
            wstate["i"] = i + 1
            if fw.dry:
                plan.append(key)
            else:
                assert plan[i] == key, (plan[i], key)
                while wstate["issued"] < min(len(plan), i + NSLOT):
                    issue_slab(wstate["issued"])
                    wstate["issued"] += 1
            slot = wr[i % NSLOT]
            return slot[:, 0:nk * ncol].rearrange("p (k c) -> p k c", k=nk), R("wr", i % NSLOT)

        def mm(out, lhsT, rhs, start, stop, r, w):
            fw.op("pe", lambda e: e.matmul(out, lhsT, rhs, start=start, stop=stop), r=r, w=w)

        def rstd_from(bank, npart, ncol, scale, rdeps):
            ri = nxt("rs", 2)
            rs = rs_t[ri]
            fw.op("act", lambda e: e.activation(out=rs[0:npart, 0:ncol], in_=ps[bank][0:npart, 0:ncol],
                                                func=AF.Sqrt, bias=EPS, scale=scale),
                  r=[R("ps", bank)], w=[R("rs", ri)])
            fw.op("dve", lambda e: e.reciprocal(out=rs[0:npart, 0:ncol], in_=rs[0:npart, 0:ncol]),
                  r=[R("rs", ri)], w=[R("rs", ri)])
            return ri

        def rmsnorm_fm(src, src_res, dst, dst_res, nch, ncols, gcol, ntile):
            for tt in range(ntile):
                t0, t1 = tt * 512, min(ncols, tt * 512 + 512)
                n = t1 - t0
                bank = 6 + nxt("aux", 2)
                for c in range(nch):
                    si = nxt("sq", 3)
                    fw.op("act", lambda e, c=c, si=si: e.activation(out=sq_t[si][:, 0:n], in_=src(c, t0, t1),
                                                                  func=AF.Square),
                          r=[src_res(c, tt)], w=[R("sq", si)])
                    mm(ps[bank][:, 0:n], ones_bf[:, :], sq_t[si][:, 0:n], c == 0, c == nch - 1,
                       r=[R("sq", si), R("ones")], w=[R("ps", bank)])
                ri = rstd_from(bank, 128, n, 1.0 / (nch * 128), None)
                for c in range(nch):
                    fw.op("dve", lambda e, c=c: e.scalar_tensor_tensor(
                        out=dst(c, t0, t1), in0=src(c, t0, t1), scalar=G[:, gcol + c:gcol + c + 1],
                        in1=rs_t[ri][:, 0:n], op0=ALU.mult, op1=ALU.mult),
                        r=[src_res(c, tt), R("rs", ri), R("G")], w=[dst_res(c, tt)])

        def head_norm(bank, n, gcol, dst_ap, dst_res, extra=None, div=128.0, sb_dst=None, sb_res=None):
            si = nxt("sq", 3)
            fw.op("act", lambda e: e.activation(out=sq_t[si][:, 0:n], in_=ps[bank][:, 0:n], func=AF.Square),
                  r=[R("ps", bank)], w=[R("sq", si)])
            ab = 6 + nxt("aux", 2)
            mm(ps[ab][:, 0:n], ones_bf[:, :], sq_t[si][:, 0:n], True, extra is None,
               r=[R("sq", si), R("ones")], w=[R("ps", ab)])
            if extra is not None:
                mm(ps[ab][:, 0:n], ones_bf[0:64, :], extra[0], False, True,
                   r=[extra[1], R("ones")], w=[R("ps", ab)])
            ri = rstd_from(ab, 128, n, 1.0 / div, None)
            if sb_dst is not None:
                fw.op("dve", lambda e: e.scalar_tensor_tensor(
                    out=sb_dst, in0=ps[bank][:, 0:n], scalar=G[:, gcol:gcol + 1], in1=rs_t[ri][:, 0:n],
                    op0=ALU.mult, op1=ALU.mult), r=[R("ps", bank), R("rs", ri), R("G")], w=[sb_res])
            else:
                sti = nxt("st", 4)
                fw.op("dve", lambda e: e.scalar_tensor_tensor(
                    out=st_t[sti][:, 0:n], in0=ps[bank][:, 0:n], scalar=G[:, gcol:gcol + 1],
                    in1=rs_t[ri][:, 0:n], op0=ALU.mult, op1=ALU.mult),
                    r=[R("ps", bank), R("rs", ri), R("G")], w=[R("st", sti)])
                fw.dma("sp", dst_ap, st_t[sti][:, 0:n], r=[R("st", sti)], w=[dst_res])
            return ri

        def evac_store(bank, npart, n, dst_ap, dst_res, func=AF.Copy):
            sti = nxt("st", 4)
            fw.op("act", lambda e: e.activation(out=st_t[sti][0:npart, 0:n], in_=ps[bank][0:npart, 0:n], func=func),
                  r=[R("ps", bank)], w=[R("st", sti)])
            fw.dma("sp", dst_ap, st_t[sti][0:npart, 0:n], r=[R("st", sti)], w=[dst_res])

        def proj_fm(slab, slab_res, m0, msz, nk, rhs, rhs_res, tt, bank, ncol=512):
            for kc in range(nk):
                mm(ps[bank][0:msz, 0:ncol], slab[:, kc, m0:m0 + msz], rhs(kc), kc == 0, kc == nk - 1,
                   r=[slab_res, rhs_res(kc, tt)], w=[R("ps", bank)])

        hres = lambda c, tt: R("h", c, tt)
        xres = lambda c, tt: R("x", c, tt)

        def emit_consts():
            fw.op("dve", lambda e: e.memset(ones_bf[:, :], 1.0), w=[R("ones")])
            fw.op("dve", lambda e: e.memset(ones_f[:, :], 1.0), w=[R("onesf")])
            fw.dma("sp", tri[:, :], tri_d, w=[R("tri")])
            fw.dma("sp", ident[:, :], ident_d, w=[R("ident")])

        def attention(br, hf, li):
            scale = (192.0 if br == "m" else 128.0) ** -0.5
            if br == "c":
                kb_lo = max(0, hf * 8 - 4)
            else:
                kb_lo = 0
            kb_hi = hf * 8 + 8
            nkb = kb_hi - kb_lo
            if br == "m":
                fence([("qn", 1), ("qr", 1), ("kn", 1), ("kr", 1), ("vt", 1)], ("big",))
            if br in ("m", "f"):
                fence([("mask",)], ("mask", "band", "E"))
                fw.dma("sp", mask_t, mmask_d if br == "m" else fmask_d, w=[R("mask")])
            else:
                fence([("band",), ("E", 0), ("E", 1)], ("mask", "band", "E"))
                fw.dma("sp", band_t, band_d, w=[R("band")])
            bufs = [dict(qn=qn_t, qr=qr_t, kn=kn_t, kr=kr_t, v=v_t),
                    dict(qn=big[:, 8, :], qr=big[0:64, 9, :],
                         kn=big[:, 10:12, :].rearrange("p a t -> p (a t)"),
                         kr=big[0:64, 12:14, :].rearrange("p a t -> p (a t)"),
                         v=big[:, 14:16, :].rearrange("p a (b d) -> p (a b) d", d=128))]

            def load_head(hd, si):
                bs = bufs[si]
                fw.dma("sp", bs["qn"][:, :], qTn[br][hd], r=[R("qTn", br, hd)], w=[R("qn", si)])
                fw.dma("sp", bs["kn"][:, 0:nkb * 128], kTn[br][hd][:, kb_lo * 128:kb_hi * 128],
                       r=[R("kTn", br, hd)], w=[R("kn", si)])
                if br == "m":
                    fw.dma("sp", bs["qr"][:, :], qTr[hd], r=[R("qTr", hd)], w=[R("qr", si)])
                    fw.dma("sp", bs["kr"][:, 0:nkb * 128], kTr[hd][:, kb_lo * 128:kb_hi * 128],
                           r=[R("kTr", hd)], w=[R("kr", si)])
                fw.dma("sp", bs["v"][:, 0:nkb, :],
                       vv[br][kb_lo * 128:kb_hi * 128, hd * 128:(hd + 1) * 128].rearrange("(b p) d -> p b d", p=128),
                       r=[R("v", br)], w=[R("vt", si)])
                pend = []
                if br == "c":
                    for q0 in range(0, EW, 512):
                        q1 = min(EW, q0 + 512)
                        ti = nxt("tmp", 2)
                        fw.dma("sp", tmp_t[ti][:, 0:q1 - q0], Tzd[li, hd][:, q0:q1], w=[R("tmp", ti)])
                        pend.append((ti, q0, q1))
                return pend

            def post_load(si, pend):
                for ti, q0, q1 in pend:
                    fw.op("dve", lambda e: e.scalar_tensor_tensor(
                        out=E_ts[si][:, q0:q1], in0=tmp_t[ti][:, 0:q1 - q0], scalar=1.0 / scale,
                        in1=band_t[:, q0:q1], op0=ALU.mult, op1=ALU.add),
                        r=[R("tmp", ti), R("band")], w=[R("E", si)])

            pend0 = load_head(0, 0)
            post_load(0, pend0)
            for hd in range(8):
                si = hd % 2
                bs = bufs[si]
                qn_b, qr_b, kn_b, kr_b, v_b = bs["qn"], bs["qr"], bs["kn"], bs["kr"], bs["v"]
                E_t = E_ts[si]
                pend_next = load_head(hd + 1, 1 - si) if hd + 1 < 8 else None
                seq = []
                for j in range(2):
                    qs = hf * 8 + j * 4
                    if br == "c":
                        blocks = list(range(max(0, qs - 4), qs + 4))
                    else:
                        blocks = list(range(0, qs + 4))
                    for bi, kb in enumerate(blocks):
                        seq.append((j, bi, kb, len(blocks), qs))
                sbank_of = {}
                olb = {}

                def emit_qk(t):
                    j, bi, kb, nb, qs = seq[t]
                    kl = kb - kb_lo
                    sbk = nxt("sbank", 4)
                    sbank_of[t] = sbk
                    ksl = slice(kl * 128, kl * 128 + 128)
                    qsl = slice(j * 512, j * 512 + 512)
                    if br == "c":
                        add_ap, add_res = E_t[:, 896 - 128 * (kb - (qs - 4)):896 - 128 * (kb - (qs - 4)) + 512], R("E", si)
                    elif kb >= qs:
                        add_ap, add_res = mask_t[:, (kb - qs) * 512:(kb - qs) * 512 + 512], R("mask")
                    else:
                        add_ap = None
                    mm(ps[sbk][:, :], kn_b[:, ksl], qn_b[:, qsl], True, br != "m" and add_ap is None,
                       r=[R("kn", si), R("qn", si)], w=[R("ps", sbk)])
                    if br == "m":
                        mm(ps[sbk][:, :], kr_b[:, ksl], qr_b[:, qsl], False, add_ap is None,
                           r=[R("kr", si), R("qr", si)], w=[R("ps", sbk)])
                    if add_ap is not None:
                        mm(ps[sbk][:, :], ident[:, :], add_ap, False, True,
                           r=[R("ident"), add_res], w=[R("ps", sbk)])

                def emit_exp(t):
                    j, bi, kb, nb, qs = seq[t]
                    sbk = sbank_of[t]
                    pi = nxt("p", 4)
                    if br == "f":
                        if kb >= hf * 8:
                            bias_ap = bias_own[:, j, kb - hf * 8, hd:hd + 1]
                            bres = R("bias_own")
                        else:
                            bias_ap = bias_oth[:, j, kb, hd:hd + 1]
                            bres = R("bias_oth")
                        fw.op("act", lambda e: e.activation(
                            out=p_t[pi][:, :], in_=ps[sbk][:, :], func=AF.Exp, bias=bias_ap, scale=scale),
                            r=[R("ps", sbk), bres], w=[R("p", pi)])
                    else:
                        fw.op("act", lambda e: e.activation(
                            out=p_t[pi][:, :], in_=ps[sbk][:, :], func=AF.Exp, scale=scale),
                            r=[R("ps", sbk)], w=[R("p", pi)])
                    return pi

                def emit_pv(t, pi):
                    j, bi, kb, nb, qs = seq[t]
                    kl = kb - kb_lo
                    if bi == 0:
                        oi = nxt("ol", 2)
                        olb[j] = (4 + 2 * oi, 5 + 2 * oi)
                    ob, lb = olb[j]
                    first_b, last_b = bi == 0, bi == nb - 1
                    mm(ps[ob][:, :], v_b[:, kl, :], p_t[pi][:, :], first_b, last_b,
                       r=[R("vt", si), R("p", pi)], w=[R("ps", ob)])
                    mm(ps[lb][:, :], ones_bf[:, :], p_t[pi][:, :], first_b, last_b,
                       r=[R("ones"), R("p", pi)], w=[R("ps", lb)])
                    if last_b:
                        qsl = slice(j * 512, j * 512 + 512)
                        ri = nxt("rs", 2)
                        fw.op("dve", lambda e: e.reciprocal(out=rs_t[ri][:, :], in_=ps[lb][:, :]),
                              r=[R("ps", lb)], w=[R("rs", ri)])
                        fw.op("dve", lambda e: e.tensor_tensor(
                            out=big[:, hd, qsl], in0=ps[ob][:, :], in1=rs_t[ri][:, :], op=ALU.mult),
                            r=[R("ps", ob), R("rs", ri)], w=[R("big", hd, j)])

                LA = 3
                for t in range(min(LA, len(seq))):
                    emit_qk(t)
                for t in range(len(seq)):
                    pi = emit_exp(t)
                    if t + LA < len(seq):
                        emit_qk(t + LA)
                    emit_pv(t, pi)
                    if t == len(seq) // 2 and pend_next is not None:
                        post_load(1 - si, pend_next)
            if br == "c":
                fence([("big", c, tt) for c in range(8, 16) for tt in range(2)], ("qn", "qr", "kn", "kr", "vt"))

        def merge(n, li):
            for sl in range(4):
                slab, sres = wget("w_br%d" % n, li, 0, 8, sl * 512, 512)
                for m in range(4):
                    c = sl * 4 + m
                    for tt in range(2):
                        tsl = slice(tt * 512, tt * 512 + 512)
                        bank = nxt("main", 4)
                        proj_fm(slab, sres, m * 128, 128, 8, lambda kc: big[:, kc, tsl],
                                lambda kc, tt_: R("big", kc, tt_), tt, bank)
                        gi = nxt("gt", 2)
                        fw.dma("sp", gt_t[gi][:, :], gat[n * 16 + c][:, tsl], r=[R("gat", n * 16 + c, tt)],
                               w=[R("gt", gi)])
                        if n == 0:
                            fw.op("dve", lambda e, bank=bank, gi=gi, c=c, tsl=tsl: e.tensor_tensor(
                                out=h[:, c, tsl], in0=ps[bank][:, :], in1=gt_t[gi][:, :], op=ALU.mult),
                                r=[R("ps", bank), R("gt", gi)], w=[R("h", c, tt)])
                        else:
                            ti = nxt("tmp", 2)
                            fw.op("dve", lambda e, bank=bank, gi=gi, ti=ti: e.tensor_tensor(
                                out=tmp_t[ti][:, :], in0=ps[bank][:, :], in1=gt_t[gi][:, :], op=ALU.mult),
                                r=[R("ps", bank), R("gt", gi)], w=[R("tmp", ti)])
                            fw.op("dve", lambda e, ti=ti, c=c, tsl=tsl: e.tensor_tensor(
                                out=h[:, c, tsl], in0=h[:, c, tsl], in1=tmp_t[ti][:, :], op=ALU.add),
                                r=[R("tmp", ti), R("h", c, tt)], w=[R("h", c, tt)])

        def add_to_x(bank, c, tt):
            tsl = slice(tt * 512, tt * 512 + 512)
            fw.op("dve", lambda e: e.tensor_tensor(out=x[:, c, tsl], in0=x[:, c, tsl], in1=ps[bank][:, :],
                                                   op=ALU.add), r=[R("ps", bank), R("x", c, tt)], w=[R("x", c, tt)])

        def emit_pass(li, hf, src_ap, dst_ap, is_final):
            tok0 = hf * T
            mark("norm")
            fw.dma("sp", x[:, :, :], src_ap[:, tok0:tok0 + T].rearrange("(c p) t -> p c t", p=128),
                   r=[R("xs", hf)], w=[R("x", c, tt) for c in range(16) for tt in range(2)])
            fw.dma("sp", G[:, :], Gd[li], w=[R("G")])
            rmsnorm_fm(lambda c, a, b: x[:, c, a:b], xres, lambda c, a, b: h[:, c, a:b], hres, 16, T, G_MIX, 2)

            hr = lambda tt: (lambda kc: h[:, kc, tt * 512:tt * 512 + 512])
            mark("z_mla")
            for sl in range(3):
                slab, sres = wget("w_in", li, 0, 16, sl * 256, 256)
                for m in range(2):
                    zc = sl * 2 + m
                    for tt in range(2):
                        bank = nxt("main", 6)
                        proj_fm(slab, sres, m * 128, 128, 16, hr(tt), hres, tt, bank)
                        fw.op("act", lambda e, bank=bank, zc=zc, tt=tt: e.activation(
                            out=big[:, 8 + zc, tt * 512:tt * 512 + 512], in_=ps[bank][:, :], func=AF.Copy),
                            r=[R("ps", bank)], w=[R("big", 8 + zc, tt)])
            slab, sres = wget("w_krx", li, 0, 16, 0, 128)
            for which in range(2):
                for tt in range(2):
                    bank = nxt("main", 6)
                    proj_fm(slab, sres, which * 64, 64, 16, hr(tt), hres, tt, bank)
                    fw.op("act", lambda e, bank=bank, which=which, tt=tt: e.activation(
                        out=big[0:64, 14 + which, tt * 512:tt * 512 + 512], in_=ps[bank][0:64, :], func=AF.Copy),
                        r=[R("ps", bank)], w=[R("big", 14 + which, tt)])
            mark("z_qkv")
            for br, cbase, gq, gk in (("f", C_FOX, G_FQ, G_FK), ("c", C_CH, G_CQ2, G_CK2)):
                for sl in range(8):
                    slab, sres = wget("w_in", li, 0, 16, cbase + sl * 256, 256)
                    for m in range(2):
                        hidx = sl * 2 + m
                        for tt in range(2):
                            bank = nxt("main", 6)
                            proj_fm(slab, sres, m * 128, 128, 16, hr(tt), hres, tt, bank)
                            if hidx < 8:
                                head_norm(bank, 512, gq, qTn[br][hidx][:, tt * 512:tt * 512 + 512],
                                          R("qTn", br, hidx))
                            else:
                                head_norm(bank, 512, gk,
                                          kTn[br][hidx - 8][:, tok0 + tt * 512:tok0 + tt * 512 + 512],
                                          R("kTn", br, hidx - 8))
                for sl in range(4):
                    slab, sres = wget("w_in", li, 0, 16, cbase + 2048 + sl * 256, 256)
                    for tb in range(8):
                        bank = nxt("main", 6)
                        for kc in range(16):
                            mm(ps[bank][:, 0:256], h[:, kc, tb * 128:tb * 128 + 128], slab[:, kc, :], kc == 0, kc == 15,
                               r=[sres, R("h", kc, tb // 4)], w=[R("ps", bank)])
                        evac_store(bank, 128, 256, vv[br][tok0 + tb * 128:tok0 + tb * 128 + 128, sl * 256:sl * 256 + 256],
                                   R("v", br))
                if br == "f":
                    slab, sres = wget("w_in", li, 0, 16, C_FOXF, 8)
                    fbank = 6 + nxt("aux", 2)
                    for tb in range(8):
                        for kc in range(16):
                            mm(ps[fbank][:, tb * 8:tb * 8 + 8], h[:, kc, tb * 128:tb * 128 + 128], slab[:, kc, :],
                               kc == 0, kc == 15, r=[sres, R("h", kc, tb // 4)], w=[R("ps", fbank)])
                    fw.op("dve", lambda e: e.tensor_tensor(out=ff[:, :], in0=ps[fbank][:, 0:64], in1=G[:, G_BF:G_BF + 64],
                                                           op=ALU.add), r=[R("ps", fbank), R("G")], w=[R("ff")])
                    fw.op("act", lambda e: e.activation(out=ff[:, :], in_=ff[:, :], func=AF.Exp, scale=-1.0),
                          r=[R("ff")], w=[R("ff")])
                    fw.op("act", lambda e: e.activation(out=ff[:, :], in_=ff[:, :], func=AF.Ln, bias=1.0),
                          r=[R("ff")], w=[R("ff")])
                    fw.op("dve", lambda e: e.memset(prefx[:, 0:8], 0.0), w=[R("prefx")])
                    for b in range(1, 9):
                        fw.op("dve", lambda e, b=b: e.tensor_tensor(
                            out=prefx[:, b * 8:b * 8 + 8], in0=prefx[:, b * 8 - 8:b * 8], in1=ff[:, b * 8 - 8:b * 8],
                            op=ALU.add), r=[R("prefx"), R("ff")], w=[R("prefx")])
                    cb = 6 + nxt("aux", 2)
                    mm(ps[cb][:, 0:64], tri[:, :], ff[:, :], True, True, r=[R("tri"), R("ff")], w=[R("ps", cb)])
                    mm(ps[cb][:, 64:136], ones_f[:, :], prefx[:, :], True, True, r=[R("onesf"), R("prefx")],
                       w=[R("ps", cb)])
                    fw.op("act", lambda e: e.activation(out=Rsb[:, :], in_=ps[cb][:, 64:136], func=AF.Copy),
                          r=[R("ps", cb)], w=[R("Rsb")])
                    fw.op("dve", lambda e: e.tensor_tensor(out=Dtok[:, :], in0=ps[cb][:, 0:64], in1=Rsb[:, 0:64],
                                                           op=ALU.add), r=[R("ps", cb), R("Rsb")], w=[R("Dtok")])
                    for j in range(2):
                        rc = (4 * j + 2) * 8
                        for kb in range(8):
                            fw.op("dve", lambda e, j=j, kb=kb, rc=rc: e.tensor_tensor(
                                out=bias_own[:, j, kb, :], in0=Dtok[:, kb * 8:kb * 8 + 8], in1=Rsb[:, rc:rc + 8],
                                op=ALU.subtract), r=[R("Dtok"), R("Rsb")], w=[R("bias_own")])
                    if hf == 0:
                        for kb in range(8):
                            fw.op("dve", lambda e, kb=kb: e.tensor_tensor(
                                out=rem_t[:, kb * 8:kb * 8 + 8], in0=Rsb[:, 64:72], in1=Dtok[:, kb * 8:kb * 8 + 8],
                                op=ALU.subtract), r=[R("Dtok"), R("Rsb")], w=[R("rem_t")])
                        fw.dma("sp", rem0, rem_t[:, :], r=[R("rem_t")], w=[R("rem0")])
                    else:
                        fw.dma("sp", rem_t[:, :], rem0, r=[R("rem0")], w=[R("rem_t")])
                        for j in range(2):
                            rc = (4 * j + 2) * 8
                            for kb in range(8):
                                fw.op("dve", lambda e, j=j, kb=kb, rc=rc: e.scalar_tensor_tensor(
                                    out=bias_oth[:, j, kb, :], in0=rem_t[:, kb * 8:kb * 8 + 8], scalar=-1.0,
                                    in1=Rsb[:, rc:rc + 8], op0=ALU.mult, op1=ALU.subtract),
                                    r=[R("rem_t"), R("Rsb")], w=[R("bias_oth")])
            mark("z_gates")
            for sl in range(24):
                slab, sres = wget("w_in", li, 0, 16, C_GATE + sl * 256, 256)
                for m in range(2):
                    gc = sl * 2 + m
                    for tt in range(2):
                        bank = nxt("main", 6)
                        proj_fm(slab, sres, m * 128, 128, 16, hr(tt), hres, tt, bank)
                        evac_store(bank, 128, 512, gat[gc][:, tt * 512:tt * 512 + 512], R("gat", gc, tt),
                                   func=AF.Sigmoid)

            mark("mla_prep")
            cqs = lambda c, a, b: big[:, 8 + c, a:b]
            cqr = lambda c, tt: R("big", 8 + c, tt)
            rmsnorm_fm(cqs, cqr, cqs, cqr, 4, T, G_CQ, 2)
            cks = lambda c, a, b: big[:, 12 + c, a:b]
            ckr = lambda c, tt: R("big", 12 + c, tt)
            rmsnorm_fm(cks, ckr, cks, ckr, 2, T, G_CKV, 2)
            for tt in range(2):
                tsl = slice(tt * 512, tt * 512 + 512)
                gsl = slice(tok0 + tt * 512, tok0 + tt * 512 + 512)
                t1, t2 = nxt("tmp", 2), nxt("tmp", 2)
                fw.dma("sp", tmp_t[t1][0:64, :], ctab_d[:, gsl], w=[R("tmp", t1)])
                fw.dma("sp", tmp_t[t2][0:64, :], stab_d[:, gsl], w=[R("tmp", t2)])
                fw.op("dve", lambda e, t1=t1, tsl=tsl: e.scalar_tensor_tensor(
                    out=tmp_t[t1][0:64, :], in0=big[0:64, 14, tsl], scalar=G[0:64, G_MKR:G_MKR + 1],
                    in1=tmp_t[t1][0:64, :], op0=ALU.mult, op1=ALU.mult),
                    r=[R("big", 14, tt), R("G"), R("tmp", t1)], w=[R("tmp", t1)])
                fw.op("dve", lambda e, t2=t2, tsl=tsl: e.scalar_tensor_tensor(
                    out=tmp_t[t2][0:64, :], in0=big[0:64, 15, tsl], scalar=G[0:64, G_MKS:G_MKS + 1],
                    in1=tmp_t[t2][0:64, :], op0=ALU.mult, op1=ALU.mult),
                    r=[R("big", 15, tt), R("G"), R("tmp", t2)], w=[R("tmp", t2)])
                fw.op("dve", lambda e, t1=t1, t2=t2, tsl=tsl: e.tensor_tensor(
                    out=kbase[:, tsl], in0=tmp_t[t1][0:64, :], in1=tmp_t[t2][0:64, :], op=ALU.add),
                    r=[R("tmp", t1), R("tmp", t2)], w=[R("kbase", tt)])
                fw.op("act", lambda e, tsl=tsl: e.activation(out=sqkr[:, tsl], in_=big[0:64, 14, tsl], func=AF.Square),
                      r=[R("big", 14, tt)], w=[R("sqkr", tt)])
            qit = [(sl, hh, tt) for sl in range(2) for hh in range(4) for tt in range(2)]
            qslab = {}
            qbanks = {}

            def q_s1(i):
                sl, hh, tt = qit[i]
                if sl not in qslab:
                    qslab[sl] = wget("w_uqx", li, 0, 4, sl * 1024, 1024)
                slab, sres = qslab[sl]
                tsl = slice(tt * 512, tt * 512 + 512)
                rhs = lambda kc: big[:, 8 + kc, tsl]
                bA, b1, b2 = nxt("main", 6), nxt("main", 6), nxt("main", 6)
                proj_fm(slab, sres, hh * 256, 128, 4, rhs, cqr, tt, bA)
                proj_fm(slab, sres, hh * 256 + 128, 64, 4, rhs, cqr, tt, b1)
                proj_fm(slab, sres, hh * 256 + 192, 64, 4, rhs, cqr, tt, b2)
                qbanks[i] = (bA, b1, b2)

            def q_s2(i):
                sl, hh, tt = qit[i]
                hd = sl * 4 + hh
                bA, b1, b2 = qbanks[i]
                tsl = slice(tt * 512, tt * 512 + 512)
                gsl = slice(tok0 + tt * 512, tok0 + tt * 512 + 512)
                si = nxt("sq", 3)
                fw.op("act", lambda e: e.activation(out=sq_t[si][0:64, :], in_=ps[b1][0:64, :], func=AF.Square),
                      r=[R("ps", b1)], w=[R("sq", si)])
                t1, t2 = nxt("tmp", 2), nxt("tmp", 2)
                fw.dma("sp", tmp_t[t1][0:64, :], ctab_d[:, gsl], w=[R("tmp", t1)])
                fw.dma("sp", tmp_t[t2][0:64, :], stab_d[:, gsl], w=[R("tmp", t2)])
                ri = head_norm(bA, 512, G_MQN, qTn["m"][hd][:, tsl], R("qTn", "m", hd),
                               extra=(sq_t[si][0:64, :], R("sq", si)), div=192.0)
                fw.op("dve", lambda e: e.tensor_tensor(
                    out=tmp_t[t1][0:64, :], in0=tmp_t[t1][0:64, :], in1=rs_t[ri][0:64, :], op=ALU.mult),
                    r=[R("tmp", t1), R("rs", ri)], w=[R("tmp", t1)])
                fw.op("dve", lambda e: e.tensor_tensor(
                    out=tmp_t[t2][0:64, :], in0=tmp_t[t2][0:64, :], in1=rs_t[ri][0:64, :], op=ALU.mult),
                    r=[R("tmp", t2), R("rs", ri)], w=[R("tmp", t2)])
                fw.op("dve", lambda e: e.scalar_tensor_tensor(
                    out=tmp_t[t1][0:64, :], in0=ps[b1][0:64, :], scalar=G[0:64, G_MQR:G_MQR + 1],
                    in1=tmp_t[t1][0:64, :], op0=ALU.mult, op1=ALU.mult),
                    r=[R("ps", b1), R("G"), R("tmp", t1)], w=[R("tmp", t1)])
                fw.op("dve", lambda e: e.scalar_tensor_tensor(
                    out=tmp_t[t2][0:64, :], in0=ps[b2][0:64, :], scalar=G[0:64, G_MQS:G_MQS + 1],
                    in1=tmp_t[t2][0:64, :], op0=ALU.mult, op1=ALU.mult),
                    r=[R("ps", b2), R("G"), R("tmp", t2)], w=[R("tmp", t2)])
                sti = nxt("st", 4)
                fw.op("dve", lambda e: e.tensor_tensor(
                    out=st_t[sti][0:64, :], in0=tmp_t[t1][0:64, :], in1=tmp_t[t2][0:64, :], op=ALU.add),
                    r=[R("tmp", t1), R("tmp", t2)], w=[R("st", sti)])
                fw.dma("sp", qTr[hd][:, tsl], st_t[sti][0:64, :], r=[R("st", sti)], w=[R("qTr", hd)])

            q_s1(0)
            for i in range(len(qit)):
                if i + 1 < len(qit):
                    q_s1(i + 1)
                q_s2(i)
            slab, sres = wget("w_ukv", li, 0, 2, 0, 2048)
            kit = [(hd, tt) for hd in range(8) for tt in range(2)]
            kbank = {}

            def k_s1(i):
                hd, tt = kit[i]
                tsl = slice(tt * 512, tt * 512 + 512)
                bA = nxt("main", 6)
                proj_fm(slab, sres, hd * 256, 128, 2, lambda kc: big[:, 12 + kc, tsl], ckr, tt, bA)
                kbank[i] = bA

            def k_s2(i):
                hd, tt = kit[i]
                bA = kbank[i]
                tsl = slice(tt * 512, tt * 512 + 512)
                gsl = slice(tok0 + tt * 512, tok0 + tt * 512 + 512)
                ri = head_norm(bA, 512, G_MKN, kTn["m"][hd][:, gsl], R("kTn", "m", hd),
                               extra=(sqkr[:, tsl], R("sqkr", tt)), div=192.0)
                sti = nxt("st", 4)
                fw.op("dve", lambda e: e.tensor_tensor(
                    out=st_t[sti][0:64, :], in0=kbase[:, tsl], in1=rs_t[ri][0:64, :], op=ALU.mult),
                    r=[R("kbase", tt), R("rs", ri)], w=[R("st", sti)])
                fw.dma("sp", kTr[hd][:, gsl], st_t[sti][0:64, :], r=[R("st", sti)], w=[R("kTr", hd)])

            k_s1(0)
            k_s1(1)
            for i in range(len(kit)):
                if i + 2 < len(kit):
                    k_s1(i + 2)
                k_s2(i)
            for tb in range(8):
                for hg in range(2):
                    bank = nxt("main", 6)
                    for hh in range(4):
                        hd = hg * 4 + hh
                        for kc in range(2):
                            mm(ps[bank][:, hh * 128:hh * 128 + 128], big[:, 12 + kc, tb * 128:tb * 128 + 128],
                               slab[:, kc, hd * 256 + 128:hd * 256 + 256], kc == 0, kc == 1,
                               r=[sres, R("big", 12 + kc, tb // 4)], w=[R("ps", bank)])
                    evac_store(bank, 128, 512, vv["m"][tok0 + tb * 128:tok0 + tb * 128 + 128, hg * 512:hg * 512 + 512],
                               R("v", "m"))

            for n, br in enumerate(("m", "f", "c")):
                mark("att_" + br)
                attention(br, hf, li)
                mark("merge_" + br)
                merge(n, li)
            mark("w_out")
            for sl in range(8):
                slab, sres = wget("w_out", li, 0, 16, sl * 256, 256)
                for m in range(2):
                    c = sl * 2 + m
                    for tt in range(2):
                        bank = nxt("main", 6)
                        proj_fm(slab, sres, m * 128, 128, 16, hr(tt), hres, tt, bank)
                        add_to_x(bank, c, tt)

            mark("cross")
            rmsnorm_fm(lambda c, a, b: x[:, c, a:b], xres, lambda c, a, b: h[:, c, a:b], hres, 16, T, G_CROSS, 2)
            fw.dma("pool", big[:, :, 0:256], memT.rearrange("(c p) m -> p c m", p=128),
                   w=[R("big", c, 0) for c in range(16)])
            mems = lambda c, a, b: big[:, c, a:b]
            memr = lambda c, tt: R("big", c, 0)
            rmsnorm_fm(mems, memr, mems, memr, 16, 256, G_MEM, 1)
            for sl in range(4):
                slab, sres = wget("w_xkv", li, 0, 16, sl * 256, 256)
                if sl < 2:
                    for m in range(2):
                        hd = sl * 2 + m
                        bank = nxt("main", 6)
                        proj_fm(slab, sres, m * 128, 128, 16, lambda kc: big[:, kc, 0:256], memr, 0, bank, ncol=256)
                        head_norm(bank, 256, G_XK, None, None, sb_dst=big[:, hd, 256:512], sb_res=R("xk", hd))
                else:
                    for mb in range(2):
                        bank = nxt("main", 6)
                        for kc in range(16):
                            mm(ps[bank][:, 0:256], big[:, kc, mb * 128:mb * 128 + 128], slab[:, kc, :], kc == 0, kc == 15,
                               r=[sres, R("big", kc, 0)], w=[R("ps", bank)])
                        c0 = 512 + (sl - 2) * 256
                        fw.op("act", lambda e, bank=bank, mb=mb, c0=c0: e.activation(
                            out=big[:, mb, c0:c0 + 256], in_=ps[bank][:, 0:256], func=AF.Copy),
                            r=[R("ps", bank)], w=[R("xv", mb, sl - 2)])
            for sl in range(2):
                slab, sres = wget("w_xq", li, 0, 16, sl * 256, 256)
                for m in range(2):
                    hd = sl * 2 + m
                    for tt in range(2):
                        bank = nxt("main", 6)
                        proj_fm(slab, sres, m * 128, 128, 16, hr(tt), hres, tt, bank)
                        head_norm(bank, 512, G_XQ, None, None, sb_dst=big[:, 4 + hd, tt * 512:tt * 512 + 512],
                                  sb_res=R("xq", hd, tt))
            xscale = 128.0 ** -0.5
            xseq = [(hd, j, mb) for hd in range(4) for j in range(2) for mb in range(2)]
            xsb = {}
            xol = {}

            def x_qk(t):
                hd, j, mb = xseq[t]
                sbk = nxt("sbank", 4)
                xsb[t] = sbk
                mm(ps[sbk][:, :], big[:, hd, 256 + mb * 128:256 + mb * 128 + 128], big[:, 4 + hd, j * 512:j * 512 + 512],
                   True, True, r=[R("xk", hd), R("xq", hd, j)], w=[R("ps", sbk)])

            def x_rest(t):
                hd, j, mb = xseq[t]
                sbk = xsb[t]
                pi = nxt("p", 4)
                fw.op("act", lambda e: e.activation(out=p_t[pi][:, :], in_=ps[sbk][:, :], func=AF.Exp, scale=xscale),
                      r=[R("ps", sbk)], w=[R("p", pi)])
                if t + 2 < len(xseq):
                    x_qk(t + 2)
                if mb == 0:
                    oi = nxt("ol", 2)
                    xol[(hd, j)] = (4 + 2 * oi, 5 + 2 * oi)
                ob, lb = xol[(hd, j)]
                mm(ps[ob][:, :], big[:, mb, 512 + hd * 128:512 + hd * 128 + 128], p_t[pi][:, :], mb == 0, mb == 1,
                   r=[R("xv", mb, hd // 2), R("p", pi)], w=[R("ps", ob)])
                mm(ps[lb][:, :], ones_bf[:, :], p_t[pi][:, :], mb == 0, mb == 1,
                   r=[R("ones"), R("p", pi)], w=[R("ps", lb)])
                if mb == 1:
                    qsl = slice(j * 512, j * 512 + 512)
                    ri = nxt("rs", 2)
                    fw.op("dve", lambda e: e.reciprocal(out=rs_t[ri][:, :], in_=ps[lb][:, :]),
                          r=[R("ps", lb)], w=[R("rs", ri)])
                    fw.op("dve", lambda e: e.tensor_tensor(
                        out=big[:, 8 + hd, qsl], in0=ps[ob][:, :], in1=rs_t[ri][:, :], op=ALU.mult),
                        r=[R("ps", ob), R("rs", ri)], w=[R("xo", hd, j)])

            x_qk(0)
            x_qk(1)
            for t in range(len(xseq)):
                x_rest(t)
            for sl in range(4):
                slab, sres = wget("w_xo", li, 0, 4, sl * 512, 512)
                for m in range(4):
                    c = sl * 4 + m
                    for tt in range(2):
                        tsl = slice(tt * 512, tt * 512 + 512)
                        bank = nxt("main", 4)
                        proj_fm(slab, sres, m * 128, 128, 4, lambda kc: big[:, 8 + kc, tsl],
                                lambda kc, tt_: R("xo", kc, tt_), tt, bank)
                        add_to_x(bank, c, tt)

            mark("mlp")
            rmsnorm_fm(lambda c, a, b: x[:, c, a:b], xres, lambda c, a, b: h[:, c, a:b], hres, 16, T, G_MLP, 2)
            fence([("big", c, tt) for c in range(16) for tt in range(2)],
                  ("big", "xk", "xv", "xq", "xo"))
            for fs in range(4):
                for sl in range(8):
                    slab, sres = wget("w_1", li, 0, 16, fs * 2048 + sl * 256, 256)
                    for m in range(2):
                        uc = sl * 2 + m
                        for tt in range(2):
                            tsl = slice(tt * 512, tt * 512 + 512)
                            bank = nxt("main", 6)
                            proj_fm(slab, sres, m * 128, 128, 16, hr(tt), hres, tt, bank)
                            ti = nxt("tmp", 2)
                            fw.op("act", lambda e, ti=ti, bank=bank: e.activation(out=tmp_t[ti][:, :], in_=ps[bank][:, :],
                                                                                func=AF.Relu),
                                  r=[R("ps", bank)], w=[R("tmp", ti)])
                            fw.op("dve", lambda e, ti=ti, uc=uc, tsl=tsl: e.tensor_tensor(
                                out=big[:, uc, tsl], in0=tmp_t[ti][:, :], in1=tmp_t[ti][:, :], op=ALU.mult),
                                r=[R("tmp", ti)], w=[R("big", uc, tt)])
                for sl in range(8):
                    slab, sres = wget("w_2", li, fs * 16, 16, sl * 256, 256)
                    for m in range(2):
                        c = sl * 2 + m
                        for tt in range(2):
                            tsl = slice(tt * 512, tt * 512 + 512)
                            bank = nxt("main", 6)
                            proj_fm(slab, sres, m * 128, 128, 16, lambda kc: big[:, kc, tsl],
                                    lambda kc, tt_: R("big", kc, tt_), tt, bank)
                            add_to_x(bank, c, tt)
            fw.dma("sp", dst_ap[:, tok0:tok0 + T].rearrange("(c p) t -> p c t", p=128), x[:, :, :],
                   r=[R("x", c, tt) for c in range(16) for tt in range(2)], w=[R("xs", hf)], is_out=is_final)

        def emit_all():
            rot.clear()
            wstate["i"] = 0
            emit_consts()
            for li in range(nl):
                for hf in range(2):
                    src = xT_in if li == 0 else xs
                    fin = li == nl - 1
                    emit_pass(li, hf, src, oT if fin else xs, fin)
            fw.finish()

        fw.dry = True
        emit_all()
        fw.dry = False
        emit_all()
    return nc


def _host_consts():
    bf = ml_dtypes.bfloat16
    s = np.arange(128)[:, None]
    t = np.arange(512)[None, :]
    NEGM = np.float32(-30000.0)
    fm = np.where(np.concatenate([((o * 128 + s) <= t) for o in range(4)], axis=1), np.float32(0), NEGM).astype(bf)
    mm_ = np.where(np.concatenate([(((o * 128 + s) // 64) <= (t // 64)) for o in range(4)], axis=1),
                   np.float32(0), NEGM).astype(bf)
    u = np.arange(EW)[None, :]
    qc = (u - 384) // 64
    kc = s // 64
    band = np.where((kc >= qc - 8) & (kc <= qc), np.float32(0), NEGM).astype(bf)
    tri = (s <= np.arange(128)[None, :]).astype(np.float32)
    pos = np.arange(S, dtype=np.float32)
    inv = (10000.0 ** (-np.arange(0, 64, 2, dtype=np.float32) / 64)).astype(np.float32)
    ang = pos[:, None] * inv[None, :]
    cos, sin = np.cos(ang).astype(np.float32).T, np.sin(ang).astype(np.float32).T
    ctab = np.ascontiguousarray(np.concatenate([cos, cos], 0))
    stab = np.ascontiguousarray(np.concatenate([-sin, sin], 0))
    return dict(fmask=np.ascontiguousarray(fm), mmask=np.ascontiguousarray(mm_), band=np.ascontiguousarray(band),
                tri=tri, ctab=ctab, stab=stab, ident=np.eye(128, dtype=np.float32).astype(bf))


def _host_layer_tables(inp, ls):
    L = len(ls)
    G = np.zeros((L, 128, NG), np.float32)
    sw = np.concatenate([np.arange(32, 64), np.arange(0, 32)])
    for i, l in enumerate(ls):
        def fm(v, n):
            return np.asarray(v).reshape(n, 128).T
        G[i, :, G_MIX:G_MIX + 16] = fm(inp["g_mix"][l], 16)
        G[i, :, G_CROSS:G_CROSS + 16] = fm(inp["g_cross"][l], 16)
        G[i, :, G_MLP:G_MLP + 16] = fm(inp["g_mlp"][l], 16)
        G[i, :, G_MEM:G_MEM + 16] = fm(inp["g_mem"][l], 16)
        G[i, :, G_CQ:G_CQ + 4] = fm(inp["g_cq"][l], 4)
        G[i, :, G_CKV:G_CKV + 2] = fm(inp["g_ckv"][l], 2)
        gq, gk = np.asarray(inp["g_mla_q"][l]), np.asarray(inp["g_mla_k"][l])
        G[i, :, G_MQN] = gq[:128]
        G[i, :64, G_MQR] = gq[128:]
        G[i, :64, G_MQS] = gq[128:][sw]
        G[i, :, G_MKN] = gk[:128]
        G[i, :64, G_MKR] = gk[128:]
        G[i, :64, G_MKS] = gk[128:][sw]
        for col, nm in ((G_FQ, "g_fox_q"), (G_FK, "g_fox_k"), (G_CQ2, "g_ch_q"), (G_CK2, "g_ch_k"),
                        (G_XQ, "g_x_q"), (G_XK, "g_x_k")):
            G[i, :, col] = np.asarray(inp[nm][l])
        G[i, :, G_BF:G_BF + 64] = np.tile(np.asarray(inp["b_f"][l])[None, :], (128, 8))
    sidx = np.arange(128)[:, None]
    uidx = np.arange(EW)[None, :]
    idx = np.clip(uidx - 384 - sidx, -128, 128) + 128
    Tz = np.ascontiguousarray(np.stack([np.asarray(inp["rel_bias"][l])[:, idx] for l in ls], 0)).astype(np.float32)
    w_uq = np.stack([np.asarray(inp["w_uq"][l]) for l in ls], 0)
    cols = []
    for hd in range(8):
        b = hd * 192
        cols += list(range(b, b + 128)) + list(range(b + 128, b + 192)) + list(b + 128 + sw)
    w_uqx = np.ascontiguousarray(w_uq[:, :, cols])
    krc = 768 + np.concatenate([np.arange(64), sw])
    w_krx = np.ascontiguousarray(np.stack([np.asarray(inp["w_in"][l][:, krc]) for l in ls], 0))
    return G, Tz, w_uqx, w_krx


_CACHE = {}
PHASES = []


def _get_program(nl, first, last):
    key = (nl, first, last)
    if key not in _CACHE:
        _CACHE[key] = build_program(nl, first, last)
    return _CACHE[key]


def _run_layers(inp, xT_list, ls, consts):
    nl = len(ls)
    nc = _get_program(nl, True, True)
    G, Tz, w_uqx, w_krx = _host_layer_tables(inp, ls)
    sel = (lambda a: np.ascontiguousarray(np.asarray(a)[ls[0]:ls[-1] + 1]))
    shared = dict(w_in=sel(inp["w_in"]), w_krx=w_krx, w_uqx=w_uqx, w_ukv=sel(inp["w_ukv"]), w_br=sel(inp["w_br"]),
                  w_out=sel(inp["w_out"]), w_xq=sel(inp["w_xq"]), w_xkv=sel(inp["w_xkv"]), w_xo=sel(inp["w_xo"]),
                  w_1=sel(inp["w_1"]), w_2=sel(inp["w_2"]), G=G, Tz=Tz, **consts)
    in_maps = []
    mem = np.asarray(inp["mem"])
    zero = None
    for c in range(N_LAUNCH_CORES):
        if c in CORE_OF_BATCH:
            bidx = CORE_OF_BATCH.index(c)
            m = dict(shared)
            m["xT"] = xT_list[bidx]
            m["memT"] = np.ascontiguousarray(mem[bidx].T)
        else:
            if zero is None:
                zero = {k: np.zeros_like(v) for k, v in shared.items()}
                zero["xT"] = np.zeros_like(xT_list[0])
                zero["memT"] = np.zeros((D, 256), np.float32)
            m = zero
        in_maps.append(m)
    res = run_bass_kernel_spmd(nc, in_maps, core_ids=list(range(N_LAUNCH_CORES)))
    return [np.asarray(res.results[c]["oT"]) for c in CORE_OF_BATCH]


N_LAUNCH_CORES = 8
CORE_OF_BATCH = [0, 1, 4, 5]
LAUNCH_GROUPS = [[0, 1, 2, 3]]


def kernel(**inp):
    x = np.asarray(inp["x"])
    consts = _host_consts()
    xT = [np.ascontiguousarray(x[b].T) for b in range(NCORES)]
    for ls in LAUNCH_GROUPS:
        xT = _run_layers(inp, xT, ls, consts)
    out = np.stack([xT[b].T for b in range(4)], 0).astype(np.float32)
    return np.ascontiguousarray(out)
```

```python
import numpy as np
import ml_dtypes
import concourse.bass as bass
import concourse.mybir as mybir
from concourse.bass_utils import run_bass_kernel_spmd
from contextlib import ExitStack
from bisect import bisect_left

F32 = mybir.dt.float32
BF16 = mybir.dt.bfloat16
AF = mybir.ActivationFunctionType
ALU = mybir.AluOpType

D = 2048
S = 2048
T = 1024
DEPTH = 4
D_IN = 13128
EPS = 1e-6
C_FOX = 832
C_FOXF = 3904
C_CH = 3912
C_GATE = 6984
NCORES = 4
G_MIX, G_CROSS, G_MLP, G_MEM, G_CQ, G_CKV = 0, 16, 32, 48, 64, 68
G_MQN, G_MQR, G_MQS, G_MKN, G_MKR, G_MKS = 70, 71, 72, 73, 74, 75
G_FQ, G_FK, G_CQ2, G_CK2, G_XQ, G_XK = 76, 77, 78, 79, 80, 81
G_BF = 82
NG = 146
EW = 1408


class Res:
    __slots__ = ("w", "wd", "re", "rd")

    def __init__(self):
        self.w = None
        self.wd = []
        self.re = {}
        self.rd = []


class EngQ:
    def __init__(self, name, eng, sem):
        self.name, self.eng, self.sem = name, eng, sem
        self.n = 0
        self.count = 0
        self.inc_idx = []
        self.inc_cnt = []
        self.last_handle = None
        self.last_idx = 0
        self.waited = {}

    def ensure_inc(self, idx):
        pos = bisect_left(self.inc_idx, idx)
        if pos < len(self.inc_idx):
            return self.inc_cnt[pos]
        assert self.last_idx >= idx
        self.count += 1
        self.last_handle.then_inc(self.sem, 1)
        self.inc_idx.append(self.last_idx)
        self.inc_cnt.append(self.count)
        return self.count


class FW:
    NDS = 24

    def __init__(self, nc, es):
        self.nc = nc
        self.dry = False
        self.engs = {}
        for name, eng in [("pe", nc.tensor), ("act", nc.scalar), ("dve", nc.vector),
                          ("pool", nc.gpsimd), ("sp", nc.sync)]:
            self.engs[name] = EngQ(name, eng, es.enter_context(nc.semaphore("s_" + name)))
        self.dsem = [es.enter_context(nc.semaphore("s_d%d" % i)) for i in range(self.NDS)]
        self.dcnt = [0] * self.NDS
        self.di = 0
        self.out_tokens = []

    def _wait(self, E, tok):
        if tok[0] == "e":
            Sq = self.engs[tok[1]]
            if Sq is E and E.name == "pe":
                return
            cnt = Sq.ensure_inc(tok[2])
            if E.waited.get(Sq.name, 0) >= cnt:
                return
            E.eng.wait_ge(Sq.sem, cnt)
            E.waited[Sq.name] = cnt
        else:
            key = ("d", tok[1])
            if E.waited.get(key, 0) >= tok[2]:
                return
            E.eng.wait_ge(self.dsem[tok[1]], tok[2])
            E.waited[key] = tok[2]

    def _deps(self, E, r, w, dma_write=False):
        deps = []
        for x in r:
            if x.w is not None:
                deps.append(x.w)
            deps.extend(x.wd)
        for x in w:
            if x.w is not None:
                deps.append(x.w)
            if not dma_write:
                deps.extend(x.wd)
            deps.extend(x.re.values())
            deps.extend(x.rd)
        for t in deps:
            self._wait(E, t)

    def _mark(self, tok, r, w):
        for x in r:
            if tok[0] == "e":
                x.re[tok[1]] = tok
            else:
                x.rd.append(tok)
        for x in w:
            if tok[0] == "e":
                x.w = tok
                x.wd = []
            else:
                x.w = None
                x.wd.append(tok)
                if len(x.wd) > 64:
                    x.wd = x.wd[-64:]
            x.re = {}
            x.rd = []

    def op(self, ename, fn, r=(), w=()):
        if self.dry:
            return
        E = self.engs[ename]
        self._deps(E, r, w)
        h = fn(E.eng)
        E.n += 1
        E.last_handle = h
        E.last_idx = E.n
        self._mark(("e", ename, E.n), r, w)

    def dma(self, qname, out, in_, r=(), w=(), is_out=False):
        if self.dry:
            return
        E = self.engs[qname]
        self._deps(E, r, w, dma_write=True)
        i = self.di % self.NDS
        self.di += 1
        if self.dcnt[i] > 0:
            self._wait(E, ("d", i, self.dcnt[i]))
        self.dcnt[i] += 16
        E.eng.dma_start(out=out, in_=in_).then_inc(self.dsem[i], 16)
        tok = ("d", i, self.dcnt[i])
        self._mark(tok, r, w)
        if is_out:
            self.out_tokens.append(tok)

    def finish(self):
        E = self.engs["sp"]
        for tok in self.out_tokens:
            self._wait(E, tok)


def build_program(nl, first, last, dbg=False):
    nc = bass.Bass("TRN2", target_bir_lowering=False)

    def din(name, shape, dt=F32):
        return nc.dram_tensor(name, shape, dt, kind="ExternalInput").ap()

    def dscr(name, shape, dt):
        return nc.dram_tensor(name, shape, dt, kind="Internal").ap()

    xT_in = din("xT", [D, S])
    memT = din("memT", [D, 256])
    w_in = din("w_in", [nl, D, D_IN])
    w_krx = din("w_krx", [nl, D, 128])
    w_uqx = din("w_uqx", [nl, 512, 2048])
    w_ukv = din("w_ukv", [nl, 256, 2048])
    w_br = din("w_br", [nl, 3, 1024, D])
    w_out = din("w_out", [nl, D, D])
    w_xq = din("w_xq", [nl, D, 512])
    w_xkv = din("w_xkv", [nl, D, 1024])
    w_xo = din("w_xo", [nl, 512, D])
    w_1 = din("w_1", [nl, D, 8192])
    w_2 = din("w_2", [nl, 8192, D])
    Gd = din("G", [nl, 128, NG])
    Tzd = din("Tz", [nl, 8, 128, EW])
    ctab_d = din("ctab", [64, S])
    stab_d = din("stab", [64, S])
    fmask_d = din("fmask", [128, 2048], BF16)
    mmask_d = din("mmask", [128, 2048], BF16)
    band_d = din("band", [128, EW], BF16)
    tri_d = din("tri", [128, 128])
    ident_d = din("ident", [128, 128], BF16)
    oT = nc.dram_tensor("oT", [D, S], F32, kind="ExternalOutput").ap()

    xs = dscr("xs", [D, S], F32)
    qTn = {b: dscr("qTn_" + b, [8, 128, T], BF16) for b in ("m", "f", "c")}
    qTr = dscr("qTr_m", [8, 64, T], BF16)
    kTn = {b: dscr("kTn_" + b, [8, 128, S], BF16) for b in ("m", "f", "c")}
    kTr = dscr("kTr_m", [8, 64, S], BF16)
    vv = {b: dscr("v_" + b, [S, 1024], BF16) for b in ("m", "f", "c")}
    gat = dscr("gat", [48, 128, T], BF16)
    rem0 = dscr("rem0", [128, 64], F32)

    es = ExitStack()
    with es:
        fw = FW(nc, es)

        def sb(name, shape, dt):
            return es.enter_context(nc.sbuf_tensor("sb_" + name, shape, dt))

        x = sb("x", [128, 16, T], F32)
        h = sb("h", [128, 16, T], BF16)
        big = sb("big", [128, 16, T], BF16)
        NSLOT = 3
        wr = [sb("wr%d" % i, [128, 4096], BF16) for i in range(NSLOT)]
        qn_t = sb("qn_t", [128, T], BF16)
        qr_t = sb("qr_t", [64, T], BF16)
        kn_t = sb("kn_t", [128, S], BF16)
        kr_t = sb("kr_t", [64, S], BF16)
        v_t = sb("v_t", [128, 16, 128], BF16)
        sq_t = [sb("sq%d" % i, [128, 512], BF16) for i in range(3)]
        rs_t = [sb("rs%d" % i, [128, 512], F32) for i in range(2)]
        tmp_t = [sb("tmp%d" % i, [128, 512], F32) for i in range(2)]
        st_t = [sb("st%d" % i, [128, 512], BF16) for i in range(4)]
        gt_t = [sb("gt%d" % i, [128, 512], BF16) for i in range(2)]
        p_t = [sb("p%d" % i, [128, 512], BF16) for i in range(4)]
        ME = sb("ME", [128, 3 * EW], BF16)
        mask_t = ME[:, 0:2048]
        band_t = ME[:, 0:EW]
        E_ts = [ME[:, EW:2 * EW], ME[:, 2 * EW:3 * EW]]
        kbase = sb("kbase", [64, T], BF16)
        sqkr = sb("sqkr", [64, T], BF16)
        G = sb("G", [128, NG], F32)
        ones_bf = sb("ones_bf", [128, 128], BF16)
        ones_f = sb("ones_f", [128, 128], F32)
        tri = sb("tri", [128, 128], F32)
        ident = sb("ident", [128, 128], BF16)
        ff = sb("ff", [128, 64], F32)
        prefx = sb("prefx", [128, 72], F32)
        Dtok = sb("Dtok", [128, 64], F32)
        Rsb = sb("Rsb", [128, 72], F32)
        rem_t = sb("rem_t", [128, 64], F32)
        bias_own = sb("bias_own", [128, 2, 8, 8], F32)
        bias_oth = sb("bias_oth", [128, 2, 8, 8], F32)
        ps = [es.enter_context(nc.psum_tensor("pp%d" % i, [128, 512], F32)) for i in range(8)]

        RR = {}

        def R(*key):
            r = RR.get(key)
            if r is None:
                r = RR[key] = Res()
            return r

        def fence(new_keys, old_prefixes):
            toks = []
            for k, r in RR.items():
                if k[0] in old_prefixes:
                    if r.w is not None:
                        toks.append(r.w)
                    toks.extend(r.wd)
                    toks.extend(r.re.values())
                    toks.extend(r.rd)
            for k in new_keys:
                r = R(*k)
                for t in toks:
                    if t[0] == "e":
                        old = r.re.get(t[1])
                        if old is None or old[2] < t[2]:
                            r.re[t[1]] = t
                    else:
                        r.rd.append(t)

        rot = {}

        def mark(name):
            if not fw.dry:
                PHASES.append((name, fw.engs["pe"].n))

        def nxt(name, n):
            i = rot.get(name, 0)
            rot[name] = i + 1
            return i % n

        plan = []
        wstate = {"i": 0, "issued": 0}
        wsrc = {"w_in": w_in, "w_krx": w_krx, "w_uqx": w_uqx, "w_ukv": w_ukv, "w_out": w_out,
                "w_xq": w_xq, "w_xkv": w_xkv, "w_xo": w_xo, "w_1": w_1, "w_2": w_2}

        def issue_slab(j):
            key = plan[j]
            name, li, k0, nk, c0, ncol = key
            if name.startswith("w_br"):
                src = w_br[li, int(name[4])]
            else:
                src = wsrc[name][li]
            slot = wr[j % NSLOT]
            fw.dma("pool", slot[:, 0:nk * ncol].rearrange("p (k c) -> p k c", k=nk),
                   src[k0 * 128:(k0 + nk) * 128, c0:c0 + ncol].rearrange("(k p) c -> p k c", p=128),
                   w=[R("wr", j % NSLOT)])

        def wget(name, li, k0, nk, c0, ncol):
            key = (name, li, k0, nk, c0, ncol)
            i = wstate["i"]
            wstate["i"] = i + 1
            if fw.dry:
                plan.append(key)
            else:
                assert plan[i] == key, (plan[i], key)
                while wstate["issued"] < min(len(plan), i + NSLOT):
                    issue_slab(wstate["issued"])
                    wstate["issued"] += 1
            slot = wr[i % NSLOT]
            return slot[:, 0:nk * ncol].rearrange("p (k c) -> p k c", k=nk), R("wr", i % NSLOT)

        def mm(out, lhsT, rhs, start, stop, r, w):
            fw.op("pe", lambda e: e.matmul(out, lhsT, rhs, start=start, stop=stop), r=r, w=w)

        def rstd_from(bank, npart, ncol, scale, rdeps):
            ri = nxt("rs", 2)
            rs = rs_t[ri]
            fw.op("act", lambda e: e.activation(out=rs[0:npart, 0:ncol], in_=ps[bank][0:npart, 0:ncol],
                                                func=AF.Sqrt, bias=EPS, scale=scale),
                  r=[R("ps", bank)], w=[R("rs", ri)])
            fw.op("dve", lambda e: e.reciprocal(out=rs[0:npart, 0:ncol], in_=rs[0:npart, 0:ncol]),
                  r=[R("rs", ri)], w=[R("rs", ri)])
            return ri

        def rmsnorm_fm(src, src_res, dst, dst_res, nch, ncols, gcol, ntile):
            for tt in range(ntile):
                t0, t1 = tt * 512, min(ncols, tt * 512 + 512)
                n = t1 - t0
                bank = 6 + nxt("aux", 2)
                for c in range(nch):
                    si = nxt("sq", 3)
                    fw.op("act", lambda e, c=c, si=si: e.activation(out=sq_t[si][:, 0:n], in_=src(c, t0, t1),
                                                                  func=AF.Square),
                          r=[src_res(c, tt)], w=[R("sq", si)])
                    mm(ps[bank][:, 0:n], ones_bf[:, :], sq_t[si][:, 0:n], c == 0, c == nch - 1,
                       r=[R("sq", si), R("ones")], w=[R("ps", bank)])
                ri = rstd_from(bank, 128, n, 1.0 / (nch * 128), None)
                for c in range(nch):
                    fw.op("dve", lambda e, c=c: e.scalar_tensor_tensor(
                        out=dst(c, t0, t1), in0=src(c, t0, t1), scalar=G[:, gcol + c:gcol + c + 1],
                        in1=rs_t[ri][:, 0:n], op0=ALU.mult, op1=ALU.mult),
                        r=[src_res(c, tt), R("rs", ri), R("G")], w=[dst_res(c, tt)])

        def head_norm(bank, n, gcol, dst_ap, dst_res, extra=None, div=128.0, sb_dst=None, sb_res=None):
            si = nxt("sq", 3)
            fw.op("act", lambda e: e.activation(out=sq_t[si][:, 0:n], in_=ps[bank][:, 0:n], func=AF.Square),
                  r=[R("ps", bank)], w=[R("sq", si)])
            ab = 6 + nxt("aux", 2)
            mm(ps[ab][:, 0:n], ones_bf[:, :], sq_t[si][:, 0:n], True, extra is None,
               r=[R("sq", si), R("ones")], w=[R("ps", ab)])
            if extra is not None:
                mm(ps[ab][:, 0:n], ones_bf[0:64, :], extra[0], False, True,
                   r=[extra[1], R("ones")], w=[R("ps", ab)])
            ri = rstd_from(ab, 128, n, 1.0 / div, None)
            if sb_dst is not None:
                fw.op("dve", lambda e: e.scalar_tensor_tensor(
                    out=sb_dst, in0=ps[bank][:, 0:n], scalar=G[:, gcol:gcol + 1], in1=rs_t[ri][:, 0:n],
                    op0=ALU.mult, op1=ALU.mult), r=[R("ps", bank), R("rs", ri), R("G")], w=[sb_res])
            else:
                sti = nxt("st", 4)
                fw.op("dve", lambda e: e.scalar_tensor_tensor(
                    out=st_t[sti][:, 0:n], in0=ps[bank][:, 0:n], scalar=G[:, gcol:gcol + 1],
                    in1=rs_t[ri][:, 0:n], op0=ALU.mult, op1=ALU.mult),
                    r=[R("ps", bank), R("rs", ri), R("G")], w=[R("st", sti)])
                fw.dma("sp", dst_ap, st_t[sti][:, 0:n], r=[R("st", sti)], w=[dst_res])
            return ri

        def evac_store(bank, npart, n, dst_ap, dst_res, func=AF.Copy):
            sti = nxt("st", 4)
            fw.op("act", lambda e: e.activation(out=st_t[sti][0:npart, 0:n], in_=ps[bank][0:npart, 0:n], func=func),
                  r=[R("ps", bank)], w=[R("st", sti)])
            fw.dma("sp", dst_ap, st_t[sti][0:npart, 0:n], r=[R("st", sti)], w=[dst_res])

        def proj_fm(slab, slab_res, m0, msz, nk, rhs, rhs_res, tt, bank, ncol=512):
            for kc in range(nk):
                mm(ps[bank][0:msz, 0:ncol], slab[:, kc, m0:m0 + msz], rhs(kc), kc == 0, kc == nk - 1,
                   r=[slab_res, rhs_res(kc, tt)], w=[R("ps", bank)])

        hres = lambda c, tt: R("h", c, tt)
        xres = lambda c, tt: R("x", c, tt)

        def emit_consts():
            fw.op("dve", lambda e: e.memset(ones_bf[:, :], 1.0), w=[R("ones")])
            fw.op("dve", lambda e: e.memset(ones_f[:, :], 1.0), w=[R("onesf")])
            fw.dma("sp", tri[:, :], tri_d, w=[R("tri")])
            fw.dma("sp", ident[:, :], ident_d, w=[R("ident")])

        def attention(br, hf, li):
            scale = (192.0 if br == "m" else 128.0) ** -0.5
            if br == "c":
                kb_lo = max(0, hf * 8 - 4)
            else:
                kb_lo = 0
            kb_hi = hf * 8 + 8
            nkb = kb_hi - kb_lo
            if br == "m":
                fence([("qn", 1), ("qr", 1), ("kn", 1), ("kr", 1), ("vt", 1)], ("big",))
            if br in ("m", "f"):
                fence([("mask",)], ("mask", "band", "E", "ropeC", "ropeS"))
                fw.dma("sp", mask_t, mmask_d if br == "m" else fmask_d, w=[R("mask")])
            else:
                fence([("band",), ("E", 0), ("E", 1)], ("mask", "band", "E", "ropeC", "ropeS"))
                fw.dma("sp", band_t, band_d, w=[R("band")])
            bufs = [dict(qn=qn_t, qr=qr_t, kn=kn_t, kr=kr_t, v=v_t),
                    dict(qn=big[:, 8, :], qr=big[0:64, 9, :],
                         kn=big[:, 10:12, :].rearrange("p a t -> p (a t)"),
                         kr=big[0:64, 12:14, :].rearrange("p a t -> p (a t)"),
                         v=big[:, 14:16, :].rearrange("p a (b d) -> p (a b) d", d=128))]

            def load_head(hd, si):
                bs = bufs[si]
                fw.dma("sp", bs["qn"][:, :], qTn[br][hd], r=[R("qTn", br, hd)], w=[R("qn", si)])
                fw.dma("sp", bs["kn"][:, 0:nkb * 128], kTn[br][hd][:, kb_lo * 128:kb_hi * 128],
                       r=[R("kTn", br, hd)], w=[R("kn", si)])
                if br == "m":
                    fw.dma("sp", bs["qr"][:, :], qTr[hd], r=[R("qTr", hd)], w=[R("qr", si)])
                    fw.dma("sp", bs["kr"][:, 0:nkb * 128], kTr[hd][:, kb_lo * 128:kb_hi * 128],
                           r=[R("kTr", hd)], w=[R("kr", si)])
                fw.dma("sp", bs["v"][:, 0:nkb, :],
                       vv[br][kb_lo * 128:kb_hi * 128, hd * 128:(hd + 1) * 128].rearrange("(b p) d -> p b d", p=128),
                       r=[R("v", br)], w=[R("vt", si)])
                pend = []
                if br == "c":
                    for q0 in range(0, EW, 512):
                        q1 = min(EW, q0 + 512)
                        ti = nxt("tmp", 2)
                        fw.dma("sp", tmp_t[ti][:, 0:q1 - q0], Tzd[li, hd][:, q0:q1], w=[R("tmp", ti)])
                        pend.append((ti, q0, q1))
                return pend

            def post_load(si, pend):
                for ti, q0, q1 in pend:
                    fw.op("dve", lambda e: e.scalar_tensor_tensor(
                        out=E_ts[si][:, q0:q1], in0=tmp_t[ti][:, 0:q1 - q0], scalar=1.0 / scale,
                        in1=band_t[:, q0:q1], op0=ALU.mult, op1=ALU.add),
                        r=[R("tmp", ti), R("band")], w=[R("E", si)])

            pend0 = load_head(0, 0)
            post_load(0, pend0)
            for hd in range(8):
                si = hd % 2
                bs = bufs[si]
                qn_b, qr_b, kn_b, kr_b, v_b = bs["qn"], bs["qr"], bs["kn"], bs["kr"], bs["v"]
                E_t = E_ts[si]
                pend_next = load_head(hd + 1, 1 - si) if hd + 1 < 8 else None
                seq = []
                for j in range(2):
                    qs = hf * 8 + j * 4
                    if br == "c":
                        blocks = list(range(max(0, qs - 4), qs + 4))
                    else:
                        blocks = list(range(0, qs + 4))
                    for bi, kb in enumerate(blocks):
                        seq.append((j, bi, kb, len(blocks), qs))
                sbank_of = {}
                olb = {}

                def emit_qk(t):
                    j, bi, kb, nb, qs = seq[t]
                    kl = kb - kb_lo
                    sbk = nxt("sbank", 6)
                    sbank_of[t] = sbk
                    ksl = slice(kl * 128, kl * 128 + 128)
                    qsl = slice(j * 512, j * 512 + 512)
                    if br == "c":
                        add_ap, add_res = E_t[:, 896 - 128 * (kb - (qs - 4)):896 - 128 * (kb - (qs - 4)) + 512], R("E", si)
                    elif kb >= qs:
                        add_ap, add_res = mask_t[:, (kb - qs) * 512:(kb - qs) * 512 + 512], R("mask")
                    else:
                        add_ap = None
                    mm(ps[sbk][:, :], kn_b[:, ksl], qn_b[:, qsl], True, br != "m" and add_ap is None,
                       r=[R("kn", si), R("qn", si)], w=[R("ps", sbk)])
                    if br == "m":
                        mm(ps[sbk][:, :], kr_b[:, ksl], qr_b[:, qsl], False, add_ap is None,
                           r=[R("kr", si), R("qr", si)], w=[R("ps", sbk)])
                    if add_ap is not None:
                        mm(ps[sbk][:, :], ident[:, :], add_ap, False, True,
                           r=[R("ident"), add_res], w=[R("ps", sbk)])

                def emit_exp(t):
                    j, bi, kb, nb, qs = seq[t]
                    sbk = sbank_of[t]
                    pi = nxt("p", 4)
                    if br == "f":
                        if kb >= hf * 8:
                            bias_ap = bias_own[:, j, kb - hf * 8, hd:hd + 1]
                            bres = R("bias_own")
                        else:
                            bias_ap = bias_oth[:, j, kb, hd:hd + 1]
                            bres = R("bias_oth")
                        fw.op("act", lambda e: e.activation(
                            out=p_t[pi][:, :], in_=ps[sbk][:, :], func=AF.Exp, bias=bias_ap, scale=scale),
                            r=[R("ps", sbk), bres], w=[R("p", pi)])
                    else:
                        fw.op("act", lambda e: e.activation(
                            out=p_t[pi][:, :], in_=ps[sbk][:, :], func=AF.Exp, scale=scale),
                            r=[R("ps", sbk)], w=[R("p", pi)])
                    return pi

                def emit_pv(t, pi):
                    j, bi, kb, nb, qs = seq[t]
                    kl = kb - kb_lo
                    if bi == 0:
                        olb[j] = (6, 7)
                    ob, lb = olb[j]
                    first_b, last_b = bi == 0, bi == nb - 1
                    mm(ps[ob][:, :], v_b[:, kl, :], p_t[pi][:, :], first_b, last_b,
                       r=[R("vt", si), R("p", pi)], w=[R("ps", ob)])
                    mm(ps[lb][:, :], ones_bf[:, :], p_t[pi][:, :], first_b, last_b,
                       r=[R("ones"), R("p", pi)], w=[R("ps", lb)])
                    if last_b:
                        qsl = slice(j * 512, j * 512 + 512)
                        ri = nxt("rs", 2)
                        fw.op("dve", lambda e: e.reciprocal(out=rs_t[ri][:, :], in_=ps[lb][:, :]),
                              r=[R("ps", lb)], w=[R("rs", ri)])
                        fw.op("dve", lambda e: e.tensor_tensor(
                            out=big[:, hd, qsl], in0=ps[ob][:, :], in1=rs_t[ri][:, :], op=ALU.mult),
                            r=[R("ps", ob), R("rs", ri)], w=[R("big", hd, j)])

                LA = 5
                for t in range(min(LA, len(seq))):
                    emit_qk(t)
                for t in range(len(seq)):
                    pi = emit_exp(t)
                    if t + LA < len(seq):
                        emit_qk(t + LA)
                    emit_pv(t, pi)
                    if t == len(seq) // 2 and pend_next is not None:
                        post_load(1 - si, pend_next)
            if br == "c":
                fence([("big", c, tt) for c in range(8, 16) for tt in range(2)], ("qn", "qr", "kn", "kr", "vt"))

        def merge(n, li):
            for sl in range(4):
                slab, sres = wget("w_br%d" % n, li, 0, 8, sl * 512, 512)
                for m in range(4):
                    c = sl * 4 + m
                    for tt in range(2):
                        tsl = slice(tt * 512, tt * 512 + 512)
                        bank = nxt("main", 4)
                        proj_fm(slab, sres, m * 128, 128, 8, lambda kc: big[:, kc, tsl],
                                lambda kc, tt_: R("big", kc, tt_), tt, bank)
                        gi = nxt("gt", 2)
                        fw.dma("sp", gt_t[gi][:, :], gat[n * 16 + c][:, tsl], r=[R("gat", n * 16 + c, tt)],
                               w=[R("gt", gi)])
                        if n == 0:
                            fw.op("dve", lambda e, bank=bank, gi=gi, c=c, tsl=tsl: e.tensor_tensor(
                                out=h[:, c, tsl], in0=ps[bank][:, :], in1=gt_t[gi][:, :], op=ALU.mult),
                                r=[R("ps", bank), R("gt", gi)], w=[R("h", c, tt)])
                        else:
                            ti = nxt("tmp", 2)
                            fw.op("dve", lambda e, bank=bank, gi=gi, ti=ti: e.tensor_tensor(
                                out=tmp_t[ti][:, :], in0=ps[bank][:, :], in1=gt_t[gi][:, :], op=ALU.mult),
                                r=[R("ps", bank), R("gt", gi)], w=[R("tmp", ti)])
                            fw.op("dve", lambda e, ti=ti, c=c, tsl=tsl: e.tensor_tensor(
                                out=h[:, c, tsl], in0=h[:, c, tsl], in1=tmp_t[ti][:, :], op=ALU.add),
                                r=[R("tmp", ti), R("h", c, tt)], w=[R("h", c, tt)])

        def add_to_x(bank, c, tt):
            tsl = slice(tt * 512, tt * 512 + 512)
            fw.op("dve", lambda e: e.tensor_tensor(out=x[:, c, tsl], in0=x[:, c, tsl], in1=ps[bank][:, :],
                                                   op=ALU.add), r=[R("ps", bank), R("x", c, tt)], w=[R("x", c, tt)])

        def emit_pass(li, hf, src_ap, dst_ap, is_final):
            tok0 = hf * T
            mark("norm")
            fw.dma("sp", x[:, :, :], src_ap[:, tok0:tok0 + T].rearrange("(c p) t -> p c t", p=128),
                   r=[R("xs", hf)], w=[R("x", c, tt) for c in range(16) for tt in range(2)])
            fw.dma("sp", G[:, :], Gd[li], w=[R("G")])
            rmsnorm_fm(lambda c, a, b: x[:, c, a:b], xres, lambda c, a, b: h[:, c, a:b], hres, 16, T, G_MIX, 2)

            hr = lambda tt: (lambda kc: h[:, kc, tt * 512:tt * 512 + 512])
            mark("z_mla")
            for sl in range(3):
                slab, sres = wget("w_in", li, 0, 16, sl * 256, 256)
                for m in range(2):
                    zc = sl * 2 + m
                    for tt in range(2):
                        bank = nxt("main", 6)
                        proj_fm(slab, sres, m * 128, 128, 16, hr(tt), hres, tt, bank)
                        fw.op("act", lambda e, bank=bank, zc=zc, tt=tt: e.activation(
                            out=big[:, 8 + zc, tt * 512:tt * 512 + 512], in_=ps[bank][:, :], func=AF.Copy),
                            r=[R("ps", bank)], w=[R("big", 8 + zc, tt)])
            slab, sres = wget("w_krx", li, 0, 16, 0, 128)
            for which in range(2):
                for tt in range(2):
                    bank = nxt("main", 6)
                    proj_fm(slab, sres, which * 64, 64, 16, hr(tt), hres, tt, bank)
                    fw.op("act", lambda e, bank=bank, which=which, tt=tt: e.activation(
                        out=big[0:64, 14 + which, tt * 512:tt * 512 + 512], in_=ps[bank][0:64, :], func=AF.Copy),
                        r=[R("ps", bank)], w=[R("big", 14 + which, tt)])
            mark("z_qkv")
            for br, cbase, gq, gk in (("f", C_FOX, G_FQ, G_FK), ("c", C_CH, G_CQ2, G_CK2)):
                for sl in range(8):
                    slab, sres = wget("w_in", li, 0, 16, cbase + sl * 256, 256)
                    for m in range(2):
                        hidx = sl * 2 + m
                        for tt in range(2):
                            bank = nxt("main", 6)
                            proj_fm(slab, sres, m * 128, 128, 16, hr(tt), hres, tt, bank)
                            if hidx < 8:
                                head_norm(bank, 512, gq, qTn[br][hidx][:, tt * 512:tt * 512 + 512],
                                          R("qTn", br, hidx))
                            else:
                                head_norm(bank, 512, gk,
                                          kTn[br][hidx - 8][:, tok0 + tt * 512:tok0 + tt * 512 + 512],
                                          R("kTn", br, hidx - 8))
                for sl in range(4):
                    slab, sres = wget("w_in", li, 0, 16, cbase + 2048 + sl * 256, 256)
                    for tb in range(8):
                        bank = nxt("main", 6)
                        for kc in range(16):
                            mm(ps[bank][:, 0:256], h[:, kc, tb * 128:tb * 128 + 128], slab[:, kc, :], kc == 0, kc == 15,
                               r=[sres, R("h", kc, tb // 4)], w=[R("ps", bank)])
                        evac_store(bank, 128, 256, vv[br][tok0 + tb * 128:tok0 + tb * 128 + 128, sl * 256:sl * 256 + 256],
                                   R("v", br))
                if br == "f":
                    slab, sres = wget("w_in", li, 0, 16, C_FOXF, 8)
                    fbank = 6 + nxt("aux", 2)
                    for tb in range(8):
                        for kc in range(16):
                            mm(ps[fbank][:, tb * 8:tb * 8 + 8], h[:, kc, tb * 128:tb * 128 + 128], slab[:, kc, :],
                               kc == 0, kc == 15, r=[sres, R("h", kc, tb // 4)], w=[R("ps", fbank)])
                    fw.op("dve", lambda e: e.tensor_tensor(out=ff[:, :], in0=ps[fbank][:, 0:64], in1=G[:, G_BF:G_BF + 64],
                                                           op=ALU.add), r=[R("ps", fbank), R("G")], w=[R("ff")])
                    fw.op("act", lambda e: e.activation(out=ff[:, :], in_=ff[:, :], func=AF.Exp, scale=-1.0),
                          r=[R("ff")], w=[R("ff")])
                    fw.op("act", lambda e: e.activation(out=ff[:, :], in_=ff[:, :], func=AF.Ln, bias=1.0),
                          r=[R("ff")], w=[R("ff")])
                    fw.op("dve", lambda e: e.memset(prefx[:, 0:8], 0.0), w=[R("prefx")])
                    for b in range(1, 9):
                        fw.op("dve", lambda e, b=b: e.tensor_tensor(
                            out=prefx[:, b * 8:b * 8 + 8], in0=prefx[:, b * 8 - 8:b * 8], in1=ff[:, b * 8 - 8:b * 8],
                            op=ALU.add), r=[R("prefx"), R("ff")], w=[R("prefx")])
                    cb = 6 + nxt("aux", 2)
                    mm(ps[cb][:, 0:64], tri[:, :], ff[:, :], True, True, r=[R("tri"), R("ff")], w=[R("ps", cb)])
                    mm(ps[cb][:, 64:136], ones_f[:, :], prefx[:, :], True, True, r=[R("onesf"), R("prefx")],
                       w=[R("ps", cb)])
                    fw.op("act", lambda e: e.activation(out=Rsb[:, :], in_=ps[cb][:, 64:136], func=AF.Copy),
                          r=[R("ps", cb)], w=[R("Rsb")])
                    fw.op("dve", lambda e: e.tensor_tensor(out=Dtok[:, :], in0=ps[cb][:, 0:64], in1=Rsb[:, 0:64],
                                                           op=ALU.add), r=[R("ps", cb), R("Rsb")], w=[R("Dtok")])
                    for j in range(2):
                        rc = (4 * j + 2) * 8
                        for kb in range(8):
                            fw.op("dve", lambda e, j=j, kb=kb, rc=rc: e.tensor_tensor(
                                out=bias_own[:, j, kb, :], in0=Dtok[:, kb * 8:kb * 8 + 8], in1=Rsb[:, rc:rc + 8],
                                op=ALU.subtract), r=[R("Dtok"), R("Rsb")], w=[R("bias_own")])
                    if hf == 0:
                        for kb in range(8):
                            fw.op("dve", lambda e, kb=kb: e.tensor_tensor(
                                out=rem_t[:, kb * 8:kb * 8 + 8], in0=Rsb[:, 64:72], in1=Dtok[:, kb * 8:kb * 8 + 8],
                                op=ALU.subtract), r=[R("Dtok"), R("Rsb")], w=[R("rem_t")])
                        fw.dma("sp", rem0, rem_t[:, :], r=[R("rem_t")], w=[R("rem0")])
                    else:
                        fw.dma("sp", rem_t[:, :], rem0, r=[R("rem0")], w=[R("rem_t")])
                        for j in range(2):
                            rc = (4 * j + 2) * 8
                            for kb in range(8):
                                fw.op("dve", lambda e, j=j, kb=kb, rc=rc: e.scalar_tensor_tensor(
                                    out=bias_oth[:, j, kb, :], in0=rem_t[:, kb * 8:kb * 8 + 8], scalar=-1.0,
                                    in1=Rsb[:, rc:rc + 8], op0=ALU.mult, op1=ALU.subtract),
                                    r=[R("rem_t"), R("Rsb")], w=[R("bias_oth")])
            mark("z_gates")
            for sl in range(24):
                slab, sres = wget("w_in", li, 0, 16, C_GATE + sl * 256, 256)
                for m in range(2):
                    gc = sl * 2 + m
                    for tt in range(2):
                        bank = nxt("main", 6)
                        proj_fm(slab, sres, m * 128, 128, 16, hr(tt), hres, tt, bank)
                        evac_store(bank, 128, 512, gat[gc][:, tt * 512:tt * 512 + 512], R("gat", gc, tt),
                                   func=AF.Sigmoid)

            mark("mla_prep")
            cqs = lambda c, a, b: big[:, 8 + c, a:b]
            cqr = lambda c, tt: R("big", 8 + c, tt)
            rmsnorm_fm(cqs, cqr, cqs, cqr, 4, T, G_CQ, 2)
            cks = lambda c, a, b: big[:, 12 + c, a:b]
            ckr = lambda c, tt: R("big", 12 + c, tt)
            rmsnorm_fm(cks, ckr, cks, ckr, 2, T, G_CKV, 2)
            for tt in range(2):
                tsl = slice(tt * 512, tt * 512 + 512)
                gsl = slice(tok0 + tt * 512, tok0 + tt * 512 + 512)
                t1, t2 = nxt("tmp", 2), nxt("tmp", 2)
                fw.dma("sp", tmp_t[t1][0:64, :], ctab_d[:, gsl], w=[R("tmp", t1)])
                fw.dma("sp", tmp_t[t2][0:64, :], stab_d[:, gsl], w=[R("tmp", t2)])
                fw.op("dve", lambda e, t1=t1, tsl=tsl: e.scalar_tensor_tensor(
                    out=tmp_t[t1][0:64, :], in0=big[0:64, 14, tsl], scalar=G[0:64, G_MKR:G_MKR + 1],
                    in1=tmp_t[t1][0:64, :], op0=ALU.mult, op1=ALU.mult),
                    r=[R("big", 14, tt), R("G"), R("tmp", t1)], w=[R("tmp", t1)])
                fw.op("dve", lambda e, t2=t2, tsl=tsl: e.scalar_tensor_tensor(
                    out=tmp_t[t2][0:64, :], in0=big[0:64, 15, tsl], scalar=G[0:64, G_MKS:G_MKS + 1],
                    in1=tmp_t[t2][0:64, :], op0=ALU.mult, op1=ALU.mult),
                    r=[R("big", 15, tt), R("G"), R("tmp", t2)], w=[R("tmp", t2)])
                fw.op("dve", lambda e, t1=t1, t2=t2, tsl=tsl: e.tensor_tensor(
                    out=kbase[:, tsl], in0=tmp_t[t1][0:64, :], in1=tmp_t[t2][0:64, :], op=ALU.add),
                    r=[R("tmp", t1), R("tmp", t2)], w=[R("kbase", tt)])
                fw.op("act", lambda e, tsl=tsl: e.activation(out=sqkr[:, tsl], in_=big[0:64, 14, tsl], func=AF.Square),
                      r=[R("big", 14, tt)], w=[R("sqkr", tt)])
            fence([("ropeC",), ("ropeS",)], ("mask", "band", "E"))
            ropeC, ropeS = ME[0:64, 0:T], ME[0:64, T:2 * T]
            fw.dma("pool", ropeC, ctab_d[:, tok0:tok0 + T], w=[R("ropeC")])
            fw.dma("pool", ropeS, stab_d[:, tok0:tok0 + T], w=[R("ropeS")])
            fw.op("dve", lambda e: e.tensor_scalar(out=ropeC, in0=ropeC, scalar1=G[0:64, G_MQR:G_MQR + 1], scalar2=None,
                                                   op0=ALU.mult), r=[R("ropeC"), R("G")], w=[R("ropeC")])
            fw.op("dve", lambda e: e.tensor_scalar(out=ropeS, in0=ropeS, scalar1=G[0:64, G_MQS:G_MQS + 1], scalar2=None,
                                                   op0=ALU.mult), r=[R("ropeS"), R("G")], w=[R("ropeS")])
            qit = [(sl, hh, tt) for sl in range(2) for hh in range(4) for tt in range(2)]
            qslab = {}
            qbanks = {}

            def q_s1(i):
                sl, hh, tt = qit[i]
                if sl not in qslab:
                    qslab[sl] = wget("w_uqx", li, 0, 4, sl * 1024, 1024)
                slab, sres = qslab[sl]
                tsl = slice(tt * 512, tt * 512 + 512)
                rhs = lambda kc: big[:, 8 + kc, tsl]
                bA, b1, b2 = nxt("main", 6), nxt("main", 6), nxt("main", 6)
                proj_fm(slab, sres, hh * 256, 128, 4, rhs, cqr, tt, bA)
                proj_fm(slab, sres, hh * 256 + 128, 64, 4, rhs, cqr, tt, b1)
                proj_fm(slab, sres, hh * 256 + 192, 64, 4, rhs, cqr, tt, b2)
                qbanks[i] = (bA, b1, b2)

            def q_s2(i):
                sl, hh, tt = qit[i]
                hd = sl * 4 + hh
                bA, b1, b2 = qbanks[i]
                tsl = slice(tt * 512, tt * 512 + 512)
                gsl = slice(tok0 + tt * 512, tok0 + tt * 512 + 512)
                si = nxt("sq", 3)
                fw.op("act", lambda e: e.activation(out=sq_t[si][0:64, :], in_=ps[b1][0:64, :], func=AF.Square),
                      r=[R("ps", b1)], w=[R("sq", si)])
                t1, t2 = nxt("tmp", 2), nxt("tmp", 2)
                fw.op("dve", lambda e: e.tensor_tensor(
                    out=tmp_t[t1][0:64, :], in0=ps[b1][0:64, :], in1=ropeC[:, tsl], op=ALU.mult),
                    r=[R("ps", b1), R("ropeC")], w=[R("tmp", t1)])
                fw.op("dve", lambda e: e.tensor_tensor(
                    out=tmp_t[t2][0:64, :], in0=ps[b2][0:64, :], in1=ropeS[:, tsl], op=ALU.mult),
                    r=[R("ps", b2), R("ropeS")], w=[R("tmp", t2)])
                fw.op("dve", lambda e: e.tensor_tensor(
                    out=tmp_t[t1][0:64, :], in0=tmp_t[t1][0:64, :], in1=tmp_t[t2][0:64, :], op=ALU.add),
                    r=[R("tmp", t1), R("tmp", t2)], w=[R("tmp", t1)])
                ri = head_norm(bA, 512, G_MQN, qTn["m"][hd][:, tsl], R("qTn", "m", hd),
                               extra=(sq_t[si][0:64, :], R("sq", si)), div=192.0)
                sti = nxt("st", 4)
                fw.op("dve", lambda e: e.tensor_tensor(
                    out=st_t[sti][0:64, :], in0=tmp_t[t1][0:64, :], in1=rs_t[ri][0:64, :], op=ALU.mult),
                    r=[R("tmp", t1), R("rs", ri)], w=[R("st", sti)])
                fw.dma("sp", qTr[hd][:, tsl], st_t[sti][0:64, :], r=[R("st", sti)], w=[R("qTr", hd)])

            q_s1(0)
            for i in range(len(qit)):
                if i + 1 < len(qit):
                    q_s1(i + 1)
                q_s2(i)
            slab, sres = wget("w_ukv", li, 0, 2, 0, 2048)
            kit = [(hd, tt) for hd in range(8) for tt in range(2)]
            kbank = {}

            def k_s1(i):
                hd, tt = kit[i]
                tsl = slice(tt * 512, tt * 512 + 512)
                bA = nxt("main", 6)
                proj_fm(slab, sres, hd * 256, 128, 2, lambda kc: big[:, 12 + kc, tsl], ckr, tt, bA)
                kbank[i] = bA

            def k_s2(i):
                hd, tt = kit[i]
                bA = kbank[i]
                tsl = slice(tt * 512, tt * 512 + 512)
                gsl = slice(tok0 + tt * 512, tok0 + tt * 512 + 512)
                ri = head_norm(bA, 512, G_MKN, kTn["m"][hd][:, gsl], R("kTn", "m", hd),
                               extra=(sqkr[:, tsl], R("sqkr", tt)), div=192.0)
                sti = nxt("st", 4)
                fw.op("dve", lambda e: e.tensor_tensor(
                    out=st_t[sti][0:64, :], in0=kbase[:, tsl], in1=rs_t[ri][0:64, :], op=ALU.mult),
                    r=[R("kbase", tt), R("rs", ri)], w=[R("st", sti)])
                fw.dma("sp", kTr[hd][:, gsl], st_t[sti][0:64, :], r=[R("st", sti)], w=[R("kTr", hd)])

            k_s1(0)
            k_s1(1)
            for i in range(len(kit)):
                if i + 2 < len(kit):
                    k_s1(i + 2)
                k_s2(i)
            for tb in range(8):
                for hg in range(2):
                    bank = nxt("main", 6)
                    for hh in range(4):
                        hd = hg * 4 + hh
                        for kc in range(2):
                            mm(ps[bank][:, hh * 128:hh * 128 + 128], big[:, 12 + kc, tb * 128:tb * 128 + 128],
                               slab[:, kc, hd * 256 + 128:hd * 256 + 256], kc == 0, kc == 1,
                               r=[sres, R("big", 12 + kc, tb // 4)], w=[R("ps", bank)])
                    evac_store(bank, 128, 512, vv["m"][tok0 + tb * 128:tok0 + tb * 128 + 128, hg * 512:hg * 512 + 512],
                               R("v", "m"))

            for n, br in enumerate(("m", "f", "c")):
                mark("att_" + br)
                attention(br, hf, li)
                mark("merge_" + br)
                merge(n, li)
            mark("w_out")
            for sl in range(8):
                slab, sres = wget("w_out", li, 0, 16, sl * 256, 256)
                for m in range(2):
                    c = sl * 2 + m
                    for tt in range(2):
                        bank = nxt("main", 6)
                        proj_fm(slab, sres, m * 128, 128, 16, hr(tt), hres, tt, bank)
                        add_to_x(bank, c, tt)

            mark("cross")
            rmsnorm_fm(lambda c, a, b: x[:, c, a:b], xres, lambda c, a, b: h[:, c, a:b], hres, 16, T, G_CROSS, 2)
            fw.dma("pool", big[:, :, 0:256], memT.rearrange("(c p) m -> p c m", p=128),
                   w=[R("big", c, 0) for c in range(16)])
            mems = lambda c, a, b: big[:, c, a:b]
            memr = lambda c, tt: R("big", c, 0)
            rmsnorm_fm(mems, memr, mems, memr, 16, 256, G_MEM, 1)
            for sl in range(4):
                slab, sres = wget("w_xkv", li, 0, 16, sl * 256, 256)
                if sl < 2:
                    for m in range(2):
                        hd = sl * 2 + m
                        bank = nxt("main", 6)
                        proj_fm(slab, sres, m * 128, 128, 16, lambda kc: big[:, kc, 0:256], memr, 0, bank, ncol=256)
                        head_norm(bank, 256, G_XK, None, None, sb_dst=big[:, hd, 256:512], sb_res=R("xk", hd))
                else:
                    for mb in range(2):
                        bank = nxt("main", 6)
                        for kc in range(16):
                            mm(ps[bank][:, 0:256], big[:, kc, mb * 128:mb * 128 + 128], slab[:, kc, :], kc == 0, kc == 15,
                               r=[sres, R("big", kc, 0)], w=[R("ps", bank)])
                        c0 = 512 + (sl - 2) * 256
                        fw.op("act", lambda e, bank=bank, mb=mb, c0=c0: e.activation(
                            out=big[:, mb, c0:c0 + 256], in_=ps[bank][:, 0:256], func=AF.Copy),
                            r=[R("ps", bank)], w=[R("xv", mb, sl - 2)])
            for sl in range(2):
                slab, sres = wget("w_xq", li, 0, 16, sl * 256, 256)
                for m in range(2):
                    hd = sl * 2 + m
                    for tt in range(2):
                        bank = nxt("main", 6)
                        proj_fm(slab, sres, m * 128, 128, 16, hr(tt), hres, tt, bank)
                        head_norm(bank, 512, G_XQ, None, None, sb_dst=big[:, 4 + hd, tt * 512:tt * 512 + 512],
                                  sb_res=R("xq", hd, tt))
            xscale = 128.0 ** -0.5
            xseq = [(hd, j, mb) for hd in range(4) for j in range(2) for mb in range(2)]
            xsb = {}
            xol = {}

            def x_qk(t):
                hd, j, mb = xseq[t]
                sbk = nxt("sbank", 4)
                xsb[t] = sbk
                mm(ps[sbk][:, :], big[:, hd, 256 + mb * 128:256 + mb * 128 + 128], big[:, 4 + hd, j * 512:j * 512 + 512],
                   True, True, r=[R("xk", hd), R("xq", hd, j)], w=[R("ps", sbk)])

            def x_rest(t):
                hd, j, mb = xseq[t]
                sbk = xsb[t]
                pi = nxt("p", 4)
                fw.op("act", lambda e: e.activation(out=p_t[pi][:, :], in_=ps[sbk][:, :], func=AF.Exp, scale=xscale),
                      r=[R("ps", sbk)], w=[R("p", pi)])
                if t + 2 < len(xseq):
                    x_qk(t + 2)
                if mb == 0:
                    oi = nxt("ol", 2)
                    xol[(hd, j)] = (4 + 2 * oi, 5 + 2 * oi)
                ob, lb = xol[(hd, j)]
                mm(ps[ob][:, :], big[:, mb, 512 + hd * 128:512 + hd * 128 + 128], p_t[pi][:, :], mb == 0, mb == 1,
                   r=[R("xv", mb, hd // 2), R("p", pi)], w=[R("ps", ob)])
                mm(ps[lb][:, :], ones_bf[:, :], p_t[pi][:, :], mb == 0, mb == 1,
                   r=[R("ones"), R("p", pi)], w=[R("ps", lb)])
                if mb == 1:
                    qsl = slice(j * 512, j * 512 + 512)
                    ri = nxt("rs", 2)
                    fw.op("dve", lambda e: e.reciprocal(out=rs_t[ri][:, :], in_=ps[lb][:, :]),
                          r=[R("ps", lb)], w=[R("rs", ri)])
                    fw.op("dve", lambda e: e.tensor_tensor(
                        out=big[:, 8 + hd, qsl], in0=ps[ob][:, :], in1=rs_t[ri][:, :], op=ALU.mult),
                        r=[R("ps", ob), R("rs", ri)], w=[R("xo", hd, j)])

            x_qk(0)
            x_qk(1)
            for t in range(len(xseq)):
                x_rest(t)
            for sl in range(4):
                slab, sres = wget("w_xo", li, 0, 4, sl * 512, 512)
                for m in range(4):
                    c = sl * 4 + m
                    for tt in range(2):
                        tsl = slice(tt * 512, tt * 512 + 512)
                        bank = nxt("main", 4)
                        proj_fm(slab, sres, m * 128, 128, 4, lambda kc: big[:, 8 + kc, tsl],
                                lambda kc, tt_: R("xo", kc, tt_), tt, bank)
                        add_to_x(bank, c, tt)

            mark("mlp")
            rmsnorm_fm(lambda c, a, b: x[:, c, a:b], xres, lambda c, a, b: h[:, c, a:b], hres, 16, T, G_MLP, 2)
            fence([("big", c, tt) for c in range(16) for tt in range(2)],
                  ("big", "xk", "xv", "xq", "xo"))
            for fs in range(4):
                for sl in range(8):
                    slab, sres = wget("w_1", li, 0, 16, fs * 2048 + sl * 256, 256)
                    for m in range(2):
                        uc = sl * 2 + m
                        for tt in range(2):
                            tsl = slice(tt * 512, tt * 512 + 512)
                            bank = nxt("main", 6)
                            proj_fm(slab, sres, m * 128, 128, 16, hr(tt), hres, tt, bank)
                            ti = nxt("tmp", 2)
                            fw.op("act", lambda e, ti=ti, bank=bank: e.activation(out=tmp_t[ti][:, :], in_=ps[bank][:, :],
                                                                                func=AF.Relu),
                                  r=[R("ps", bank)], w=[R("tmp", ti)])
                            fw.op("dve", lambda e, ti=ti, uc=uc, tsl=tsl: e.tensor_tensor(
                                out=big[:, uc, tsl], in0=tmp_t[ti][:, :], in1=tmp_t[ti][:, :], op=ALU.mult),
                                r=[R("tmp", ti)], w=[R("big", uc, tt)])
                for sl in range(8):
                    slab, sres = wget("w_2", li, fs * 16, 16, sl * 256, 256)
                    for m in range(2):
                        c = sl * 2 + m
                        for tt in range(2):
                            tsl = slice(tt * 512, tt * 512 + 512)
                            bank = nxt("main", 6)
                            proj_fm(slab, sres, m * 128, 128, 16, lambda kc: big[:, kc, tsl],
                                    lambda kc, tt_: R("big", kc, tt_), tt, bank)
                            add_to_x(bank, c, tt)
            fw.dma("sp", dst_ap[:, tok0:tok0 + T].rearrange("(c p) t -> p c t", p=128), x[:, :, :],
                   r=[R("x", c, tt) for c in range(16) for tt in range(2)], w=[R("xs", hf)], is_out=is_final)

        def emit_all():
            rot.clear()
            wstate["i"] = 0
            emit_consts()
            for li in range(nl):
                for hf in range(2):
                    src = xT_in if li == 0 else xs
                    fin = li == nl - 1
                    emit_pass(li, hf, src, oT if fin else xs, fin)
            fw.finish()

        fw.dry = True
        emit_all()
        fw.dry = False
        emit_all()
    return nc


def _host_consts():
    bf = ml_dtypes.bfloat16
    s = np.arange(128)[:, None]
    t = np.arange(512)[None, :]
    NEGM = np.float32(-30000.0)
    fm = np.where(np.concatenate([((o * 128 + s) <= t) for o in range(4)], axis=1), np.float32(0), NEGM).astype(bf)
    mm_ = np.where(np.concatenate([(((o * 128 + s) // 64) <= (t // 64)) for o in range(4)], axis=1),
                   np.float32(0), NEGM).astype(bf)
    u = np.arange(EW)[None, :]
    qc = (u - 384) // 64
    kc = s // 64
    band = np.where((kc >= qc - 8) & (kc <= qc), np.float32(0), NEGM).astype(bf)
    tri = (s <= np.arange(128)[None, :]).astype(np.float32)
    pos = np.arange(S, dtype=np.float32)
    inv = (10000.0 ** (-np.arange(0, 64, 2, dtype=np.float32) / 64)).astype(np.float32)
    ang = pos[:, None] * inv[None, :]
    cos, sin = np.cos(ang).astype(np.float32).T, np.sin(ang).astype(np.float32).T
    ctab = np.ascontiguousarray(np.concatenate([cos, cos], 0))
    stab = np.ascontiguousarray(np.concatenate([-sin, sin], 0))
    return dict(fmask=np.ascontiguousarray(fm), mmask=np.ascontiguousarray(mm_), band=np.ascontiguousarray(band),
                tri=tri, ctab=ctab, stab=stab, ident=np.eye(128, dtype=np.float32).astype(bf))


def _host_layer_tables(inp, ls):
    L = len(ls)
    G = np.zeros((L, 128, NG), np.float32)
    sw = np.concatenate([np.arange(32, 64), np.arange(0, 32)])
    for i, l in enumerate(ls):
        def fm(v, n):
            return np.asarray(v).reshape(n, 128).T
        G[i, :, G_MIX:G_MIX + 16] = fm(inp["g_mix"][l], 16)
        G[i, :, G_CROSS:G_CROSS + 16] = fm(inp["g_cross"][l], 16)
        G[i, :, G_MLP:G_MLP + 16] = fm(inp["g_mlp"][l], 16)
        G[i, :, G_MEM:G_MEM + 16] = fm(inp["g_mem"][l], 16)
        G[i, :, G_CQ:G_CQ + 4] = fm(inp["g_cq"][l], 4)
        G[i, :, G_CKV:G_CKV + 2] = fm(inp["g_ckv"][l], 2)
        gq, gk = np.asarray(inp["g_mla_q"][l]), np.asarray(inp["g_mla_k"][l])
        G[i, :, G_MQN] = gq[:128]
        G[i, :64, G_MQR] = gq[128:]
        G[i, :64, G_MQS] = gq[128:][sw]
        G[i, :, G_MKN] = gk[:128]
        G[i, :64, G_MKR] = gk[128:]
        G[i, :64, G_MKS] = gk[128:][sw]
        for col, nm in ((G_FQ, "g_fox_q"), (G_FK, "g_fox_k"), (G_CQ2, "g_ch_q"), (G_CK2, "g_ch_k"),
                        (G_XQ, "g_x_q"), (G_XK, "g_x_k")):
            G[i, :, col] = np.asarray(inp[nm][l])
        G[i, :, G_BF:G_BF + 64] = np.tile(np.asarray(inp["b_f"][l])[None, :], (128, 8))
    sidx = np.arange(128)[:, None]
    uidx = np.arange(EW)[None, :]
    idx = np.clip(uidx - 384 - sidx, -128, 128) + 128
    Tz = np.ascontiguousarray(np.stack([np.asarray(inp["rel_bias"][l])[:, idx] for l in ls], 0)).astype(np.float32)
    w_uq = np.stack([np.asarray(inp["w_uq"][l]) for l in ls], 0)
    cols = []
    for hd in range(8):
        b = hd * 192
        cols += list(range(b, b + 128)) + list(range(b + 128, b + 192)) + list(b + 128 + sw)
    w_uqx = np.ascontiguousarray(w_uq[:, :, cols])
    krc = 768 + np.concatenate([np.arange(64), sw])
    w_krx = np.ascontiguousarray(np.stack([np.asarray(inp["w_in"][l][:, krc]) for l in ls], 0))
    return G, Tz, w_uqx, w_krx


_CACHE = {}
PHASES = []


def _get_program(nl, first, last):
    key = (nl, first, last)
    if key not in _CACHE:
        _CACHE[key] = build_program(nl, first, last)
    return _CACHE[key]


def _run_layers(inp, xT_list, ls, consts):
    nl = len(ls)
    nc = _get_program(nl, True, True)
    G, Tz, w_uqx, w_krx = _host_layer_tables(inp, ls)
    sel = (lambda a: np.ascontiguousarray(np.asarray(a)[ls[0]:ls[-1] + 1]))
    shared = dict(w_in=sel(inp["w_in"]), w_krx=w_krx, w_uqx=w_uqx, w_ukv=sel(inp["w_ukv"]), w_br=sel(inp["w_br"]),
                  w_out=sel(inp["w_out"]), w_xq=sel(inp["w_xq"]), w_xkv=sel(inp["w_xkv"]), w_xo=sel(inp["w_xo"]),
                  w_1=sel(inp["w_1"]), w_2=sel(inp["w_2"]), G=G, Tz=Tz, **consts)
    in_maps = []
    mem = np.asarray(inp["mem"])
    zero = None
    for c in range(N_LAUNCH_CORES):
        if c in CORE_OF_BATCH:
            bidx = CORE_OF_BATCH.index(c)
            m = dict(shared)
            m["xT"] = xT_list[bidx]
            m["memT"] = np.ascontiguousarray(mem[bidx].T)
        else:
            if zero is None:
                zero = {k: np.zeros_like(v) for k, v in shared.items()}
                zero["xT"] = np.zeros_like(xT_list[0])
                zero["memT"] = np.zeros((D, 256), np.float32)
            m = zero
        in_maps.append(m)
    res = run_bass_kernel_spmd(nc, in_maps, core_ids=list(range(N_LAUNCH_CORES)))
    return [np.asarray(res.results[c]["oT"]) for c in CORE_OF_BATCH]


N_LAUNCH_CORES = 8
CORE_OF_BATCH = [0, 1, 4, 5]
LAUNCH_GROUPS = [[0, 1, 2, 3]]


def kernel(**inp):
    x = np.asarray(inp["x"])
    consts = _host_consts()
    xT = [np.ascontiguousarray(x[b].T) for b in range(NCORES)]
    for ls in LAUNCH_GROUPS:
        xT = _run_layers(inp, xT, ls, consts)
    out = np.stack([xT[b].T for b in range(4)], 0).astype(np.float32)
    return np.ascontiguousarray(out)
```

```python
import numpy as np
import ml_dtypes
import concourse.bass as bass
import concourse.mybir as mybir
from concourse.bass_utils import run_bass_kernel_spmd
from contextlib import ExitStack
from bisect import bisect_left

F32 = mybir.dt.float32
BF16 = mybir.dt.bfloat16
AF = mybir.ActivationFunctionType
ALU = mybir.AluOpType

D = 2048
S = 2048
T = 1024
DEPTH = 4
D_IN = 13128
EPS = 1e-6
C_FOX = 832
C_FOXF = 3904
C_CH = 3912
C_GATE = 6984
NCORES = 4
G_MIX, G_CROSS, G_MLP, G_MEM, G_CQ, G_CKV = 0, 16, 32, 48, 64, 68
G_MQN, G_MQR, G_MQS, G_MKN, G_MKR, G_MKS = 70, 71, 72, 73, 74, 75
G_FQ, G_FK, G_CQ2, G_CK2, G_XQ, G_XK = 76, 77, 78, 79, 80, 81
G_BF = 82
NG = 146
EW = 1408


class Res:
    __slots__ = ("w", "wd", "re", "rd")

    def __init__(self):
        self.w = None
        self.wd = []
        self.re = {}
        self.rd = []


class EngQ:
    def __init__(self, name, eng, sem):
        self.name, self.eng, self.sem = name, eng, sem
        self.n = 0
        self.count = 0
        self.inc_idx = []
        self.inc_cnt = []
        self.last_handle = None
        self.last_idx = 0
        self.waited = {}

    def ensure_inc(self, idx):
        pos = bisect_left(self.inc_idx, idx)
        if pos < len(self.inc_idx):
            return self.inc_cnt[pos]
        assert self.last_idx >= idx
        self.count += 1
        self.last_handle.then_inc(self.sem, 1)
        self.inc_idx.append(self.last_idx)
        self.inc_cnt.append(self.count)
        return self.count


class FW:
    NDS = 24

    def __init__(self, nc, es):
        self.nc = nc
        self.dry = False
        self.engs = {}
        for name, eng in [("pe", nc.tensor), ("act", nc.scalar), ("dve", nc.vector),
                          ("pool", nc.gpsimd), ("sp", nc.sync)]:
            self.engs[name] = EngQ(name, eng, es.enter_context(nc.semaphore("s_" + name)))
        self.dsem = [es.enter_context(nc.semaphore("s_d%d" % i)) for i in range(self.NDS)]
        self.dcnt = [0] * self.NDS
        self.di = 0
        self.out_tokens = []

    def _wait(self, E, tok):
        if tok[0] == "e":
            Sq = self.engs[tok[1]]
            if Sq is E and E.name == "pe":
                return
            cnt = Sq.ensure_inc(tok[2])
            if E.waited.get(Sq.name, 0) >= cnt:
                return
            E.eng.wait_ge(Sq.sem, cnt)
            E.waited[Sq.name] = cnt
        else:
            key = ("d", tok[1])
            if E.waited.get(key, 0) >= tok[2]:
                return
            E.eng.wait_ge(self.dsem[tok[1]], tok[2])
            E.waited[key] = tok[2]

    def _deps(self, E, r, w, dma_write=False):
        deps = []
        for x in r:
            if x.w is not None:
                deps.append(x.w)
            deps.extend(x.wd)
        for x in w:
            if x.w is not None:
                deps.append(x.w)
            if not dma_write:
                deps.extend(x.wd)
            deps.extend(x.re.values())
            deps.extend(x.rd)
        for t in deps:
            self._wait(E, t)

    def _mark(self, tok, r, w):
        for x in r:
            if tok[0] == "e":
                x.re[tok[1]] = tok
            else:
                x.rd.append(tok)
        for x in w:
            if tok[0] == "e":
                x.w = tok
                x.wd = []
            else:
                x.w = None
                x.wd.append(tok)
                if len(x.wd) > 64:
                    x.wd = x.wd[-64:]
            x.re = {}
            x.rd = []

    def op(self, ename, fn, r=(), w=()):
        if self.dry:
            return
        E = self.engs[ename]
        self._deps(E, r, w)
        h = fn(E.eng)
        E.n += 1
        E.last_handle = h
        E.last_idx = E.n
        self._mark(("e", ename, E.n), r, w)

    def dma(self, qname, out, in_, r=(), w=(), is_out=False):
        if self.dry:
            return
        E = self.engs[qname]
        self._deps(E, r, w, dma_write=True)
        i = self.di % self.NDS
        self.di += 1
        if self.dcnt[i] > 0:
            self._wait(E, ("d", i, self.dcnt[i]))
        self.dcnt[i] += 16
        E.eng.dma_start(out=out, in_=in_).then_inc(self.dsem[i], 16)
        tok = ("d", i, self.dcnt[i])
        self._mark(tok, r, w)
        if is_out:
            self.out_tokens.append(tok)

    def finish(self):
        E = self.engs["sp"]
        for tok in self.out_tokens:
            self._wait(E, tok)


def build_program(nl, first, last, dbg=False):
    nc = bass.Bass("TRN2", target_bir_lowering=False)

    def din(name, shape, dt=F32):
        return nc.dram_tensor(name, shape, dt, kind="ExternalInput").ap()

    def dscr(name, shape, dt):
        return nc.dram_tensor(name, shape, dt, kind="Internal").ap()

    xT_in = din("xT", [D, S])
    memT = din("memT", [D, 256])
    w_in = din("w_in", [nl, D, D_IN])
    w_krx = din("w_krx", [nl, D, 128])
    w_uqx = din("w_uqx", [nl, 512, 2048])
    w_ukv = din("w_ukv", [nl, 256, 2048])
    w_br = din("w_br", [nl, 3, 1024, D])
    w_out = din("w_out", [nl, D, D])
    w_xq = din("w_xq", [nl, D, 512])
    w_xkv = din("w_xkv", [nl, D, 1024])
    w_xo = din("w_xo", [nl, 512, D])
    w_1 = din("w_1", [nl, D, 8192])
    w_2 = din("w_2", [nl, 8192, D])
    Gd = din("G", [nl, 128, NG])
    Tzd = din("Tz", [nl, 8, 128, EW])
    ctab_d = din("ctab", [64, S])
    stab_d = din("stab", [64, S])
    fmask_d = din("fmask", [128, 2048], BF16)
    mmask_d = din("mmask", [128, 2048], BF16)
    band_d = din("band", [128, EW], BF16)
    tri_d = din("tri", [128, 128])
    ident_d = din("ident", [128, 128], BF16)
    oT = nc.dram_tensor("oT", [D, S], F32, kind="ExternalOutput").ap()

    xs = dscr("xs", [D, S], F32)
    qTn = {b: dscr("qTn_" + b, [8, 128, T], BF16) for b in ("m", "f", "c")}
    qTr = dscr("qTr_m", [8, 64, T], BF16)
    kTn = {b: dscr("kTn_" + b, [8, 128, S], BF16) for b in ("m", "f", "c")}
    kTr = dscr("kTr_m", [8, 64, S], BF16)
    vv = {b: dscr("v_" + b, [S, 1024], BF16) for b in ("m", "f", "c")}
    gat = dscr("gat", [48, 128, T], BF16)
    rem0 = dscr("rem0", [128, 64], F32)
    kx_d = dscr("kx_d", [128, 4, 256], BF16)
    vx_d = dscr("vx_d", [128, 2, 512], BF16)

    es = ExitStack()
    with es:
        fw = FW(nc, es)

        def sb(name, shape, dt):
            return es.enter_context(nc.sbuf_tensor("sb_" + name, shape, dt))

        x = sb("x", [128, 16, T], F32)
        h = sb("h", [128, 16, T], BF16)
        big = sb("big", [128, 16, T], BF16)
        NSLOT = 3
        wr = [sb("wr%d" % i, [128, 4096], BF16) for i in range(NSLOT)]
        qn_t = sb("qn_t", [128, T], BF16)
        qr_t = sb("qr_t", [64, T], BF16)
        kn_t = sb("kn_t", [128, S], BF16)
        kr_t = sb("kr_t", [64, S], BF16)
        v_t = sb("v_t", [128, 16, 128], BF16)
        sq_t = [sb("sq%d" % i, [128, 512], BF16) for i in range(3)]
        rs_t = [sb("rs%d" % i, [128, 512], F32) for i in range(2)]
        tmp_t = [sb("tmp%d" % i, [128, 512], F32) for i in range(2)]
        st_t = [sb("st%d" % i, [128, 512], BF16) for i in range(4)]
        gt_t = [sb("gt%d" % i, [128, 512], BF16) for i in range(2)]
        p_t = [sb("p%d" % i, [128, 512], BF16) for i in range(4)]
        ME = sb("ME", [128, 3 * EW], BF16)
        mask_t = ME[:, 0:2048]
        band_t = ME[:, 0:EW]
        E_ts = [ME[:, EW:2 * EW], ME[:, 2 * EW:3 * EW]]
        kbase = sb("kbase", [64, T], BF16)
        sqkr = sb("sqkr", [64, T], BF16)
        G = sb("G", [128, NG], F32)
        ones_bf = sb("ones_bf", [128, 128], BF16)
        ones_f = sb("ones_f", [128, 128], F32)
        tri = sb("tri", [128, 128], F32)
        ident = sb("ident", [128, 128], BF16)
        ff = sb("ff", [128, 64], F32)
        prefx = sb("prefx", [128, 72], F32)
        Dtok = sb("Dtok", [128, 64], F32)
        Rsb = sb("Rsb", [128, 72], F32)
        rem_t = sb("rem_t", [128, 64], F32)
        bias_own = sb("bias_own", [128, 2, 8, 8], F32)
        bias_oth = sb("bias_oth", [128, 2, 8, 8], F32)
        ps = [es.enter_context(nc.psum_tensor("pp%d" % i, [128, 512], F32)) for i in range(8)]

        RR = {}

        def R(*key):
            r = RR.get(key)
            if r is None:
                r = RR[key] = Res()
            return r

        def fence(new_keys, old_prefixes):
            toks = []
            for k, r in RR.items():
                if k[0] in old_prefixes:
                    if r.w is not None:
                        toks.append(r.w)
                    toks.extend(r.wd)
                    toks.extend(r.re.values())
                    toks.extend(r.rd)
            for k in new_keys:
                r = R(*k)
                for t in toks:
                    if t[0] == "e":
                        old = r.re.get(t[1])
                        if old is None or old[2] < t[2]:
                            r.re[t[1]] = t
                    else:
                        r.rd.append(t)

        rot = {}

        def mark(name):
            if not fw.dry:
                PHASES.append((name, fw.engs["pe"].n))

        def nxt(name, n):
            i = rot.get(name, 0)
            rot[name] = i + 1
            return i % n

        plan = []
        wstate = {"i": 0, "issued": 0}
        wsrc = {"w_in": w_in, "w_krx": w_krx, "w_uqx": w_uqx, "w_ukv": w_ukv, "w_out": w_out,
                "w_xq": w_xq, "w_xkv": w_xkv, "w_xo": w_xo, "w_1": w_1, "w_2": w_2}

        def issue_slab(j):
            key = plan[j]
            name, li, k0, nk, c0, ncol = key
            if name.startswith("w_br"):
                src = w_br[li, int(name[4])]
            else:
                src = wsrc[name][li]
            slot = wr[j % NSLOT]
            fw.dma("pool", slot[:, 0:nk * ncol].rearrange("p (k c) -> p k c", k=nk),
                   src[k0 * 128:(k0 + nk) * 128, c0:c0 + ncol].rearrange("(k p) c -> p k c", p=128),
                   w=[R("wr", j % NSLOT)])

        def wget(name, li, k0, nk, c0, ncol):
            key = (name, li, k0, nk, c0, ncol)
            i = wstate["i"]
            wstate["i"] = i + 1
            if fw.dry:
                plan.append(key)
            else:
                assert plan[i] == key, (plan[i], key)
                while wstate["issued"] < min(len(plan), i + NSLOT):
                    issue_slab(wstate["issued"])
                    wstate["issued"] += 1
            slot = wr[i % NSLOT]
            return slot[:, 0:nk * ncol].rearrange("p (k c) -> p k c", k=nk), R("wr", i % NSLOT)

        def mm(out, lhsT, rhs, start, stop, r, w):
            fw.op("pe", lambda e: e.matmul(out, lhsT, rhs, start=start, stop=stop), r=r, w=w)

        def rstd_from(bank, npart, ncol, scale, rdeps):
            ri = nxt("rs", 2)
            rs = rs_t[ri]
            fw.op("act", lambda e: e.activation(out=rs[0:npart, 0:ncol], in_=ps[bank][0:npart, 0:ncol],
                                                func=AF.Sqrt, bias=EPS, scale=scale),
                  r=[R("ps", bank)], w=[R("rs", ri)])
            fw.op("dve", lambda e: e.reciprocal(out=rs[0:npart, 0:ncol], in_=rs[0:npart, 0:ncol]),
                  r=[R("rs", ri)], w=[R("rs", ri)])
            return ri

        def rmsnorm_fm(src, src_res, dst, dst_res, nch, ncols, gcol, ntile):
            for tt in range(ntile):
                t0, t1 = tt * 512, min(ncols, tt * 512 + 512)
                n = t1 - t0
                bank = 6 + nxt("aux", 2)
                for c in range(nch):
                    si = nxt("sq", 3)
                    fw.op("act", lambda e, c=c, si=si: e.activation(out=sq_t[si][:, 0:n], in_=src(c, t0, t1),
                                                                  func=AF.Square),
                          r=[src_res(c, tt)], w=[R("sq", si)])
                    mm(ps[bank][:, 0:n], ones_bf[:, :], sq_t[si][:, 0:n], c == 0, c == nch - 1,
                       r=[R("sq", si), R("ones")], w=[R("ps", bank)])
                ri = rstd_from(bank, 128, n, 1.0 / (nch * 128), None)
                for c in range(nch):
                    fw.op("dve", lambda e, c=c: e.scalar_tensor_tensor(
                        out=dst(c, t0, t1), in0=src(c, t0, t1), scalar=G[:, gcol + c:gcol + c + 1],
                        in1=rs_t[ri][:, 0:n], op0=ALU.mult, op1=ALU.mult),
                        r=[src_res(c, tt), R("rs", ri), R("G")], w=[dst_res(c, tt)])

        def head_norm(bank, n, gcol, dst_ap, dst_res, extra=None, div=128.0, sb_dst=None, sb_res=None):
            si = nxt("sq", 3)
            fw.op("act", lambda e: e.activation(out=sq_t[si][:, 0:n], in_=ps[bank][:, 0:n], func=AF.Square),
                  r=[R("ps", bank)], w=[R("sq", si)])
            ab = 6 + nxt("aux", 2)
            mm(ps[ab][:, 0:n], ones_bf[:, :], sq_t[si][:, 0:n], True, extra is None,
               r=[R("sq", si), R("ones")], w=[R("ps", ab)])
            if extra is not None:
                mm(ps[ab][:, 0:n], ones_bf[0:64, :], extra[0], False, True,
                   r=[extra[1], R("ones")], w=[R("ps", ab)])
            ri = rstd_from(ab, 128, n, 1.0 / div, None)
            if sb_dst is not None:
                fw.op("dve", lambda e: e.scalar_tensor_tensor(
                    out=sb_dst, in0=ps[bank][:, 0:n], scalar=G[:, gcol:gcol + 1], in1=rs_t[ri][:, 0:n],
                    op0=ALU.mult, op1=ALU.mult), r=[R("ps", bank), R("rs", ri), R("G")], w=[sb_res])
            else:
                sti = nxt("st", 4)
                fw.op("dve", lambda e: e.scalar_tensor_tensor(
                    out=st_t[sti][:, 0:n], in0=ps[bank][:, 0:n], scalar=G[:, gcol:gcol + 1],
                    in1=rs_t[ri][:, 0:n], op0=ALU.mult, op1=ALU.mult),
                    r=[R("ps", bank), R("rs", ri), R("G")], w=[R("st", sti)])
                fw.dma("sp", dst_ap, st_t[sti][:, 0:n], r=[R("st", sti)], w=[dst_res])
            return ri

        def evac_store(bank, npart, n, dst_ap, dst_res, func=AF.Copy):
            sti = nxt("st", 4)
            fw.op("act", lambda e: e.activation(out=st_t[sti][0:npart, 0:n], in_=ps[bank][0:npart, 0:n], func=func),
                  r=[R("ps", bank)], w=[R("st", sti)])
            fw.dma("sp", dst_ap, st_t[sti][0:npart, 0:n], r=[R("st", sti)], w=[dst_res])

        def proj_fm(slab, slab_res, m0, msz, nk, rhs, rhs_res, tt, bank, ncol=512):
            for kc in range(nk):
                mm(ps[bank][0:msz, 0:ncol], slab[:, kc, m0:m0 + msz], rhs(kc), kc == 0, kc == nk - 1,
                   r=[slab_res, rhs_res(kc, tt)], w=[R("ps", bank)])

        hres = lambda c, tt: R("h", c, tt)
        xres = lambda c, tt: R("x", c, tt)

        def emit_consts():
            fw.op("dve", lambda e: e.memset(ones_bf[:, :], 1.0), w=[R("ones")])
            fw.op("dve", lambda e: e.memset(ones_f[:, :], 1.0), w=[R("onesf")])
            fw.dma("sp", tri[:, :], tri_d, w=[R("tri")])
            fw.dma("sp", ident[:, :], ident_d, w=[R("ident")])

        def attention(br, hf, li):
            scale = (192.0 if br == "m" else 128.0) ** -0.5
            if br == "c":
                kb_lo = max(0, hf * 8 - 4)
            else:
                kb_lo = 0
            kb_hi = hf * 8 + 8
            nkb = kb_hi - kb_lo
            if br == "m":
                fence([("qn", 1), ("qr", 1), ("kn", 1), ("kr", 1), ("vt", 1)], ("big",))
            if br in ("m", "f"):
                fence([("mask",)], ("mask", "band", "E", "ropeC", "ropeS"))
                fw.dma("sp", mask_t, mmask_d if br == "m" else fmask_d, w=[R("mask")])
            else:
                fence([("band",), ("E", 0), ("E", 1)], ("mask", "band", "E", "ropeC", "ropeS"))
                fw.dma("sp", band_t, band_d, w=[R("band")])
            bufs = [dict(qn=qn_t, qr=qr_t, kn=kn_t, kr=kr_t, v=v_t),
                    dict(qn=big[:, 8, :], qr=big[0:64, 9, :],
                         kn=big[:, 10:12, :].rearrange("p a t -> p (a t)"),
                         kr=big[0:64, 12:14, :].rearrange("p a t -> p (a t)"),
                         v=big[:, 14:16, :].rearrange("p a (b d) -> p (a b) d", d=128))]

            def load_head(hd, si):
                bs = bufs[si]
                fw.dma("sp", bs["qn"][:, :], qTn[br][hd], r=[R("qTn", br, hd)], w=[R("qn", si)])
                fw.dma("sp", bs["kn"][:, 0:nkb * 128], kTn[br][hd][:, kb_lo * 128:kb_hi * 128],
                       r=[R("kTn", br, hd)], w=[R("kn", si)])
                if br == "m":
                    fw.dma("sp", bs["qr"][:, :], qTr[hd], r=[R("qTr", hd)], w=[R("qr", si)])
                    fw.dma("sp", bs["kr"][:, 0:nkb * 128], kTr[hd][:, kb_lo * 128:kb_hi * 128],
                           r=[R("kTr", hd)], w=[R("kr", si)])
                fw.dma("sp", bs["v"][:, 0:nkb, :],
                       vv[br][kb_lo * 128:kb_hi * 128, hd * 128:(hd + 1) * 128].rearrange("(b p) d -> p b d", p=128),
                       r=[R("v", br)], w=[R("vt", si)])
                pend = []
                if br == "c":
                    for q0 in range(0, EW, 512):
                        q1 = min(EW, q0 + 512)
                        ti = nxt("tmp", 2)
                        fw.dma("sp", tmp_t[ti][:, 0:q1 - q0], Tzd[li, hd][:, q0:q1], w=[R("tmp", ti)])
                        pend.append((ti, q0, q1))
                return pend

            def post_load(si, pend):
                for ti, q0, q1 in pend:
                    fw.op("dve", lambda e: e.scalar_tensor_tensor(
                        out=E_ts[si][:, q0:q1], in0=tmp_t[ti][:, 0:q1 - q0], scalar=1.0 / scale,
                        in1=band_t[:, q0:q1], op0=ALU.mult, op1=ALU.add),
                        r=[R("tmp", ti), R("band")], w=[R("E", si)])

            pend0 = load_head(0, 0)
            post_load(0, pend0)
            for hd in range(8):
                si = hd % 2
                bs = bufs[si]
                qn_b, qr_b, kn_b, kr_b, v_b = bs["qn"], bs["qr"], bs["kn"], bs["kr"], bs["v"]
                E_t = E_ts[si]
                pend_next = load_head(hd + 1, 1 - si) if hd + 1 < 8 else None
                seq = []
                for j in range(2):
                    qs = hf * 8 + j * 4
                    if br == "c":
                        blocks = list(range(max(0, qs - 4), qs + 4))
                    else:
                        blocks = list(range(0, qs + 4))
                    for bi, kb in enumerate(blocks):
                        seq.append((j, bi, kb, len(blocks), qs))
                sbank_of = {}
                olb = {}

                def emit_qk(t):
                    j, bi, kb, nb, qs = seq[t]
                    kl = kb - kb_lo
                    sbk = nxt("sbank", 6)
                    sbank_of[t] = sbk
                    ksl = slice(kl * 128, kl * 128 + 128)
                    qsl = slice(j * 512, j * 512 + 512)
                    if br == "c":
                        add_ap, add_res = E_t[:, 896 - 128 * (kb - (qs - 4)):896 - 128 * (kb - (qs - 4)) + 512], R("E", si)
                    elif kb >= qs:
                        add_ap, add_res = mask_t[:, (kb - qs) * 512:(kb - qs) * 512 + 512], R("mask")
                    else:
                        add_ap = None
                    mm(ps[sbk][:, :], kn_b[:, ksl], qn_b[:, qsl], True, br != "m" and add_ap is None,
                       r=[R("kn", si), R("qn", si)], w=[R("ps", sbk)])
                    if br == "m":
                        mm(ps[sbk][:, :], kr_b[:, ksl], qr_b[:, qsl], False, add_ap is None,
                           r=[R("kr", si), R("qr", si)], w=[R("ps", sbk)])
                    if add_ap is not None:
                        mm(ps[sbk][:, :], ident[:, :], add_ap, False, True,
                           r=[R("ident"), add_res], w=[R("ps", sbk)])

                def emit_exp(t):
                    j, bi, kb, nb, qs = seq[t]
                    sbk = sbank_of[t]
                    pi = nxt("p", 4)
                    if br == "f":
                        if kb >= hf * 8:
                            bias_ap = bias_own[:, j, kb - hf * 8, hd:hd + 1]
                            bres = R("bias_own")
                        else:
                            bias_ap = bias_oth[:, j, kb, hd:hd + 1]
                            bres = R("bias_oth")
                        fw.op("act", lambda e: e.activation(
                            out=p_t[pi][:, :], in_=ps[sbk][:, :], func=AF.Exp, bias=bias_ap, scale=scale),
                            r=[R("ps", sbk), bres], w=[R("p", pi)])
                    else:
                        fw.op("act", lambda e: e.activation(
                            out=p_t[pi][:, :], in_=ps[sbk][:, :], func=AF.Exp, scale=scale),
                            r=[R("ps", sbk)], w=[R("p", pi)])
                    return pi

                def emit_pv(t, pi):
                    j, bi, kb, nb, qs = seq[t]
                    kl = kb - kb_lo
                    if bi == 0:
                        olb[j] = (6, 7)
                    ob, lb = olb[j]
                    first_b, last_b = bi == 0, bi == nb - 1
                    mm(ps[ob][:, :], v_b[:, kl, :], p_t[pi][:, :], first_b, last_b,
                       r=[R("vt", si), R("p", pi)], w=[R("ps", ob)])
                    mm(ps[lb][:, :], ones_bf[:, :], p_t[pi][:, :], first_b, last_b,
                       r=[R("ones"), R("p", pi)], w=[R("ps", lb)])
                    if last_b:
                        qsl = slice(j * 512, j * 512 + 512)
                        ri = nxt("rs", 2)
                        fw.op("dve", lambda e: e.reciprocal(out=rs_t[ri][:, :], in_=ps[lb][:, :]),
                              r=[R("ps", lb)], w=[R("rs", ri)])
                        fw.op("dve", lambda e: e.tensor_tensor(
                            out=big[:, hd, qsl], in0=ps[ob][:, :], in1=rs_t[ri][:, :], op=ALU.mult),
                            r=[R("ps", ob), R("rs", ri)], w=[R("big", hd, j)])

                LA = 5
                for t in range(min(LA, len(seq))):
                    emit_qk(t)
                for t in range(len(seq)):
                    pi = emit_exp(t)
                    if t + LA < len(seq):
                        emit_qk(t + LA)
                    emit_pv(t, pi)
                    if t == len(seq) // 2 and pend_next is not None:
                        post_load(1 - si, pend_next)
            if br == "c":
                fence([("big", c, tt) for c in range(8, 16) for tt in range(2)], ("qn", "qr", "kn", "kr", "vt"))

        def merge(n, li):
            for sl in range(4):
                slab, sres = wget("w_br%d" % n, li, 0, 8, sl * 512, 512)
                for m in range(4):
                    c = sl * 4 + m
                    for tt in range(2):
                        tsl = slice(tt * 512, tt * 512 + 512)
                        bank = nxt("main", 4)
                        proj_fm(slab, sres, m * 128, 128, 8, lambda kc: big[:, kc, tsl],
                                lambda kc, tt_: R("big", kc, tt_), tt, bank)
                        gi = nxt("gt", 2)
                        fw.dma("sp", gt_t[gi][:, :], gat[n * 16 + c][:, tsl], r=[R("gat", n * 16 + c, tt)],
                               w=[R("gt", gi)])
                        if n == 0:
                            fw.op("dve", lambda e, bank=bank, gi=gi, c=c, tsl=tsl: e.tensor_tensor(
                                out=h[:, c, tsl], in0=ps[bank][:, :], in1=gt_t[gi][:, :], op=ALU.mult),
                                r=[R("ps", bank), R("gt", gi)], w=[R("h", c, tt)])
                        else:
                            ti = nxt("tmp", 2)
                            fw.op("dve", lambda e, bank=bank, gi=gi, ti=ti: e.tensor_tensor(
                                out=tmp_t[ti][:, :], in0=ps[bank][:, :], in1=gt_t[gi][:, :], op=ALU.mult),
                                r=[R("ps", bank), R("gt", gi)], w=[R("tmp", ti)])
                            fw.op("dve", lambda e, ti=ti, c=c, tsl=tsl: e.tensor_tensor(
                                out=h[:, c, tsl], in0=h[:, c, tsl], in1=tmp_t[ti][:, :], op=ALU.add),
                                r=[R("tmp", ti), R("h", c, tt)], w=[R("h", c, tt)])

        def add_to_x(bank, c, tt):
            tsl = slice(tt * 512, tt * 512 + 512)
            fw.op("dve", lambda e: e.tensor_tensor(out=x[:, c, tsl], in0=x[:, c, tsl], in1=ps[bank][:, :],
                                                   op=ALU.add), r=[R("ps", bank), R("x", c, tt)], w=[R("x", c, tt)])

        def emit_pass(li, hf, src_ap, dst_ap, is_final):
            tok0 = hf * T
            mark("norm")
            for g4 in range(4):
                fw.dma("sp", x[:, 4 * g4:4 * g4 + 4, :],
                       src_ap[g4 * 512:g4 * 512 + 512, tok0:tok0 + T].rearrange("(c p) t -> p c t", p=128),
                       r=[R("xs", hf)], w=[R("x", c, tt) for c in range(4 * g4, 4 * g4 + 4) for tt in range(2)])
            fw.dma("sp", G[:, :], Gd[li], w=[R("G")])
            rmsnorm_fm(lambda c, a, b: x[:, c, a:b], xres, lambda c, a, b: h[:, c, a:b], hres, 16, T, G_MIX, 2)

            hr = lambda tt: (lambda kc: h[:, kc, tt * 512:tt * 512 + 512])
            mark("z_mla")
            for sl in range(3):
                slab, sres = wget("w_in", li, 0, 16, sl * 256, 256)
                for m in range(2):
                    zc = sl * 2 + m
                    for tt in range(2):
                        bank = nxt("main", 6)
                        proj_fm(slab, sres, m * 128, 128, 16, hr(tt), hres, tt, bank)
                        fw.op("act", lambda e, bank=bank, zc=zc, tt=tt: e.activation(
                            out=big[:, 8 + zc, tt * 512:tt * 512 + 512], in_=ps[bank][:, :], func=AF.Copy),
                            r=[R("ps", bank)], w=[R("big", 8 + zc, tt)])
            slab, sres = wget("w_krx", li, 0, 16, 0, 128)
            for which in range(2):
                for tt in range(2):
                    bank = nxt("main", 6)
                    proj_fm(slab, sres, which * 64, 64, 16, hr(tt), hres, tt, bank)
                    fw.op("act", lambda e, bank=bank, which=which, tt=tt: e.activation(
                        out=big[0:64, 14 + which, tt * 512:tt * 512 + 512], in_=ps[bank][0:64, :], func=AF.Copy),
                        r=[R("ps", bank)], w=[R("big", 14 + which, tt)])
            mark("z_qkv")
            for br, cbase, gq, gk in (("f", C_FOX, G_FQ, G_FK), ("c", C_CH, G_CQ2, G_CK2)):
                for sl in range(8):
                    slab, sres = wget("w_in", li, 0, 16, cbase + sl * 256, 256)
                    for m in range(2):
                        hidx = sl * 2 + m
                        for tt in range(2):
                            bank = nxt("main", 6)
                            proj_fm(slab, sres, m * 128, 128, 16, hr(tt), hres, tt, bank)
                            if hidx < 8:
                                head_norm(bank, 512, gq, qTn[br][hidx][:, tt * 512:tt * 512 + 512],
                                          R("qTn", br, hidx))
                            else:
                                head_norm(bank, 512, gk,
                                          kTn[br][hidx - 8][:, tok0 + tt * 512:tok0 + tt * 512 + 512],
                                          R("kTn", br, hidx - 8))
                for sl in range(4):
                    slab, sres = wget("w_in", li, 0, 16, cbase + 2048 + sl * 256, 256)
                    for tb in range(8):
                        bank = nxt("main", 6)
                        for kc in range(16):
                            mm(ps[bank][:, 0:256], h[:, kc, tb * 128:tb * 128 + 128], slab[:, kc, :], kc == 0, kc == 15,
                               r=[sres, R("h", kc, tb // 4)], w=[R("ps", bank)])
                        evac_store(bank, 128, 256, vv[br][tok0 + tb * 128:tok0 + tb * 128 + 128, sl * 256:sl * 256 + 256],
                                   R("v", br))
                if br == "f":
                    slab, sres = wget("w_in", li, 0, 16, C_FOXF, 8)
                    fbank = 6 + nxt("aux", 2)
                    for tb in range(8):
                        for kc in range(16):
                            mm(ps[fbank][:, tb * 8:tb * 8 + 8], h[:, kc, tb * 128:tb * 128 + 128], slab[:, kc, :],
                               kc == 0, kc == 15, r=[sres, R("h", kc, tb // 4)], w=[R("ps", fbank)])
                    fw.op("dve", lambda e: e.tensor_tensor(out=ff[:, :], in0=ps[fbank][:, 0:64], in1=G[:, G_BF:G_BF + 64],
                                                           op=ALU.add), r=[R("ps", fbank), R("G")], w=[R("ff")])
                    fw.op("act", lambda e: e.activation(out=ff[:, :], in_=ff[:, :], func=AF.Exp, scale=-1.0),
                          r=[R("ff")], w=[R("ff")])
                    fw.op("act", lambda e: e.activation(out=ff[:, :], in_=ff[:, :], func=AF.Ln, bias=1.0),
                          r=[R("ff")], w=[R("ff")])
                    fw.op("dve", lambda e: e.memset(prefx[:, 0:8], 0.0), w=[R("prefx")])
                    for b in range(1, 9):
                        fw.op("dve", lambda e, b=b: e.tensor_tensor(
                            out=prefx[:, b * 8:b * 8 + 8], in0=prefx[:, b * 8 - 8:b * 8], in1=ff[:, b * 8 - 8:b * 8],
                            op=ALU.add), r=[R("prefx"), R("ff")], w=[R("prefx")])
                    cb = 6 + nxt("aux", 2)
                    mm(ps[cb][:, 0:64], tri[:, :], ff[:, :], True, True, r=[R("tri"), R("ff")], w=[R("ps", cb)])
                    mm(ps[cb][:, 64:136], ones_f[:, :], prefx[:, :], True, True, r=[R("onesf"), R("prefx")],
                       w=[R("ps", cb)])
                    fw.op("act", lambda e: e.activation(out=Rsb[:, :], in_=ps[cb][:, 64:136], func=AF.Copy),
                          r=[R("ps", cb)], w=[R("Rsb")])
                    fw.op("dve", lambda e: e.tensor_tensor(out=Dtok[:, :], in0=ps[cb][:, 0:64], in1=Rsb[:, 0:64],
                                                           op=ALU.add), r=[R("ps", cb), R("Rsb")], w=[R("Dtok")])
                    for j in range(2):
                        rc = (4 * j + 2) * 8
                        for kb in range(8):
                            fw.op("dve", lambda e, j=j, kb=kb, rc=rc: e.tensor_tensor(
                                out=bias_own[:, j, kb, :], in0=Dtok[:, kb * 8:kb * 8 + 8], in1=Rsb[:, rc:rc + 8],
                                op=ALU.subtract), r=[R("Dtok"), R("Rsb")], w=[R("bias_own")])
                    if hf == 0:
                        for kb in range(8):
                            fw.op("dve", lambda e, kb=kb: e.tensor_tensor(
                                out=rem_t[:, kb * 8:kb * 8 + 8], in0=Rsb[:, 64:72], in1=Dtok[:, kb * 8:kb * 8 + 8],
                                op=ALU.subtract), r=[R("Dtok"), R("Rsb")], w=[R("rem_t")])
                        fw.dma("sp", rem0, rem_t[:, :], r=[R("rem_t")], w=[R("rem0")])
                    else:
                        fw.dma("sp", rem_t[:, :], rem0, r=[R("rem0")], w=[R("rem_t")])
                        for j in range(2):
                            rc = (4 * j + 2) * 8
                            for kb in range(8):
                                fw.op("dve", lambda e, j=j, kb=kb, rc=rc: e.scalar_tensor_tensor(
                                    out=bias_oth[:, j, kb, :], in0=rem_t[:, kb * 8:kb * 8 + 8], scalar=-1.0,
                                    in1=Rsb[:, rc:rc + 8], op0=ALU.mult, op1=ALU.subtract),
                                    r=[R("rem_t"), R("Rsb")], w=[R("bias_oth")])
            mark("z_gates")
            for sl in range(24):
                slab, sres = wget("w_in", li, 0, 16, C_GATE + sl * 256, 256)
                for m in range(2):
                    gc = sl * 2 + m
                    for tt in range(2):
                        bank = nxt("main", 6)
                        proj_fm(slab, sres, m * 128, 128, 16, hr(tt), hres, tt, bank)
                        evac_store(bank, 128, 512, gat[gc][:, tt * 512:tt * 512 + 512], R("gat", gc, tt),
                                   func=AF.Sigmoid)

            mark("mla_prep")
            cqs = lambda c, a, b: big[:, 8 + c, a:b]
            cqr = lambda c, tt: R("big", 8 + c, tt)
            rmsnorm_fm(cqs, cqr, cqs, cqr, 4, T, G_CQ, 2)
            cks = lambda c, a, b: big[:, 12 + c, a:b]
            ckr = lambda c, tt: R("big", 12 + c, tt)
            rmsnorm_fm(cks, ckr, cks, ckr, 2, T, G_CKV, 2)
            for tt in range(2):
                tsl = slice(tt * 512, tt * 512 + 512)
                gsl = slice(tok0 + tt * 512, tok0 + tt * 512 + 512)
                t1, t2 = nxt("tmp", 2), nxt("tmp", 2)
                fw.dma("sp", tmp_t[t1][0:64, :], ctab_d[:, gsl], w=[R("tmp", t1)])
                fw.dma("sp", tmp_t[t2][0:64, :], stab_d[:, gsl], w=[R("tmp", t2)])
                fw.op("dve", lambda e, t1=t1, tsl=tsl: e.scalar_tensor_tensor(
                    out=tmp_t[t1][0:64, :], in0=big[0:64, 14, tsl], scalar=G[0:64, G_MKR:G_MKR + 1],
                    in1=tmp_t[t1][0:64, :], op0=ALU.mult, op1=ALU.mult),
                    r=[R("big", 14, tt), R("G"), R("tmp", t1)], w=[R("tmp", t1)])
                fw.op("dve", lambda e, t2=t2, tsl=tsl: e.scalar_tensor_tensor(
                    out=tmp_t[t2][0:64, :], in0=big[0:64, 15, tsl], scalar=G[0:64, G_MKS:G_MKS + 1],
                    in1=tmp_t[t2][0:64, :], op0=ALU.mult, op1=ALU.mult),
                    r=[R("big", 15, tt), R("G"), R("tmp", t2)], w=[R("tmp", t2)])
                fw.op("dve", lambda e, t1=t1, t2=t2, tsl=tsl: e.tensor_tensor(
                    out=kbase[:, tsl], in0=tmp_t[t1][0:64, :], in1=tmp_t[t2][0:64, :], op=ALU.add),
                    r=[R("tmp", t1), R("tmp", t2)], w=[R("kbase", tt)])
                fw.op("act", lambda e, tsl=tsl: e.activation(out=sqkr[:, tsl], in_=big[0:64, 14, tsl], func=AF.Square),
                      r=[R("big", 14, tt)], w=[R("sqkr", tt)])
            fence([("ropeC",), ("ropeS",)], ("mask", "band", "E"))
            ropeC, ropeS = ME[0:64, 0:T], ME[0:64, T:2 * T]
            fw.dma("pool", ropeC, ctab_d[:, tok0:tok0 + T], w=[R("ropeC")])
            fw.dma("pool", ropeS, stab_d[:, tok0:tok0 + T], w=[R("ropeS")])
            fw.op("dve", lambda e: e.tensor_scalar(out=ropeC, in0=ropeC, scalar1=G[0:64, G_MQR:G_MQR + 1], scalar2=None,
                                                   op0=ALU.mult), r=[R("ropeC"), R("G")], w=[R("ropeC")])
            fw.op("dve", lambda e: e.tensor_scalar(out=ropeS, in0=ropeS, scalar1=G[0:64, G_MQS:G_MQS + 1], scalar2=None,
                                                   op0=ALU.mult), r=[R("ropeS"), R("G")], w=[R("ropeS")])
            qit = [(sl, hh, tt) for sl in range(2) for hh in range(4) for tt in range(2)]
            qslab = {}
            qbanks = {}

            def q_s1(i):
                sl, hh, tt = qit[i]
                if sl not in qslab:
                    qslab[sl] = wget("w_uqx", li, 0, 4, sl * 1024, 1024)
                slab, sres = qslab[sl]
                tsl = slice(tt * 512, tt * 512 + 512)
                rhs = lambda kc: big[:, 8 + kc, tsl]
                bA, b1, b2 = nxt("main", 6), nxt("main", 6), nxt("main", 6)
                proj_fm(slab, sres, hh * 256, 128, 4, rhs, cqr, tt, bA)
                proj_fm(slab, sres, hh * 256 + 128, 64, 4, rhs, cqr, tt, b1)
                proj_fm(slab, sres, hh * 256 + 192, 64, 4, rhs, cqr, tt, b2)
                qbanks[i] = (bA, b1, b2)

            def q_s2(i):
                sl, hh, tt = qit[i]
                hd = sl * 4 + hh
                bA, b1, b2 = qbanks[i]
                tsl = slice(tt * 512, tt * 512 + 512)
                gsl = slice(tok0 + tt * 512, tok0 + tt * 512 + 512)
                si = nxt("sq", 3)
                fw.op("act", lambda e: e.activation(out=sq_t[si][0:64, :], in_=ps[b1][0:64, :], func=AF.Square),
                      r=[R("ps", b1)], w=[R("sq", si)])
                t1, t2 = nxt("tmp", 2), nxt("tmp", 2)
                fw.op("dve", lambda e: e.tensor_tensor(
                    out=tmp_t[t1][0:64, :], in0=ps[b1][0:64, :], in1=ropeC[:, tsl], op=ALU.mult),
                    r=[R("ps", b1), R("ropeC")], w=[R("tmp", t1)])
                fw.op("dve", lambda e: e.tensor_tensor(
                    out=tmp_t[t2][0:64, :], in0=ps[b2][0:64, :], in1=ropeS[:, tsl], op=ALU.mult),
                    r=[R("ps", b2), R("ropeS")], w=[R("tmp", t2)])
                fw.op("dve", lambda e: e.tensor_tensor(
                    out=tmp_t[t1][0:64, :], in0=tmp_t[t1][0:64, :], in1=tmp_t[t2][0:64, :], op=ALU.add),
                    r=[R("tmp", t1), R("tmp", t2)], w=[R("tmp", t1)])
                ri = head_norm(bA, 512, G_MQN, qTn["m"][hd][:, tsl], R("qTn", "m", hd),
                               extra=(sq_t[si][0:64, :], R("sq", si)), div=192.0)
                sti = nxt("st", 4)
                fw.op("dve", lambda e: e.tensor_tensor(
                    out=st_t[sti][0:64, :], in0=tmp_t[t1][0:64, :], in1=rs_t[ri][0:64, :], op=ALU.mult),
                    r=[R("tmp", t1), R("rs", ri)], w=[R("st", sti)])
                fw.dma("sp", qTr[hd][:, tsl], st_t[sti][0:64, :], r=[R("st", sti)], w=[R("qTr", hd)])

            q_s1(0)
            for i in range(len(qit)):
                if i + 1 < len(qit):
                    q_s1(i + 1)
                q_s2(i)
            slab, sres = wget("w_ukv", li, 0, 2, 0, 2048)
            kit = [(hd, tt) for hd in range(8) for tt in range(2)]
            kbank = {}

            def k_s1(i):
                hd, tt = kit[i]
                tsl = slice(tt * 512, tt * 512 + 512)
                bA = nxt("main", 6)
                proj_fm(slab, sres, hd * 256, 128, 2, lambda kc: big[:, 12 + kc, tsl], ckr, tt, bA)
                kbank[i] = bA

            def k_s2(i):
                hd, tt = kit[i]
                bA = kbank[i]
                tsl = slice(tt * 512, tt * 512 + 512)
                gsl = slice(tok0 + tt * 512, tok0 + tt * 512 + 512)
                ri = head_norm(bA, 512, G_MKN, kTn["m"][hd][:, gsl], R("kTn", "m", hd),
                               extra=(sqkr[:, tsl], R("sqkr", tt)), div=192.0)
                sti = nxt("st", 4)
                fw.op("dve", lambda e: e.tensor_tensor(
                    out=st_t[sti][0:64, :], in0=kbase[:, tsl], in1=rs_t[ri][0:64, :], op=ALU.mult),
                    r=[R("kbase", tt), R("rs", ri)], w=[R("st", sti)])
                fw.dma("sp", kTr[hd][:, gsl], st_t[sti][0:64, :], r=[R("st", sti)], w=[R("kTr", hd)])

            k_s1(0)
            k_s1(1)
            for i in range(len(kit)):
                if i + 2 < len(kit):
                    k_s1(i + 2)
                k_s2(i)
            for tb in range(8):
                for hg in range(2):
                    bank = nxt("main", 6)
                    for hh in range(4):
                        hd = hg * 4 + hh
                        for kc in range(2):
                            mm(ps[bank][:, hh * 128:hh * 128 + 128], big[:, 12 + kc, tb * 128:tb * 128 + 128],
                               slab[:, kc, hd * 256 + 128:hd * 256 + 256], kc == 0, kc == 1,
                               r=[sres, R("big", 12 + kc, tb // 4)], w=[R("ps", bank)])
                    evac_store(bank, 128, 512, vv["m"][tok0 + tb * 128:tok0 + tb * 128 + 128, hg * 512:hg * 512 + 512],
                               R("v", "m"))

            for n, br in enumerate(("m", "f", "c")):
                mark("att_" + br)
                attention(br, hf, li)
                mark("merge_" + br)
                merge(n, li)
            mark("w_out")
            for sl in range(8):
                slab, sres = wget("w_out", li, 0, 16, sl * 256, 256)
                for m in range(2):
                    c = sl * 2 + m
                    for tt in range(2):
                        bank = nxt("main", 6)
                        proj_fm(slab, sres, m * 128, 128, 16, hr(tt), hres, tt, bank)
                        add_to_x(bank, c, tt)

            mark("cross")
            rmsnorm_fm(lambda c, a, b: x[:, c, a:b], xres, lambda c, a, b: h[:, c, a:b], hres, 16, T, G_CROSS, 2)
            if hf == 0:
                fw.dma("pool", big[:, :, 0:256], memT.rearrange("(c p) m -> p c m", p=128),
                       w=[R("big", c, 0) for c in range(16)])
                mems = lambda c, a, b: big[:, c, a:b]
                memr = lambda c, tt: R("big", c, 0)
                rmsnorm_fm(mems, memr, mems, memr, 16, 256, G_MEM, 1)
                for sl in range(4):
                    slab, sres = wget("w_xkv", li, 0, 16, sl * 256, 256)
                    if sl < 2:
                        for m in range(2):
                            hd = sl * 2 + m
                            bank = nxt("main", 6)
                            proj_fm(slab, sres, m * 128, 128, 16, lambda kc: big[:, kc, 0:256], memr, 0, bank, ncol=256)
                            head_norm(bank, 256, G_XK, None, None, sb_dst=big[:, hd, 256:512], sb_res=R("xk", hd))
                    else:
                        for mb in range(2):
                            bank = nxt("main", 6)
                            for kc in range(16):
                                mm(ps[bank][:, 0:256], big[:, kc, mb * 128:mb * 128 + 128], slab[:, kc, :], kc == 0, kc == 15,
                                   r=[sres, R("big", kc, 0)], w=[R("ps", bank)])
                            c0 = 512 + (sl - 2) * 256
                            fw.op("act", lambda e, bank=bank, mb=mb, c0=c0: e.activation(
                                out=big[:, mb, c0:c0 + 256], in_=ps[bank][:, 0:256], func=AF.Copy),
                                r=[R("ps", bank)], w=[R("xv", mb, sl - 2)])
                fw.dma("sp", kx_d, big[:, 0:4, 256:512], r=[R("xk", hd_) for hd_ in range(4)], w=[R("kxd")])
                fw.dma("sp", vx_d, big[:, 0:2, 512:1024],
                       r=[R("xv", mb_, s_) for mb_ in range(2) for s_ in range(2)], w=[R("vxd")])
            else:
                fw.dma("sp", big[:, 0:4, 256:512], kx_d, r=[R("kxd")],
                       w=[R("xk", hd_) for hd_ in range(4)] + [R("big", c_, 0) for c_ in range(4)])
                fw.dma("sp", big[:, 0:2, 512:1024], vx_d, r=[R("vxd")],
                       w=[R("xv", mb_, s_) for mb_ in range(2) for s_ in range(2)] + [R("big", c_, 1) for c_ in range(2)])
            for sl in range(2):
                slab, sres = wget("w_xq", li, 0, 16, sl * 256, 256)
                for m in range(2):
                    hd = sl * 2 + m
                    for tt in range(2):
                        bank = nxt("main", 6)
                        proj_fm(slab, sres, m * 128, 128, 16, hr(tt), hres, tt, bank)
                        head_norm(bank, 512, G_XQ, None, None, sb_dst=big[:, 4 + hd, tt * 512:tt * 512 + 512],
                                  sb_res=R("xq", hd, tt))
            xscale = 128.0 ** -0.5
            xseq = [(hd, j, mb) for hd in range(4) for j in range(2) for mb in range(2)]
            xsb = {}
            xol = {}

            def x_qk(t):
                hd, j, mb = xseq[t]
                sbk = nxt("sbank", 4)
                xsb[t] = sbk
                mm(ps[sbk][:, :], big[:, hd, 256 + mb * 128:256 + mb * 128 + 128], big[:, 4 + hd, j * 512:j * 512 + 512],
                   True, True, r=[R("xk", hd), R("xq", hd, j)], w=[R("ps", sbk)])

            def x_rest(t):
                hd, j, mb = xseq[t]
                sbk = xsb[t]
                pi = nxt("p", 4)
                fw.op("act", lambda e: e.activation(out=p_t[pi][:, :], in_=ps[sbk][:, :], func=AF.Exp, scale=xscale),
                      r=[R("ps", sbk)], w=[R("p", pi)])
                if t + 2 < len(xseq):
                    x_qk(t + 2)
                if mb == 0:
                    oi = nxt("ol", 2)
                    xol[(hd, j)] = (4 + 2 * oi, 5 + 2 * oi)
                ob, lb = xol[(hd, j)]
                mm(ps[ob][:, :], big[:, mb, 512 + hd * 128:512 + hd * 128 + 128], p_t[pi][:, :], mb == 0, mb == 1,
                   r=[R("xv", mb, hd // 2), R("p", pi)], w=[R("ps", ob)])
                mm(ps[lb][:, :], ones_bf[:, :], p_t[pi][:, :], mb == 0, mb == 1,
                   r=[R("ones"), R("p", pi)], w=[R("ps", lb)])
                if mb == 1:
                    qsl = slice(j * 512, j * 512 + 512)
                    ri = nxt("rs", 2)
                    fw.op("dve", lambda e: e.reciprocal(out=rs_t[ri][:, :], in_=ps[lb][:, :]),
                          r=[R("ps", lb)], w=[R("rs", ri)])
                    fw.op("dve", lambda e: e.tensor_tensor(
                        out=big[:, 8 + hd, qsl], in0=ps[ob][:, :], in1=rs_t[ri][:, :], op=ALU.mult),
                        r=[R("ps", ob), R("rs", ri)], w=[R("xo", hd, j)])

            x_qk(0)
            x_qk(1)
            for t in range(len(xseq)):
                x_rest(t)
            for sl in range(4):
                slab, sres = wget("w_xo", li, 0, 4, sl * 512, 512)
                for m in range(4):
                    c = sl * 4 + m
                    for tt in range(2):
                        tsl = slice(tt * 512, tt * 512 + 512)
                        bank = nxt("main", 4)
                        proj_fm(slab, sres, m * 128, 128, 4, lambda kc: big[:, 8 + kc, tsl],
                                lambda kc, tt_: R("xo", kc, tt_), tt, bank)
                        add_to_x(bank, c, tt)

            mark("mlp")
            rmsnorm_fm(lambda c, a, b: x[:, c, a:b], xres, lambda c, a, b: h[:, c, a:b], hres, 16, T, G_MLP, 2)
            fence([("big", c, tt) for c in range(16) for tt in range(2)],
                  ("big", "xk", "xv", "xq", "xo"))
            for fs in range(4):
                for sl in range(8):
                    slab, sres = wget("w_1", li, 0, 16, fs * 2048 + sl * 256, 256)
                    for m in range(2):
                        uc = sl * 2 + m
                        for tt in range(2):
                            tsl = slice(tt * 512, tt * 512 + 512)
                            bank = nxt("main", 6)
                            proj_fm(slab, sres, m * 128, 128, 16, hr(tt), hres, tt, bank)
                            ti = nxt("tmp", 2)
                            fw.op("act", lambda e, ti=ti, bank=bank: e.activation(out=tmp_t[ti][:, :], in_=ps[bank][:, :],
                                                                                func=AF.Relu),
                                  r=[R("ps", bank)], w=[R("tmp", ti)])
                            fw.op("dve", lambda e, ti=ti, uc=uc, tsl=tsl: e.tensor_tensor(
                                out=big[:, uc, tsl], in0=tmp_t[ti][:, :], in1=tmp_t[ti][:, :], op=ALU.mult),
                                r=[R("tmp", ti)], w=[R("big", uc, tt)])
                for sl in range(8):
                    slab, sres = wget("w_2", li, fs * 16, 16, sl * 256, 256)
                    for m in range(2):
                        c = sl * 2 + m
                        for tt in range(2):
                            tsl = slice(tt * 512, tt * 512 + 512)
                            bank = nxt("main", 6)
                            proj_fm(slab, sres, m * 128, 128, 16, lambda kc: big[:, kc, tsl],
                                    lambda kc, tt_: R("big", kc, tt_), tt, bank)
                            add_to_x(bank, c, tt)
            fw.dma("sp", dst_ap[:, tok0:tok0 + T].rearrange("(c p) t -> p c t", p=128), x[:, :, :],
                   r=[R("x", c, tt) for c in range(16) for tt in range(2)], w=[R("xs", hf)], is_out=is_final)

        def emit_all():
            rot.clear()
            wstate["i"] = 0
            emit_consts()
            for li in range(nl):
                for hf in range(2):
                    src = xT_in if li == 0 else xs
                    fin = li == nl - 1
                    emit_pass(li, hf, src, oT if fin else xs, fin)
            fw.finish()

        fw.dry = True
        emit_all()
        fw.dry = False
        emit_all()
    return nc


def _host_consts():
    bf = ml_dtypes.bfloat16
    s = np.arange(128)[:, None]
    t = np.arange(512)[None, :]
    NEGM = np.float32(-30000.0)
    fm = np.where(np.concatenate([((o * 128 + s) <= t) for o in range(4)], axis=1), np.float32(0), NEGM).astype(bf)
    mm_ = np.where(np.concatenate([(((o * 128 + s) // 64) <= (t // 64)) for o in range(4)], axis=1),
                   np.float32(0), NEGM).astype(bf)
    u = np.arange(EW)[None, :]
    qc = (u - 384) // 64
    kc = s // 64
    band = np.where((kc >= qc - 8) & (kc <= qc), np.float32(0), NEGM).astype(bf)
    tri = (s <= np.arange(128)[None, :]).astype(np.float32)
    pos = np.arange(S, dtype=np.float32)
    inv = (10000.0 ** (-np.arange(0, 64, 2, dtype=np.float32) / 64)).astype(np.float32)
    ang = pos[:, None] * inv[None, :]
    cos, sin = np.cos(ang).astype(np.float32).T, np.sin(ang).astype(np.float32).T
    ctab = np.ascontiguousarray(np.concatenate([cos, cos], 0))
    stab = np.ascontiguousarray(np.concatenate([-sin, sin], 0))
    return dict(fmask=np.ascontiguousarray(fm), mmask=np.ascontiguousarray(mm_), band=np.ascontiguousarray(band),
                tri=tri, ctab=ctab, stab=stab, ident=np.eye(128, dtype=np.float32).astype(bf))


def _host_layer_tables(inp, ls):
    L = len(ls)
    G = np.zeros((L, 128, NG), np.float32)
    sw = np.concatenate([np.arange(32, 64), np.arange(0, 32)])
    for i, l in enumerate(ls):
        def fm(v, n):
            return np.asarray(v).reshape(n, 128).T
        G[i, :, G_MIX:G_MIX + 16] = fm(inp["g_mix"][l], 16)
        G[i, :, G_CROSS:G_CROSS + 16] = fm(inp["g_cross"][l], 16)
        G[i, :, G_MLP:G_MLP + 16] = fm(inp["g_mlp"][l], 16)
        G[i, :, G_MEM:G_MEM + 16] = fm(inp["g_mem"][l], 16)
        G[i, :, G_CQ:G_CQ + 4] = fm(inp["g_cq"][l], 4)
        G[i, :, G_CKV:G_CKV + 2] = fm(inp["g_ckv"][l], 2)
        gq, gk = np.asarray(inp["g_mla_q"][l]), np.asarray(inp["g_mla_k"][l])
        G[i, :, G_MQN] = gq[:128]
        G[i, :64, G_MQR] = gq[128:]
        G[i, :64, G_MQS] = gq[128:][sw]
        G[i, :, G_MKN] = gk[:128]
        G[i, :64, G_MKR] = gk[128:]
        G[i, :64, G_MKS] = gk[128:][sw]
        for col, nm in ((G_FQ, "g_fox_q"), (G_FK, "g_fox_k"), (G_CQ2, "g_ch_q"), (G_CK2, "g_ch_k"),
                        (G_XQ, "g_x_q"), (G_XK, "g_x_k")):
            G[i, :, col] = np.asarray(inp[nm][l])
        G[i, :, G_BF:G_BF + 64] = np.tile(np.asarray(inp["b_f"][l])[None, :], (128, 8))
    sidx = np.arange(128)[:, None]
    uidx = np.arange(EW)[None, :]
    idx = np.clip(uidx - 384 - sidx, -128, 128) + 128
    Tz = np.ascontiguousarray(np.stack([np.asarray(inp["rel_bias"][l])[:, idx] for l in ls], 0)).astype(np.float32)
    w_uq = np.stack([np.asarray(inp["w_uq"][l]) for l in ls], 0)
    cols = []
    for hd in range(8):
        b = hd * 192
        cols += list(range(b, b + 128)) + list(range(b + 128, b + 192)) + list(b + 128 + sw)
    w_uqx = np.ascontiguousarray(w_uq[:, :, cols])
    krc = 768 + np.concatenate([np.arange(64), sw])
    w_krx = np.ascontiguousarray(np.stack([np.asarray(inp["w_in"][l][:, krc]) for l in ls], 0))
    return G, Tz, w_uqx, w_krx


_CACHE = {}
PHASES = []


def _get_program(nl, first, last):
    key = (nl, first, last)
    if key not in _CACHE:
        _CACHE[key] = build_program(nl, first, last)
    return _CACHE[key]


def _run_layers(inp, xT_list, ls, consts):
    nl = len(ls)
    nc = _get_program(nl, True, True)
    G, Tz, w_uqx, w_krx = _host_layer_tables(inp, ls)
    sel = (lambda a: np.ascontiguousarray(np.asarray(a)[ls[0]:ls[-1] + 1]))
    shared = dict(w_in=sel(inp["w_in"]), w_krx=w_krx, w_uqx=w_uqx, w_ukv=sel(inp["w_ukv"]), w_br=sel(inp["w_br"]),
                  w_out=sel(inp["w_out"]), w_xq=sel(inp["w_xq"]), w_xkv=sel(inp["w_xkv"]), w_xo=sel(inp["w_xo"]),
                  w_1=sel(inp["w_1"]), w_2=sel(inp["w_2"]), G=G, Tz=Tz, **consts)
    in_maps = []
    mem = np.asarray(inp["mem"])
    zero = None
    for c in range(N_LAUNCH_CORES):
        if c in CORE_OF_BATCH:
            bidx = CORE_OF_BATCH.index(c)
            m = dict(shared)
            m["xT"] = xT_list[bidx]
            m["memT"] = np.ascontiguousarray(mem[bidx].T)
        else:
            if zero is None:
                zero = {k: np.zeros_like(v) for k, v in shared.items()}
                zero["xT"] = np.zeros_like(xT_list[0])
                zero["memT"] = np.zeros((D, 256), np.float32)
            m = zero
        in_maps.append(m)
    res = run_bass_kernel_spmd(nc, in_maps, core_ids=list(range(N_LAUNCH_CORES)))
    return [np.asarray(res.results[c]["oT"]) for c in CORE_OF_BATCH]


N_LAUNCH_CORES = 8
CORE_OF_BATCH = [0, 1, 4, 5]
LAUNCH_GROUPS = [[0, 1, 2, 3]]


def kernel(**inp):
    x = np.asarray(inp["x"])
    consts = _host_consts()
    xT = [np.ascontiguousarray(x[b].T) for b in range(NCORES)]
    for ls in LAUNCH_GROUPS:
        xT = _run_layers(inp, xT, ls, consts)
    out = np.stack([xT[b].T for b in range(4)], 0).astype(np.float32)
    return np.ascontiguousarray(out)
```
